# Optimizing a Trainium2 kernel written in Bass

```python
import math
import jax, jax.numpy as jnp
from jax import lax
import numpy as np

D_MODEL = 1024
BATCH = 2
SEQ = 8192
DEPTH = 4

RMS_EPS = 1e-6
Q_BLOCK = 128

MLA_HEADS = 4
MLA_NOPE = 64
MLA_ROPE = 32
MLA_V = 64
MLA_Q_LORA = 256
MLA_KV_LORA = 128
MLA_WIDTH = MLA_HEADS * MLA_V
ROPE_THETA = 10000.0

GDN_HEADS = 4
GDN_DK = 128
GDN_DV = 128
GDN_CONV = 4
GDN_CHUNK = 64
GDN_WIDTH = GDN_HEADS * GDN_DV

MOBA_HEADS = 4
MOBA_DH = 64
MOBA_BLOCK = 256
MOBA_TOPK = 3
MOBA_WIDTH = MOBA_HEADS * MOBA_DH

T5_BUCKETS = 32
T5_MAX_DIST = 2048

MIX_WIDTH = MLA_WIDTH + GDN_WIDTH + MOBA_WIDTH

IN_SIZES = (
    MLA_Q_LORA, MLA_KV_LORA, MLA_ROPE, MLA_WIDTH,
    GDN_HEADS * GDN_DK, GDN_HEADS * GDN_DK, GDN_WIDTH, GDN_WIDTH,
    GDN_HEADS, GDN_HEADS,
    MOBA_WIDTH, MOBA_WIDTH, MOBA_WIDTH, MOBA_WIDTH,
)
IN_COLS = sum(IN_SIZES)

kernel_name = "hybrid_mla_gdn_moba_parallel_heads"


def rmsnorm(x, w):
    xf = x.astype(jnp.float32)
    y = xf * lax.rsqrt(jnp.mean(xf * xf, axis=-1, keepdims=True) + RMS_EPS)
    return (y * w.astype(jnp.float32)).astype(x.dtype)


def l2norm(x):
    xf = x.astype(jnp.float32)
    return xf * lax.rsqrt(jnp.sum(xf * xf, axis=-1, keepdims=True) + 1e-6)


def rope_tables(seq, dim):
    inv = 1.0 / (ROPE_THETA ** (jnp.arange(0, dim, 2, dtype=jnp.float32) / dim))
    ang = jnp.arange(seq, dtype=jnp.float32)[:, None] * inv[None, :]
    return jnp.cos(ang), jnp.sin(ang)


def apply_rope(x, cos, sin):
    half = x.shape[-1] // 2
    xf = x.astype(jnp.float32)
    x1, x2 = xf[..., :half], xf[..., half:]
    c, s = cos[None, :, None, :], sin[None, :, None, :]
    return jnp.concatenate([x1 * c - x2 * s, x1 * s + x2 * c], axis=-1).astype(x.dtype)


def t5_bucket(rel):
    n = jnp.maximum(rel, 0)
    max_exact = T5_BUCKETS // 2
    nf = jnp.maximum(n, 1).astype(jnp.float32)
    large = max_exact + (jnp.log(nf / max_exact) / math.log(T5_MAX_DIST / max_exact)
                         * (T5_BUCKETS - max_exact)).astype(jnp.int32)
    large = jnp.minimum(large, T5_BUCKETS - 1)
    return jnp.where(n < max_exact, n, large)


def causal_block_attention(q, k, v, scale):
    B, S, H, Dk = q.shape
    Dv = v.shape[-1]
    nqb = S // Q_BLOCK
    qb = q.reshape(B, nqb, Q_BLOCK, H, Dk).transpose(1, 0, 2, 3, 4)
    kpos = jnp.arange(S)

    def one(args):
        i, qi = args
        logits = jnp.einsum('bqhd,bkhd->bhqk', qi, k, preferred_element_type=jnp.float32) * scale
        qpos = i * Q_BLOCK + jnp.arange(Q_BLOCK)
        logits = jnp.where(kpos[None, :] <= qpos[:, None], logits, -jnp.inf)
        p = jax.nn.softmax(logits, axis=-1).astype(v.dtype)
        return jnp.einsum('bhqk,bkhd->bqhd', p, v)

    out = lax.map(one, (jnp.arange(nqb), qb))
    return out.transpose(1, 0, 2, 3, 4).reshape(B, S, H, Dv)


def mla_mixer(c_q, c_kv, k_rope, q_norm_w, w_uq, kv_norm_w, w_ukv, cos, sin):
    B, S, _ = c_q.shape
    q = (rmsnorm(c_q, q_norm_w) @ w_uq).reshape(B, S, MLA_HEADS, MLA_NOPE + MLA_ROPE)
    q_nope, q_pe = q[..., :MLA_NOPE], q[..., MLA_NOPE:]
    q_pe = apply_rope(q_pe, cos, sin)
    kv = (rmsnorm(c_kv, kv_norm_w) @ w_ukv).reshape(B, S, MLA_HEADS, MLA_NOPE + MLA_V)
    k_nope, v = kv[..., :MLA_NOPE], kv[..., MLA_NOPE:]
    k_pe = apply_rope(k_rope[:, :, None, :], cos, sin)
    k = jnp.concatenate([k_nope, jnp.broadcast_to(k_pe, (B, S, MLA_HEADS, MLA_ROPE))], axis=-1)
    qh = jnp.concatenate([q_nope, q_pe], axis=-1)
    o = causal_block_attention(qh, k, v, (MLA_NOPE + MLA_ROPE) ** -0.5)
    return o.reshape(B, S, MLA_WIDTH)


def causal_conv(x, w):
    K = w.shape[0]
    S = x.shape[1]
    xp = jnp.pad(x, ((0, 0), (K - 1, 0), (0, 0)))
    return sum(xp[:, j:j + S] * w[j] for j in range(K))


def chunk_gated_delta_rule(q, k, v, beta, g):
    B, S, H, DK = q.shape
    DV = v.shape[-1]
    C = GDN_CHUNK
    N = S // C

    def chunks(t):
        return t.reshape(B, N, C, H, t.shape[-1]).transpose(0, 3, 1, 2, 4)

    qc, kc, vc = chunks(q), chunks(k), chunks(v)
    bc = beta.reshape(B, N, C, H).transpose(0, 3, 1, 2)
    gc = jnp.cumsum(g.reshape(B, N, C, H).transpose(0, 3, 1, 2), axis=-1)

    idx = jnp.arange(C)
    strict = idx[:, None] > idx[None, :]
    incl = idx[:, None] >= idx[None, :]
    diff = gc[..., :, None] - gc[..., None, :]
    decay = jnp.exp(jnp.where(incl, diff, -jnp.inf))

    kk = jnp.einsum('bhntd,bhnjd->bhntj', kc, kc)
    L = jnp.where(strict, bc[..., :, None] * kk * decay, 0.0)
    rhs = jnp.concatenate([kc * (bc * jnp.exp(gc))[..., None], vc * bc[..., None]], axis=-1)
    sol = lax.linalg.triangular_solve(L + jnp.eye(C, dtype=L.dtype), rhs,
                                      left_side=True, lower=True, unit_diagonal=True)
    W, U0 = sol[..., :DK], sol[..., DK:]
    A_qk = jnp.einsum('bhntd,bhnjd->bhntj', qc, kc) * decay
    q_g = qc * jnp.exp(gc)[..., None]
    k_end = kc * jnp.exp(gc[..., -1:] - gc)[..., None]
    g_end = jnp.exp(gc[..., -1])

    def step(state, xs):
        W_n, U0_n, A_n, qg_n, ke_n, ge_n = xs
        U = U0_n - jnp.einsum('bhcd,bhde->bhce', W_n, state)
        o = jnp.einsum('bhcd,bhde->bhce', qg_n, state) + jnp.einsum('bhtj,bhje->bhte', A_n, U)
        new_state = ge_n[..., None, None] * state + jnp.einsum('bhcd,bhce->bhde', ke_n, U)
        return new_state, o

    xs = tuple(jnp.moveaxis(t, 2, 0) for t in (W, U0, A_qk, q_g, k_end, g_end))
    s0 = jnp.zeros((B, H, DK, DV), jnp.float32)
    _, o = lax.scan(step, s0, xs)
    return o.transpose(1, 0, 3, 2, 4).reshape(B, S, H, DV)


def gdn_mixer(q, k, v, z, b, a, conv_w, A_log, dt_bias, norm_w):
    B, S, _ = q.shape
    qkv = jax.nn.silu(causal_conv(jnp.concatenate([q, k, v], axis=-1), conv_w))
    nk = GDN_HEADS * GDN_DK
    q, k, v = qkv[..., :nk], qkv[..., nk:2 * nk], qkv[..., 2 * nk:]
    q = l2norm(q.reshape(B, S, GDN_HEADS, GDN_DK)) * (GDN_DK ** -0.5)
    k = l2norm(k.reshape(B, S, GDN_HEADS, GDN_DK))
    v = v.reshape(B, S, GDN_HEADS, GDN_DV).astype(jnp.float32)
    beta = jax.nn.sigmoid(b.astype(jnp.float32))
    g = -jnp.exp(A_log.astype(jnp.float32)) * jax.nn.softplus(
        a.astype(jnp.float32) + dt_bias.astype(jnp.float32))
    o = chunk_gated_delta_rule(q, k, v, beta, g).astype(z.dtype)
    o = rmsnorm(o, norm_w) * jax.nn.silu(z.reshape(B, S, GDN_HEADS, GDN_DV))
    return o.reshape(B, S, GDN_WIDTH)


def moba_mixer(q, k, v, t5_table):
    B, S, _ = q.shape
    H, Dh, BS = MOBA_HEADS, MOBA_DH, MOBA_BLOCK
    q = q.reshape(B, S, H, Dh).transpose(0, 2, 1, 3)
    k = k.reshape(B, S, H, Dh).transpose(0, 2, 1, 3)
    v = v.reshape(B, S, H, Dh).transpose(0, 2, 1, 3)
    NB = -(-S // BS)
    pad = NB * BS - S
    kp = jnp.pad(k, ((0, 0), (0, 0), (0, pad), (0, 0)))
    vp = jnp.pad(v, ((0, 0), (0, 0), (0, pad), (0, 0)))
    kblk = kp.reshape(B, H, NB, BS, Dh)
    vblk = vp.reshape(B, H, NB, BS, Dh)
    kmean = jnp.mean(kblk.astype(jnp.float32), axis=3).astype(k.dtype)
    topk = min(MOBA_TOPK, NB)
    scale = Dh ** -0.5
    nqb = S // Q_BLOCK
    qb = jnp.moveaxis(q.reshape(B, H, nqb, Q_BLOCK, Dh), 2, 0)
    bi = jnp.arange(B)[:, None, None, None]
    hi = jnp.arange(H)[None, :, None, None]
    hi5 = jnp.arange(H)[None, :, None, None, None]
    blk_iota = jnp.arange(BS)

    def one(args):
        i, qi = args
        qpos = i * Q_BLOCK + jnp.arange(Q_BLOCK)
        own = (i * Q_BLOCK) // BS
        gate = jnp.einsum('bhqd,bhnd->bhqn', qi, kmean, preferred_element_type=jnp.float32)
        gate = jnp.where(jnp.arange(NB) < own, gate, -jnp.inf)
        _, sel = lax.top_k(gate, topk)
        valid = sel < own
        ksel = kblk[bi, hi, sel]
        vsel = vblk[bi, hi, sel]
        kpos_sel = sel[..., None] * BS + blk_iota
        bias_sel = t5_table[t5_bucket(qpos[None, None, :, None, None] - kpos_sel), hi5]
        l_sel = jnp.einsum('bhqd,bhqnkd->bhqnk', qi, ksel, preferred_element_type=jnp.float32) * scale
        l_sel = jnp.where(valid[..., None], l_sel + bias_sel.astype(jnp.float32), -jnp.inf)
        kown = lax.dynamic_slice_in_dim(kp, own * BS, BS, axis=2)
        vown = lax.dynamic_slice_in_dim(vp, own * BS, BS, axis=2)
        kpos_own = own * BS + blk_iota
        rel_own = qpos[:, None] - kpos_own[None, :]
        bias_own = t5_table[t5_bucket(rel_own)].transpose(2, 0, 1)[None]
        l_own = jnp.einsum('bhqd,bhkd->bhqk', qi, kown, preferred_element_type=jnp.float32) * scale
        l_own = jnp.where(rel_own[None, None] >= 0, l_own + bias_own.astype(jnp.float32), -jnp.inf)
        logits = jnp.concatenate([l_sel.reshape(B, H, Q_BLOCK, topk * BS), l_own], axis=-1)
        p = jax.nn.softmax(logits, axis=-1).astype(v.dtype)
        p_sel = p[..., :topk * BS].reshape(B, H, Q_BLOCK, topk, BS)
        p_own = p[..., topk * BS:]
        return (jnp.einsum('bhqnk,bhqnkd->bhqd', p_sel, vsel)
                + jnp.einsum('bhqk,bhkd->bhqd', p_own, vown))

    out = lax.map(one, (jnp.arange(nqb), qb))
    out = out.transpose(1, 2, 0, 3, 4).reshape(B, H, S, Dh)
    return out.transpose(0, 2, 1, 3).reshape(B, S, MOBA_WIDTH)


def hybrid_layer(x, norm_w, w_in, mla_q_norm, mla_w_uq, mla_kv_norm, mla_w_ukv,
                 gdn_conv_w, gdn_A_log, gdn_dt_bias, gdn_norm_w, w_out, t5_table, cos, sin):
    h = rmsnorm(x, norm_w)
    proj = h @ w_in
    split_points = np.cumsum(IN_SIZES)[:-1].tolist()
    (m_cq, m_ckv, m_kr, m_gate, g_q, g_k, g_v, g_z, g_b, g_a,
     c_q, c_k, c_v, c_gate) = jnp.split(proj, split_points, axis=-1)
    y_mla = mla_mixer(m_cq, m_ckv, m_kr, mla_q_norm, mla_w_uq, mla_kv_norm, mla_w_ukv,
                      cos, sin) * jax.nn.silu(m_gate)
    y_gdn = gdn_mixer(g_q, g_k, g_v, g_z, g_b, g_a, gdn_conv_w, gdn_A_log, gdn_dt_bias, gdn_norm_w)
    y_moba = moba_mixer(c_q, c_k, c_v, t5_table) * jax.nn.silu(c_gate)
    y = jnp.concatenate([y_mla, y_gdn, y_moba], axis=-1)
    return x + y @ w_out


def setup_inputs(seed: int = 0) -> dict:
    key = jax.random.key(seed)
    ks = jax.random.split(key, 20)
    nrm = jax.random.normal
    f32 = jnp.float32
    dt = jnp.exp(jax.random.uniform(ks[10], (DEPTH, GDN_HEADS), f32, math.log(1e-3), math.log(1e-1)))
    return {
        "x": nrm(ks[0], (BATCH, SEQ, D_MODEL), f32),
        "norm_w": 1.0 + 0.02 * nrm(ks[1], (DEPTH, D_MODEL), f32),
        "w_in": nrm(ks[2], (DEPTH, D_MODEL, IN_COLS), f32) * D_MODEL ** -0.5,
        "mla_q_norm": 1.0 + 0.02 * nrm(ks[3], (DEPTH, MLA_Q_LORA), f32),
        "mla_w_uq": nrm(ks[4], (DEPTH, MLA_Q_LORA, MLA_HEADS * (MLA_NOPE + MLA_ROPE)), f32) * MLA_Q_LORA ** -0.5,
        "mla_kv_norm": 1.0 + 0.02 * nrm(ks[5], (DEPTH, MLA_KV_LORA), f32),
        "mla_w_ukv": nrm(ks[6], (DEPTH, MLA_KV_LORA, MLA_HEADS * (MLA_NOPE + MLA_V)), f32) * MLA_KV_LORA ** -0.5,
        "gdn_conv_w": nrm(ks[7], (DEPTH, GDN_CONV, 2 * GDN_HEADS * GDN_DK + GDN_WIDTH), f32) * GDN_CONV ** -0.5,
        "gdn_A_log": jnp.log(jax.random.uniform(ks[8], (DEPTH, GDN_HEADS), f32, 1.0, 16.0)),
        "gdn_dt_bias": jnp.log(jnp.expm1(dt)),
        "gdn_norm_w": 1.0 + 0.02 * nrm(ks[9], (DEPTH, GDN_DV), f32),
        "w_out": nrm(ks[11], (DEPTH, MIX_WIDTH, D_MODEL), f32) * (MIX_WIDTH * 2 * DEPTH) ** -0.5,
        "t5_table": 0.5 * nrm(ks[12], (T5_BUCKETS, MOBA_HEADS), f32),
        "final_norm_w": 1.0 + 0.02 * nrm(ks[13], (D_MODEL,), f32),
    }


def reference(x, norm_w, w_in, mla_q_norm, mla_w_uq, mla_kv_norm, mla_w_ukv,
              gdn_conv_w, gdn_A_log, gdn_dt_bias, gdn_norm_w, w_out, t5_table, final_norm_w):
    S = x.shape[1]
    cos, sin = rope_tables(S, MLA_ROPE)
    for l in range(DEPTH):
        x = hybrid_layer(x, norm_w[l], w_in[l], mla_q_norm[l], mla_w_uq[l], mla_kv_norm[l],
                         mla_w_ukv[l], gdn_conv_w[l], gdn_A_log[l], gdn_dt_bias[l],
                         gdn_norm_w[l], w_out[l], t5_table, cos, sin)
    return rmsnorm(x, final_norm_w)
```

```python
import math
import os
from contextlib import ExitStack

import numpy as np
import concourse.bass as bass
import concourse.mybir as mybir
from concourse.bass_utils import run_bass_kernel_spmd

F32 = mybir.dt.float32
BF16 = mybir.dt.bfloat16
AF = mybir.ActivationFunctionType
ALU = mybir.AluOpType
AX = mybir.AxisListType

SEQ = 8192
DM = 1024
DEPTH = 4
TT = 512
NT = SEQ // TT
EPS = 1e-6
TW = [128] * 8 + [96, 96, 64, 2]
TOFF = [sum(TW[:i]) for i in range(len(TW))]
NCOL = sum(TW)

ENGS = ("pe", "act", "dve", "pool", "sp")


class Sched:
    def __init__(self, nc, stack, n_dma_sems=40):
        self.nc = nc
        self.q = {e: [] for e in ENGS}
        self.sem = {e: stack.enter_context(nc.semaphore("s_" + e)) for e in ENGS if e != "sp"}
        self.cnt = {e: 0 for e in ENGS}
        self.seen = {e: {} for e in ENGS}
        self.dsem = [stack.enter_context(nc.semaphore("d%d" % i)) for i in range(n_dma_sems)]
        self.dcnt = [0] * n_dma_sems
        self.dnext = 0
        self.lastw = {}
        self.readers = {}
        self.semobj = dict(self.sem)
        for i, s in enumerate(self.dsem):
            self.semobj["d%d" % i] = s
        self.semobj["cc"] = stack.enter_context(nc.semaphore("s_cc"))
        self.cccnt = 0

    def _deps(self, eng, reads, writes):
        deps = {}

        def add(src, val, kind):
            if src == eng and eng == "pe":
                return
            if deps.get(src, 0) < val:
                deps[src] = val
        for k in reads:
            w = self.lastw.get(k)
            if w:
                add(w[0], w[1], "raw")
        for k in writes:
            w = self.lastw.get(k)
            if w:
                add(w[0], w[1], "waw")
            for s, v in self.readers.get(k, {}).items():
                add(s, v, "war")
        waits = []
        for s, v in deps.items():
            if self.seen[eng].get(s, 0) < v:
                self.seen[eng][s] = v
                waits.append((s, v))
        return waits

    def _commit(self, src, val, reads, writes):
        for k in reads:
            self.readers.setdefault(k, {})[src] = val
        for k in writes:
            self.lastw[k] = (src, val)
            self.readers[k] = {}

    def op(self, eng, fn, reads=(), writes=(), sig=True):
        waits = self._deps(eng, reads, writes)
        val = self.cnt[eng] + 1
        if sig:
            self.cnt[eng] = val
        self._commit(eng, val, reads, writes)
        self.q[eng].append((waits, fn, (eng, 1) if sig else None))

    def dma(self, eng, fn, reads=(), writes=()):
        i = self.dnext
        self.dnext = (self.dnext + 1) % len(self.dsem)
        name = "d%d" % i
        waits = self._deps(eng, reads, writes)
        if self.dcnt[i] > 0 and self.seen[eng].get(name, 0) < self.dcnt[i]:
            self.seen[eng][name] = self.dcnt[i]
            waits.append((name, self.dcnt[i]))
        self.dcnt[i] += 16
        self._commit(name, self.dcnt[i], reads, writes)
        self.q[eng].append((waits, fn, (name, 16)))

    def coll(self, fn, reads=(), writes=()):
        if "cc" not in self.semobj:
            raise RuntimeError("no cc semaphore")
        waits = self._deps("pool", reads, writes)
        self.cccnt += 1
        self._commit("cc", self.cccnt, reads, writes)
        self.q["pool"].append((waits, fn, ("cc", 1)))

    def drain_dmas(self, eng="sp"):
        waits = []
        for i in range(len(self.dsem)):
            name = "d%d" % i
            if self.dcnt[i] > 0 and self.seen[eng].get(name, 0) < self.dcnt[i]:
                self.seen[eng][name] = self.dcnt[i]
                waits.append((name, self.dcnt[i]))
        if self.cccnt > 0 and self.seen[eng].get("cc", 0) < self.cccnt:
            self.seen[eng]["cc"] = self.cccnt
            waits.append(("cc", self.cccnt))
        self.q[eng].append((waits, None, None))

    def replay(self):
        nc = self.nc
        with nc.Block() as block:
            engmap = {"pe": block.tensor, "act": block.scalar, "dve": block.vector,
                      "pool": block.gpsimd, "sp": block.sync}
            for e in ENGS:
                items = self.q[e]
                if not items:
                    continue

                def body(engine, items=items):
                    for waits, fn, inc in items:
                        for s, v in waits:
                            engine.wait_ge(self.semobj[s], v)
                        if fn is None:
                            continue
                        ins = fn(engine)
                        if inc is not None:
                            ins.then_inc(self.semobj[inc[0]], inc[1])
                engmap[e](body)
        self.q = {e: [] for e in ENGS}


class Ctx:
    def __init__(self, nc, stack):
        self.nc = nc
        self.S = Sched(nc, stack)
        self.ps = [stack.enter_context(nc.psum_tensor("ps%d" % i, [128, 512], F32)) for i in range(8)]
        self.psn = 0
        self.uid = 0
        self.fused = False
        self.d = {}

    def bank(self):
        i = self.psn
        self.psn = (self.psn + 1) % 8
        return i

    def din(self, name, shape, dt):
        t = self.nc.dram_tensor(name, list(shape), dt, kind="ExternalInput").ap()
        self.d[name] = t
        return t

    def dout(self, name, shape, dt):
        t = self.nc.dram_tensor(name, list(shape), dt, kind="ExternalOutput").ap()
        self.d[name] = t
        return t

    def dint(self, name, shape, dt):
        t = self.nc.dram_tensor(name, list(shape), dt, kind="Internal").ap()
        self.d[name] = t
        return t


def mm_group(C, out_fn, pairs, bank, reads, extra_writes=()):
    S = C.S
    n = len(pairs)
    key = "ps%d" % bank
    for i, (l, r) in enumerate(pairs):
        S.op("pe", (lambda e, l=l, r=r, i=i: e.matmul(out_fn(), l, r, start=(i == 0), stop=(i == n - 1))),
             reads=reads, writes=[key] + list(extra_writes), sig=(i == n - 1))
    return key


def stage_proj(C, has_prev, do_proj, final=False, tiles=range(NT)):
    nc, S, d = C.nc, C.S, C.d
    with ExitStack() as st:
        def sb(name, shape, dt):
            C.uid += 1
            return st.enter_context(nc.sbuf_tensor("sb%d_%s" % (C.uid, name), list(shape), dt))
        xs = [sb("xs%d" % i, [128, 8, TT], F32) for i in range(2)]
        eps_t = sb("eps_t", [128, 1], F32)
        S.op("dve", lambda e: e.memset(eps_t[:], EPS), writes=["eps"])
        if has_prev:
            ys = [sb("ys%d" % i, [128, 8, TT], BF16) for i in range(2)]
            wo = sb("wo", [128, 8, DM], BF16)
            wstage = sb("wstage", [128, NCOL], F32)
            for k in range(8):
                S.dma("sp", lambda e, k=k: e.dma_start(out=wstage[:, 0:DM], in_=d["wo"][k * 128:(k + 1) * 128, :]),
                      reads=["d_wo"], writes=["wstage"])
                if k % 2 == 0:
                    S.op("dve", lambda e, k=k: e.tensor_copy(out=wo[:, k, :], in_=wstage[:, 0:DM]),
                         reads=["wstage"], writes=["wo"])
                else:
                    S.op("act", lambda e, k=k: e.activation(out=wo[:, k, :], in_=wstage[:, 0:DM], func=AF.Copy),
                         reads=["wstage"], writes=["wo"])
        if do_proj or final:
            xsq = sb("xsq", [128, 8, TT], BF16)
            ones = sb("ones", [128, 128], BF16)
            S.op("dve", lambda e: e.memset(ones[:], 1.0), writes=["ones"])
            rstd1 = sb("rstd1", [128, TT], F32)
            lnt = sb("lnt", [128, TT], F32)
        if final:
            fnw = sb("fnw", [128, 8], F32)
            S.dma("sp", lambda e: e.dma_start(out=fnw[:], in_=d["fnw"][:, :]), reads=["d_fnw"], writes=["fnw"])
            outs = sb("outs", [128, 8, TT], F32)
        if do_proj:
            if not has_prev:
                wstage = sb("wstage", [128, NCOL], F32)
            xb = sb("xb", [128, 8, TT], BF16)
            win = sb("win", [128, 8, NCOL], BF16)
            nw = sb("nw", [128, 8], F32)
            S.dma("sp", lambda e: e.dma_start(out=nw[:], in_=d["nw"][:, :]), reads=["d_nw"], writes=["nw"])
            for k in range(8):
                S.dma("sp", lambda e, k=k: e.dma_start(out=wstage[:], in_=d["w_in"][k * 128:(k + 1) * 128, :]),
                      reads=["d_w_in"], writes=["wstage"])
                S.op("dve", lambda e, k=k: e.tensor_scalar(out=win[:, k, :], in0=wstage[:], scalar1=nw[:, k:k + 1],
                                                           scalar2=None, op0=ALU.mult),
                     reads=["wstage", "nw"], writes=["win"])
            wuq = sb("wuq", [128, 2, 192], BF16)
            qnw = sb("qnw", [128, 2], F32)
            S.dma("sp", lambda e: e.dma_start(out=qnw[:], in_=d["qnw"][:, :]), reads=["d_qnw"], writes=["qnw"])
            for k in range(2):
                S.dma("sp", lambda e, k=k: e.dma_start(out=wstage[:, 0:192], in_=d["wuq"][k * 128:(k + 1) * 128, :]),
                      reads=["d_wuq"], writes=["wstage"])
                S.op("dve", lambda e, k=k: e.tensor_scalar(out=wuq[:, k, :], in0=wstage[:, 0:192], scalar1=qnw[:, k:k + 1],
                                                           scalar2=None, op0=ALU.mult),
                     reads=["wstage", "qnw"], writes=["wuq"])
            wukv = sb("wukv", [128, 128], BF16)
            kvnw = sb("kvnw", [128, 1], F32)
            S.dma("sp", lambda e: e.dma_start(out=kvnw[:], in_=d["kvnw"][:, :]), reads=["d_kvnw"], writes=["kvnw"])
            S.dma("sp", lambda e: e.dma_start(out=wstage[:, 0:128], in_=d["wukv"][:, :]), reads=["d_wukv"], writes=["wstage"])
            S.op("dve", lambda e: e.tensor_scalar(out=wukv[:], in0=wstage[:, 0:128], scalar1=kvnw[:, 0:1],
                                                  scalar2=None, op0=ALU.mult),
                 reads=["wstage", "kvnw"], writes=["wukv"])
            identf = sb("identf", [128, 128], F32)
            identb = sb("identb", [128, 128], BF16)
            S.dma("sp", lambda e: e.dma_start(out=identf[:], in_=d["ident"][:, :]), reads=["d_ident"], writes=["identf"])
            S.op("dve", lambda e: e.tensor_copy(out=identb[:], in_=identf[:]), reads=["identf"], writes=["identb"])
            ctab = sb("ctab", [128, TT], F32)
            stab = sb("stab", [128, TT], F32)
            cqf = sb("cqf", [128, 2, TT], F32)
            cqsq = sb("cqsq", [128, 2, TT], BF16)
            cqb = sb("cqb", [128, 2, TT], BF16)
            rstd2 = sb("rstd2", [128, TT], F32)
            rstd3 = sb("rstd3", [128, TT], F32)
            ckvf = sb("ckvf", [128, TT], F32)
            ckvsq = sb("ckvsq", [128, TT], BF16)
            ckvb = sb("ckvb", [128, TT], BF16)
            QTs = sb("QTs", [96, TT], BF16)
            KTs = sb("KTs", [96, TT], BF16)
            t1 = sb("t1", [128, TT], F32)
            t2 = sb("t2", [128, TT], F32)
            t3 = sb("t3", [128, TT], F32)
            t4 = sb("t4", [128, TT], F32)
            vT = sb("vT", [64, TT], BF16)
            Vs = sb("Vs", [128, 4, 64], BF16)
            mvT = sb("mvT", [64, TT], BF16)
            mVs = sb("mVs", [128, 4, 64], BF16)
            mqf = sb("mqf", [64, TT], F32)
            mqb = sb("mqb", [64, TT], BF16)
            mkf = sb("mkf", [64, TT], F32)
            mkb = sb("mkb", [64, TT], BF16)
            km = sb("km", [64, 32], F32)
            gf = sb("gf", [128, TT], F32)
            gs = sb("gs", [128, TT], BF16)
            gq3 = [sb("gq3_%d" % i, [128, TT], F32) for i in range(3)]
            zf = sb("zf", [128, TT], F32)
            zs = sb("zs", [128, TT], BF16)
            bas = sb("bas", [2, TT], F32)

        def rstd_from(bank, n, out_t, rows=128):
            key = "ps%d" % bank
            S.op("act", lambda e: e.activation(out=lnt[0:rows, :], in_=C.ps[bank][0:rows, :], func=AF.Ln,
                                               bias=eps_t[0:rows, 0:1], scale=1.0 / n),
                 reads=["eps"], writes=[key, "lnt"])
            S.op("act", lambda e: e.activation(out=out_t[0:rows, :], in_=lnt[0:rows, :], func=AF.Exp, scale=-0.5),
                 reads=["lnt"], writes=[out_t.name if hasattr(out_t, "name") else "rstd"])

        def do_tile(ti):
            b = ti % 2
            c0 = ti * TT
            xk = "xs%d" % b
            xsb = xs[b]
            for k4 in range(2):
                S.dma("sp", lambda e, k4=k4, xsb=xsb: e.dma_start(
                    out=xsb[:, 4 * k4:4 * k4 + 4, :],
                    in_=d["xT"][4 * k4 * 128:(4 * k4 + 4) * 128, c0:c0 + TT].rearrange("(k p) t -> p k t", p=128)),
                    reads=["d_xT"], writes=[xk])
            if has_prev:
                yk = "ys%d" % b
                ysb = ys[b]
                S.dma("sp", lambda e, ysb=ysb: e.dma_start(
                    out=ysb[:], in_=(d["yg%d" % (c0 // 2048)][:, c0 % 2048:c0 % 2048 + TT] if C.fused else d["yT"][:, c0:c0 + TT]
                                      ).rearrange("(k p) t -> p k t", p=128)),
                    reads=[("d_yg%d" % (c0 // 2048)) if C.fused else "d_yT"], writes=[yk])
                for c in range(8):
                    bk = C.bank()
                    key = mm_group(C, (lambda bk=bk: C.ps[bk][:, :]),
                                   [(wo[:, k, c * 128:(c + 1) * 128], ysb[:, k, :]) for k in range(8)],
                                   bk, reads=["wo", yk])
                    S.op("dve", lambda e, c=c, bk=bk, xsb=xsb: e.tensor_tensor(out=xsb[:, c, :], in0=C.ps[bk][:, :], in1=xsb[:, c, :], op=ALU.add),
                         reads=[], writes=[key, xk])
                if not final:
                    for k4 in range(2):
                        S.dma("sp", lambda e, k4=k4, xsb=xsb: e.dma_start(
                            out=d["xTo"][4 * k4 * 128:(4 * k4 + 4) * 128, c0:c0 + TT].rearrange("(k p) t -> p k t", p=128),
                            in_=xsb[:, 4 * k4:4 * k4 + 4, :]),
                            reads=[xk], writes=["d_xTo"])
            if not (do_proj or final):
                return
            S.op("act", lambda e, xsb=xsb: e.activation(out=xsq[:], in_=xsb[:], func=AF.Square), reads=[xk], writes=["xsq"])
            bk = C.bank()
            key = mm_group(C, (lambda bk=bk: C.ps[bk][:, :]), [(ones[:], xsq[:, k, :]) for k in range(8)], bk,
                           reads=["ones", "xsq"])
            S.op("act", lambda e, bk=bk: e.activation(out=lnt[:], in_=C.ps[bk][:, :], func=AF.Ln, bias=eps_t[:, 0:1], scale=1.0 / DM),
                 reads=["eps"], writes=[key, "lnt"])
            S.op("act", lambda e: e.activation(out=rstd1[:], in_=lnt[:], func=AF.Exp, scale=-0.5), reads=["lnt"], writes=["rstd1"])
            if final:
                for k in range(8):
                    S.op("dve", lambda e, k=k, xsb=xsb: e.scalar_tensor_tensor(
                        out=outs[:, k, :], in0=xsb[:, k, :], scalar=fnw[:, k:k + 1], in1=rstd1[:], op0=ALU.mult, op1=ALU.mult),
                        reads=[xk, "fnw", "rstd1"], writes=["outs"])
                for k4 in range(2):
                    S.dma("sp", lambda e, k4=k4: e.dma_start(
                        out=d["outT"][4 * k4 * 128:(4 * k4 + 4) * 128, c0:c0 + TT].rearrange("(k p) t -> p k t", p=128),
                        in_=outs[:, 4 * k4:4 * k4 + 4, :]),
                        reads=["outs"], writes=["d_outT"])
                return
            S.op("dve", lambda e, xsb=xsb: e.tensor_copy(out=xb[:], in_=xsb[:]), reads=[xk], writes=["xb"])
            S.dma("sp", lambda e: e.dma_start(out=ctab[64:96, :], in_=d["ctab"][:, c0:c0 + TT]), reads=["d_ctab"], writes=["ctab"])
            S.dma("sp", lambda e: e.dma_start(out=stab[64:96, :], in_=d["stab"][:, c0:c0 + TT]), reads=["d_stab"], writes=["stab"])

            def proj_tile(t):
                bk = C.bank()
                w = TW[t]
                key = mm_group(C, (lambda bk=bk, w=w: C.ps[bk][0:w, :]),
                               [(win[:, k, TOFF[t]:TOFF[t] + w], xb[:, k, :]) for k in range(8)], bk,
                               reads=["win", "xb"])
                return bk, key

            for j in range(2):
                bk, key = proj_tile(j)
                S.op("dve", lambda e, bk=bk, j=j: e.tensor_tensor(out=cqf[:, j, :], in0=C.ps[bk][:, :], in1=rstd1[:], op=ALU.mult),
                     reads=["rstd1"], writes=[key, "cqf"])
            S.op("act", lambda e: e.activation(out=cqsq[:], in_=cqf[:], func=AF.Square), reads=["cqf"], writes=["cqsq"])
            S.op("dve", lambda e: e.tensor_copy(out=cqb[:], in_=cqf[:]), reads=["cqf"], writes=["cqb"])
            bk = C.bank()
            key = mm_group(C, (lambda bk=bk: C.ps[bk][:, :]), [(ones[:], cqsq[:, j, :]) for j in range(2)], bk, reads=["ones", "cqsq"])
            S.op("act", lambda e, bk=bk: e.activation(out=lnt[:], in_=C.ps[bk][:, :], func=AF.Ln, bias=eps_t[:, 0:1], scale=1.0 / 256),
                 reads=["eps"], writes=[key, "lnt"])
            S.op("act", lambda e: e.activation(out=rstd2[:], in_=lnt[:], func=AF.Exp, scale=-0.5), reads=["lnt"], writes=["rstd2"])
            bq = C.bank()
            keyq = mm_group(C, (lambda bq=bq: C.ps[bq][0:96, :]), [(wuq[:, j, 0:96], cqb[:, j, :]) for j in range(2)], bq, reads=["wuq", "cqb"])
            br = C.bank()
            keyr = mm_group(C, (lambda br=br: C.ps[br][0:96, :]), [(wuq[:, j, 96:192], cqb[:, j, :]) for j in range(2)], br, reads=["wuq", "cqb"])
            S.op("dve", lambda e, bq=bq: e.tensor_tensor(out=QTs[0:64, :], in0=C.ps[bq][0:64, :], in1=rstd2[0:64, :], op=ALU.mult),
                 reads=["rstd2"], writes=[keyq, "QTs"])
            S.op("dve", lambda e, bq=bq: e.tensor_tensor(out=t1[64:96, :], in0=C.ps[bq][64:96, :], in1=ctab[64:96, :], op=ALU.mult),
                 reads=["ctab"], writes=[keyq, "t1"])
            S.op("dve", lambda e, br=br: e.tensor_tensor(out=t2[64:96, :], in0=C.ps[br][64:96, :], in1=stab[64:96, :], op=ALU.mult),
                 reads=["stab"], writes=[keyr, "t2"])
            S.op("dve", lambda e: e.tensor_tensor(out=t1[64:96, :], in0=t1[64:96, :], in1=t2[64:96, :], op=ALU.add),
                 reads=["t2", "t1"], writes=["t1"])
            S.op("dve", lambda e: e.tensor_tensor(out=QTs[64:96, :], in0=t1[64:96, :], in1=rstd2[64:96, :], op=ALU.mult),
                 reads=["t1", "rstd2"], writes=["QTs"])
            S.dma("sp", lambda e: e.dma_start(out=d["QT_mla"][:, c0:c0 + TT], in_=QTs[:]), reads=["QTs"], writes=["d_QT_mla"])
            bk, key = proj_tile(2)
            S.op("dve", lambda e, bk=bk: e.tensor_tensor(out=ckvf[:], in0=C.ps[bk][:, :], in1=rstd1[:], op=ALU.mult),
                 reads=["rstd1"], writes=[key, "ckvf"])
            S.op("act", lambda e: e.activation(out=ckvsq[:], in_=ckvf[:], func=AF.Square), reads=["ckvf"], writes=["ckvsq"])
            S.op("act", lambda e: e.activation(out=ckvb[:], in_=ckvf[:], func=AF.Copy), reads=["ckvf"], writes=["ckvb"])
            bk = C.bank()
            key = mm_group(C, (lambda bk=bk: C.ps[bk][:, :]), [(ones[:], ckvsq[:])], bk, reads=["ones", "ckvsq"])
            S.op("act", lambda e, bk=bk: e.activation(out=lnt[:], in_=C.ps[bk][:, :], func=AF.Ln, bias=eps_t[:, 0:1], scale=1.0 / 128),
                 reads=["eps"], writes=[key, "lnt"])
            S.op("act", lambda e: e.activation(out=rstd3[:], in_=lnt[:], func=AF.Exp, scale=-0.5), reads=["lnt"], writes=["rstd3"])
            bk = C.bank()
            key = mm_group(C, (lambda bk=bk: C.ps[bk][0:64, :]), [(wukv[:, 0:64], ckvb[:])], bk, reads=["wukv", "ckvb"])
            S.op("dve", lambda e, bk=bk: e.tensor_tensor(out=KTs[0:64, :], in0=C.ps[bk][0:64, :], in1=rstd3[0:64, :], op=ALU.mult),
                 reads=["rstd3"], writes=[key, "KTs"])
            bk = C.bank()
            key = mm_group(C, (lambda bk=bk: C.ps[bk][0:64, :]), [(wukv[:, 64:128], ckvb[:])], bk, reads=["wukv", "ckvb"])
            S.op("dve", lambda e, bk=bk: e.tensor_tensor(out=vT[:], in0=C.ps[bk][0:64, :], in1=rstd3[0:64, :], op=ALU.mult),
                 reads=["rstd3"], writes=[key, "vT"])

            def transpose_v(src, srck, dst, dstk, dname):
                bk = C.bank()
                key = "ps%d" % bk
                for j in range(4):
                    S.op("pe", lambda e, j=j, bk=bk: e.transpose(
                        C.ps[bk][:, :].bitcast(BF16)[:, j * 64:(j + 1) * 64], src[0:64, j * 128:(j + 1) * 128], identb[0:64, 0:64]),
                        reads=[srck, "identb"], writes=[key], sig=(j == 3))
                S.op("act", lambda e, bk=bk: e.activation(out=dst[:].rearrange("p a b -> p (a b)"),
                                                          in_=C.ps[bk][:, :].bitcast(BF16)[:, 0:256], func=AF.Copy),
                     reads=[], writes=[key, dstk])
                S.dma("sp", lambda e: e.dma_start(
                    out=d[dname][c0:c0 + TT, :].rearrange("(a p) v -> p a v", p=128), in_=dst[:]),
                    reads=[dstk], writes=["d_" + dname])
            transpose_v(vT, "vT", Vs, "Vs", "V_mla")
            bk, key = proj_tile(8)
            S.op("dve", lambda e, bk=bk: e.tensor_tensor(out=mqf[:], in0=C.ps[bk][0:64, :], in1=rstd1[0:64, :], op=ALU.mult),
                 reads=["rstd1"], writes=[key, "mqf"])
            S.op("dve", lambda e, bk=bk: e.tensor_tensor(out=t3[64:96, :], in0=C.ps[bk][64:96, :], in1=ctab[64:96, :], op=ALU.mult),
                 reads=["ctab"], writes=[key, "t3"])
            S.op("act", lambda e: e.activation(out=mqb[:], in_=mqf[:], func=AF.Copy, scale=0.125),
                 reads=["mqf"], writes=["mqb"])
            S.dma("sp", lambda e: e.dma_start(out=d["QTf_moba"][:, c0:c0 + TT], in_=mqf[:]), reads=["mqf"], writes=["d_QTf_moba"])
            S.dma("sp", lambda e: e.dma_start(out=d["QT_moba"][:, c0:c0 + TT], in_=mqb[:]), reads=["mqb"], writes=["d_QT_moba"])
            bk, key = proj_tile(9)
            S.op("dve", lambda e, bk=bk: e.tensor_tensor(out=mkf[:], in0=C.ps[bk][0:64, :], in1=rstd1[0:64, :], op=ALU.mult),
                 reads=["rstd1"], writes=[key, "mkf"])
            S.op("dve", lambda e, bk=bk: e.tensor_tensor(out=t4[64:96, :], in0=C.ps[bk][64:96, :], in1=stab[64:96, :], op=ALU.mult),
                 reads=["stab"], writes=[key, "t4"])
            S.op("act", lambda e: e.activation(out=mkb[:], in_=mkf[:], func=AF.Copy), reads=["mkf"], writes=["mkb"])
            S.op("dve", lambda e: e.tensor_reduce(out=km[:, 2 * ti:2 * ti + 2], in_=mkf[:].rearrange("p (a b) -> p a b", b=256),
                                                  op=ALU.add, axis=AX.X),
                 reads=["mkf"], writes=["km"])
            S.dma("sp", lambda e: e.dma_start(out=d["KT_moba"][:, c0:c0 + TT], in_=mkb[:]), reads=["mkb"], writes=["d_KT_moba"])
            S.op("dve", lambda e: e.tensor_tensor(out=t3[64:96, :], in0=t3[64:96, :], in1=t4[64:96, :], op=ALU.add),
                 reads=["t3", "t4"], writes=["t3"])
            S.op("dve", lambda e: e.tensor_tensor(out=KTs[64:96, :], in0=t3[64:96, :], in1=rstd1[64:96, :], op=ALU.mult),
                 reads=["t3", "rstd1"], writes=["KTs"])
            S.dma("sp", lambda e: e.dma_start(out=d["KT_mla"][:, c0:c0 + TT], in_=KTs[:]), reads=["KTs"], writes=["d_KT_mla"])
            bk, key = proj_tile(10)
            S.op("dve", lambda e, bk=bk: e.tensor_tensor(out=mvT[:], in0=C.ps[bk][0:64, :], in1=rstd1[0:64, :], op=ALU.mult),
                 reads=["rstd1"], writes=[key, "mvT"])
            transpose_v(mvT, "mvT", mVs, "mVs", "V_moba")
            bk, key = proj_tile(3)
            S.op("dve", lambda e, bk=bk: e.tensor_tensor(out=gf[:], in0=C.ps[bk][:, :], in1=rstd1[:], op=ALU.mult),
                 reads=["rstd1"], writes=[key, "gf"])
            S.op("act", lambda e: e.activation(out=gs[:], in_=gf[:], func=AF.Silu), reads=["gf"], writes=["gs"])
            S.dma("sp", lambda e: e.dma_start(out=d["GT_mla"][:, c0:c0 + TT], in_=gs[0:64, :]), reads=["gs"], writes=["d_GT_mla"])
            S.dma("sp", lambda e: e.dma_start(out=d["GT_moba"][:, c0:c0 + TT], in_=gs[64:128, :]), reads=["gs"], writes=["d_GT_moba"])
            for j, nm in enumerate(("gqT", "gkT", "gvT")):
                bk, key = proj_tile(4 + j)
                S.op("dve", lambda e, bk=bk, j=j: e.tensor_tensor(out=gq3[j][:], in0=C.ps[bk][:, :], in1=rstd1[:], op=ALU.mult),
                     reads=["rstd1"], writes=[key, "gq3_%d" % j])
                S.dma("sp", lambda e, j=j, nm=nm: e.dma_start(out=d[nm][:, c0:c0 + TT], in_=gq3[j][:]),
                      reads=["gq3_%d" % j], writes=["d_" + nm])
            bk, key = proj_tile(7)
            S.op("dve", lambda e, bk=bk: e.tensor_tensor(out=zf[:], in0=C.ps[bk][:, :], in1=rstd1[:], op=ALU.mult),
                 reads=["rstd1"], writes=[key, "zf"])
            S.op("act", lambda e: e.activation(out=zs[:], in_=zf[:], func=AF.Silu), reads=["zf"], writes=["zs"])
            S.dma("sp", lambda e: e.dma_start(out=d["gzT"][:, c0:c0 + TT], in_=zs[:]), reads=["zs"], writes=["d_gzT"])
            bk, key = proj_tile(11)
            S.op("dve", lambda e, bk=bk: e.tensor_tensor(out=bas[:], in0=C.ps[bk][0:2, :], in1=rstd1[0:2, :], op=ALU.mult),
                 reads=["rstd1"], writes=[key, "bas"])
            S.dma("sp", lambda e: e.dma_start(out=d["baT"][:, c0:c0 + TT], in_=bas[:]), reads=["bas"], writes=["d_baT"])
        for ti in tiles:
            do_tile(ti)
        if do_proj and not final:
            S.op("dve", lambda e: e.tensor_scalar(out=km[:], in0=km[:], scalar1=1.0 / 256, scalar2=None, op0=ALU.mult),
                 reads=["km"], writes=["km"])
            S.dma("sp", lambda e: e.dma_start(out=d["kmT"][:, :], in_=km[:]), reads=["km"], writes=["d_kmT"])
        S.drain_dmas("sp")
        S.replay()


def rope_consts():
    inv = (1.0 / (10000.0 ** (np.arange(0, 32, 2, dtype=np.float32) / 32))).astype(np.float32)
    ang = (np.arange(SEQ, dtype=np.float32)[:, None] * inv[None, :]).astype(np.float32)
    cos = np.cos(ang).astype(np.float32).T
    sin = np.sin(ang).astype(np.float32).T
    ctab = np.ascontiguousarray(np.concatenate([cos, cos], 0))
    stab = np.ascontiguousarray(np.concatenate([-sin, sin], 0))
    return ctab, stab


def in_cols(h):
    cq0, ckv0, kr0, mg0, gq0, gk0, gv0, gz0, gb0, ga0, mq0, mk0, mv0, cg0 = (
        0, 256, 384, 416, 672, 1184, 1696, 2208, 2720, 2724, 2728, 2984, 3240, 3496)
    r = lambda a, n: list(range(a, a + n))
    cols = []
    cols += r(cq0, 256) + r(ckv0, 128)
    cols += r(mg0 + 64 * h, 64) + r(cg0 + 64 * h, 64)
    cols += r(gq0 + 128 * h, 128) + r(gk0 + 128 * h, 128) + r(gv0 + 128 * h, 128) + r(gz0 + 128 * h, 128)
    cols += r(mq0 + 64 * h, 64) + r(kr0, 32)
    cols += r(mk0 + 64 * h, 64) + r(kr0 + 16, 16) + r(kr0, 16)
    cols += r(mv0 + 64 * h, 64)
    cols += [gb0 + h, ga0 + h]
    assert len(cols) == NCOL
    return cols


def prep_layer(I, l, h):
    f = np.float32
    out = {}
    out["w_in"] = np.ascontiguousarray(I["w_in"][l][:, in_cols(h)]).astype(f)
    out["nw"] = np.ascontiguousarray(I["norm_w"][l].reshape(8, 128).T).astype(f)
    wq = I["mla_w_uq"][l][:, 96 * h:96 * h + 96]
    wuq = np.zeros((256, 192), f)
    wuq[:, 0:96] = wq
    wuq[:, 160:176] = wq[:, 80:96]
    wuq[:, 176:192] = wq[:, 64:80]
    out["wuq"] = wuq
    out["qnw"] = np.ascontiguousarray(I["mla_q_norm"][l].reshape(2, 128).T).astype(f)
    out["wukv"] = np.ascontiguousarray(I["mla_w_ukv"][l][:, 128 * h:128 * h + 128]).astype(f)
    out["kvnw"] = np.ascontiguousarray(I["mla_kv_norm"][l].reshape(128, 1)).astype(f)
    return out


PROJ_OUTS = [("QT_mla", (96, SEQ), BF16), ("KT_mla", (96, SEQ), BF16), ("V_mla", (SEQ, 64), BF16),
             ("QTf_moba", (64, SEQ), F32), ("QT_moba", (64, SEQ), BF16), ("KT_moba", (64, SEQ), BF16),
             ("V_moba", (SEQ, 64), BF16), ("kmT", (64, 32), F32),
             ("GT_mla", (64, SEQ), BF16), ("GT_moba", (64, SEQ), BF16),
             ("gqT", (128, SEQ), F32), ("gkT", (128, SEQ), F32), ("gvT", (128, SEQ), F32),
             ("gzT", (128, SEQ), BF16), ("baT", (2, SEQ), F32)]
PROJ_INS = [("w_in", (DM, NCOL)), ("nw", (128, 8)), ("wuq", (256, 192)), ("qnw", (128, 2)),
            ("wukv", (128, 128)), ("kvnw", (128, 1))]


def t5_consts():
    e = np.arange(3072)
    dd = e - 511
    n = np.maximum(dd, 0)
    nf = np.maximum(n, 1).astype(np.float32)
    large = 16 + (np.log(nf / np.float32(16)) / np.float32(math.log(2048 / 16)) * np.float32(16)).astype(np.int32)
    large = np.minimum(large, 31)
    bucket = np.where(n < 16, n, large)
    OH = np.zeros((32, 3072), np.float32)
    OH[bucket, e] = 1.0
    OH[:, dd < 0] = 0.0
    return OH


def moba_consts():
    pen = np.zeros((32, 32), np.float32)
    for own in range(32):
        pen[own, own] = 1e30
        pen[own, own + 1:] = -1e30
    E = np.zeros((32, SEQ), np.float32)
    for j in range(32):
        E[j, j * 256:(j + 1) * 256] = 1.0
    return pen, E


def stage_attn(C, moba, qtiles=range(NT)):
    nc, S, d = C.nc, C.S, C.d
    pre = "moba" if moba else "mla"
    KD = 96
    scale = 1.0 if moba else 96 ** -0.5
    rowbase = 64 if moba else 0
    with ExitStack() as st:
        def sb(name, shape, dt):
            C.uid += 1
            return st.enter_context(nc.sbuf_tensor("sb%d_%s" % (C.uid, name), list(shape), dt))
        QT = sb("QT", [96, SEQ], BF16)
        KT = sb("KT", [96, SEQ], BF16)
        Va = sb("Va", [128, 64, 65], BF16)
        GT = sb("GT", [64, SEQ], BF16)
        Pb = [sb("P%d" % i, [128, TT], BF16) for i in range(4)]
        osb = sb("osb", [65, TT], F32)
        rec = sb("rec", [64, TT], F32)
        otmp = sb("otmp", [64, TT], F32)
        ysb = sb("ysb", [64, TT], BF16)
        if C.fused:
            ym = [sb("ym%d" % j, [64, TT], BF16) for j in range(4)]
            hm = sb("hm", [128, 4], F32)
            S.dma("sp", lambda e: e.dma_start(out=hm[:], in_=d["hm"][:, :]), reads=["d_hm"], writes=["hm"])
        sel = sb("sel", [65, 64], F32)
        nrows = 64 if moba else 96
        for q4 in range(4):
            cs = slice(q4 * 2048, (q4 + 1) * 2048)
            S.dma("sp", lambda e, cs=cs: e.dma_start(out=QT[0:nrows, cs], in_=d["QT_" + pre][:, cs]), reads=["d_QT_" + pre], writes=["QT"])
            S.dma("sp", lambda e, cs=cs: e.dma_start(out=KT[0:nrows, cs], in_=d["KT_" + pre][:, cs]), reads=["d_KT_" + pre], writes=["KT"])
            S.dma("sp", lambda e, cs=cs: e.dma_start(out=GT[:, cs], in_=d["GT_" + pre][:, cs]), reads=["d_GT_" + pre], writes=["GT"])
        for a4 in range(16):
            S.dma("sp", lambda e, a4=a4: e.dma_start(
                out=Va[:, 4 * a4:4 * a4 + 4, 0:64],
                in_=d["V_" + pre][a4 * 512:(a4 + 1) * 512, :].rearrange("(a p) v -> p a v", p=128)),
                reads=["d_V_" + pre], writes=["Va"])
        S.op("dve", lambda e: e.memset(Va[:, :, 64:65], 1.0), writes=["Va"])
        S.dma("sp", lambda e: e.dma_start(out=sel[:], in_=d["sel"][:, :]), reads=["d_sel"], writes=["sel"])
        if not moba:
            trif = sb("trif", [128, 128], F32)
            tri = sb("tri", [128, 128], BF16)
            S.dma("sp", lambda e: e.dma_start(out=trif[:], in_=d["tri"][:, :]), reads=["d_tri"], writes=["trif"])
            S.op("dve", lambda e: e.tensor_copy(out=tri[:], in_=trif[:]), reads=["trif"], writes=["tri"])
        else:
            t5c = sb("t5c", [32, 1], F32)
            et = sb("et", [32, 1], F32)
            b31 = sb("b31", [128, 1], F32)
            OH = sb("OH", [32, 3072], F32)
            fvs = sb("fvs", [1, 3072], F32)
            ES32 = sb("ES32", [128, 2560], F32)
            ES = sb("ES", [128, 2560], BF16)
            S.dma("sp", lambda e: e.dma_start(out=t5c[:], in_=d["t5h"][:, :]), reads=["d_t5h"], writes=["t5c"])
            S.dma("sp", lambda e: e.dma_start(out=b31[:], in_=d["t5h"][31:32, :].partition_broadcast(128).rearrange("p a b -> p (a b)")),
                  reads=["d_t5h"], writes=["b31"])
            S.dma("sp", lambda e: e.dma_start(out=OH[:], in_=d["OH"][:, :]), reads=["d_OH"], writes=["OH"])
            S.op("act", lambda e: e.activation(out=et[:], in_=t5c[:], func=AF.Exp), reads=["t5c"], writes=["et"])
            for j in range(6):
                key = mm_group(C, (lambda: C.ps[5][0:1, :]), [(et[:, 0:1], OH[:, j * 512:(j + 1) * 512])], 5, reads=["et", "OH"])
                S.op("dve", lambda e, j=j: e.tensor_copy(out=fvs[:, j * 512:(j + 1) * 512], in_=C.ps[5][0:1, :]), reads=[], writes=[key, "fvs"])
            S.dma("sp", lambda e: e.dma_start(out=d["fv"][:, :], in_=fvs[:]), reads=["fvs"], writes=["d_fv"])
            for ki in range(128):
                S.dma("sp", lambda e, ki=ki: e.dma_start(
                    out=ES32[ki:ki + 1, :], in_=d["fv"][:, 127 - ki:127 - ki + 2560]), reads=["d_fv"], writes=["ES32"])
            S.op("dve", lambda e: e.tensor_copy(out=ES[:], in_=ES32[:]), reads=["ES32"], writes=["ES"])
            for q4 in range(4):
                cs = slice(q4 * 2048, (q4 + 1) * 2048)
                S.dma("sp", lambda e, cs=cs: e.dma_start(out=KT[64:96, cs], in_=d["Eb"][:, cs]), reads=["d_Eb"], writes=["KT"])
            QTf = sb("QTf", [64, SEQ], F32)
            kmT = sb("kmT", [64, 32], F32)
            pen = sb("pen", [128, 32 * 32], F32)
            identf = sb("identf", [128, 128], F32)
            identb = sb("identb", [128, 128], BF16)
            S.dma("sp", lambda e: e.dma_start(out=identf[:], in_=d["ident"][:, :]), reads=["d_ident"], writes=["identf"])
            S.op("dve", lambda e: e.tensor_copy(out=identb[:], in_=identf[:]), reads=["identf"], writes=["identb"])
            for q4 in range(4):
                cs = slice(q4 * 2048, (q4 + 1) * 2048)
                S.dma("sp", lambda e, cs=cs: e.dma_start(out=QTf[:, cs], in_=d["QTf_moba"][:, cs]), reads=["d_QTf_moba"], writes=["QTf"])
            S.dma("sp", lambda e: e.dma_start(out=kmT[:], in_=d["kmT"][:, :]), reads=["d_kmT"], writes=["kmT"])
            S.dma("sp", lambda e: e.dma_start(out=pen[:], in_=d["pen"][:, :].rearrange("a b -> (a b)").partition_broadcast(128)),
                  reads=["d_pen"], writes=["pen"])
            gm = [sb("gm%d" % i, [128, 32], F32) for i in range(4)]
            m8 = [sb("m8%d" % i, [128, 8], F32) for i in range(4)]
            thr = [sb("thr%d" % i, [128, 1], F32) for i in range(4)]
            Mq = [sb("Mq%d" % i, [128, 96], BF16) for i in range(4)]
            for i in range(4):
                S.op("dve", lambda e, i=i: e.memset(Mq[i][:], 0.0), writes=["Mq%d" % i])
            for g in range(16):
                for j in range(4):
                    i = g * 4 + j
                    own = i // 2
                    key = mm_group(C, (lambda j=j: C.ps[6][:, j * 32:(j + 1) * 32]), [(QTf[:, i * 128:(i + 1) * 128], kmT[:, :])], 6,
                                   reads=["QTf", "kmT"])
                    S.op("dve", lambda e, j=j, own=own: e.tensor_tensor(out=gm[j][:], in0=C.ps[6][:, j * 32:(j + 1) * 32],
                                                                      in1=pen[:, own * 32:(own + 1) * 32], op=ALU.add),
                         reads=["pen"], writes=[key, "gm%d" % j])
                    S.op("dve", lambda e, j=j: e.max(out=m8[j][:], in_=gm[j][:]), reads=["gm%d" % j], writes=["m8%d" % j])
                    S.op("dve", lambda e, j=j: e.tensor_scalar(out=thr[j][:], in0=m8[j][:, 3:4], scalar1=-1e29, scalar2=None, op0=ALU.max),
                         reads=["m8%d" % j], writes=["thr%d" % j])
                    S.op("dve", lambda e, j=j: e.tensor_scalar(out=Mq[j][:, 64:96], in0=gm[j][:], scalar1=thr[j][:, 0:1], scalar2=-30000.0,
                                                               op0=ALU.is_lt, op1=ALU.mult),
                         reads=["gm%d" % j, "thr%d" % j], writes=["Mq%d" % j])
                    S.op("pe", lambda e, j=j: e.transpose(C.ps[7][0:96, :].bitcast(BF16)[:, j * 128:(j + 1) * 128], Mq[j][:], identb[:]),
                         reads=["Mq%d" % j, "identb"], writes=["ps7"], sig=True)
                S.op("act", lambda e, g=g: e.activation(out=QT[64:96, g * 512:(g + 1) * 512],
                                                        in_=C.ps[7][64:96, :].bitcast(BF16)[:, 0:512], func=AF.Copy),
                     reads=[], writes=["ps7", "QT"])

        SB = (0, 1, 2, 3)
        OB = (4, 5)
        NB = 4
        DEPTHQ = 3
        items = []
        for qi, qt in enumerate(qtiles):
            for kt in range(4 * qt + 4):
                items.append((qi, qt, kt))

        def front(idx):
            qi, qt, kt = items[idx]
            q0 = qt * TT
            k0 = kt * 128
            diag = kt >= 4 * qt
            koff = (kt - 4 * qt) * 128 if diag else 0
            sbk = SB[idx % NB]
            skey = "ps%d" % sbk
            P = Pb[idx % NB]
            pkey = "P%d" % (idx % NB)
            S.op("pe", lambda e: e.matmul(
                C.ps[sbk][:, koff:TT], KT[0:KD, k0:k0 + 128], QT[0:KD, q0 + koff:q0 + TT], start=True, stop=True),
                reads=["KT", "QT"], writes=[skey])
            far = moba and (q0 - k0 >= 1664)
            if far:
                S.op("act", lambda e: e.activation(
                    out=P[:, koff:TT], in_=C.ps[sbk][:, koff:TT], func=AF.Exp, bias=b31[:, 0:1], scale=scale),
                    reads=["b31"], writes=[skey, pkey])
            else:
                S.op("act", lambda e: e.activation(
                    out=P[:, koff:TT], in_=C.ps[sbk][:, koff:TT], func=AF.Exp, scale=scale),
                    reads=[], writes=[skey, pkey])
                if moba:
                    s0 = q0 - k0 + 384
                    S.op("pool", lambda e: e.tensor_tensor(
                        out=P[:, koff:TT], in0=P[:, koff:TT], in1=ES[:, s0 + koff:s0 + TT], op=ALU.mult),
                        reads=["ES", pkey], writes=[pkey])
                elif diag:
                    S.op("pool", lambda e: e.tensor_tensor(
                        out=P[:, koff:koff + 128], in0=P[:, koff:koff + 128], in1=tri[:], op=ALU.mult),
                        reads=["tri", pkey], writes=[pkey])

        def back(idx):
            qi, qt, kt = items[idx]
            q0 = qt * TT
            diag = kt >= 4 * qt
            koff = (kt - 4 * qt) * 128 if diag else 0
            nkt = 4 * qt + 4
            ob = OB[qi % 2]
            okey = "ps%d" % ob
            P = Pb[idx % NB]
            pkey = "P%d" % (idx % NB)
            S.op("pe", lambda e: e.matmul(
                C.ps[ob][0:65, koff:TT], Va[:, kt, :], P[:, koff:TT], start=(kt == 0), stop=(kt == nkt - 1)),
                reads=[pkey, "Va"], writes=[okey])
            if kt != nkt - 1:
                return
            S.op("act", lambda e: e.activation(out=osb[:], in_=C.ps[ob][0:65, :], func=AF.Copy), reads=[], writes=[okey, "osb"])
            key = mm_group(C, (lambda: C.ps[6][0:64, :]), [(sel[:], osb[:])], 6, reads=["sel", "osb"])
            S.op("dve", lambda e: e.reciprocal(out=rec[:], in_=C.ps[6][0:64, :]), reads=[], writes=[key, "rec"])
            S.op("dve", lambda e: e.tensor_tensor(out=otmp[:], in0=osb[0:64, :], in1=rec[:], op=ALU.mult), reads=["osb", "rec"], writes=["otmp"])
            if not C.fused:
                S.op("dve", lambda e: e.tensor_tensor(out=ysb[:], in0=otmp[:], in1=GT[:, q0:q0 + TT], op=ALU.mult),
                     reads=["otmp", "GT"], writes=["ysb"])
                S.dma("sp", lambda e: e.dma_start(out=d["yT_h"][rowbase:rowbase + 64, q0:q0 + TT], in_=ysb[:]),
                      reads=["ysb"], writes=["d_yT_h"])
            else:
                qq, qc = q0 // 2048, q0 % 2048
                for j in range(4):
                    S.op("dve", lambda e, j=j: e.scalar_tensor_tensor(out=ym[j][:], in0=otmp[:], scalar=hm[0:64, j:j + 1],
                                                                      in1=GT[:, q0:q0 + TT], op0=ALU.mult, op1=ALU.mult),
                         reads=["otmp", "GT", "hm"], writes=["ym%d" % j])
                    S.dma("sp", lambda e, j=j: e.dma_start(
                        out=d["ypad%d" % qq][256 * j + rowbase:256 * j + rowbase + 64, qc:qc + TT], in_=ym[j][:]),
                        reads=["ym%d" % j], writes=["d_ypad%d" % qq])

        n_it = len(items)
        for idx in range(n_it + DEPTHQ):
            if idx < n_it:
                front(idx)
            if idx - DEPTHQ >= 0:
                back(idx - DEPTHQ)
        S.drain_dmas("sp")
        S.replay()


GC = 128
NCH = SEQ // GC


def gdn_consts():
    i = np.arange(128)
    umask = (i[:, None] <= i[None, :]).astype(np.float32)
    m2 = (i[:, None] > i[None, :]).astype(np.float32)
    neg = np.where(i[:, None] < i[None, :], -30000.0, 0.0).astype(np.float32)
    sl = (i[:, None] > i[None, :]).astype(np.float32)
    return umask, m2, neg, sl


def level_masks():
    i = np.arange(128)
    out = np.zeros((128, 14, 128), np.float32)
    for l in range(7):
        b = 1 << l
        bi = i // b
        m = ((bi[:, None] % 2 == 1) & (bi[None, :] == bi[:, None] - 1)).astype(np.float32)
        out[:, 2 * l, :] = m
        out[:, 2 * l + 1, :] = m.T
    return out.reshape(128, 14 * 128)


def stage_gdn(C, nchunks=NCH, G=4, stop_after=None):
    nc, S, d = C.nc, C.S, C.d
    NSET = 2 * G
    with ExitStack() as st:
        def sb(name, shape, dt):
            C.uid += 1
            return st.enter_context(nc.sbuf_tensor("sb%d_%s" % (C.uid, name), list(shape), dt))
        identf = sb("identf", [128, 128], F32)
        umask = sb("umask", [128, 128], F32)
        m2 = sb("m2", [128, 128], F32)
        neg = sb("neg", [128, 128], F32)
        slm = sb("slm", [128, 128], F32)
        onesf = sb("onesf", [128, 128], F32)
        identb = sb("identb", [128, 128], BF16)
        lvf = sb("lvf", [128, 14 * 128], F32)
        lvm = sb("lvm", [128, 14 * 128], BF16)
        S.dma("sp", lambda e: e.dma_start(out=lvf[:], in_=d["lvlm"][:, :]), reads=["d_lvlm"], writes=["lvf"])
        S.op("dve", lambda e: e.tensor_copy(out=lvm[:], in_=lvf[:]), reads=["lvf"], writes=["lvm"])
        for nm, t in (("ident", identf), ("umask", umask), ("m2", m2), ("neg", neg), ("slm", slm)):
            S.dma("sp", lambda e, nm=nm, t=t: e.dma_start(out=t[:], in_=d[nm][:, :]), reads=["d_" + nm], writes=[nm])
        S.op("dve", lambda e: e.memset(onesf[:], 1.0), writes=["onesf"])
        S.op("dve", lambda e: e.tensor_copy(out=identb[:], in_=identf[:]), reads=["ident"], writes=["identb"])
        eps_t = sb("eps_t", [128, 1], F32)
        S.op("dve", lambda e: e.memset(eps_t[:], EPS), writes=["eps"])
        cw = sb("cw", [128, 12], F32)
        S.dma("sp", lambda e: e.dma_start(out=cw[:], in_=d["cw"][:, :]), reads=["d_cw"], writes=["cw"])
        gsc = sb("gsc", [128, 4], F32)
        S.dma("sp", lambda e: e.dma_start(out=gsc[:, 0:2], in_=d["gsc"][:, :].rearrange("a b -> (a b)").partition_broadcast(128)),
              reads=["d_gsc"], writes=["gsc"])
        gnw = sb("gnw", [128, 1], F32)
        S.dma("sp", lambda e: e.dma_start(out=gnw[:], in_=d["gnw"][:, :]), reads=["d_gnw"], writes=["gnw"])
        S.op("act", lambda e: e.activation(out=gsc[:, 2:3], in_=gsc[:, 0:1], func=AF.Exp), reads=["gsc"], writes=["gsc2"])
        S.op("dve", lambda e: e.tensor_scalar(out=gsc[:, 3:4], in0=gsc[:, 2:3], scalar1=-1.0, scalar2=None, op0=ALU.mult),
             reads=["gsc2"], writes=["gsc3"])
        ba = [sb("ba%d" % i, [2, TT], F32) for i in range(2)]
        batok = sb("batok", [128, NCH, 2], F32)
        for c in range(NCH):
            bi = (c // 4) % 2
            if c % 4 == 0:
                S.dma("sp", lambda e, c=c, bi=bi: e.dma_start(out=ba[bi][:], in_=d["baT"][:, c * 128:c * 128 + TT]),
                      reads=["d_baT"], writes=["ba%d" % bi])
            S.op("pe", lambda e, c=c, bi=bi: e.matmul(C.ps[0][:, 2 * c:2 * c + 2], ba[bi][0:2, (c % 4) * 128:(c % 4 + 1) * 128], identf[0:2, 0:2],
                                                      start=True, stop=True),
                 reads=["ba%d" % bi, "ident"], writes=["ps0"], sig=True)
        S.op("dve", lambda e: e.tensor_copy(out=batok[:].rearrange("p a b -> p (a b)"), in_=C.ps[0][:, 0:2 * NCH]), reads=[], writes=["ps0", "batok"])
        beta = sb("beta", [128, NCH], F32)
        nbeta = sb("nbeta", [128, NCH], F32)
        gg = sb("gg", [128, NCH], F32)
        tmpa = sb("tmpa", [128, NCH], F32)
        gc = sb("gc", [128, NCH], F32)
        gce = sb("gce", [128, NCH], F32)
        egc = sb("egc", [128, NCH], F32)
        bke = sb("bke", [128, NCH], F32)
        eend = sb("eend", [128, NCH], F32)
        gend = sb("gend", [128, NCH], F32)
        S.op("act", lambda e: e.activation(out=beta[:], in_=batok[:, :, 0], func=AF.Sigmoid), reads=["batok"], writes=["beta"])
        S.op("dve", lambda e: e.tensor_scalar(out=nbeta[:], in0=beta[:], scalar1=-1.0, scalar2=None, op0=ALU.mult), reads=["beta"], writes=["nbeta"])
        S.op("act", lambda e: e.activation(out=tmpa[:], in_=batok[:, :, 1], func=AF.Exp, bias=gsc[:, 1:2], scale=1.0), reads=["batok", "gsc"], writes=["tmpa"])
        one_t = sb("one_t", [128, 1], F32)
        S.op("dve", lambda e: e.memset(one_t[:], 1.0), writes=["one_t"])
        S.op("act", lambda e: e.activation(out=tmpa[:], in_=tmpa[:], func=AF.Ln, bias=one_t[:, 0:1], scale=1.0), reads=["tmpa", "one_t"], writes=["tmpa"])
        S.op("dve", lambda e: e.tensor_scalar(out=gg[:], in0=tmpa[:], scalar1=gsc[:, 3:4], scalar2=None, op0=ALU.mult), reads=["tmpa", "gsc3"], writes=["gg"])
        key = mm_group(C, (lambda: C.ps[1][:, 0:NCH]), [(umask[:], gg[:])], 1, reads=["umask", "gg"])
        S.op("dve", lambda e: e.tensor_copy(out=gc[:], in_=C.ps[1][:, 0:NCH]), reads=[], writes=[key, "gc"])
        key = mm_group(C, (lambda: C.ps[2][:, 0:NCH]), [(onesf[:], gg[:])], 2, reads=["onesf", "gg"])
        S.op("dve", lambda e: e.tensor_copy(out=gce[:], in_=C.ps[2][:, 0:NCH]), reads=[], writes=[key, "gce"])
        S.op("act", lambda e: e.activation(out=egc[:], in_=gc[:], func=AF.Exp), reads=["gc"], writes=["egc"])
        S.op("act", lambda e: e.activation(out=gend[:], in_=gce[:], func=AF.Exp), reads=["gce"], writes=["gend"])
        S.op("dve", lambda e: e.tensor_tensor(out=bke[:], in0=beta[:], in1=egc[:], op=ALU.mult), reads=["beta", "egc"], writes=["bke"])
        S.op("dve", lambda e: e.tensor_tensor(out=eend[:], in0=gce[:], in1=gc[:], op=ALU.subtract), reads=["gce", "gc"], writes=["eend"])
        S.op("act", lambda e: e.activation(out=eend[:], in_=eend[:], func=AF.Exp), reads=["eend"], writes=["eend"])
        PERTOK = ["beta", "nbeta", "gg", "egc", "bke", "eend", "gend"]
        if stop_after == 1:
            S.drain_dmas("sp"); S.replay(); return

        QnT = sb("QnT", [128, SEQ], BF16)
        KnT = sb("KnT", [128, SEQ], BF16)
        Kb = sb("Kb", [128, NCH, 128], BF16)
        Kend = sb("Kend", [128, NCH, 128], BF16)
        Vb = sb("Vb", [128, NCH, 128], BF16)
        xin = [[sb("xin%d_%d" % (j, i), [128, 3 + TT], F32) for i in range(2)] for j in range(3)]
        cacc = [sb("cacc%d" % j, [128, TT], F32) for j in range(3)]
        sact = [sb("sact%d" % j, [128, TT], F32) for j in range(3)]
        sq2 = [sb("sq2%d" % j, [128, TT], F32) for j in range(2)]
        lnt = sb("lnt", [128, TT], F32)
        rr = [sb("rr%d" % j, [128, TT], F32) for j in range(2)]
        knf = sb("knf", [128, TT], F32)
        names3 = ("gqT", "gkT", "gvT")
        ntile_a = (nchunks * GC + TT - 1) // TT

        def phase_a(ti):
            c0 = ti * TT
            b = ti % 2
            for j in range(3):
                xt = xin[j][b]
                xk = "xin%d_%d" % (j, b)
                if ti == 0:
                    S.op("pool", lambda e, xt=xt: e.memset(xt[:, 0:3], 0.0), writes=[xk])
                    S.dma("sp", lambda e, xt=xt, j=j: e.dma_start(out=xt[:, 3:3 + TT], in_=d[names3[j]][:, 0:TT]),
                          reads=["d_" + names3[j]], writes=[xk])
                else:
                    S.dma("sp", lambda e, xt=xt, j=j: e.dma_start(out=xt[:, :], in_=d[names3[j]][:, c0 - 3:c0 + TT]),
                          reads=["d_" + names3[j]], writes=[xk])
                ck = "cacc%d" % j
                S.op("dve", lambda e, xt=xt, j=j: e.tensor_scalar(out=cacc[j][:], in0=xt[:, 0:TT], scalar1=cw[:, 4 * j:4 * j + 1],
                                                                 scalar2=None, op0=ALU.mult), reads=[xk, "cw"], writes=[ck])
                for tap in range(1, 4):
                    S.op("dve", lambda e, xt=xt, j=j, tap=tap: e.scalar_tensor_tensor(
                        out=cacc[j][:], in0=xt[:, tap:tap + TT], scalar=cw[:, 4 * j + tap:4 * j + tap + 1], in1=cacc[j][:],
                        op0=ALU.mult, op1=ALU.add), reads=[xk, "cw", ck], writes=[ck])
                S.op("act", lambda e, j=j: e.activation(out=sact[j][:], in_=cacc[j][:], func=AF.Silu), reads=[ck], writes=["sact%d" % j])
            for j in range(2):
                S.op("act", lambda e, j=j: e.activation(out=sq2[j][:], in_=sact[j][:], func=AF.Square), reads=["sact%d" % j], writes=["sq2%d" % j])
                bk = C.bank()
                key = mm_group(C, (lambda bk=bk: C.ps[bk][:, :]), [(onesf[:], sq2[j][:])], bk, reads=["onesf", "sq2%d" % j])
                S.op("act", lambda e, bk=bk: e.activation(out=lnt[:], in_=C.ps[bk][:, :], func=AF.Ln, bias=eps_t[:, 0:1], scale=1.0),
                     reads=["eps"], writes=[key, "lnt"])
                S.op("act", lambda e, j=j: e.activation(out=rr[j][:], in_=lnt[:], func=AF.Exp, scale=-0.5), reads=["lnt"], writes=["rr%d" % j])
            S.op("dve", lambda e: e.scalar_tensor_tensor(out=QnT[:, c0:c0 + TT], in0=sact[0][:], scalar=float(128 ** -0.5), in1=rr[0][:],
                                                         op0=ALU.mult, op1=ALU.mult), reads=["sact0", "rr0"], writes=["QnT"])
            S.op("dve", lambda e: e.tensor_tensor(out=knf[:], in0=sact[1][:], in1=rr[1][:], op=ALU.mult), reads=["sact1", "rr1"], writes=["knf"])
            S.op("act", lambda e: e.activation(out=KnT[:, c0:c0 + TT], in_=knf[:], func=AF.Copy), reads=["knf"], writes=["KnT"])
            for a in range(4):
                c = ti * 4 + a
                bk = C.bank()
                key = "ps%d" % bk
                S.op("pe", lambda e, bk=bk, a=a: e.transpose(C.ps[bk][:, 0:128], knf[:, a * 128:(a + 1) * 128], identf[:]),
                     reads=["knf", "ident"], writes=[key])
                S.op("pe", lambda e, bk=bk, a=a: e.transpose(C.ps[bk][:, 128:256], sact[2][:, a * 128:(a + 1) * 128], identf[:]),
                     reads=["sact2", "ident"], writes=[key])
                S.op("act", lambda e, bk=bk, c=c: e.activation(out=Kb[:, c, :], in_=C.ps[bk][:, 0:128], func=AF.Copy, scale=bke[:, c:c + 1]),
                     reads=["bke"], writes=[key, "Kb"])
                S.op("dve", lambda e, bk=bk, c=c: e.tensor_scalar(out=Kend[:, c, :], in0=C.ps[bk][:, 0:128], scalar1=eend[:, c:c + 1], scalar2=None, op0=ALU.mult),
                     reads=["eend"], writes=[key, "Kend"])
                S.op("act", lambda e, bk=bk, c=c: e.activation(out=Vb[:, c, :], in_=C.ps[bk][:, 128:256], func=AF.Copy, scale=beta[:, c:c + 1]),
                     reads=["beta"], writes=[key, "Vb"])
        for ti in range(ntile_a):
            phase_a(ti)
        if stop_after == 2:
            S.drain_dmas("sp"); S.replay(); return

        def bufset(name, dt, n=NSET):
            return [sb("%s%d" % (name, i), [128, 128], dt) for i in range(n)]
        G1 = bufset("G1", F32)
        Dm = bufset("Dm", F32)
        Xf = bufset("Xf", F32)
        Xb_ = bufset("Xb", BF16)
        Yb = bufset("Yb", BF16)
        Mb = bufset("Mb", BF16)
        Xo = bufset("Xo", BF16)
        Yo = bufset("Yo", BF16)
        Hb = bufset("Hb", BF16)
        Gb = bufset("Gb", BF16)
        Am = bufset("Am", F32)
        AT = bufset("AT", BF16)
        TTb = bufset("TTb", BF16)
        WT = bufset("WT", BF16)
        U0 = bufset("U0", F32)
        Ub = bufset("Ub", BF16, 2)
        Sf = sb("Sf", [128, 128], F32)
        Sbb = [sb("Sbb%d" % i, [128, 128], BF16) for i in range(2)]
        otmp = sb("otmp", [128, 128], F32)
        osb = sb("osb", [128, 128], F32)
        osq = sb("osq", [128, 128], F32)
        onr = sb("onr", [128, 128], F32)
        ssq = sb("ssq", [128, 1], F32)
        lno = sb("lno", [128, 1], F32)
        rso = sb("rso", [128, 1], F32)
        gz = [sb("gz%d" % i, [128, TT], BF16) for i in range(2)]
        yst = [sb("yst%d" % i, [128, TT], BF16) for i in range(2)]
        if C.fused:
            ystm = [[sb("ystm%d_%d" % (i, j), [128, TT], BF16) for j in range(4)] for i in range(2)]
            hm = sb("hm", [128, 4], F32)
            gnwm = sb("gnwm", [128, 4], F32)
            S.dma("sp", lambda e: e.dma_start(out=hm[:], in_=d["hm"][:, :]), reads=["d_hm"], writes=["hm"])
            S.op("dve", lambda e: e.tensor_scalar(out=gnwm[:], in0=hm[:], scalar1=gnw[:, 0:1], scalar2=None, op0=ALU.mult),
                 reads=["hm", "gnw"], writes=["gnwm"])
        S.op("dve", lambda e: e.memset(Sf[:], 0.0), writes=["Sf"])
        S.op("dve", lambda e: e.memset(Sbb[0][:], 0.0), writes=["Sbb0"])

        def mm1(out_bank, lhsT, rhs, reads, cols=128):
            return mm_group(C, (lambda: C.ps[out_bank][:, 0:cols]), [(lhsT, rhs)], out_bank, reads=reads)

        def pre_steps(c):
            s = c % NSET
            ck = slice(c * GC, (c + 1) * GC)
            k = lambda nm: "%s%d" % (nm, s)
            steps = []
            st8 = {}

            def s1a():
                S.op("act", lambda e: e.activation(out=G1[s][:], in_=umask[:], func=AF.Copy, scale=gg[:, c:c + 1]),
                     reads=["umask", "gg"], writes=[k("G1")])
            steps.append(s1a)

            def s1b():
                bk = C.bank()
                key = mm_group(C, (lambda: C.ps[bk][:, 0:128]), [(G1[s][:], m2[:]), (identf[:], neg[:])], bk, reads=[k("G1"), "m2", "ident", "neg"])
                S.op("act", lambda e: e.activation(out=Dm[s][:], in_=C.ps[bk][:, 0:128], func=AF.Exp), reads=[], writes=[key, k("Dm")])
            steps.append(s1b)

            def s2():
                bk = C.bank()
                key = mm1(bk, KnT[:, ck], KnT[:, ck], ["KnT"])
                S.op("dve", lambda e: e.scalar_tensor_tensor(out=Xf[s][:], in0=C.ps[bk][:, 0:128], scalar=nbeta[:, c:c + 1], in1=Dm[s][:],
                                                             op0=ALU.mult, op1=ALU.mult), reads=["nbeta", k("Dm")], writes=[key, k("Xf")])
                S.op("dve", lambda e: e.tensor_tensor(out=Xf[s][:], in0=Xf[s][:], in1=slm[:], op=ALU.mult), reads=[k("Xf"), "slm"], writes=[k("Xf")])
                S.op("act", lambda e: e.activation(out=Xb_[s][:], in_=Xf[s][:], func=AF.Copy), reads=[k("Xf")], writes=[k("Xb")])
            steps.append(s2)

            def s3a():
                bk = C.bank()
                key = mm1(bk, QnT[:, ck], KnT[:, ck], ["QnT", "KnT"])
                S.op("dve", lambda e: e.tensor_tensor(out=Am[s][:], in0=C.ps[bk][:, 0:128], in1=Dm[s][:], op=ALU.mult),
                     reads=[k("Dm")], writes=[key, k("Am")])
            steps.append(s3a)

            def s4():
                bk = C.bank()
                key = "ps%d" % bk
                S.op("pe", lambda e: e.transpose(C.ps[bk][:, 0:128], Xf[s][:], identf[:]), reads=[k("Xf"), "ident"], writes=[key])
                S.op("act", lambda e: e.activation(out=Yb[s][:], in_=C.ps[bk][:, 0:128], func=AF.Copy), reads=[], writes=[key, k("Yb")])
                S.op("pool", lambda e: e.tensor_tensor(out=Xo[s][:], in0=Xb_[s][:], in1=lvm[:, 0:128], op=ALU.mult),
                     reads=[k("Xb"), "lvm"], writes=[k("Xo")])
                S.op("dve", lambda e: e.tensor_tensor(out=Mb[s][:], in0=Xo[s][:], in1=identb[:], op=ALU.add),
                     reads=[k("Xo"), "identb"], writes=[k("Mb")])
            steps.append(s4)

            def s3b():
                bk2 = C.bank()
                key2 = "ps%d" % bk2
                S.op("pe", lambda e: e.transpose(C.ps[bk2][:, 0:128], Am[s][:], identf[:]), reads=[k("Am"), "ident"], writes=[key2])
                S.op("act", lambda e: e.activation(out=AT[s][:], in_=C.ps[bk2][:, 0:128], func=AF.Copy), reads=[], writes=[key2, k("AT")])
                S.op("pool", lambda e: e.tensor_tensor(out=Yo[s][:], in0=Yb[s][:], in1=lvm[:, 128:256], op=ALU.mult),
                     reads=[k("Yb"), "lvm"], writes=[k("Yo")])
                S.op("dve", lambda e: e.tensor_tensor(out=TTb[s][:], in0=Yo[s][:], in1=identb[:], op=ALU.add),
                     reads=[k("Yo"), "identb"], writes=[k("TTb")])
            steps.append(s3b)
            for l in range(1, 7):
                def la(l=l):
                    if l <= 5:
                        S.op("pool", lambda e: e.tensor_tensor(out=Yo[s][:], in0=Yb[s][:], in1=lvm[:, (2 * l + 1) * 128:(2 * l + 2) * 128], op=ALU.mult),
                             reads=[k("Yb"), "lvm"], writes=[k("Yo")])
                    S.op("pool", lambda e: e.tensor_tensor(out=Xo[s][:], in0=Xb_[s][:], in1=lvm[:, (2 * l) * 128:(2 * l + 1) * 128], op=ALU.mult),
                         reads=[k("Xb"), "lvm"], writes=[k("Xo")])
                steps.append(la)

                def lb(l=l):
                    if l <= 5:
                        bh = C.bank()
                        keyh = mm1(bh, Yo[s][:], Mb[s][:], [k("Yo"), k("Mb")])
                    bg = C.bank()
                    keyg = mm1(bg, Xo[s][:], TTb[s][:], [k("Xo"), k("TTb")])
                    if l <= 5:
                        S.op("act", lambda e: e.activation(out=Hb[s][:], in_=C.ps[bh][:, 0:128], func=AF.Copy), reads=[], writes=[keyh, k("Hb")])
                    if l % 2 == 0:
                        S.op("act", lambda e: e.activation(out=Gb[s][:], in_=C.ps[bg][:, 0:128], func=AF.Copy), reads=[], writes=[keyg, k("Gb")])
                    else:
                        S.op("dve", lambda e: e.tensor_copy(out=Gb[s][:], in_=C.ps[bg][:, 0:128]), reads=[], writes=[keyg, k("Gb")])
                steps.append(lb)

                def lc(l=l):
                    if l <= 5:
                        bm = C.bank()
                        keym = mm1(bm, TTb[s][:], Hb[s][:], [k("TTb"), k("Hb")])
                    bw = C.bank()
                    keyw = mm1(bw, Mb[s][:], Gb[s][:], [k("Mb"), k("Gb")])
                    if l <= 5:
                        S.op("dve", lambda e: e.tensor_tensor(out=Mb[s][:], in0=Mb[s][:], in1=C.ps[bm][:, 0:128], op=ALU.add),
                             reads=[k("Mb")], writes=[keym, k("Mb")])
                    S.op("dve", lambda e: e.tensor_tensor(out=TTb[s][:], in0=TTb[s][:], in1=C.ps[bw][:, 0:128], op=ALU.add),
                         reads=[k("TTb")], writes=[keyw, k("TTb")])
                steps.append(lc)

            def s5():
                bk = C.bank()
                key = mm1(bk, Kb[:, c, :], TTb[s][:], ["Kb", k("TTb")])
                bk2 = C.bank()
                key2 = mm1(bk2, TTb[s][:], Vb[:, c, :], ["Vb", k("TTb")])
                S.op("act", lambda e: e.activation(out=WT[s][:], in_=C.ps[bk][:, 0:128], func=AF.Copy), reads=[], writes=[key, k("WT")])
                S.op("dve", lambda e: e.tensor_copy(out=U0[s][:], in_=C.ps[bk2][:, 0:128]), reads=[], writes=[key2, k("U0")])
            steps.append(s5)
            return steps

        def scan_steps(c):
            s = c % NSET
            ck = slice(c * GC, (c + 1) * GC)
            k = lambda nm: "%s%d" % (nm, s)
            u = c % 2
            sbi, sbo = c % 2, (c + 1) % 2
            stt = {}

            def sa():
                b1 = C.bank()
                key1 = mm1(b1, WT[s][:], Sbb[sbi][:], [k("WT"), "Sbb%d" % sbi])
                b2 = C.bank()
                key2 = mm1(b2, QnT[:, ck], Sbb[sbi][:], ["QnT", "Sbb%d" % sbi])
                S.op("dve", lambda e: e.tensor_tensor(out=Ub[u][:], in0=U0[s][:], in1=C.ps[b1][:, 0:128], op=ALU.subtract),
                     reads=[k("U0")], writes=[key1, "Ub%d" % u])
                S.op("act", lambda e: e.activation(out=otmp[:], in_=C.ps[b2][:, 0:128], func=AF.Copy, scale=egc[:, c:c + 1]),
                     reads=["egc"], writes=[key2, "otmp"])

            def sb_():
                b4 = C.bank()
                key4 = mm1(b4, Kend[:, c, :], Ub[u][:], ["Kend", "Ub%d" % u])
                b3 = C.bank()
                key3 = mm1(b3, AT[s][:], Ub[u][:], [k("AT"), "Ub%d" % u])
                S.op("dve", lambda e: e.scalar_tensor_tensor(out=Sf[:], in0=Sf[:], scalar=gend[:, c:c + 1], in1=C.ps[b4][:, 0:128],
                                                             op0=ALU.mult, op1=ALU.add), reads=["gend", "Sf"], writes=[key4, "Sf"])
                S.op("act", lambda e: e.activation(out=Sbb[sbo][:], in_=Sf[:], func=AF.Copy), reads=["Sf"], writes=["Sbb%d" % sbo])
                S.op("dve", lambda e: e.tensor_tensor(out=osb[:], in0=otmp[:], in1=C.ps[b3][:, 0:128], op=ALU.add),
                     reads=["otmp"], writes=[key3, "osb"])

            def sc():
                S.op("act", lambda e: e.activation(out=osq[:], in_=osb[:], func=AF.Square, accum_out=ssq[:, 0:1]), reads=["osb"], writes=["osq", "ssq"])
                S.op("act", lambda e: e.activation(out=lno[:], in_=ssq[:], func=AF.Ln, bias=eps_t[:, 0:1], scale=1.0 / 128), reads=["ssq", "eps"], writes=["lno"])
                S.op("act", lambda e: e.activation(out=rso[:], in_=lno[:], func=AF.Exp, scale=-0.5), reads=["lno"], writes=["rso"])
                S.op("act", lambda e: e.activation(out=onr[:], in_=osb[:], func=AF.Copy, scale=rso[:, 0:1]),
                     reads=["osb", "rso"], writes=["onr"])

            def sd():
                b5 = C.bank()
                key5 = "ps%d" % b5
                S.op("pe", lambda e: e.transpose(C.ps[b5][:, 0:128], onr[:], identf[:]), reads=["onr", "ident"], writes=[key5])
                yb = (c // 4) % 2
                a = c % 4
                if a == 0:
                    tz = (c // 4) * TT
                    S.dma("sp", lambda e: e.dma_start(out=gz[yb][:], in_=d["gzT"][:, tz:tz + TT]), reads=["d_gzT"], writes=["gz%d" % yb])
                if not C.fused:
                    S.op("dve", lambda e: e.scalar_tensor_tensor(out=yst[yb][:, a * 128:(a + 1) * 128], in0=C.ps[b5][:, 0:128], scalar=gnw[:, 0:1],
                                                                 in1=gz[yb][:, a * 128:(a + 1) * 128], op0=ALU.mult, op1=ALU.mult),
                         reads=["gnw", "gz%d" % yb], writes=[key5, "yst%d" % yb])
                    if a == 3:
                        t0 = (c // 4) * TT
                        S.dma("sp", lambda e: e.dma_start(out=d["yT_h"][128:256, t0:t0 + TT], in_=yst[yb][:]), reads=["yst%d" % yb], writes=["d_yT_h"])
                else:
                    for j in range(4):
                        S.op("dve", lambda e, j=j: e.scalar_tensor_tensor(out=ystm[yb][j][:, a * 128:(a + 1) * 128], in0=C.ps[b5][:, 0:128],
                                                                          scalar=gnwm[:, j:j + 1], in1=gz[yb][:, a * 128:(a + 1) * 128],
                                                                          op0=ALU.mult, op1=ALU.mult),
                             reads=["gnwm", "gz%d" % yb], writes=[key5, "ystm%d_%d" % (yb, j)])
                    if a == 3:
                        t0 = (c // 4) * TT
                        qq, qc = t0 // 2048, t0 % 2048
                        for j in range(4):
                            S.dma("sp", lambda e, j=j: e.dma_start(out=d["ypad%d" % qq][256 * j + 128:256 * j + 256, qc:qc + TT], in_=ystm[yb][j][:]),
                                  reads=["ystm%d_%d" % (yb, j)], writes=["d_ypad%d" % qq])
            return [sa, sb_, sc, sd]

        groups = [list(range(g, min(g + G, nchunks))) for g in range(0, nchunks, G)]
        prev = []
        for grp in groups + [[]]:
            lists = [pre_steps(c) for c in grp]
            nst = max([len(l) for l in lists] + [0])
            pending = []
            for c in prev:
                pending += scan_steps(c)
            for si in range(nst):
                for l in lists:
                    if si < len(l):
                        l[si]()
                if pending:
                    pending.pop(0)()
            while pending:
                pending.pop(0)()
            prev = grp
        S.drain_dmas("sp")
        S.replay()


CONST_INS = [("ctab", (32, SEQ), F32), ("stab", (32, SEQ), F32), ("ident", (128, 128), F32), ("sel", (65, 64), F32),
             ("tri", (128, 128), F32), ("OH", (32, 3072), F32), ("Eb", (32, SEQ), BF16), ("pen", (32, 32), F32),
             ("umask", (128, 128), F32), ("m2", (128, 128), F32), ("neg", (128, 128), F32), ("slm", (128, 128), F32),
             ("lvlm", (128, 14 * 128), F32)]
LAYER_INS = PROJ_INS + [("t5h", (32, 1)), ("cw", (128, 12)), ("gsc", (1, 2)), ("gnw", (128, 1))]


def host_consts():
    import ml_dtypes
    ctab, stab = rope_consts()
    pen, E = moba_consts()
    umask, m2, neg, slm = gdn_consts()
    sel = np.zeros((65, 64), np.float32)
    sel[64, :] = 1.0
    return {"ctab": ctab, "stab": stab, "ident": np.eye(128, dtype=np.float32), "sel": sel,
            "tri": np.triu(np.ones((128, 128), np.float32)), "OH": t5_consts(), "Eb": E.astype(ml_dtypes.bfloat16),
            "pen": pen, "umask": umask, "m2": m2, "neg": neg, "slm": slm, "lvlm": level_masks()}


def prep_layer_all(I, l, h):
    out = prep_layer(I, l, h)
    f = np.float32
    out["t5h"] = np.ascontiguousarray(I["t5_table"][:, h:h + 1]).astype(f)
    cwf = I["gdn_conv_w"][l]
    cw = np.zeros((128, 12), f)
    for j in range(3):
        for tap in range(4):
            cw[:, 4 * j + tap] = cwf[tap, j * 512 + 128 * h:j * 512 + 128 * h + 128]
    out["cw"] = cw
    out["gsc"] = np.array([[I["gdn_A_log"][l, h], I["gdn_dt_bias"][l, h]]], f)
    out["gnw"] = np.ascontiguousarray(I["gdn_norm_w"][l].reshape(128, 1)).astype(f)
    return out


def wo_perm():
    rows = []
    for h in range(4):
        rows += list(range(64 * h, 64 * h + 64))
        rows += list(range(768 + 64 * h, 768 + 64 * h + 64))
        rows += list(range(256 + 128 * h, 256 + 128 * h + 128))
    return rows


def build_layer_program(has_prev, do_layer, final):
    nc = bass.Bass("TRN2", target_bir_lowering=False)
    with ExitStack() as st:
        C = Ctx(nc, st)
        C.din("xT", (DM, SEQ), F32)
        if has_prev:
            C.din("yT", (DM, SEQ), BF16)
            C.din("wo", (DM, DM), F32)
            if not final:
                C.dout("xTo", (DM, SEQ), F32)
        if final:
            C.din("fnw", (128, 8), F32)
            C.dout("outT", (DM, SEQ), F32)
        if do_layer:
            for n, s_, dt in CONST_INS:
                C.din(n, s_, dt)
            for n, s_ in LAYER_INS:
                C.din(n, s_, F32)
            for n, s_, dt in PROJ_OUTS:
                C.dint(n, s_, dt)
            C.dint("fv", (1, 3072), F32)
            C.dout("yT_h", (256, SEQ), BF16)
        stage_proj(C, has_prev=has_prev, do_proj=do_layer, final=final)
        if do_layer:
            stage_attn(C, False)
            stage_attn(C, True)
            stage_gdn(C)
    return nc


def build_fused_program(depth=DEPTH, do_coll=True):
    nc = bass.Bass("TRN2", target_bir_lowering=False)
    with ExitStack() as st:
        C = Ctx(nc, st)
        C.fused = True
        S = C.S
        C.din("xT", (DM, SEQ), F32)
        C.din("hm", (128, 4), F32)
        C.din("fnw", (128, 8), F32)
        for n, s_, dt in CONST_INS:
            C.din(n, s_, dt)
        for l in range(depth):
            for n, s_ in LAYER_INS:
                C.din("%s_%d" % (n, l), s_, F32)
            C.din("wo_%d" % l, (DM, DM), F32)
        C.dout("outT", (DM, SEQ), F32)
        C.dint("xTi", (DM, SEQ), F32)
        for n, s_, dt in PROJ_OUTS:
            C.dint(n, s_, dt)
        C.dint("fv", (1, 3072), F32)
        for q in range(4):
            C.dint("ypad%d" % q, (DM, 2048), BF16)
            C.dint("yg%d" % q, (DM, 2048), BF16)
        x_ext = C.d["xT"]
        for l in range(depth):
            for n, s_ in LAYER_INS:
                C.d[n] = C.d["%s_%d" % (n, l)]
            if l > 0:
                C.d["wo"] = C.d["wo_%d" % (l - 1)]
            C.d["xT"] = x_ext if l <= 1 else C.d["xTi"]
            C.d["xTo"] = C.d["xTi"]
            import os
            ST = os.environ.get("STAGES", "pamg")
            if "p" in ST:
                stage_proj(C, has_prev=(l > 0), do_proj=True, final=False)
            if "a" in ST:
                stage_attn(C, False)
            if "m" in ST:
                stage_attn(C, True)
            if "g" in ST:
                stage_gdn(C)
            for q in range(4 if do_coll else 0):
                S.coll(lambda e, q=q: e.collective_compute(
                    "AllReduce", ALU.add, replica_groups=[[0, 1, 2, 3], [4, 5, 6, 7]],
                    ins=[C.d["ypad%d" % q].opt()], outs=[C.d["yg%d" % q].opt()]),
                    reads=["d_ypad%d" % q], writes=["d_yg%d" % q])
            S.drain_dmas("sp")
            S.replay()
        C.d["wo"] = C.d["wo_%d" % (depth - 1)]
        C.d["xT"] = C.d["xTi"] if depth > 1 else x_ext
        if "f" in os.environ.get("STAGES", "pamgf"):
            stage_proj(C, has_prev=True, do_proj=False, final=True)
    return nc


_FUSED = []


def kernel(x, norm_w, w_in, mla_q_norm, mla_w_uq, mla_kv_norm, mla_w_ukv, gdn_conv_w, gdn_A_log, gdn_dt_bias,
           gdn_norm_w, w_out, t5_table, final_norm_w):
    I = dict(x=np.asarray(x), norm_w=np.asarray(norm_w), w_in=np.asarray(w_in), mla_q_norm=np.asarray(mla_q_norm),
             mla_w_uq=np.asarray(mla_w_uq), mla_kv_norm=np.asarray(mla_kv_norm), mla_w_ukv=np.asarray(mla_w_ukv),
             gdn_conv_w=np.asarray(gdn_conv_w), gdn_A_log=np.asarray(gdn_A_log), gdn_dt_bias=np.asarray(gdn_dt_bias),
             gdn_norm_w=np.asarray(gdn_norm_w), w_out=np.asarray(w_out), t5_table=np.asarray(t5_table),
             final_norm_w=np.asarray(final_norm_w))
    if not _FUSED:
        _FUSED.append(build_fused_program())
    nc = _FUSED[0]
    consts = host_consts()
    perm = wo_perm()
    fnw = np.ascontiguousarray(I["final_norm_w"].reshape(8, 128).T).astype(np.float32)
    wos = [np.ascontiguousarray(I["w_out"][l][perm, :]).astype(np.float32) for l in range(DEPTH)]
    xT = [np.ascontiguousarray(I["x"][b].T).astype(np.float32) for b in range(2)]
    maps = []
    for c in range(8):
        b, h = c // 4, c % 4
        m = dict(consts)
        m["xT"] = xT[b]
        hm = np.zeros((128, 4), np.float32)
        hm[:, h] = 1.0
        m["hm"] = hm
        m["fnw"] = fnw
        for l in range(DEPTH):
            for k, v in prep_layer_all(I, l, h).items():
                m["%s_%d" % (k, l)] = v
            m["wo_%d" % l] = wos[l]
        maps.append(m)
    res = run_bass_kernel_spmd(nc, maps, core_ids=list(range(8)))
    out = np.stack([np.asarray(res.results[4 * b]["outT"]).T for b in range(2)], axis=0)
    return np.ascontiguousarray(out).astype(np.float32)
```

```python
import math
import os
from contextlib import ExitStack

import numpy as np
import concourse.bass as bass
import concourse.mybir as mybir
from concourse.bass_utils import run_bass_kernel_spmd

F32 = mybir.dt.float32
BF16 = mybir.dt.bfloat16
AF = mybir.ActivationFunctionType
ALU = mybir.AluOpType
AX = mybir.AxisListType

SEQ = 8192
DM = 1024
DEPTH = 4
TT = 512
NT = SEQ // TT
EPS = 1e-6
TW = [128] * 8 + [96, 96, 64, 2]
TOFF = [sum(TW[:i]) for i in range(len(TW))]
NCOL = sum(TW)

ENGS = ("pe", "act", "dve", "pool", "sp")


class Sched:
    def __init__(self, nc, stack, n_dma_sems=40):
        self.nc = nc
        self.q = {e: [] for e in ENGS}
        self.sem = {e: stack.enter_context(nc.semaphore("s_" + e)) for e in ENGS if e != "sp"}
        self.cnt = {e: 0 for e in ENGS}
        self.seen = {e: {} for e in ENGS}
        self.dsem = [stack.enter_context(nc.semaphore("d%d" % i)) for i in range(n_dma_sems)]
        self.dcnt = [0] * n_dma_sems
        self.dnext = 0
        self.lastw = {}
        self.readers = {}
        self.semobj = dict(self.sem)
        for i, s in enumerate(self.dsem):
            self.semobj["d%d" % i] = s
        self.semobj["cc"] = stack.enter_context(nc.semaphore("s_cc"))
        self.cccnt = 0

    def _deps(self, eng, reads, writes):
        deps = {}

        def add(src, val, kind):
            if src == eng and eng == "pe":
                return
            if deps.get(src, 0) < val:
                deps[src] = val
        for k in reads:
            w = self.lastw.get(k)
            if w:
                add(w[0], w[1], "raw")
        for k in writes:
            w = self.lastw.get(k)
            if w:
                add(w[0], w[1], "waw")
            for s, v in self.readers.get(k, {}).items():
                add(s, v, "war")
        waits = []
        for s, v in deps.items():
            if self.seen[eng].get(s, 0) < v:
                self.seen[eng][s] = v
                waits.append((s, v))
        return waits

    def _commit(self, src, val, reads, writes):
        for k in reads:
            self.readers.setdefault(k, {})[src] = val
        for k in writes:
            self.lastw[k] = (src, val)
            self.readers[k] = {}

    def op(self, eng, fn, reads=(), writes=(), sig=True):
        waits = self._deps(eng, reads, writes)
        val = self.cnt[eng] + 1
        if sig:
            self.cnt[eng] = val
        self._commit(eng, val, reads, writes)
        self.q[eng].append((waits, fn, (eng, 1) if sig else None))

    def dma(self, eng, fn, reads=(), writes=()):
        i = self.dnext
        self.dnext = (self.dnext + 1) % len(self.dsem)
        name = "d%d" % i
        waits = self._deps(eng, reads, writes)
        if self.dcnt[i] > 0 and self.seen[eng].get(name, 0) < self.dcnt[i]:
            self.seen[eng][name] = self.dcnt[i]
            waits.append((name, self.dcnt[i]))
        self.dcnt[i] += 16
        self._commit(name, self.dcnt[i], reads, writes)
        self.q[eng].append((waits, fn, (name, 16)))

    def coll(self, fn, reads=(), writes=()):
        if "cc" not in self.semobj:
            raise RuntimeError("no cc semaphore")
        waits = self._deps("pool", reads, writes)
        self.cccnt += 1
        self._commit("cc", self.cccnt, reads, writes)
        self.q["pool"].append((waits, fn, ("cc", 1)))

    def drain_dmas(self, eng="sp", include_cc=False):
        waits = []
        for i in range(len(self.dsem)):
            name = "d%d" % i
            if self.dcnt[i] > 0 and self.seen[eng].get(name, 0) < self.dcnt[i]:
                self.seen[eng][name] = self.dcnt[i]
                waits.append((name, self.dcnt[i]))
        if include_cc and self.cccnt > 0 and self.seen[eng].get("cc", 0) < self.cccnt:
            self.seen[eng]["cc"] = self.cccnt
            waits.append(("cc", self.cccnt))
        self.q[eng].append((waits, None, None))

    def replay(self):
        nc = self.nc
        with nc.Block() as block:
            engmap = {"pe": block.tensor, "act": block.scalar, "dve": block.vector,
                      "pool": block.gpsimd, "sp": block.sync}
            for e in ENGS:
                items = self.q[e]
                if not items:
                    continue

                def body(engine, items=items):
                    for waits, fn, inc in items:
                        for s, v in waits:
                            engine.wait_ge(self.semobj[s], v)
                        if fn is None:
                            continue
                        ins = fn(engine)
                        if inc is not None:
                            ins.then_inc(self.semobj[inc[0]], inc[1])
                engmap[e](body)
        self.q = {e: [] for e in ENGS}


class Ctx:
    def __init__(self, nc, stack):
        self.nc = nc
        self.S = Sched(nc, stack)
        self.ps = [stack.enter_context(nc.psum_tensor("ps%d" % i, [128, 512], F32)) for i in range(8)]
        self.psn = 0
        self.uid = 0
        self.fused = False
        self.d = {}

    def bank(self):
        i = self.psn
        self.psn = (self.psn + 1) % 8
        return i

    def din(self, name, shape, dt):
        t = self.nc.dram_tensor(name, list(shape), dt, kind="ExternalInput").ap()
        self.d[name] = t
        return t

    def dout(self, name, shape, dt):
        t = self.nc.dram_tensor(name, list(shape), dt, kind="ExternalOutput").ap()
        self.d[name] = t
        return t

    def dint(self, name, shape, dt):
        t = self.nc.dram_tensor(name, list(shape), dt, kind="Internal").ap()
        self.d[name] = t
        return t


def mm_group(C, out_fn, pairs, bank, reads, extra_writes=()):
    S = C.S
    n = len(pairs)
    key = "ps%d" % bank
    for i, (l, r) in enumerate(pairs):
        S.op("pe", (lambda e, l=l, r=r, i=i: e.matmul(out_fn(), l, r, start=(i == 0), stop=(i == n - 1))),
             reads=reads, writes=[key] + list(extra_writes), sig=(i == n - 1))
    return key


def stage_proj(C, has_prev, do_proj, final=False, tiles=range(NT)):
    nc, S, d = C.nc, C.S, C.d
    with ExitStack() as st:
        def sb(name, shape, dt):
            C.uid += 1
            return st.enter_context(nc.sbuf_tensor("sb%d_%s" % (C.uid, name), list(shape), dt))
        xs = [sb("xs%d" % i, [128, 8, TT], F32) for i in range(2)]
        eps_t = sb("eps_t", [128, 1], F32)
        S.op("dve", lambda e: e.memset(eps_t[:], EPS), writes=["eps"])
        if has_prev:
            ys = [sb("ys%d" % i, [128, 8, TT], BF16) for i in range(2)]
            wo = sb("wo", [128, 8, DM], BF16)
            wstage = sb("wstage", [128, NCOL], F32)
            for k in range(8):
                S.dma("sp", lambda e, k=k: e.dma_start(out=wstage[:, 0:DM], in_=d["wo"][k * 128:(k + 1) * 128, :]),
                      reads=["d_wo"], writes=["wstage"])
                if k % 2 == 0:
                    S.op("dve", lambda e, k=k: e.tensor_copy(out=wo[:, k, :], in_=wstage[:, 0:DM]),
                         reads=["wstage"], writes=["wo"])
                else:
                    S.op("act", lambda e, k=k: e.activation(out=wo[:, k, :], in_=wstage[:, 0:DM], func=AF.Copy),
                         reads=["wstage"], writes=["wo"])
        if do_proj or final:
            xsq = sb("xsq", [128, 8, TT], BF16)
            ones = sb("ones", [128, 128], BF16)
            S.op("dve", lambda e: e.memset(ones[:], 1.0), writes=["ones"])
            rstd1 = sb("rstd1", [128, TT], F32)
            lnt = sb("lnt", [128, TT], F32)
        if final:
            fnw = sb("fnw", [128, 8], F32)
            S.dma("sp", lambda e: e.dma_start(out=fnw[:], in_=d["fnw"][:, :]), reads=["d_fnw"], writes=["fnw"])
            outs = sb("outs", [128, 8, TT], F32)
        if do_proj:
            if not has_prev:
                wstage = sb("wstage", [128, NCOL], F32)
            xb = sb("xb", [128, 8, TT], BF16)
            win = sb("win", [128, 8, NCOL], BF16)
            nw = sb("nw", [128, 8], F32)
            S.dma("sp", lambda e: e.dma_start(out=nw[:], in_=d["nw"][:, :]), reads=["d_nw"], writes=["nw"])
            for k in range(8):
                S.dma("sp", lambda e, k=k: e.dma_start(out=wstage[:], in_=d["w_in"][k * 128:(k + 1) * 128, :]),
                      reads=["d_w_in"], writes=["wstage"])
                S.op("dve", lambda e, k=k: e.tensor_scalar(out=win[:, k, :], in0=wstage[:], scalar1=nw[:, k:k + 1],
                                                           scalar2=None, op0=ALU.mult),
                     reads=["wstage", "nw"], writes=["win"])
            wuq = sb("wuq", [128, 2, 192], BF16)
            qnw = sb("qnw", [128, 2], F32)
            S.dma("sp", lambda e: e.dma_start(out=qnw[:], in_=d["qnw"][:, :]), reads=["d_qnw"], writes=["qnw"])
            for k in range(2):
                S.dma("sp", lambda e, k=k: e.dma_start(out=wstage[:, 0:192], in_=d["wuq"][k * 128:(k + 1) * 128, :]),
                      reads=["d_wuq"], writes=["wstage"])
                S.op("dve", lambda e, k=k: e.tensor_scalar(out=wuq[:, k, :], in0=wstage[:, 0:192], scalar1=qnw[:, k:k + 1],
                                                           scalar2=None, op0=ALU.mult),
                     reads=["wstage", "qnw"], writes=["wuq"])
            wukv = sb("wukv", [128, 128], BF16)
            kvnw = sb("kvnw", [128, 1], F32)
            S.dma("sp", lambda e: e.dma_start(out=kvnw[:], in_=d["kvnw"][:, :]), reads=["d_kvnw"], writes=["kvnw"])
            S.dma("sp", lambda e: e.dma_start(out=wstage[:, 0:128], in_=d["wukv"][:, :]), reads=["d_wukv"], writes=["wstage"])
            S.op("dve", lambda e: e.tensor_scalar(out=wukv[:], in0=wstage[:, 0:128], scalar1=kvnw[:, 0:1],
                                                  scalar2=None, op0=ALU.mult),
                 reads=["wstage", "kvnw"], writes=["wukv"])
            identf = sb("identf", [128, 128], F32)
            identb = sb("identb", [128, 128], BF16)
            S.dma("sp", lambda e: e.dma_start(out=identf[:], in_=d["ident"][:, :]), reads=["d_ident"], writes=["identf"])
            S.op("dve", lambda e: e.tensor_copy(out=identb[:], in_=identf[:]), reads=["identf"], writes=["identb"])
            ctab = sb("ctab", [128, TT], F32)
            stab = sb("stab", [128, TT], F32)
            cqf = sb("cqf", [128, 2, TT], F32)
            cqsq = sb("cqsq", [128, 2, TT], BF16)
            cqb = sb("cqb", [128, 2, TT], BF16)
            rstd2 = sb("rstd2", [128, TT], F32)
            rstd3 = sb("rstd3", [128, TT], F32)
            ckvf = sb("ckvf", [128, TT], F32)
            ckvsq = sb("ckvsq", [128, TT], BF16)
            ckvb = sb("ckvb", [128, TT], BF16)
            QTs = sb("QTs", [96, TT], BF16)
            KTs = sb("KTs", [96, TT], BF16)
            t1 = sb("t1", [128, TT], F32)
            t2 = sb("t2", [128, TT], F32)
            t3 = sb("t3", [128, TT], F32)
            t4 = sb("t4", [128, TT], F32)
            vT = sb("vT", [64, TT], BF16)
            Vs = sb("Vs", [128, 4, 64], BF16)
            mvT = sb("mvT", [64, TT], BF16)
            mVs = sb("mVs", [128, 4, 64], BF16)
            mqf = sb("mqf", [64, TT], F32)
            mqb = sb("mqb", [64, TT], BF16)
            mkf = sb("mkf", [64, TT], F32)
            mkb = sb("mkb", [64, TT], BF16)
            km = sb("km", [64, 32], F32)
            gf = sb("gf", [128, TT], F32)
            gs = sb("gs", [128, TT], BF16)
            gq3 = [sb("gq3_%d" % i, [128, TT], F32) for i in range(3)]
            zf = sb("zf", [128, TT], F32)
            zs = sb("zs", [128, TT], BF16)
            bas = sb("bas", [2, TT], F32)

        def rstd_from(bank, n, out_t, rows=128):
            key = "ps%d" % bank
            S.op("act", lambda e: e.activation(out=lnt[0:rows, :], in_=C.ps[bank][0:rows, :], func=AF.Ln,
                                               bias=eps_t[0:rows, 0:1], scale=1.0 / n),
                 reads=["eps"], writes=[key, "lnt"])
            S.op("act", lambda e: e.activation(out=out_t[0:rows, :], in_=lnt[0:rows, :], func=AF.Exp, scale=-0.5),
                 reads=["lnt"], writes=[out_t.name if hasattr(out_t, "name") else "rstd"])

        def do_tile(ti):
            b = ti % 2
            c0 = ti * TT
            xk = "xs%d" % b
            xsb = xs[b]
            for k4 in range(2):
                S.dma("sp", lambda e, k4=k4, xsb=xsb: e.dma_start(
                    out=xsb[:, 4 * k4:4 * k4 + 4, :],
                    in_=d["xT"][4 * k4 * 128:(4 * k4 + 4) * 128, c0:c0 + TT].rearrange("(k p) t -> p k t", p=128)),
                    reads=["d_xT"], writes=[xk])
            if has_prev:
                yk = "ys%d" % b
                ysb = ys[b]
                S.dma("sp", lambda e, ysb=ysb: e.dma_start(
                    out=ysb[:], in_=(d["yg%d" % (c0 // 2048)][:, c0 % 2048:c0 % 2048 + TT] if C.fused else d["yT"][:, c0:c0 + TT]
                                      ).rearrange("(k p) t -> p k t", p=128)),
                    reads=[("d_yg%d" % (c0 // 2048)) if C.fused else "d_yT"], writes=[yk])
                for c in range(8):
                    bk = C.bank()
                    key = mm_group(C, (lambda bk=bk: C.ps[bk][:, :]),
                                   [(wo[:, k, c * 128:(c + 1) * 128], ysb[:, k, :]) for k in range(8)],
                                   bk, reads=["wo", yk])
                    S.op("dve", lambda e, c=c, bk=bk, xsb=xsb: e.tensor_tensor(out=xsb[:, c, :], in0=C.ps[bk][:, :], in1=xsb[:, c, :], op=ALU.add),
                         reads=[], writes=[key, xk])
                if not final:
                    for k4 in range(2):
                        S.dma("sp", lambda e, k4=k4, xsb=xsb: e.dma_start(
                            out=d["xTo"][4 * k4 * 128:(4 * k4 + 4) * 128, c0:c0 + TT].rearrange("(k p) t -> p k t", p=128),
                            in_=xsb[:, 4 * k4:4 * k4 + 4, :]),
                            reads=[xk], writes=["d_xTo"])
            if not (do_proj or final):
                return
            S.op("act", lambda e, xsb=xsb: e.activation(out=xsq[:], in_=xsb[:], func=AF.Square), reads=[xk], writes=["xsq"])
            bk = C.bank()
            key = mm_group(C, (lambda bk=bk: C.ps[bk][:, :]), [(ones[:], xsq[:, k, :]) for k in range(8)], bk,
                           reads=["ones", "xsq"])
            S.op("act", lambda e, bk=bk: e.activation(out=lnt[:], in_=C.ps[bk][:, :], func=AF.Ln, bias=eps_t[:, 0:1], scale=1.0 / DM),
                 reads=["eps"], writes=[key, "lnt"])
            S.op("act", lambda e: e.activation(out=rstd1[:], in_=lnt[:], func=AF.Exp, scale=-0.5), reads=["lnt"], writes=["rstd1"])
            if final:
                for k in range(8):
                    S.op("dve", lambda e, k=k, xsb=xsb: e.scalar_tensor_tensor(
                        out=outs[:, k, :], in0=xsb[:, k, :], scalar=fnw[:, k:k + 1], in1=rstd1[:], op0=ALU.mult, op1=ALU.mult),
                        reads=[xk, "fnw", "rstd1"], writes=["outs"])
                for k4 in range(2):
                    S.dma("sp", lambda e, k4=k4: e.dma_start(
                        out=d["outT"][4 * k4 * 128:(4 * k4 + 4) * 128, c0:c0 + TT].rearrange("(k p) t -> p k t", p=128),
                        in_=outs[:, 4 * k4:4 * k4 + 4, :]),
                        reads=["outs"], writes=["d_outT"])
                return
            S.op("dve", lambda e, xsb=xsb: e.tensor_copy(out=xb[:], in_=xsb[:]), reads=[xk], writes=["xb"])
            S.dma("sp", lambda e: e.dma_start(out=ctab[64:96, :], in_=d["ctab"][:, c0:c0 + TT]), reads=["d_ctab"], writes=["ctab"])
            S.dma("sp", lambda e: e.dma_start(out=stab[64:96, :], in_=d["stab"][:, c0:c0 + TT]), reads=["d_stab"], writes=["stab"])

            def proj_tile(t):
                bk = C.bank()
                w = TW[t]
                key = mm_group(C, (lambda bk=bk, w=w: C.ps[bk][0:w, :]),
                               [(win[:, k, TOFF[t]:TOFF[t] + w], xb[:, k, :]) for k in range(8)], bk,
                               reads=["win", "xb"])
                return bk, key

            for j in range(2):
                bk, key = proj_tile(j)
                S.op("dve", lambda e, bk=bk, j=j: e.tensor_tensor(out=cqf[:, j, :], in0=C.ps[bk][:, :], in1=rstd1[:], op=ALU.mult),
                     reads=["rstd1"], writes=[key, "cqf"])
            S.op("act", lambda e: e.activation(out=cqsq[:], in_=cqf[:], func=AF.Square), reads=["cqf"], writes=["cqsq"])
            S.op("dve", lambda e: e.tensor_copy(out=cqb[:], in_=cqf[:]), reads=["cqf"], writes=["cqb"])
            bk = C.bank()
            key = mm_group(C, (lambda bk=bk: C.ps[bk][:, :]), [(ones[:], cqsq[:, j, :]) for j in range(2)], bk, reads=["ones", "cqsq"])
            S.op("act", lambda e, bk=bk: e.activation(out=lnt[:], in_=C.ps[bk][:, :], func=AF.Ln, bias=eps_t[:, 0:1], scale=1.0 / 256),
                 reads=["eps"], writes=[key, "lnt"])
            S.op("act", lambda e: e.activation(out=rstd2[:], in_=lnt[:], func=AF.Exp, scale=-0.5), reads=["lnt"], writes=["rstd2"])
            bq = C.bank()
            keyq = mm_group(C, (lambda bq=bq: C.ps[bq][0:96, :]), [(wuq[:, j, 0:96], cqb[:, j, :]) for j in range(2)], bq, reads=["wuq", "cqb"])
            br = C.bank()
            keyr = mm_group(C, (lambda br=br: C.ps[br][0:96, :]), [(wuq[:, j, 96:192], cqb[:, j, :]) for j in range(2)], br, reads=["wuq", "cqb"])
            S.op("dve", lambda e, bq=bq: e.tensor_tensor(out=QTs[0:64, :], in0=C.ps[bq][0:64, :], in1=rstd2[0:64, :], op=ALU.mult),
                 reads=["rstd2"], writes=[keyq, "QTs"])
            S.op("dve", lambda e, bq=bq: e.tensor_tensor(out=t1[64:96, :], in0=C.ps[bq][64:96, :], in1=ctab[64:96, :], op=ALU.mult),
                 reads=["ctab"], writes=[keyq, "t1"])
            S.op("dve", lambda e, br=br: e.tensor_tensor(out=t2[64:96, :], in0=C.ps[br][64:96, :], in1=stab[64:96, :], op=ALU.mult),
                 reads=["stab"], writes=[keyr, "t2"])
            S.op("dve", lambda e: e.tensor_tensor(out=t1[64:96, :], in0=t1[64:96, :], in1=t2[64:96, :], op=ALU.add),
                 reads=["t2", "t1"], writes=["t1"])
            S.op("dve", lambda e: e.tensor_tensor(out=QTs[64:96, :], in0=t1[64:96, :], in1=rstd2[64:96, :], op=ALU.mult),
                 reads=["t1", "rstd2"], writes=["QTs"])
            S.dma("sp", lambda e: e.dma_start(out=d["QT_mla"][:, c0:c0 + TT], in_=QTs[:]), reads=["QTs"], writes=["d_QT_mla"])
            bk, key = proj_tile(2)
            S.op("dve", lambda e, bk=bk: e.tensor_tensor(out=ckvf[:], in0=C.ps[bk][:, :], in1=rstd1[:], op=ALU.mult),
                 reads=["rstd1"], writes=[key, "ckvf"])
            S.op("act", lambda e: e.activation(out=ckvsq[:], in_=ckvf[:], func=AF.Square), reads=["ckvf"], writes=["ckvsq"])
            S.op("act", lambda e: e.activation(out=ckvb[:], in_=ckvf[:], func=AF.Copy), reads=["ckvf"], writes=["ckvb"])
            bk = C.bank()
            key = mm_group(C, (lambda bk=bk: C.ps[bk][:, :]), [(ones[:], ckvsq[:])], bk, reads=["ones", "ckvsq"])
            S.op("act", lambda e, bk=bk: e.activation(out=lnt[:], in_=C.ps[bk][:, :], func=AF.Ln, bias=eps_t[:, 0:1], scale=1.0 / 128),
                 reads=["eps"], writes=[key, "lnt"])
            S.op("act", lambda e: e.activation(out=rstd3[:], in_=lnt[:], func=AF.Exp, scale=-0.5), reads=["lnt"], writes=["rstd3"])
            bk = C.bank()
            key = mm_group(C, (lambda bk=bk: C.ps[bk][0:64, :]), [(wukv[:, 0:64], ckvb[:])], bk, reads=["wukv", "ckvb"])
            S.op("dve", lambda e, bk=bk: e.tensor_tensor(out=KTs[0:64, :], in0=C.ps[bk][0:64, :], in1=rstd3[0:64, :], op=ALU.mult),
                 reads=["rstd3"], writes=[key, "KTs"])
            bk = C.bank()
            key = mm_group(C, (lambda bk=bk: C.ps[bk][0:64, :]), [(wukv[:, 64:128], ckvb[:])], bk, reads=["wukv", "ckvb"])
            S.op("dve", lambda e, bk=bk: e.tensor_tensor(out=vT[:], in0=C.ps[bk][0:64, :], in1=rstd3[0:64, :], op=ALU.mult),
                 reads=["rstd3"], writes=[key, "vT"])

            def transpose_v(src, srck, dst, dstk, dname):
                bk = C.bank()
                key = "ps%d" % bk
                for j in range(4):
                    S.op("pe", lambda e, j=j, bk=bk: e.transpose(
                        C.ps[bk][:, :].bitcast(BF16)[:, j * 64:(j + 1) * 64], src[0:64, j * 128:(j + 1) * 128], identb[0:64, 0:64]),
                        reads=[srck, "identb"], writes=[key], sig=(j == 3))
                S.op("act", lambda e, bk=bk: e.activation(out=dst[:].rearrange("p a b -> p (a b)"),
                                                          in_=C.ps[bk][:, :].bitcast(BF16)[:, 0:256], func=AF.Copy),
                     reads=[], writes=[key, dstk])
                S.dma("sp", lambda e: e.dma_start(
                    out=d[dname][c0:c0 + TT, :].rearrange("(a p) v -> p a v", p=128), in_=dst[:]),
                    reads=[dstk], writes=["d_" + dname])
            transpose_v(vT, "vT", Vs, "Vs", "V_mla")
            bk, key = proj_tile(8)
            S.op("dve", lambda e, bk=bk: e.tensor_tensor(out=mqf[:], in0=C.ps[bk][0:64, :], in1=rstd1[0:64, :], op=ALU.mult),
                 reads=["rstd1"], writes=[key, "mqf"])
            S.op("dve", lambda e, bk=bk: e.tensor_tensor(out=t3[64:96, :], in0=C.ps[bk][64:96, :], in1=ctab[64:96, :], op=ALU.mult),
                 reads=["ctab"], writes=[key, "t3"])
            S.op("act", lambda e: e.activation(out=mqb[:], in_=mqf[:], func=AF.Copy, scale=0.125),
                 reads=["mqf"], writes=["mqb"])
            S.dma("sp", lambda e: e.dma_start(out=d["QTf_moba"][:, c0:c0 + TT], in_=mqf[:]), reads=["mqf"], writes=["d_QTf_moba"])
            S.dma("sp", lambda e: e.dma_start(out=d["QT_moba"][:, c0:c0 + TT], in_=mqb[:]), reads=["mqb"], writes=["d_QT_moba"])
            bk, key = proj_tile(9)
            S.op("dve", lambda e, bk=bk: e.tensor_tensor(out=mkf[:], in0=C.ps[bk][0:64, :], in1=rstd1[0:64, :], op=ALU.mult),
                 reads=["rstd1"], writes=[key, "mkf"])
            S.op("dve", lambda e, bk=bk: e.tensor_tensor(out=t4[64:96, :], in0=C.ps[bk][64:96, :], in1=stab[64:96, :], op=ALU.mult),
                 reads=["stab"], writes=[key, "t4"])
            S.op("act", lambda e: e.activation(out=mkb[:], in_=mkf[:], func=AF.Copy), reads=["mkf"], writes=["mkb"])
            S.op("dve", lambda e: e.tensor_reduce(out=km[:, 2 * ti:2 * ti + 2], in_=mkf[:].rearrange("p (a b) -> p a b", b=256),
                                                  op=ALU.add, axis=AX.X),
                 reads=["mkf"], writes=["km"])
            S.dma("sp", lambda e: e.dma_start(out=d["KT_moba"][:, c0:c0 + TT], in_=mkb[:]), reads=["mkb"], writes=["d_KT_moba"])
            S.op("dve", lambda e: e.tensor_tensor(out=t3[64:96, :], in0=t3[64:96, :], in1=t4[64:96, :], op=ALU.add),
                 reads=["t3", "t4"], writes=["t3"])
            S.op("dve", lambda e: e.tensor_tensor(out=KTs[64:96, :], in0=t3[64:96, :], in1=rstd1[64:96, :], op=ALU.mult),
                 reads=["t3", "rstd1"], writes=["KTs"])
            S.dma("sp", lambda e: e.dma_start(out=d["KT_mla"][:, c0:c0 + TT], in_=KTs[:]), reads=["KTs"], writes=["d_KT_mla"])
            bk, key = proj_tile(10)
            S.op("dve", lambda e, bk=bk: e.tensor_tensor(out=mvT[:], in0=C.ps[bk][0:64, :], in1=rstd1[0:64, :], op=ALU.mult),
                 reads=["rstd1"], writes=[key, "mvT"])
            transpose_v(mvT, "mvT", mVs, "mVs", "V_moba")
            bk, key = proj_tile(3)
            S.op("dve", lambda e, bk=bk: e.tensor_tensor(out=gf[:], in0=C.ps[bk][:, :], in1=rstd1[:], op=ALU.mult),
                 reads=["rstd1"], writes=[key, "gf"])
            S.op("act", lambda e: e.activation(out=gs[:], in_=gf[:], func=AF.Silu), reads=["gf"], writes=["gs"])
            S.dma("sp", lambda e: e.dma_start(out=d["GT_mla"][:, c0:c0 + TT], in_=gs[0:64, :]), reads=["gs"], writes=["d_GT_mla"])
            S.dma("sp", lambda e: e.dma_start(out=d["GT_moba"][:, c0:c0 + TT], in_=gs[64:128, :]), reads=["gs"], writes=["d_GT_moba"])
            for j, nm in enumerate(("gqT", "gkT", "gvT")):
                bk, key = proj_tile(4 + j)
                S.op("dve", lambda e, bk=bk, j=j: e.tensor_tensor(out=gq3[j][:], in0=C.ps[bk][:, :], in1=rstd1[:], op=ALU.mult),
                     reads=["rstd1"], writes=[key, "gq3_%d" % j])
                S.dma("sp", lambda e, j=j, nm=nm: e.dma_start(out=d[nm][:, c0:c0 + TT], in_=gq3[j][:]),
                      reads=["gq3_%d" % j], writes=["d_" + nm])
            bk, key = proj_tile(7)
            S.op("dve", lambda e, bk=bk: e.tensor_tensor(out=zf[:], in0=C.ps[bk][:, :], in1=rstd1[:], op=ALU.mult),
                 reads=["rstd1"], writes=[key, "zf"])
            S.op("act", lambda e: e.activation(out=zs[:], in_=zf[:], func=AF.Silu), reads=["zf"], writes=["zs"])
            S.dma("sp", lambda e: e.dma_start(out=d["gzT"][:, c0:c0 + TT], in_=zs[:]), reads=["zs"], writes=["d_gzT"])
            bk, key = proj_tile(11)
            S.op("dve", lambda e, bk=bk: e.tensor_tensor(out=bas[:], in0=C.ps[bk][0:2, :], in1=rstd1[0:2, :], op=ALU.mult),
                 reads=["rstd1"], writes=[key, "bas"])
            S.dma("sp", lambda e: e.dma_start(out=d["baT"][:, c0:c0 + TT], in_=bas[:]), reads=["bas"], writes=["d_baT"])
        for ti in tiles:
            do_tile(ti)
        if do_proj and not final:
            S.op("dve", lambda e: e.tensor_scalar(out=km[:], in0=km[:], scalar1=1.0 / 256, scalar2=None, op0=ALU.mult),
                 reads=["km"], writes=["km"])
            S.dma("sp", lambda e: e.dma_start(out=d["kmT"][:, :], in_=km[:]), reads=["km"], writes=["d_kmT"])
        S.drain_dmas("sp")
        S.replay()


def rope_consts():
    inv = (1.0 / (10000.0 ** (np.arange(0, 32, 2, dtype=np.float32) / 32))).astype(np.float32)
    ang = (np.arange(SEQ, dtype=np.float32)[:, None] * inv[None, :]).astype(np.float32)
    cos = np.cos(ang).astype(np.float32).T
    sin = np.sin(ang).astype(np.float32).T
    ctab = np.ascontiguousarray(np.concatenate([cos, cos], 0))
    stab = np.ascontiguousarray(np.concatenate([-sin, sin], 0))
    return ctab, stab


def in_cols(h):
    cq0, ckv0, kr0, mg0, gq0, gk0, gv0, gz0, gb0, ga0, mq0, mk0, mv0, cg0 = (
        0, 256, 384, 416, 672, 1184, 1696, 2208, 2720, 2724, 2728, 2984, 3240, 3496)
    r = lambda a, n: list(range(a, a + n))
    cols = []
    cols += r(cq0, 256) + r(ckv0, 128)
    cols += r(mg0 + 64 * h, 64) + r(cg0 + 64 * h, 64)
    cols += r(gq0 + 128 * h, 128) + r(gk0 + 128 * h, 128) + r(gv0 + 128 * h, 128) + r(gz0 + 128 * h, 128)
    cols += r(mq0 + 64 * h, 64) + r(kr0, 32)
    cols += r(mk0 + 64 * h, 64) + r(kr0 + 16, 16) + r(kr0, 16)
    cols += r(mv0 + 64 * h, 64)
    cols += [gb0 + h, ga0 + h]
    assert len(cols) == NCOL
    return cols


def prep_layer(I, l, h):
    f = np.float32
    out = {}
    out["w_in"] = np.ascontiguousarray(I["w_in"][l][:, in_cols(h)]).astype(f)
    out["nw"] = np.ascontiguousarray(I["norm_w"][l].reshape(8, 128).T).astype(f)
    wq = I["mla_w_uq"][l][:, 96 * h:96 * h + 96]
    wuq = np.zeros((256, 192), f)
    wuq[:, 0:96] = wq
    wuq[:, 160:176] = wq[:, 80:96]
    wuq[:, 176:192] = wq[:, 64:80]
    out["wuq"] = wuq
    out["qnw"] = np.ascontiguousarray(I["mla_q_norm"][l].reshape(2, 128).T).astype(f)
    out["wukv"] = np.ascontiguousarray(I["mla_w_ukv"][l][:, 128 * h:128 * h + 128]).astype(f)
    out["kvnw"] = np.ascontiguousarray(I["mla_kv_norm"][l].reshape(128, 1)).astype(f)
    return out


PROJ_OUTS = [("QT_mla", (96, SEQ), BF16), ("KT_mla", (96, SEQ), BF16), ("V_mla", (SEQ, 64), BF16),
             ("QTf_moba", (64, SEQ), F32), ("QT_moba", (64, SEQ), BF16), ("KT_moba", (64, SEQ), BF16),
             ("V_moba", (SEQ, 64), BF16), ("kmT", (64, 32), F32),
             ("GT_mla", (64, SEQ), BF16), ("GT_moba", (64, SEQ), BF16),
             ("gqT", (128, SEQ), F32), ("gkT", (128, SEQ), F32), ("gvT", (128, SEQ), F32),
             ("gzT", (128, SEQ), BF16), ("baT", (2, SEQ), F32)]
PROJ_INS = [("w_in", (DM, NCOL)), ("nw", (128, 8)), ("wuq", (256, 192)), ("qnw", (128, 2)),
            ("wukv", (128, 128)), ("kvnw", (128, 1))]


def t5_consts():
    e = np.arange(3072)
    dd = e - 511
    n = np.maximum(dd, 0)
    nf = np.maximum(n, 1).astype(np.float32)
    large = 16 + (np.log(nf / np.float32(16)) / np.float32(math.log(2048 / 16)) * np.float32(16)).astype(np.int32)
    large = np.minimum(large, 31)
    bucket = np.where(n < 16, n, large)
    OH = np.zeros((32, 3072), np.float32)
    OH[bucket, e] = 1.0
    OH[:, dd < 0] = 0.0
    return OH


def moba_consts():
    pen = np.zeros((32, 32), np.float32)
    for own in range(32):
        pen[own, own] = 1e30
        pen[own, own + 1:] = -1e30
    E = np.zeros((32, SEQ), np.float32)
    for j in range(32):
        E[j, j * 256:(j + 1) * 256] = 1.0
    return pen, E


def stage_attn(C, moba, qtiles=range(NT)):
    nc, S, d = C.nc, C.S, C.d
    pre = "moba" if moba else "mla"
    KD = 96
    scale = 1.0 if moba else 96 ** -0.5
    rowbase = 64 if moba else 0
    with ExitStack() as st:
        def sb(name, shape, dt):
            C.uid += 1
            return st.enter_context(nc.sbuf_tensor("sb%d_%s" % (C.uid, name), list(shape), dt))
        QT = sb("QT", [96, SEQ], BF16)
        KT = sb("KT", [96, SEQ], BF16)
        Va = sb("Va", [128, 64, 65], BF16)
        GT = sb("GT", [64, SEQ], BF16)
        Pb = [sb("P%d" % i, [128, TT], BF16) for i in range(4)]
        osb = sb("osb", [65, TT], F32)
        rec = sb("rec", [64, TT], F32)
        otmp = sb("otmp", [64, TT], F32)
        ysb = sb("ysb", [64, TT], BF16)
        if C.fused:
            ym = [sb("ym%d" % j, [64, TT], BF16) for j in range(4)]
            hm = sb("hm", [128, 4], F32)
            S.dma("sp", lambda e: e.dma_start(out=hm[:], in_=d["hm"][:, :]), reads=["d_hm"], writes=["hm"])
        sel = sb("sel", [65, 64], F32)
        nrows = 64 if moba else 96
        for q4 in range(4):
            cs = slice(q4 * 2048, (q4 + 1) * 2048)
            S.dma("sp", lambda e, cs=cs: e.dma_start(out=QT[0:nrows, cs], in_=d["QT_" + pre][:, cs]), reads=["d_QT_" + pre], writes=["QT"])
            S.dma("sp", lambda e, cs=cs: e.dma_start(out=KT[0:nrows, cs], in_=d["KT_" + pre][:, cs]), reads=["d_KT_" + pre], writes=["KT"])
            S.dma("sp", lambda e, cs=cs: e.dma_start(out=GT[:, cs], in_=d["GT_" + pre][:, cs]), reads=["d_GT_" + pre], writes=["GT"])
        for a4 in range(16):
            S.dma("sp", lambda e, a4=a4: e.dma_start(
                out=Va[:, 4 * a4:4 * a4 + 4, 0:64],
                in_=d["V_" + pre][a4 * 512:(a4 + 1) * 512, :].rearrange("(a p) v -> p a v", p=128)),
                reads=["d_V_" + pre], writes=["Va"])
        S.op("dve", lambda e: e.memset(Va[:, :, 64:65], 1.0), writes=["Va"])
        S.dma("sp", lambda e: e.dma_start(out=sel[:], in_=d["sel"][:, :]), reads=["d_sel"], writes=["sel"])
        if not moba:
            trif = sb("trif", [128, 128], F32)
            tri = sb("tri", [128, 128], BF16)
            S.dma("sp", lambda e: e.dma_start(out=trif[:], in_=d["tri"][:, :]), reads=["d_tri"], writes=["trif"])
            S.op("dve", lambda e: e.tensor_copy(out=tri[:], in_=trif[:]), reads=["trif"], writes=["tri"])
        else:
            t5c = sb("t5c", [32, 1], F32)
            et = sb("et", [32, 1], F32)
            b31 = sb("b31", [128, 1], F32)
            OH = sb("OH", [32, 3072], F32)
            fvs = sb("fvs", [1, 3072], F32)
            ES32 = sb("ES32", [128, 2560], F32)
            ES = sb("ES", [128, 2560], BF16)
            S.dma("sp", lambda e: e.dma_start(out=t5c[:], in_=d["t5h"][:, :]), reads=["d_t5h"], writes=["t5c"])
            S.dma("sp", lambda e: e.dma_start(out=b31[:], in_=d["t5h"][31:32, :].partition_broadcast(128).rearrange("p a b -> p (a b)")),
                  reads=["d_t5h"], writes=["b31"])
            S.dma("sp", lambda e: e.dma_start(out=OH[:], in_=d["OH"][:, :]), reads=["d_OH"], writes=["OH"])
            S.op("act", lambda e: e.activation(out=et[:], in_=t5c[:], func=AF.Exp), reads=["t5c"], writes=["et"])
            for j in range(6):
                key = mm_group(C, (lambda: C.ps[5][0:1, :]), [(et[:, 0:1], OH[:, j * 512:(j + 1) * 512])], 5, reads=["et", "OH"])
                S.op("dve", lambda e, j=j: e.tensor_copy(out=fvs[:, j * 512:(j + 1) * 512], in_=C.ps[5][0:1, :]), reads=[], writes=[key, "fvs"])
            S.dma("sp", lambda e: e.dma_start(out=d["fv"][:, :], in_=fvs[:]), reads=["fvs"], writes=["d_fv"])
            for ki in range(128):
                S.dma("sp", lambda e, ki=ki: e.dma_start(
                    out=ES32[ki:ki + 1, :], in_=d["fv"][:, 127 - ki:127 - ki + 2560]), reads=["d_fv"], writes=["ES32"])
            S.op("dve", lambda e: e.tensor_copy(out=ES[:], in_=ES32[:]), reads=["ES32"], writes=["ES"])
            for q4 in range(4):
                cs = slice(q4 * 2048, (q4 + 1) * 2048)
                S.dma("sp", lambda e, cs=cs: e.dma_start(out=KT[64:96, cs], in_=d["Eb"][:, cs]), reads=["d_Eb"], writes=["KT"])
            QTf = sb("QTf", [64, SEQ], F32)
            kmT = sb("kmT", [64, 32], F32)
            pen = sb("pen", [128, 32 * 32], F32)
            identf = sb("identf", [128, 128], F32)
            identb = sb("identb", [128, 128], BF16)
            S.dma("sp", lambda e: e.dma_start(out=identf[:], in_=d["ident"][:, :]), reads=["d_ident"], writes=["identf"])
            S.op("dve", lambda e: e.tensor_copy(out=identb[:], in_=identf[:]), reads=["identf"], writes=["identb"])
            for q4 in range(4):
                cs = slice(q4 * 2048, (q4 + 1) * 2048)
                S.dma("sp", lambda e, cs=cs: e.dma_start(out=QTf[:, cs], in_=d["QTf_moba"][:, cs]), reads=["d_QTf_moba"], writes=["QTf"])
            S.dma("sp", lambda e: e.dma_start(out=kmT[:], in_=d["kmT"][:, :]), reads=["d_kmT"], writes=["kmT"])
            S.dma("sp", lambda e: e.dma_start(out=pen[:], in_=d["pen"][:, :].rearrange("a b -> (a b)").partition_broadcast(128)),
                  reads=["d_pen"], writes=["pen"])
            gm = [sb("gm%d" % i, [128, 32], F32) for i in range(4)]
            m8 = [sb("m8%d" % i, [128, 8], F32) for i in range(4)]
            thr = [sb("thr%d" % i, [128, 1], F32) for i in range(4)]
            Mq = [sb("Mq%d" % i, [128, 96], BF16) for i in range(4)]
            for i in range(4):
                S.op("dve", lambda e, i=i: e.memset(Mq[i][:], 0.0), writes=["Mq%d" % i])
            for g in range(16):
                for j in range(4):
                    i = g * 4 + j
                    own = i // 2
                    key = mm_group(C, (lambda j=j: C.ps[6][:, j * 32:(j + 1) * 32]), [(QTf[:, i * 128:(i + 1) * 128], kmT[:, :])], 6,
                                   reads=["QTf", "kmT"])
                    S.op("dve", lambda e, j=j, own=own: e.tensor_tensor(out=gm[j][:], in0=C.ps[6][:, j * 32:(j + 1) * 32],
                                                                      in1=pen[:, own * 32:(own + 1) * 32], op=ALU.add),
                         reads=["pen"], writes=[key, "gm%d" % j])
                    S.op("dve", lambda e, j=j: e.max(out=m8[j][:], in_=gm[j][:]), reads=["gm%d" % j], writes=["m8%d" % j])
                    S.op("dve", lambda e, j=j: e.tensor_scalar(out=thr[j][:], in0=m8[j][:, 3:4], scalar1=-1e29, scalar2=None, op0=ALU.max),
                         reads=["m8%d" % j], writes=["thr%d" % j])
                    S.op("dve", lambda e, j=j: e.tensor_scalar(out=Mq[j][:, 64:96], in0=gm[j][:], scalar1=thr[j][:, 0:1], scalar2=-30000.0,
                                                               op0=ALU.is_lt, op1=ALU.mult),
                         reads=["gm%d" % j, "thr%d" % j], writes=["Mq%d" % j])
                    S.op("pe", lambda e, j=j: e.transpose(C.ps[7][0:96, :].bitcast(BF16)[:, j * 128:(j + 1) * 128], Mq[j][:], identb[:]),
                         reads=["Mq%d" % j, "identb"], writes=["ps7"], sig=True)
                S.op("act", lambda e, g=g: e.activation(out=QT[64:96, g * 512:(g + 1) * 512],
                                                        in_=C.ps[7][64:96, :].bitcast(BF16)[:, 0:512], func=AF.Copy),
                     reads=[], writes=["ps7", "QT"])

        SB = (0, 1, 2, 3)
        OB = (4, 5)
        NB = 4
        DEPTHQ = 3
        items = []
        for qi, qt in enumerate(qtiles):
            for kt in range(4 * qt + 4):
                items.append((qi, qt, kt))

        def front(idx):
            qi, qt, kt = items[idx]
            q0 = qt * TT
            k0 = kt * 128
            diag = kt >= 4 * qt
            koff = (kt - 4 * qt) * 128 if diag else 0
            sbk = SB[idx % NB]
            skey = "ps%d" % sbk
            P = Pb[idx % NB]
            pkey = "P%d" % (idx % NB)
            S.op("pe", lambda e: e.matmul(
                C.ps[sbk][:, koff:TT], KT[0:KD, k0:k0 + 128], QT[0:KD, q0 + koff:q0 + TT], start=True, stop=True),
                reads=["KT", "QT"], writes=[skey])
            far = moba and (q0 - k0 >= 1664)
            if far:
                S.op("act", lambda e: e.activation(
                    out=P[:, koff:TT], in_=C.ps[sbk][:, koff:TT], func=AF.Exp, bias=b31[:, 0:1], scale=scale),
                    reads=["b31"], writes=[skey, pkey])
            else:
                S.op("act", lambda e: e.activation(
                    out=P[:, koff:TT], in_=C.ps[sbk][:, koff:TT], func=AF.Exp, scale=scale),
                    reads=[], writes=[skey, pkey])
                if moba:
                    s0 = q0 - k0 + 384
                    S.op("pool", lambda e: e.tensor_tensor(
                        out=P[:, koff:TT], in0=P[:, koff:TT], in1=ES[:, s0 + koff:s0 + TT], op=ALU.mult),
                        reads=["ES", pkey], writes=[pkey])
                elif diag:
                    S.op("pool", lambda e: e.tensor_tensor(
                        out=P[:, koff:koff + 128], in0=P[:, koff:koff + 128], in1=tri[:], op=ALU.mult),
                        reads=["tri", pkey], writes=[pkey])

        def back(idx):
            qi, qt, kt = items[idx]
            q0 = qt * TT
            diag = kt >= 4 * qt
            koff = (kt - 4 * qt) * 128 if diag else 0
            nkt = 4 * qt + 4
            ob = OB[qi % 2]
            okey = "ps%d" % ob
            P = Pb[idx % NB]
            pkey = "P%d" % (idx % NB)
            S.op("pe", lambda e: e.matmul(
                C.ps[ob][0:65, koff:TT], Va[:, kt, :], P[:, koff:TT], start=(kt == 0), stop=(kt == nkt - 1)),
                reads=[pkey, "Va"], writes=[okey])
            if kt != nkt - 1:
                return
            S.op("act", lambda e: e.activation(out=osb[:], in_=C.ps[ob][0:65, :], func=AF.Copy), reads=[], writes=[okey, "osb"])
            key = mm_group(C, (lambda: C.ps[6][0:64, :]), [(sel[:], osb[:])], 6, reads=["sel", "osb"])
            S.op("dve", lambda e: e.reciprocal(out=rec[:], in_=C.ps[6][0:64, :]), reads=[], writes=[key, "rec"])
            S.op("dve", lambda e: e.tensor_tensor(out=otmp[:], in0=osb[0:64, :], in1=rec[:], op=ALU.mult), reads=["osb", "rec"], writes=["otmp"])
            if not C.fused:
                S.op("dve", lambda e: e.tensor_tensor(out=ysb[:], in0=otmp[:], in1=GT[:, q0:q0 + TT], op=ALU.mult),
                     reads=["otmp", "GT"], writes=["ysb"])
                S.dma("sp", lambda e: e.dma_start(out=d["yT_h"][rowbase:rowbase + 64, q0:q0 + TT], in_=ysb[:]),
                      reads=["ysb"], writes=["d_yT_h"])
            else:
                qq, qc = q0 // 2048, q0 % 2048
                for j in range(4):
                    S.op("dve", lambda e, j=j: e.scalar_tensor_tensor(out=ym[j][:], in0=otmp[:], scalar=hm[0:64, j:j + 1],
                                                                      in1=GT[:, q0:q0 + TT], op0=ALU.mult, op1=ALU.mult),
                         reads=["otmp", "GT", "hm"], writes=["ym%d" % j])
                    S.dma("sp", lambda e, j=j: e.dma_start(
                        out=d["ypad%d" % qq][256 * j + rowbase:256 * j + rowbase + 64, qc:qc + TT], in_=ym[j][:]),
                        reads=["ym%d" % j], writes=["d_ypad%d" % qq])

        n_it = len(items)
        for idx in range(n_it + DEPTHQ):
            if idx < n_it:
                front(idx)
            if idx - DEPTHQ >= 0:
                back(idx - DEPTHQ)
        S.drain_dmas("sp")
        S.replay()


GC = 128
NCH = SEQ // GC


def gdn_consts():
    i = np.arange(128)
    umask = (i[:, None] <= i[None, :]).astype(np.float32)
    m2 = (i[:, None] > i[None, :]).astype(np.float32)
    neg = np.where(i[:, None] < i[None, :], -30000.0, 0.0).astype(np.float32)
    sl = (i[:, None] > i[None, :]).astype(np.float32)
    return umask, m2, neg, sl


def level_masks():
    i = np.arange(128)
    out = np.zeros((128, 14, 128), np.float32)
    for l in range(7):
        b = 1 << l
        bi = i // b
        m = ((bi[:, None] % 2 == 1) & (bi[None, :] == bi[:, None] - 1)).astype(np.float32)
        out[:, 2 * l, :] = m
        out[:, 2 * l + 1, :] = m.T
    return out.reshape(128, 14 * 128)


def stage_gdn(C, nchunks=NCH, G=4, stop_after=None):
    nc, S, d = C.nc, C.S, C.d
    NSET = 2 * G
    with ExitStack() as st:
        def sb(name, shape, dt):
            C.uid += 1
            return st.enter_context(nc.sbuf_tensor("sb%d_%s" % (C.uid, name), list(shape), dt))
        identf = sb("identf", [128, 128], F32)
        umask = sb("umask", [128, 128], F32)
        m2 = sb("m2", [128, 128], F32)
        neg = sb("neg", [128, 128], F32)
        slm = sb("slm", [128, 128], F32)
        onesf = sb("onesf", [128, 128], F32)
        identb = sb("identb", [128, 128], BF16)
        lvf = sb("lvf", [128, 14 * 128], F32)
        lvm = sb("lvm", [128, 14 * 128], BF16)
        S.dma("sp", lambda e: e.dma_start(out=lvf[:], in_=d["lvlm"][:, :]), reads=["d_lvlm"], writes=["lvf"])
        S.op("dve", lambda e: e.tensor_copy(out=lvm[:], in_=lvf[:]), reads=["lvf"], writes=["lvm"])
        for nm, t in (("ident", identf), ("umask", umask), ("m2", m2), ("neg", neg), ("slm", slm)):
            S.dma("sp", lambda e, nm=nm, t=t: e.dma_start(out=t[:], in_=d[nm][:, :]), reads=["d_" + nm], writes=[nm])
        S.op("dve", lambda e: e.memset(onesf[:], 1.0), writes=["onesf"])
        S.op("dve", lambda e: e.tensor_copy(out=identb[:], in_=identf[:]), reads=["ident"], writes=["identb"])
        eps_t = sb("eps_t", [128, 1], F32)
        S.op("dve", lambda e: e.memset(eps_t[:], EPS), writes=["eps"])
        cw = sb("cw", [128, 12], F32)
        S.dma("sp", lambda e: e.dma_start(out=cw[:], in_=d["cw"][:, :]), reads=["d_cw"], writes=["cw"])
        gsc = sb("gsc", [128, 4], F32)
        S.dma("sp", lambda e: e.dma_start(out=gsc[:, 0:2], in_=d["gsc"][:, :].rearrange("a b -> (a b)").partition_broadcast(128)),
              reads=["d_gsc"], writes=["gsc"])
        gnw = sb("gnw", [128, 1], F32)
        S.dma("sp", lambda e: e.dma_start(out=gnw[:], in_=d["gnw"][:, :]), reads=["d_gnw"], writes=["gnw"])
        S.op("act", lambda e: e.activation(out=gsc[:, 2:3], in_=gsc[:, 0:1], func=AF.Exp), reads=["gsc"], writes=["gsc2"])
        S.op("dve", lambda e: e.tensor_scalar(out=gsc[:, 3:4], in0=gsc[:, 2:3], scalar1=-1.0, scalar2=None, op0=ALU.mult),
             reads=["gsc2"], writes=["gsc3"])
        ba = [sb("ba%d" % i, [2, TT], F32) for i in range(2)]
        batok = sb("batok", [128, NCH, 2], F32)
        for c in range(NCH):
            bi = (c // 4) % 2
            if c % 4 == 0:
                S.dma("sp", lambda e, c=c, bi=bi: e.dma_start(out=ba[bi][:], in_=d["baT"][:, c * 128:c * 128 + TT]),
                      reads=["d_baT"], writes=["ba%d" % bi])
            S.op("pe", lambda e, c=c, bi=bi: e.matmul(C.ps[0][:, 2 * c:2 * c + 2], ba[bi][0:2, (c % 4) * 128:(c % 4 + 1) * 128], identf[0:2, 0:2],
                                                      start=True, stop=True),
                 reads=["ba%d" % bi, "ident"], writes=["ps0"], sig=True)
        S.op("dve", lambda e: e.tensor_copy(out=batok[:].rearrange("p a b -> p (a b)"), in_=C.ps[0][:, 0:2 * NCH]), reads=[], writes=["ps0", "batok"])
        beta = sb("beta", [128, NCH], F32)
        nbeta = sb("nbeta", [128, NCH], F32)
        gg = sb("gg", [128, NCH], F32)
        tmpa = sb("tmpa", [128, NCH], F32)
        gc = sb("gc", [128, NCH], F32)
        gce = sb("gce", [128, NCH], F32)
        egc = sb("egc", [128, NCH], F32)
        bke = sb("bke", [128, NCH], F32)
        eend = sb("eend", [128, NCH], F32)
        gend = sb("gend", [128, NCH], F32)
        S.op("act", lambda e: e.activation(out=beta[:], in_=batok[:, :, 0], func=AF.Sigmoid), reads=["batok"], writes=["beta"])
        S.op("dve", lambda e: e.tensor_scalar(out=nbeta[:], in0=beta[:], scalar1=-1.0, scalar2=None, op0=ALU.mult), reads=["beta"], writes=["nbeta"])
        S.op("act", lambda e: e.activation(out=tmpa[:], in_=batok[:, :, 1], func=AF.Exp, bias=gsc[:, 1:2], scale=1.0), reads=["batok", "gsc"], writes=["tmpa"])
        one_t = sb("one_t", [128, 1], F32)
        S.op("dve", lambda e: e.memset(one_t[:], 1.0), writes=["one_t"])
        S.op("act", lambda e: e.activation(out=tmpa[:], in_=tmpa[:], func=AF.Ln, bias=one_t[:, 0:1], scale=1.0), reads=["tmpa", "one_t"], writes=["tmpa"])
        S.op("dve", lambda e: e.tensor_scalar(out=gg[:], in0=tmpa[:], scalar1=gsc[:, 3:4], scalar2=None, op0=ALU.mult), reads=["tmpa", "gsc3"], writes=["gg"])
        key = mm_group(C, (lambda: C.ps[1][:, 0:NCH]), [(umask[:], gg[:])], 1, reads=["umask", "gg"])
        S.op("dve", lambda e: e.tensor_copy(out=gc[:], in_=C.ps[1][:, 0:NCH]), reads=[], writes=[key, "gc"])
        key = mm_group(C, (lambda: C.ps[2][:, 0:NCH]), [(onesf[:], gg[:])], 2, reads=["onesf", "gg"])
        S.op("dve", lambda e: e.tensor_copy(out=gce[:], in_=C.ps[2][:, 0:NCH]), reads=[], writes=[key, "gce"])
        S.op("act", lambda e: e.activation(out=egc[:], in_=gc[:], func=AF.Exp), reads=["gc"], writes=["egc"])
        S.op("act", lambda e: e.activation(out=gend[:], in_=gce[:], func=AF.Exp), reads=["gce"], writes=["gend"])
        S.op("dve", lambda e: e.tensor_tensor(out=bke[:], in0=beta[:], in1=egc[:], op=ALU.mult), reads=["beta", "egc"], writes=["bke"])
        S.op("dve", lambda e: e.tensor_tensor(out=eend[:], in0=gce[:], in1=gc[:], op=ALU.subtract), reads=["gce", "gc"], writes=["eend"])
        S.op("act", lambda e: e.activation(out=eend[:], in_=eend[:], func=AF.Exp), reads=["eend"], writes=["eend"])
        PERTOK = ["beta", "nbeta", "gg", "egc", "bke", "eend", "gend"]
        if stop_after == 1:
            S.drain_dmas("sp"); S.replay(); return

        QnT = sb("QnT", [128, SEQ], BF16)
        KnT = sb("KnT", [128, SEQ], BF16)
        Kb = sb("Kb", [128, NCH, 128], BF16)
        Kend = sb("Kend", [128, NCH, 128], BF16)
        Vb = sb("Vb", [128, NCH, 128], BF16)
        xin = [[sb("xin%d_%d" % (j, i), [128, 3 + TT], F32) for i in range(2)] for j in range(3)]
        cacc = [sb("cacc%d" % j, [128, TT], F32) for j in range(3)]
        sact = [sb("sact%d" % j, [128, TT], F32) for j in range(3)]
        sq2 = [sb("sq2%d" % j, [128, TT], F32) for j in range(2)]
        lnt = sb("lnt", [128, TT], F32)
        rr = [sb("rr%d" % j, [128, TT], F32) for j in range(2)]
        knf = sb("knf", [128, TT], F32)
        names3 = ("gqT", "gkT", "gvT")
        ntile_a = (nchunks * GC + TT - 1) // TT

        def phase_a(ti):
            c0 = ti * TT
            b = ti % 2
            for j in range(3):
                xt = xin[j][b]
                xk = "xin%d_%d" % (j, b)
                if ti == 0:
                    S.op("pool", lambda e, xt=xt: e.memset(xt[:, 0:3], 0.0), writes=[xk])
                    S.dma("sp", lambda e, xt=xt, j=j: e.dma_start(out=xt[:, 3:3 + TT], in_=d[names3[j]][:, 0:TT]),
                          reads=["d_" + names3[j]], writes=[xk])
                else:
                    S.dma("sp", lambda e, xt=xt, j=j: e.dma_start(out=xt[:, :], in_=d[names3[j]][:, c0 - 3:c0 + TT]),
                          reads=["d_" + names3[j]], writes=[xk])
                ck = "cacc%d" % j
                S.op("dve", lambda e, xt=xt, j=j: e.tensor_scalar(out=cacc[j][:], in0=xt[:, 0:TT], scalar1=cw[:, 4 * j:4 * j + 1],
                                                                 scalar2=None, op0=ALU.mult), reads=[xk, "cw"], writes=[ck])
                for tap in range(1, 4):
                    S.op("dve", lambda e, xt=xt, j=j, tap=tap: e.scalar_tensor_tensor(
                        out=cacc[j][:], in0=xt[:, tap:tap + TT], scalar=cw[:, 4 * j + tap:4 * j + tap + 1], in1=cacc[j][:],
                        op0=ALU.mult, op1=ALU.add), reads=[xk, "cw", ck], writes=[ck])
                S.op("act", lambda e, j=j: e.activation(out=sact[j][:], in_=cacc[j][:], func=AF.Silu), reads=[ck], writes=["sact%d" % j])
            for j in range(2):
                S.op("act", lambda e, j=j: e.activation(out=sq2[j][:], in_=sact[j][:], func=AF.Square), reads=["sact%d" % j], writes=["sq2%d" % j])
                bk = C.bank()
                key = mm_group(C, (lambda bk=bk: C.ps[bk][:, :]), [(onesf[:], sq2[j][:])], bk, reads=["onesf", "sq2%d" % j])
                S.op("act", lambda e, bk=bk: e.activation(out=lnt[:], in_=C.ps[bk][:, :], func=AF.Ln, bias=eps_t[:, 0:1], scale=1.0),
                     reads=["eps"], writes=[key, "lnt"])
                S.op("act", lambda e, j=j: e.activation(out=rr[j][:], in_=lnt[:], func=AF.Exp, scale=-0.5), reads=["lnt"], writes=["rr%d" % j])
            S.op("dve", lambda e: e.scalar_tensor_tensor(out=QnT[:, c0:c0 + TT], in0=sact[0][:], scalar=float(128 ** -0.5), in1=rr[0][:],
                                                         op0=ALU.mult, op1=ALU.mult), reads=["sact0", "rr0"], writes=["QnT"])
            S.op("dve", lambda e: e.tensor_tensor(out=knf[:], in0=sact[1][:], in1=rr[1][:], op=ALU.mult), reads=["sact1", "rr1"], writes=["knf"])
            S.op("act", lambda e: e.activation(out=KnT[:, c0:c0 + TT], in_=knf[:], func=AF.Copy), reads=["knf"], writes=["KnT"])
            for a in range(4):
                c = ti * 4 + a
                bk = C.bank()
                key = "ps%d" % bk
                S.op("pe", lambda e, bk=bk, a=a: e.transpose(C.ps[bk][:, 0:128], knf[:, a * 128:(a + 1) * 128], identf[:]),
                     reads=["knf", "ident"], writes=[key])
                S.op("pe", lambda e, bk=bk, a=a: e.transpose(C.ps[bk][:, 128:256], sact[2][:, a * 128:(a + 1) * 128], identf[:]),
                     reads=["sact2", "ident"], writes=[key])
                S.op("act", lambda e, bk=bk, c=c: e.activation(out=Kb[:, c, :], in_=C.ps[bk][:, 0:128], func=AF.Copy, scale=bke[:, c:c + 1]),
                     reads=["bke"], writes=[key, "Kb"])
                S.op("dve", lambda e, bk=bk, c=c: e.tensor_scalar(out=Kend[:, c, :], in0=C.ps[bk][:, 0:128], scalar1=eend[:, c:c + 1], scalar2=None, op0=ALU.mult),
                     reads=["eend"], writes=[key, "Kend"])
                S.op("act", lambda e, bk=bk, c=c: e.activation(out=Vb[:, c, :], in_=C.ps[bk][:, 128:256], func=AF.Copy, scale=beta[:, c:c + 1]),
                     reads=["beta"], writes=[key, "Vb"])
        for ti in range(ntile_a):
            phase_a(ti)
        if stop_after == 2:
            S.drain_dmas("sp"); S.replay(); return

        def bufset(name, dt, n=NSET):
            return [sb("%s%d" % (name, i), [128, 128], dt) for i in range(n)]
        G1 = bufset("G1", F32)
        Dm = bufset("Dm", F32)
        Xf = bufset("Xf", F32)
        Xb_ = bufset("Xb", BF16)
        Yb = bufset("Yb", BF16)
        Mb = bufset("Mb", BF16)
        Xo = bufset("Xo", BF16)
        Yo = bufset("Yo", BF16)
        Hb = bufset("Hb", BF16)
        Gb = bufset("Gb", BF16)
        Am = bufset("Am", F32)
        AT = bufset("AT", BF16)
        TTb = bufset("TTb", BF16)
        WT = bufset("WT", BF16)
        U0 = bufset("U0", F32)
        Ub = bufset("Ub", BF16, 2)
        Sf = sb("Sf", [128, 128], F32)
        Sbb = [sb("Sbb%d" % i, [128, 128], BF16) for i in range(2)]
        otmp = sb("otmp", [128, 128], F32)
        osb = sb("osb", [128, 128], F32)
        osq = sb("osq", [128, 128], F32)
        onr = sb("onr", [128, 128], F32)
        ssq = sb("ssq", [128, 1], F32)
        lno = sb("lno", [128, 1], F32)
        rso = sb("rso", [128, 1], F32)
        gz = [sb("gz%d" % i, [128, TT], BF16) for i in range(2)]
        yst = [sb("yst%d" % i, [128, TT], BF16) for i in range(2)]
        if C.fused:
            ystm = [[sb("ystm%d_%d" % (i, j), [128, TT], BF16) for j in range(4)] for i in range(2)]
            hm = sb("hm", [128, 4], F32)
            gnwm = sb("gnwm", [128, 4], F32)
            S.dma("sp", lambda e: e.dma_start(out=hm[:], in_=d["hm"][:, :]), reads=["d_hm"], writes=["hm"])
            S.op("dve", lambda e: e.tensor_scalar(out=gnwm[:], in0=hm[:], scalar1=gnw[:, 0:1], scalar2=None, op0=ALU.mult),
                 reads=["hm", "gnw"], writes=["gnwm"])
        S.op("dve", lambda e: e.memset(Sf[:], 0.0), writes=["Sf"])
        S.op("dve", lambda e: e.memset(Sbb[0][:], 0.0), writes=["Sbb0"])

        def mm1(out_bank, lhsT, rhs, reads, cols=128):
            return mm_group(C, (lambda: C.ps[out_bank][:, 0:cols]), [(lhsT, rhs)], out_bank, reads=reads)

        def pre_steps(c):
            s = c % NSET
            ck = slice(c * GC, (c + 1) * GC)
            k = lambda nm: "%s%d" % (nm, s)
            steps = []
            st8 = {}

            def s1a():
                S.op("act", lambda e: e.activation(out=G1[s][:], in_=umask[:], func=AF.Copy, scale=gg[:, c:c + 1]),
                     reads=["umask", "gg"], writes=[k("G1")])
            steps.append(s1a)

            def s1b():
                bk = C.bank()
                key = mm_group(C, (lambda: C.ps[bk][:, 0:128]), [(G1[s][:], m2[:]), (identf[:], neg[:])], bk, reads=[k("G1"), "m2", "ident", "neg"])
                S.op("act", lambda e: e.activation(out=Dm[s][:], in_=C.ps[bk][:, 0:128], func=AF.Exp), reads=[], writes=[key, k("Dm")])
            steps.append(s1b)

            def s2():
                bk = C.bank()
                key = mm1(bk, KnT[:, ck], KnT[:, ck], ["KnT"])
                S.op("dve", lambda e: e.scalar_tensor_tensor(out=Xf[s][:], in0=C.ps[bk][:, 0:128], scalar=nbeta[:, c:c + 1], in1=Dm[s][:],
                                                             op0=ALU.mult, op1=ALU.mult), reads=["nbeta", k("Dm")], writes=[key, k("Xf")])
                S.op("dve", lambda e: e.tensor_tensor(out=Xf[s][:], in0=Xf[s][:], in1=slm[:], op=ALU.mult), reads=[k("Xf"), "slm"], writes=[k("Xf")])
                S.op("act", lambda e: e.activation(out=Xb_[s][:], in_=Xf[s][:], func=AF.Copy), reads=[k("Xf")], writes=[k("Xb")])
            steps.append(s2)

            def s3a():
                bk = C.bank()
                key = mm1(bk, QnT[:, ck], KnT[:, ck], ["QnT", "KnT"])
                S.op("dve", lambda e: e.tensor_tensor(out=Am[s][:], in0=C.ps[bk][:, 0:128], in1=Dm[s][:], op=ALU.mult),
                     reads=[k("Dm")], writes=[key, k("Am")])
            steps.append(s3a)

            def s4():
                bk = C.bank()
                key = "ps%d" % bk
                S.op("pe", lambda e: e.transpose(C.ps[bk][:, 0:128], Xf[s][:], identf[:]), reads=[k("Xf"), "ident"], writes=[key])
                S.op("act", lambda e: e.activation(out=Yb[s][:], in_=C.ps[bk][:, 0:128], func=AF.Copy), reads=[], writes=[key, k("Yb")])
                S.op("pool", lambda e: e.tensor_tensor(out=Xo[s][:], in0=Xb_[s][:], in1=lvm[:, 0:128], op=ALU.mult),
                     reads=[k("Xb"), "lvm"], writes=[k("Xo")])
                S.op("dve", lambda e: e.tensor_tensor(out=Mb[s][:], in0=Xo[s][:], in1=identb[:], op=ALU.add),
                     reads=[k("Xo"), "identb"], writes=[k("Mb")])
            steps.append(s4)

            def s3b():
                bk2 = C.bank()
                key2 = "ps%d" % bk2
                S.op("pe", lambda e: e.transpose(C.ps[bk2][:, 0:128], Am[s][:], identf[:]), reads=[k("Am"), "ident"], writes=[key2])
                S.op("act", lambda e: e.activation(out=AT[s][:], in_=C.ps[bk2][:, 0:128], func=AF.Copy), reads=[], writes=[key2, k("AT")])
                S.op("pool", lambda e: e.tensor_tensor(out=Yo[s][:], in0=Yb[s][:], in1=lvm[:, 128:256], op=ALU.mult),
                     reads=[k("Yb"), "lvm"], writes=[k("Yo")])
                S.op("dve", lambda e: e.tensor_tensor(out=TTb[s][:], in0=Yo[s][:], in1=identb[:], op=ALU.add),
                     reads=[k("Yo"), "identb"], writes=[k("TTb")])
            steps.append(s3b)
            for l in range(1, 7):
                def la(l=l):
                    if l <= 5:
                        S.op("pool", lambda e: e.tensor_tensor(out=Yo[s][:], in0=Yb[s][:], in1=lvm[:, (2 * l + 1) * 128:(2 * l + 2) * 128], op=ALU.mult),
                             reads=[k("Yb"), "lvm"], writes=[k("Yo")])
                    S.op("pool", lambda e: e.tensor_tensor(out=Xo[s][:], in0=Xb_[s][:], in1=lvm[:, (2 * l) * 128:(2 * l + 1) * 128], op=ALU.mult),
                         reads=[k("Xb"), "lvm"], writes=[k("Xo")])
                steps.append(la)

                def lb(l=l):
                    if l <= 5:
                        bh = C.bank()
                        keyh = mm1(bh, Yo[s][:], Mb[s][:], [k("Yo"), k("Mb")])
                    bg = C.bank()
                    keyg = mm1(bg, Xo[s][:], TTb[s][:], [k("Xo"), k("TTb")])
                    if l <= 5:
                        S.op("act", lambda e: e.activation(out=Hb[s][:], in_=C.ps[bh][:, 0:128], func=AF.Copy), reads=[], writes=[keyh, k("Hb")])
                    if l % 2 == 0:
                        S.op("act", lambda e: e.activation(out=Gb[s][:], in_=C.ps[bg][:, 0:128], func=AF.Copy), reads=[], writes=[keyg, k("Gb")])
                    else:
                        S.op("dve", lambda e: e.tensor_copy(out=Gb[s][:], in_=C.ps[bg][:, 0:128]), reads=[], writes=[keyg, k("Gb")])
                steps.append(lb)

                def lc(l=l):
                    if l <= 5:
                        bm = C.bank()
                        keym = mm1(bm, TTb[s][:], Hb[s][:], [k("TTb"), k("Hb")])
                    bw = C.bank()
                    keyw = mm1(bw, Mb[s][:], Gb[s][:], [k("Mb"), k("Gb")])
                    if l <= 5:
                        S.op("dve", lambda e: e.tensor_tensor(out=Mb[s][:], in0=Mb[s][:], in1=C.ps[bm][:, 0:128], op=ALU.add),
                             reads=[k("Mb")], writes=[keym, k("Mb")])
                    S.op("dve", lambda e: e.tensor_tensor(out=TTb[s][:], in0=TTb[s][:], in1=C.ps[bw][:, 0:128], op=ALU.add),
                         reads=[k("TTb")], writes=[keyw, k("TTb")])
                steps.append(lc)

            def s5():
                bk = C.bank()
                key = mm1(bk, Kb[:, c, :], TTb[s][:], ["Kb", k("TTb")])
                bk2 = C.bank()
                key2 = mm1(bk2, TTb[s][:], Vb[:, c, :], ["Vb", k("TTb")])
                S.op("act", lambda e: e.activation(out=WT[s][:], in_=C.ps[bk][:, 0:128], func=AF.Copy), reads=[], writes=[key, k("WT")])
                S.op("dve", lambda e: e.tensor_copy(out=U0[s][:], in_=C.ps[bk2][:, 0:128]), reads=[], writes=[key2, k("U0")])
            steps.append(s5)
            return steps

        def scan_steps(c):
            s = c % NSET
            ck = slice(c * GC, (c + 1) * GC)
            k = lambda nm: "%s%d" % (nm, s)
            u = c % 2
            sbi, sbo = c % 2, (c + 1) % 2
            stt = {}

            def sa():
                b1 = C.bank()
                key1 = mm1(b1, WT[s][:], Sbb[sbi][:], [k("WT"), "Sbb%d" % sbi])
                b2 = C.bank()
                key2 = mm1(b2, QnT[:, ck], Sbb[sbi][:], ["QnT", "Sbb%d" % sbi])
                S.op("dve", lambda e: e.tensor_tensor(out=Ub[u][:], in0=U0[s][:], in1=C.ps[b1][:, 0:128], op=ALU.subtract),
                     reads=[k("U0")], writes=[key1, "Ub%d" % u])
                S.op("act", lambda e: e.activation(out=otmp[:], in_=C.ps[b2][:, 0:128], func=AF.Copy, scale=egc[:, c:c + 1]),
                     reads=["egc"], writes=[key2, "otmp"])

            def sb_():
                b4 = C.bank()
                key4 = mm1(b4, Kend[:, c, :], Ub[u][:], ["Kend", "Ub%d" % u])
                b3 = C.bank()
                key3 = mm1(b3, AT[s][:], Ub[u][:], [k("AT"), "Ub%d" % u])
                S.op("dve", lambda e: e.scalar_tensor_tensor(out=Sf[:], in0=Sf[:], scalar=gend[:, c:c + 1], in1=C.ps[b4][:, 0:128],
                                                             op0=ALU.mult, op1=ALU.add), reads=["gend", "Sf"], writes=[key4, "Sf"])
                S.op("act", lambda e: e.activation(out=Sbb[sbo][:], in_=Sf[:], func=AF.Copy), reads=["Sf"], writes=["Sbb%d" % sbo])
                S.op("dve", lambda e: e.tensor_tensor(out=osb[:], in0=otmp[:], in1=C.ps[b3][:, 0:128], op=ALU.add),
                     reads=["otmp"], writes=[key3, "osb"])

            def sc():
                S.op("act", lambda e: e.activation(out=osq[:], in_=osb[:], func=AF.Square, accum_out=ssq[:, 0:1]), reads=["osb"], writes=["osq", "ssq"])
                S.op("act", lambda e: e.activation(out=lno[:], in_=ssq[:], func=AF.Ln, bias=eps_t[:, 0:1], scale=1.0 / 128), reads=["ssq", "eps"], writes=["lno"])
                S.op("act", lambda e: e.activation(out=rso[:], in_=lno[:], func=AF.Exp, scale=-0.5), reads=["lno"], writes=["rso"])
                S.op("act", lambda e: e.activation(out=onr[:], in_=osb[:], func=AF.Copy, scale=rso[:, 0:1]),
                     reads=["osb", "rso"], writes=["onr"])

            def sd():
                b5 = C.bank()
                key5 = "ps%d" % b5
                S.op("pe", lambda e: e.transpose(C.ps[b5][:, 0:128], onr[:], identf[:]), reads=["onr", "ident"], writes=[key5])
                yb = (c // 4) % 2
                a = c % 4
                if a == 0:
                    tz = (c // 4) * TT
                    S.dma("sp", lambda e: e.dma_start(out=gz[yb][:], in_=d["gzT"][:, tz:tz + TT]), reads=["d_gzT"], writes=["gz%d" % yb])
                if not C.fused:
                    S.op("dve", lambda e: e.scalar_tensor_tensor(out=yst[yb][:, a * 128:(a + 1) * 128], in0=C.ps[b5][:, 0:128], scalar=gnw[:, 0:1],
                                                                 in1=gz[yb][:, a * 128:(a + 1) * 128], op0=ALU.mult, op1=ALU.mult),
                         reads=["gnw", "gz%d" % yb], writes=[key5, "yst%d" % yb])
                    if a == 3:
                        t0 = (c // 4) * TT
                        S.dma("sp", lambda e: e.dma_start(out=d["yT_h"][128:256, t0:t0 + TT], in_=yst[yb][:]), reads=["yst%d" % yb], writes=["d_yT_h"])
                else:
                    for j in range(4):
                        S.op("dve", lambda e, j=j: e.scalar_tensor_tensor(out=ystm[yb][j][:, a * 128:(a + 1) * 128], in0=C.ps[b5][:, 0:128],
                                                                          scalar=gnwm[:, j:j + 1], in1=gz[yb][:, a * 128:(a + 1) * 128],
                                                                          op0=ALU.mult, op1=ALU.mult),
                             reads=["gnwm", "gz%d" % yb], writes=[key5, "ystm%d_%d" % (yb, j)])
                    if a == 3:
                        t0 = (c // 4) * TT
                        qq, qc = t0 // 2048, t0 % 2048
                        for j in range(4):
                            S.dma("sp", lambda e, j=j: e.dma_start(out=d["ypad%d" % qq][256 * j + 128:256 * j + 256, qc:qc + TT], in_=ystm[yb][j][:]),
                                  reads=["ystm%d_%d" % (yb, j)], writes=["d_ypad%d" % qq])
                        if qc + TT == 2048:
                            S.coll(lambda e: e.collective_compute(
                                "AllReduce", ALU.add, replica_groups=[[0, 1, 2, 3], [4, 5, 6, 7]],
                                ins=[d["ypad%d" % qq].opt()], outs=[d["yg%d" % qq].opt()]),
                                reads=["d_ypad%d" % qq], writes=["d_yg%d" % qq])
            return [sa, sb_, sc, sd]

        groups = [list(range(g, min(g + G, nchunks))) for g in range(0, nchunks, G)]
        prev = []
        for grp in groups + [[]]:
            lists = [pre_steps(c) for c in grp]
            nst = max([len(l) for l in lists] + [0])
            pending = []
            for c in prev:
                pending += scan_steps(c)
            for si in range(nst):
                for l in lists:
                    if si < len(l):
                        l[si]()
                if pending:
                    pending.pop(0)()
            while pending:
                pending.pop(0)()
            prev = grp
        S.drain_dmas("sp")
        S.replay()


CONST_INS = [("ctab", (32, SEQ), F32), ("stab", (32, SEQ), F32), ("ident", (128, 128), F32), ("sel", (65, 64), F32),
             ("tri", (128, 128), F32), ("OH", (32, 3072), F32), ("Eb", (32, SEQ), BF16), ("pen", (32, 32), F32),
             ("umask", (128, 128), F32), ("m2", (128, 128), F32), ("neg", (128, 128), F32), ("slm", (128, 128), F32),
             ("lvlm", (128, 14 * 128), F32)]
LAYER_INS = PROJ_INS + [("t5h", (32, 1)), ("cw", (128, 12)), ("gsc", (1, 2)), ("gnw", (128, 1))]


def host_consts():
    import ml_dtypes
    ctab, stab = rope_consts()
    pen, E = moba_consts()
    umask, m2, neg, slm = gdn_consts()
    sel = np.zeros((65, 64), np.float32)
    sel[64, :] = 1.0
    return {"ctab": ctab, "stab": stab, "ident": np.eye(128, dtype=np.float32), "sel": sel,
            "tri": np.triu(np.ones((128, 128), np.float32)), "OH": t5_consts(), "Eb": E.astype(ml_dtypes.bfloat16),
            "pen": pen, "umask": umask, "m2": m2, "neg": neg, "slm": slm, "lvlm": level_masks()}


def prep_layer_all(I, l, h):
    out = prep_layer(I, l, h)
    f = np.float32
    out["t5h"] = np.ascontiguousarray(I["t5_table"][:, h:h + 1]).astype(f)
    cwf = I["gdn_conv_w"][l]
    cw = np.zeros((128, 12), f)
    for j in range(3):
        for tap in range(4):
            cw[:, 4 * j + tap] = cwf[tap, j * 512 + 128 * h:j * 512 + 128 * h + 128]
    out["cw"] = cw
    out["gsc"] = np.array([[I["gdn_A_log"][l, h], I["gdn_dt_bias"][l, h]]], f)
    out["gnw"] = np.ascontiguousarray(I["gdn_norm_w"][l].reshape(128, 1)).astype(f)
    return out


def wo_perm():
    rows = []
    for h in range(4):
        rows += list(range(64 * h, 64 * h + 64))
        rows += list(range(768 + 64 * h, 768 + 64 * h + 64))
        rows += list(range(256 + 128 * h, 256 + 128 * h + 128))
    return rows


def build_layer_program(has_prev, do_layer, final):
    nc = bass.Bass("TRN2", target_bir_lowering=False)
    with ExitStack() as st:
        C = Ctx(nc, st)
        C.din("xT", (DM, SEQ), F32)
        if has_prev:
            C.din("yT", (DM, SEQ), BF16)
            C.din("wo", (DM, DM), F32)
            if not final:
                C.dout("xTo", (DM, SEQ), F32)
        if final:
            C.din("fnw", (128, 8), F32)
            C.dout("outT", (DM, SEQ), F32)
        if do_layer:
            for n, s_, dt in CONST_INS:
                C.din(n, s_, dt)
            for n, s_ in LAYER_INS:
                C.din(n, s_, F32)
            for n, s_, dt in PROJ_OUTS:
                C.dint(n, s_, dt)
            C.dint("fv", (1, 3072), F32)
            C.dout("yT_h", (256, SEQ), BF16)
        stage_proj(C, has_prev=has_prev, do_proj=do_layer, final=final)
        if do_layer:
            stage_attn(C, False)
            stage_attn(C, True)
            stage_gdn(C)
    return nc


def build_fused_program(depth=DEPTH, do_coll=True):
    nc = bass.Bass("TRN2", target_bir_lowering=False)
    with ExitStack() as st:
        C = Ctx(nc, st)
        C.fused = True
        S = C.S
        C.din("xT", (DM, SEQ), F32)
        C.din("hm", (128, 4), F32)
        C.din("fnw", (128, 8), F32)
        for n, s_, dt in CONST_INS:
            C.din(n, s_, dt)
        for l in range(depth):
            for n, s_ in LAYER_INS:
                C.din("%s_%d" % (n, l), s_, F32)
            C.din("wo_%d" % l, (DM, DM), F32)
        C.dout("outT", (DM, SEQ), F32)
        C.dint("xTi", (DM, SEQ), F32)
        for n, s_, dt in PROJ_OUTS:
            C.dint(n, s_, dt)
        C.dint("fv", (1, 3072), F32)
        for q in range(4):
            C.dint("ypad%d" % q, (DM, 2048), BF16)
            C.dint("yg%d" % q, (DM, 2048), BF16)
        x_ext = C.d["xT"]
        for l in range(depth):
            for n, s_ in LAYER_INS:
                C.d[n] = C.d["%s_%d" % (n, l)]
            if l > 0:
                C.d["wo"] = C.d["wo_%d" % (l - 1)]
            C.d["xT"] = x_ext if l <= 1 else C.d["xTi"]
            C.d["xTo"] = C.d["xTi"]
            import os
            ST = os.environ.get("STAGES", "pamg")
            if "p" in ST:
                stage_proj(C, has_prev=(l > 0), do_proj=True, final=False)
            if "a" in ST:
                stage_attn(C, False)
            if "m" in ST:
                stage_attn(C, True)
            if "g" in ST:
                stage_gdn(C)
        C.d["wo"] = C.d["wo_%d" % (depth - 1)]
        C.d["xT"] = C.d["xTi"] if depth > 1 else x_ext
        if "f" in os.environ.get("STAGES", "pamgf"):
            stage_proj(C, has_prev=True, do_proj=False, final=True)
    return nc


_FUSED = []


def kernel(x, norm_w, w_in, mla_q_norm, mla_w_uq, mla_kv_norm, mla_w_ukv, gdn_conv_w, gdn_A_log, gdn_dt_bias,
           gdn_norm_w, w_out, t5_table, final_norm_w):
    I = dict(x=np.asarray(x), norm_w=np.asarray(norm_w), w_in=np.asarray(w_in), mla_q_norm=np.asarray(mla_q_norm),
             mla_w_uq=np.asarray(mla_w_uq), mla_kv_norm=np.asarray(mla_kv_norm), mla_w_ukv=np.asarray(mla_w_ukv),
             gdn_conv_w=np.asarray(gdn_conv_w), gdn_A_log=np.asarray(gdn_A_log), gdn_dt_bias=np.asarray(gdn_dt_bias),
             gdn_norm_w=np.asarray(gdn_norm_w), w_out=np.asarray(w_out), t5_table=np.asarray(t5_table),
             final_norm_w=np.asarray(final_norm_w))
    if not _FUSED:
        _FUSED.append(build_fused_program())
    nc = _FUSED[0]
    consts = host_consts()
    perm = wo_perm()
    fnw = np.ascontiguousarray(I["final_norm_w"].reshape(8, 128).T).astype(np.float32)
    wos = [np.ascontiguousarray(I["w_out"][l][perm, :]).astype(np.float32) for l in range(DEPTH)]
    xT = [np.ascontiguousarray(I["x"][b].T).astype(np.float32) for b in range(2)]
    maps = []
    for c in range(8):
        b, h = c // 4, c % 4
        m = dict(consts)
        m["xT"] = xT[b]
        hm = np.zeros((128, 4), np.float32)
        hm[:, h] = 1.0
        m["hm"] = hm
        m["fnw"] = fnw
        for l in range(DEPTH):
            for k, v in prep_layer_all(I, l, h).items():
                m["%s_%d" % (k, l)] = v
            m["wo_%d" % l] = wos[l]
        maps.append(m)
    res = run_bass_kernel_spmd(nc, maps, core_ids=list(range(8)))
    out = np.stack([np.asarray(res.results[4 * b]["outT"]).T for b in range(2)], axis=0)
    return np.ascontiguousarray(out).astype(np.float32)
```

```python
import math
import os
from contextlib import ExitStack

import numpy as np
import concourse.bass as bass
import concourse.mybir as mybir
from concourse.bass_utils import run_bass_kernel_spmd

F32 = mybir.dt.float32
BF16 = mybir.dt.bfloat16
AF = mybir.ActivationFunctionType
ALU = mybir.AluOpType
AX = mybir.AxisListType

SEQ = 8192
DM = 1024
DEPTH = 4
TT = 512
NT = SEQ // TT
EPS = 1e-6
TW = [128] * 8 + [96, 96, 64, 2]
TOFF = [sum(TW[:i]) for i in range(len(TW))]
NCOL = sum(TW)

ENGS = ("pe", "act", "dve", "pool", "sp")


class Sched:
    def __init__(self, nc, stack, n_dma_sems=40):
        self.nc = nc
        self.q = {e: [] for e in ENGS}
        self.sem = {e: stack.enter_context(nc.semaphore("s_" + e)) for e in ENGS if e != "sp"}
        self.cnt = {e: 0 for e in ENGS}
        self.seen = {e: {} for e in ENGS}
        self.dsem = [stack.enter_context(nc.semaphore("d%d" % i)) for i in range(n_dma_sems)]
        self.dcnt = [0] * n_dma_sems
        self.dnext = 0
        self.lastw = {}
        self.readers = {}
        self.semobj = dict(self.sem)
        for i, s in enumerate(self.dsem):
            self.semobj["d%d" % i] = s
        self.semobj["cc"] = stack.enter_context(nc.semaphore("s_cc"))
        self.cccnt = 0

    def _deps(self, eng, reads, writes):
        deps = {}

        def add(src, val, kind):
            if src == eng and eng == "pe":
                return
            if deps.get(src, 0) < val:
                deps[src] = val
        for k in reads:
            w = self.lastw.get(k)
            if w:
                add(w[0], w[1], "raw")
        for k in writes:
            w = self.lastw.get(k)
            if w:
                add(w[0], w[1], "waw")
            for s, v in self.readers.get(k, {}).items():
                add(s, v, "war")
        waits = []
        for s, v in deps.items():
            if self.seen[eng].get(s, 0) < v:
                self.seen[eng][s] = v
                waits.append((s, v))
        return waits

    def _commit(self, src, val, reads, writes):
        for k in reads:
            self.readers.setdefault(k, {})[src] = val
        for k in writes:
            self.lastw[k] = (src, val)
            self.readers[k] = {}

    def op(self, eng, fn, reads=(), writes=(), sig=True):
        waits = self._deps(eng, reads, writes)
        val = self.cnt[eng] + 1
        if sig:
            self.cnt[eng] = val
        self._commit(eng, val, reads, writes)
        self.q[eng].append((waits, fn, (eng, 1) if sig else None))

    def dma(self, eng, fn, reads=(), writes=()):
        i = self.dnext
        self.dnext = (self.dnext + 1) % len(self.dsem)
        name = "d%d" % i
        waits = self._deps(eng, reads, writes)
        if self.dcnt[i] > 0 and self.seen[eng].get(name, 0) < self.dcnt[i]:
            self.seen[eng][name] = self.dcnt[i]
            waits.append((name, self.dcnt[i]))
        self.dcnt[i] += 16
        self._commit(name, self.dcnt[i], reads, writes)
        self.q[eng].append((waits, fn, (name, 16)))

    def coll(self, fn, reads=(), writes=()):
        if "cc" not in self.semobj:
            raise RuntimeError("no cc semaphore")
        waits = self._deps("pool", reads, writes)
        self.cccnt += 1
        self._commit("cc", self.cccnt, reads, writes)
        self.q["pool"].append((waits, fn, ("cc", 1)))

    def drain_dmas(self, eng="sp", include_cc=False):
        waits = []
        for i in range(len(self.dsem)):
            name = "d%d" % i
            if self.dcnt[i] > 0 and self.seen[eng].get(name, 0) < self.dcnt[i]:
                self.seen[eng][name] = self.dcnt[i]
                waits.append((name, self.dcnt[i]))
        if include_cc and self.cccnt > 0 and self.seen[eng].get("cc", 0) < self.cccnt:
            self.seen[eng]["cc"] = self.cccnt
            waits.append(("cc", self.cccnt))
        self.q[eng].append((waits, None, None))

    def replay(self):
        nc = self.nc
        with nc.Block() as block:
            engmap = {"pe": block.tensor, "act": block.scalar, "dve": block.vector,
                      "pool": block.gpsimd, "sp": block.sync}
            for e in ENGS:
                items = self.q[e]
                if not items:
                    continue

                def body(engine, items=items):
                    for waits, fn, inc in items:
                        for s, v in waits:
                            engine.wait_ge(self.semobj[s], v)
                        if fn is None:
                            continue
                        ins = fn(engine)
                        if inc is not None:
                            ins.then_inc(self.semobj[inc[0]], inc[1])
                engmap[e](body)
        self.q = {e: [] for e in ENGS}


class Ctx:
    def __init__(self, nc, stack):
        self.nc = nc
        self.S = Sched(nc, stack)
        self.ps = [stack.enter_context(nc.psum_tensor("ps%d" % i, [128, 512], F32)) for i in range(8)]
        self.psn = 0
        self.uid = 0
        self.fused = False
        self.d = {}

    def bank(self):
        i = self.psn
        self.psn = (self.psn + 1) % 8
        return i

    def din(self, name, shape, dt):
        t = self.nc.dram_tensor(name, list(shape), dt, kind="ExternalInput").ap()
        self.d[name] = t
        return t

    def dout(self, name, shape, dt):
        t = self.nc.dram_tensor(name, list(shape), dt, kind="ExternalOutput").ap()
        self.d[name] = t
        return t

    def dint(self, name, shape, dt):
        t = self.nc.dram_tensor(name, list(shape), dt, kind="Internal").ap()
        self.d[name] = t
        return t


def mm_group(C, out_fn, pairs, bank, reads, extra_writes=()):
    S = C.S
    n = len(pairs)
    key = "ps%d" % bank
    for i, (l, r) in enumerate(pairs):
        S.op("pe", (lambda e, l=l, r=r, i=i: e.matmul(out_fn(), l, r, start=(i == 0), stop=(i == n - 1))),
             reads=reads, writes=[key] + list(extra_writes), sig=(i == n - 1))
    return key


def stage_proj(C, has_prev, do_proj, final=False, tiles=range(NT)):
    nc, S, d = C.nc, C.S, C.d
    with ExitStack() as st:
        def sb(name, shape, dt):
            C.uid += 1
            return st.enter_context(nc.sbuf_tensor("sb%d_%s" % (C.uid, name), list(shape), dt))
        xs = [sb("xs%d" % i, [128, 8, TT], F32) for i in range(2)]
        eps_t = sb("eps_t", [128, 1], F32)
        S.op("dve", lambda e: e.memset(eps_t[:], EPS), writes=["eps"])
        if has_prev:
            ys = [sb("ys%d" % i, [128, 8, TT], BF16) for i in range(2)]
            wo = sb("wo", [128, 8, DM], BF16)
            wstage = sb("wstage", [128, NCOL], F32)
            for k in range(8):
                S.dma("sp", lambda e, k=k: e.dma_start(out=wstage[:, 0:DM], in_=d["wo"][k * 128:(k + 1) * 128, :]),
                      reads=["d_wo"], writes=["wstage"])
                if k % 2 == 0:
                    S.op("dve", lambda e, k=k: e.tensor_copy(out=wo[:, k, :], in_=wstage[:, 0:DM]),
                         reads=["wstage"], writes=["wo"])
                else:
                    S.op("act", lambda e, k=k: e.activation(out=wo[:, k, :], in_=wstage[:, 0:DM], func=AF.Copy),
                         reads=["wstage"], writes=["wo"])
        if do_proj or final:
            xsq = sb("xsq", [128, 8, TT], BF16)
            ones = sb("ones", [128, 128], BF16)
            S.op("dve", lambda e: e.memset(ones[:], 1.0), writes=["ones"])
            rstd1 = sb("rstd1", [128, TT], F32)
            lnt = sb("lnt", [128, TT], F32)
        if final:
            fnw = sb("fnw", [128, 8], F32)
            S.dma("sp", lambda e: e.dma_start(out=fnw[:], in_=d["fnw"][:, :]), reads=["d_fnw"], writes=["fnw"])
            outs = sb("outs", [128, 8, TT], F32)
        if do_proj:
            if not has_prev:
                wstage = sb("wstage", [128, NCOL], F32)
            xb = sb("xb", [128, 8, TT], BF16)
            win = sb("win", [128, 8, NCOL], BF16)
            nw = sb("nw", [128, 8], F32)
            S.dma("sp", lambda e: e.dma_start(out=nw[:], in_=d["nw"][:, :]), reads=["d_nw"], writes=["nw"])
            for k in range(8):
                S.dma("sp", lambda e, k=k: e.dma_start(out=wstage[:], in_=d["w_in"][k * 128:(k + 1) * 128, :]),
                      reads=["d_w_in"], writes=["wstage"])
                S.op("dve", lambda e, k=k: e.tensor_scalar(out=win[:, k, :], in0=wstage[:], scalar1=nw[:, k:k + 1],
                                                           scalar2=None, op0=ALU.mult),
                     reads=["wstage", "nw"], writes=["win"])
            wuq = sb("wuq", [128, 2, 192], BF16)
            qnw = sb("qnw", [128, 2], F32)
            S.dma("sp", lambda e: e.dma_start(out=qnw[:], in_=d["qnw"][:, :]), reads=["d_qnw"], writes=["qnw"])
            for k in range(2):
                S.dma("sp", lambda e, k=k: e.dma_start(out=wstage[:, 0:192], in_=d["wuq"][k * 128:(k + 1) * 128, :]),
                      reads=["d_wuq"], writes=["wstage"])
                S.op("dve", lambda e, k=k: e.tensor_scalar(out=wuq[:, k, :], in0=wstage[:, 0:192], scalar1=qnw[:, k:k + 1],
                                                           scalar2=None, op0=ALU.mult),
                     reads=["wstage", "qnw"], writes=["wuq"])
            wukv = sb("wukv", [128, 128], BF16)
            kvnw = sb("kvnw", [128, 1], F32)
            S.dma("sp", lambda e: e.dma_start(out=kvnw[:], in_=d["kvnw"][:, :]), reads=["d_kvnw"], writes=["kvnw"])
            S.dma("sp", lambda e: e.dma_start(out=wstage[:, 0:128], in_=d["wukv"][:, :]), reads=["d_wukv"], writes=["wstage"])
            S.op("dve", lambda e: e.tensor_scalar(out=wukv[:], in0=wstage[:, 0:128], scalar1=kvnw[:, 0:1],
                                                  scalar2=None, op0=ALU.mult),
                 reads=["wstage", "kvnw"], writes=["wukv"])
            identf = sb("identf", [128, 128], F32)
            identb = sb("identb", [128, 128], BF16)
            S.dma("sp", lambda e: e.dma_start(out=identf[:], in_=d["ident"][:, :]), reads=["d_ident"], writes=["identf"])
            S.op("dve", lambda e: e.tensor_copy(out=identb[:], in_=identf[:]), reads=["identf"], writes=["identb"])
            ctab = sb("ctab", [128, TT], F32)
            stab = sb("stab", [128, TT], F32)
            cqf = sb("cqf", [128, 2, TT], F32)
            cqsq = sb("cqsq", [128, 2, TT], BF16)
            cqb = sb("cqb", [128, 2, TT], BF16)
            rstd2 = sb("rstd2", [128, TT], F32)
            rstd3 = sb("rstd3", [128, TT], F32)
            ckvf = sb("ckvf", [128, TT], F32)
            ckvsq = sb("ckvsq", [128, TT], BF16)
            ckvb = sb("ckvb", [128, TT], BF16)
            QTs = sb("QTs", [96, TT], BF16)
            KTs = sb("KTs", [96, TT], BF16)
            t1 = sb("t1", [128, TT], F32)
            t2 = sb("t2", [128, TT], F32)
            t3 = sb("t3", [128, TT], F32)
            t4 = sb("t4", [128, TT], F32)
            vT = sb("vT", [64, TT], BF16)
            Vs = sb("Vs", [128, 4, 64], BF16)
            mvT = sb("mvT", [64, TT], BF16)
            mVs = sb("mVs", [128, 4, 64], BF16)
            mqf = sb("mqf", [64, TT], F32)
            mqb = sb("mqb", [64, TT], BF16)
            mkf = sb("mkf", [64, TT], F32)
            mkb = sb("mkb", [64, TT], BF16)
            km = sb("km", [64, 32], F32)
            gf = sb("gf", [128, TT], F32)
            gs = sb("gs", [128, TT], BF16)
            gq3 = [sb("gq3_%d" % i, [128, TT], F32) for i in range(3)]
            zf = sb("zf", [128, TT], F32)
            zs = sb("zs", [128, TT], BF16)
            bas = sb("bas", [2, TT], F32)

        def rstd_from(bank, n, out_t, rows=128):
            key = "ps%d" % bank
            S.op("act", lambda e: e.activation(out=lnt[0:rows, :], in_=C.ps[bank][0:rows, :], func=AF.Ln,
                                               bias=eps_t[0:rows, 0:1], scale=1.0 / n),
                 reads=["eps"], writes=[key, "lnt"])
            S.op("act", lambda e: e.activation(out=out_t[0:rows, :], in_=lnt[0:rows, :], func=AF.Exp, scale=-0.5),
                 reads=["lnt"], writes=[out_t.name if hasattr(out_t, "name") else "rstd"])

        def do_tile(ti):
            b = ti % 2
            c0 = ti * TT
            xk = "xs%d" % b
            xsb = xs[b]
            for k4 in range(2):
                S.dma("sp", lambda e, k4=k4, xsb=xsb: e.dma_start(
                    out=xsb[:, 4 * k4:4 * k4 + 4, :],
                    in_=d["xT"][4 * k4 * 128:(4 * k4 + 4) * 128, c0:c0 + TT].rearrange("(k p) t -> p k t", p=128)),
                    reads=["d_xT"], writes=[xk])
            if has_prev:
                yk = "ys%d" % b
                ysb = ys[b]
                S.dma("sp", lambda e, ysb=ysb: e.dma_start(
                    out=ysb[:], in_=(d["yg%d" % (c0 // 2048)][:, c0 % 2048:c0 % 2048 + TT] if C.fused else d["yT"][:, c0:c0 + TT]
                                      ).rearrange("(k p) t -> p k t", p=128)),
                    reads=[("d_yg%d" % (c0 // 2048)) if C.fused else "d_yT"], writes=[yk])
                for c in range(8):
                    bk = C.bank()
                    key = mm_group(C, (lambda bk=bk: C.ps[bk][:, :]),
                                   [(wo[:, k, c * 128:(c + 1) * 128], ysb[:, k, :]) for k in range(8)],
                                   bk, reads=["wo", yk])
                    S.op("dve", lambda e, c=c, bk=bk, xsb=xsb: e.tensor_tensor(out=xsb[:, c, :], in0=C.ps[bk][:, :], in1=xsb[:, c, :], op=ALU.add),
                         reads=[], writes=[key, xk])
                if not final:
                    for k4 in range(2):
                        S.dma("sp", lambda e, k4=k4, xsb=xsb: e.dma_start(
                            out=d["xTo"][4 * k4 * 128:(4 * k4 + 4) * 128, c0:c0 + TT].rearrange("(k p) t -> p k t", p=128),
                            in_=xsb[:, 4 * k4:4 * k4 + 4, :]),
                            reads=[xk], writes=["d_xTo"])
            if not (do_proj or final):
                return
            S.op("act", lambda e, xsb=xsb: e.activation(out=xsq[:], in_=xsb[:], func=AF.Square), reads=[xk], writes=["xsq"])
            bk = C.bank()
            key = mm_group(C, (lambda bk=bk: C.ps[bk][:, :]), [(ones[:], xsq[:, k, :]) for k in range(8)], bk,
                           reads=["ones", "xsq"])
            S.op("act", lambda e, bk=bk: e.activation(out=lnt[:], in_=C.ps[bk][:, :], func=AF.Ln, bias=eps_t[:, 0:1], scale=1.0 / DM),
                 reads=["eps"], writes=[key, "lnt"])
            S.op("act", lambda e: e.activation(out=rstd1[:], in_=lnt[:], func=AF.Exp, scale=-0.5), reads=["lnt"], writes=["rstd1"])
            if final:
                for k in range(8):
                    S.op("dve", lambda e, k=k, xsb=xsb: e.scalar_tensor_tensor(
                        out=outs[:, k, :], in0=xsb[:, k, :], scalar=fnw[:, k:k + 1], in1=rstd1[:], op0=ALU.mult, op1=ALU.mult),
                        reads=[xk, "fnw", "rstd1"], writes=["outs"])
                for k4 in range(2):
                    S.dma("sp", lambda e, k4=k4: e.dma_start(
                        out=d["outT"][4 * k4 * 128:(4 * k4 + 4) * 128, c0:c0 + TT].rearrange("(k p) t -> p k t", p=128),
                        in_=outs[:, 4 * k4:4 * k4 + 4, :]),
                        reads=["outs"], writes=["d_outT"])
                return
            S.op("dve", lambda e, xsb=xsb: e.tensor_copy(out=xb[:], in_=xsb[:]), reads=[xk], writes=["xb"])
            S.dma("sp", lambda e: e.dma_start(out=ctab[64:96, :], in_=d["ctab"][:, c0:c0 + TT]), reads=["d_ctab"], writes=["ctab"])
            S.dma("sp", lambda e: e.dma_start(out=stab[64:96, :], in_=d["stab"][:, c0:c0 + TT]), reads=["d_stab"], writes=["stab"])

            def proj_tile(t):
                bk = C.bank()
                w = TW[t]
                key = mm_group(C, (lambda bk=bk, w=w: C.ps[bk][0:w, :]),
                               [(win[:, k, TOFF[t]:TOFF[t] + w], xb[:, k, :]) for k in range(8)], bk,
                               reads=["win", "xb"])
                return bk, key

            for j in range(2):
                bk, key = proj_tile(j)
                S.op("dve", lambda e, bk=bk, j=j: e.tensor_tensor(out=cqf[:, j, :], in0=C.ps[bk][:, :], in1=rstd1[:], op=ALU.mult),
                     reads=["rstd1"], writes=[key, "cqf"])
            S.op("act", lambda e: e.activation(out=cqsq[:], in_=cqf[:], func=AF.Square), reads=["cqf"], writes=["cqsq"])
            S.op("dve", lambda e: e.tensor_copy(out=cqb[:], in_=cqf[:]), reads=["cqf"], writes=["cqb"])
            bk = C.bank()
            key = mm_group(C, (lambda bk=bk: C.ps[bk][:, :]), [(ones[:], cqsq[:, j, :]) for j in range(2)], bk, reads=["ones", "cqsq"])
            S.op("act", lambda e, bk=bk: e.activation(out=lnt[:], in_=C.ps[bk][:, :], func=AF.Ln, bias=eps_t[:, 0:1], scale=1.0 / 256),
                 reads=["eps"], writes=[key, "lnt"])
            S.op("act", lambda e: e.activation(out=rstd2[:], in_=lnt[:], func=AF.Exp, scale=-0.5), reads=["lnt"], writes=["rstd2"])
            bq = C.bank()
            keyq = mm_group(C, (lambda bq=bq: C.ps[bq][0:96, :]), [(wuq[:, j, 0:96], cqb[:, j, :]) for j in range(2)], bq, reads=["wuq", "cqb"])
            br = C.bank()
            keyr = mm_group(C, (lambda br=br: C.ps[br][0:96, :]), [(wuq[:, j, 96:192], cqb[:, j, :]) for j in range(2)], br, reads=["wuq", "cqb"])
            S.op("dve", lambda e, bq=bq: e.tensor_tensor(out=QTs[0:64, :], in0=C.ps[bq][0:64, :], in1=rstd2[0:64, :], op=ALU.mult),
                 reads=["rstd2"], writes=[keyq, "QTs"])
            S.op("dve", lambda e, bq=bq: e.tensor_tensor(out=t1[64:96, :], in0=C.ps[bq][64:96, :], in1=ctab[64:96, :], op=ALU.mult),
                 reads=["ctab"], writes=[keyq, "t1"])
            S.op("dve", lambda e, br=br: e.tensor_tensor(out=t2[64:96, :], in0=C.ps[br][64:96, :], in1=stab[64:96, :], op=ALU.mult),
                 reads=["stab"], writes=[keyr, "t2"])
            S.op("dve", lambda e: e.tensor_tensor(out=t1[64:96, :], in0=t1[64:96, :], in1=t2[64:96, :], op=ALU.add),
                 reads=["t2", "t1"], writes=["t1"])
            S.op("dve", lambda e: e.tensor_tensor(out=QTs[64:96, :], in0=t1[64:96, :], in1=rstd2[64:96, :], op=ALU.mult),
                 reads=["t1", "rstd2"], writes=["QTs"])
            S.dma("sp", lambda e: e.dma_start(out=d["QT_mla"][:, c0:c0 + TT], in_=QTs[:]), reads=["QTs"], writes=["d_QT_mla"])
            bk, key = proj_tile(2)
            S.op("dve", lambda e, bk=bk: e.tensor_tensor(out=ckvf[:], in0=C.ps[bk][:, :], in1=rstd1[:], op=ALU.mult),
                 reads=["rstd1"], writes=[key, "ckvf"])
            S.op("act", lambda e: e.activation(out=ckvsq[:], in_=ckvf[:], func=AF.Square), reads=["ckvf"], writes=["ckvsq"])
            S.op("act", lambda e: e.activation(out=ckvb[:], in_=ckvf[:], func=AF.Copy), reads=["ckvf"], writes=["ckvb"])
            bk = C.bank()
            key = mm_group(C, (lambda bk=bk: C.ps[bk][:, :]), [(ones[:], ckvsq[:])], bk, reads=["ones", "ckvsq"])
            S.op("act", lambda e, bk=bk: e.activation(out=lnt[:], in_=C.ps[bk][:, :], func=AF.Ln, bias=eps_t[:, 0:1], scale=1.0 / 128),
                 reads=["eps"], writes=[key, "lnt"])
            S.op("act", lambda e: e.activation(out=rstd3[:], in_=lnt[:], func=AF.Exp, scale=-0.5), reads=["lnt"], writes=["rstd3"])
            bk = C.bank()
            key = mm_group(C, (lambda bk=bk: C.ps[bk][0:64, :]), [(wukv[:, 0:64], ckvb[:])], bk, reads=["wukv", "ckvb"])
            S.op("dve", lambda e, bk=bk: e.tensor_tensor(out=KTs[0:64, :], in0=C.ps[bk][0:64, :], in1=rstd3[0:64, :], op=ALU.mult),
                 reads=["rstd3"], writes=[key, "KTs"])
            bk = C.bank()
            key = mm_group(C, (lambda bk=bk: C.ps[bk][0:64, :]), [(wukv[:, 64:128], ckvb[:])], bk, reads=["wukv", "ckvb"])
            S.op("dve", lambda e, bk=bk: e.tensor_tensor(out=vT[:], in0=C.ps[bk][0:64, :], in1=rstd3[0:64, :], op=ALU.mult),
                 reads=["rstd3"], writes=[key, "vT"])

            def transpose_v(src, srck, dst, dstk, dname):
                bk = C.bank()
                key = "ps%d" % bk
                for j in range(4):
                    S.op("pe", lambda e, j=j, bk=bk: e.transpose(
                        C.ps[bk][:, :].bitcast(BF16)[:, j * 64:(j + 1) * 64], src[0:64, j * 128:(j + 1) * 128], identb[0:64, 0:64]),
                        reads=[srck, "identb"], writes=[key], sig=(j == 3))
                S.op("act", lambda e, bk=bk: e.activation(out=dst[:].rearrange("p a b -> p (a b)"),
                                                          in_=C.ps[bk][:, :].bitcast(BF16)[:, 0:256], func=AF.Copy),
                     reads=[], writes=[key, dstk])
                S.dma("sp", lambda e: e.dma_start(
                    out=d[dname][c0:c0 + TT, :].rearrange("(a p) v -> p a v", p=128), in_=dst[:]),
                    reads=[dstk], writes=["d_" + dname])
            transpose_v(vT, "vT", Vs, "Vs", "V_mla")
            bk, key = proj_tile(8)
            S.op("dve", lambda e, bk=bk: e.tensor_tensor(out=mqf[:], in0=C.ps[bk][0:64, :], in1=rstd1[0:64, :], op=ALU.mult),
                 reads=["rstd1"], writes=[key, "mqf"])
            S.op("dve", lambda e, bk=bk: e.tensor_tensor(out=t3[64:96, :], in0=C.ps[bk][64:96, :], in1=ctab[64:96, :], op=ALU.mult),
                 reads=["ctab"], writes=[key, "t3"])
            S.op("act", lambda e: e.activation(out=mqb[:], in_=mqf[:], func=AF.Copy, scale=0.125),
                 reads=["mqf"], writes=["mqb"])
            S.dma("sp", lambda e: e.dma_start(out=d["QTf_moba"][:, c0:c0 + TT], in_=mqf[:]), reads=["mqf"], writes=["d_QTf_moba"])
            S.dma("sp", lambda e: e.dma_start(out=d["QT_moba"][:, c0:c0 + TT], in_=mqb[:]), reads=["mqb"], writes=["d_QT_moba"])
            bk, key = proj_tile(9)
            S.op("dve", lambda e, bk=bk: e.tensor_tensor(out=mkf[:], in0=C.ps[bk][0:64, :], in1=rstd1[0:64, :], op=ALU.mult),
                 reads=["rstd1"], writes=[key, "mkf"])
            S.op("dve", lambda e, bk=bk: e.tensor_tensor(out=t4[64:96, :], in0=C.ps[bk][64:96, :], in1=stab[64:96, :], op=ALU.mult),
                 reads=["stab"], writes=[key, "t4"])
            S.op("act", lambda e: e.activation(out=mkb[:], in_=mkf[:], func=AF.Copy), reads=["mkf"], writes=["mkb"])
            S.op("dve", lambda e: e.tensor_reduce(out=km[:, 2 * ti:2 * ti + 2], in_=mkf[:].rearrange("p (a b) -> p a b", b=256),
                                                  op=ALU.add, axis=AX.X),
                 reads=["mkf"], writes=["km"])
            S.dma("sp", lambda e: e.dma_start(out=d["KT_moba"][:, c0:c0 + TT], in_=mkb[:]), reads=["mkb"], writes=["d_KT_moba"])
            S.op("dve", lambda e: e.tensor_tensor(out=t3[64:96, :], in0=t3[64:96, :], in1=t4[64:96, :], op=ALU.add),
                 reads=["t3", "t4"], writes=["t3"])
            S.op("dve", lambda e: e.tensor_tensor(out=KTs[64:96, :], in0=t3[64:96, :], in1=rstd1[64:96, :], op=ALU.mult),
                 reads=["t3", "rstd1"], writes=["KTs"])
            S.dma("sp", lambda e: e.dma_start(out=d["KT_mla"][:, c0:c0 + TT], in_=KTs[:]), reads=["KTs"], writes=["d_KT_mla"])
            bk, key = proj_tile(10)
            S.op("dve", lambda e, bk=bk: e.tensor_tensor(out=mvT[:], in0=C.ps[bk][0:64, :], in1=rstd1[0:64, :], op=ALU.mult),
                 reads=["rstd1"], writes=[key, "mvT"])
            transpose_v(mvT, "mvT", mVs, "mVs", "V_moba")
            bk, key = proj_tile(3)
            S.op("dve", lambda e, bk=bk: e.tensor_tensor(out=gf[:], in0=C.ps[bk][:, :], in1=rstd1[:], op=ALU.mult),
                 reads=["rstd1"], writes=[key, "gf"])
            S.op("act", lambda e: e.activation(out=gs[:], in_=gf[:], func=AF.Silu), reads=["gf"], writes=["gs"])
            S.dma("sp", lambda e: e.dma_start(out=d["GT_mla"][:, c0:c0 + TT], in_=gs[0:64, :]), reads=["gs"], writes=["d_GT_mla"])
            S.dma("sp", lambda e: e.dma_start(out=d["GT_moba"][:, c0:c0 + TT], in_=gs[64:128, :]), reads=["gs"], writes=["d_GT_moba"])
            for j, nm in enumerate(("gqT", "gkT", "gvT")):
                bk, key = proj_tile(4 + j)
                S.op("dve", lambda e, bk=bk, j=j: e.tensor_tensor(out=gq3[j][:], in0=C.ps[bk][:, :], in1=rstd1[:], op=ALU.mult),
                     reads=["rstd1"], writes=[key, "gq3_%d" % j])
                S.dma("sp", lambda e, j=j, nm=nm: e.dma_start(out=d[nm][:, c0:c0 + TT], in_=gq3[j][:]),
                      reads=["gq3_%d" % j], writes=["d_" + nm])
            bk, key = proj_tile(7)
            S.op("dve", lambda e, bk=bk: e.tensor_tensor(out=zf[:], in0=C.ps[bk][:, :], in1=rstd1[:], op=ALU.mult),
                 reads=["rstd1"], writes=[key, "zf"])
            S.op("act", lambda e: e.activation(out=zs[:], in_=zf[:], func=AF.Silu), reads=["zf"], writes=["zs"])
            S.dma("sp", lambda e: e.dma_start(out=d["gzT"][:, c0:c0 + TT], in_=zs[:]), reads=["zs"], writes=["d_gzT"])
            bk, key = proj_tile(11)
            S.op("dve", lambda e, bk=bk: e.tensor_tensor(out=bas[:], in0=C.ps[bk][0:2, :], in1=rstd1[0:2, :], op=ALU.mult),
                 reads=["rstd1"], writes=[key, "bas"])
            S.dma("sp", lambda e: e.dma_start(out=d["baT"][:, c0:c0 + TT], in_=bas[:]), reads=["bas"], writes=["d_baT"])
        for ti in tiles:
            do_tile(ti)
        if do_proj and not final:
            S.op("dve", lambda e: e.tensor_scalar(out=km[:], in0=km[:], scalar1=1.0 / 256, scalar2=None, op0=ALU.mult),
                 reads=["km"], writes=["km"])
            S.dma("sp", lambda e: e.dma_start(out=d["kmT"][:, :], in_=km[:]), reads=["km"], writes=["d_kmT"])
        S.drain_dmas("sp")
        S.replay()


def rope_consts():
    inv = (1.0 / (10000.0 ** (np.arange(0, 32, 2, dtype=np.float32) / 32))).astype(np.float32)
    ang = (np.arange(SEQ, dtype=np.float32)[:, None] * inv[None, :]).astype(np.float32)
    cos = np.cos(ang).astype(np.float32).T
    sin = np.sin(ang).astype(np.float32).T
    ctab = np.ascontiguousarray(np.concatenate([cos, cos], 0))
    stab = np.ascontiguousarray(np.concatenate([-sin, sin], 0))
    return ctab, stab


def in_cols(h):
    cq0, ckv0, kr0, mg0, gq0, gk0, gv0, gz0, gb0, ga0, mq0, mk0, mv0, cg0 = (
        0, 256, 384, 416, 672, 1184, 1696, 2208, 2720, 2724, 2728, 2984, 3240, 3496)
    r = lambda a, n: list(range(a, a + n))
    cols = []
    cols += r(cq0, 256) + r(ckv0, 128)
    cols += r(mg0 + 64 * h, 64) + r(cg0 + 64 * h, 64)
    cols += r(gq0 + 128 * h, 128) + r(gk0 + 128 * h, 128) + r(gv0 + 128 * h, 128) + r(gz0 + 128 * h, 128)
    cols += r(mq0 + 64 * h, 64) + r(kr0, 32)
    cols += r(mk0 + 64 * h, 64) + r(kr0 + 16, 16) + r(kr0, 16)
    cols += r(mv0 + 64 * h, 64)
    cols += [gb0 + h, ga0 + h]
    assert len(cols) == NCOL
    return cols


def prep_layer(I, l, h):
    f = np.float32
    out = {}
    out["w_in"] = np.ascontiguousarray(I["w_in"][l][:, in_cols(h)]).astype(f)
    out["nw"] = np.ascontiguousarray(I["norm_w"][l].reshape(8, 128).T).astype(f)
    wq = I["mla_w_uq"][l][:, 96 * h:96 * h + 96]
    wuq = np.zeros((256, 192), f)
    wuq[:, 0:96] = wq
    wuq[:, 160:176] = wq[:, 80:96]
    wuq[:, 176:192] = wq[:, 64:80]
    out["wuq"] = wuq
    out["qnw"] = np.ascontiguousarray(I["mla_q_norm"][l].reshape(2, 128).T).astype(f)
    out["wukv"] = np.ascontiguousarray(I["mla_w_ukv"][l][:, 128 * h:128 * h + 128]).astype(f)
    out["kvnw"] = np.ascontiguousarray(I["mla_kv_norm"][l].reshape(128, 1)).astype(f)
    return out


PROJ_OUTS = [("QT_mla", (96, SEQ), BF16), ("KT_mla", (96, SEQ), BF16), ("V_mla", (SEQ, 64), BF16),
             ("QTf_moba", (64, SEQ), F32), ("QT_moba", (64, SEQ), BF16), ("KT_moba", (64, SEQ), BF16),
             ("V_moba", (SEQ, 64), BF16), ("kmT", (64, 32), F32),
             ("GT_mla", (64, SEQ), BF16), ("GT_moba", (64, SEQ), BF16),
             ("gqT", (128, SEQ), F32), ("gkT", (128, SEQ), F32), ("gvT", (128, SEQ), F32),
             ("gzT", (128, SEQ), BF16), ("baT", (2, SEQ), F32)]
PROJ_INS = [("w_in", (DM, NCOL)), ("nw", (128, 8)), ("wuq", (256, 192)), ("qnw", (128, 2)),
            ("wukv", (128, 128)), ("kvnw", (128, 1))]


def t5_consts():
    e = np.arange(3072)
    dd = e - 511
    n = np.maximum(dd, 0)
    nf = np.maximum(n, 1).astype(np.float32)
    large = 16 + (np.log(nf / np.float32(16)) / np.float32(math.log(2048 / 16)) * np.float32(16)).astype(np.int32)
    large = np.minimum(large, 31)
    bucket = np.where(n < 16, n, large)
    OH = np.zeros((32, 3072), np.float32)
    OH[bucket, e] = 1.0
    OH[:, dd < 0] = 0.0
    return OH


def moba_consts():
    pen = np.zeros((32, 32), np.float32)
    for own in range(32):
        pen[own, own] = 1e30
        pen[own, own + 1:] = -1e30
    E = np.zeros((32, SEQ), np.float32)
    for j in range(32):
        E[j, j * 256:(j + 1) * 256] = 1.0
    return pen, E


def stage_attn(C, moba, qtiles=range(NT)):
    nc, S, d = C.nc, C.S, C.d
    pre = "moba" if moba else "mla"
    KD = 96
    scale = 1.0 if moba else 96 ** -0.5
    rowbase = 64 if moba else 0
    with ExitStack() as st:
        def sb(name, shape, dt):
            C.uid += 1
            return st.enter_context(nc.sbuf_tensor("sb%d_%s" % (C.uid, name), list(shape), dt))
        QT = sb("QT", [96, SEQ], BF16)
        KT = sb("KT", [96, SEQ], BF16)
        Va = sb("Va", [128, 64, 65], BF16)
        GT = sb("GT", [64, SEQ], BF16)
        Pb = [sb("P%d" % i, [128, TT], BF16) for i in range(4)]
        osb = sb("osb", [65, TT], F32)
        rec = sb("rec", [64, TT], F32)
        otmp = sb("otmp", [64, TT], F32)
        ysb = sb("ysb", [64, TT], BF16)
        if C.fused:
            ym = [sb("ym%d" % j, [64, TT], BF16) for j in range(4)]
            hm = sb("hm", [128, 4], F32)
            S.dma("sp", lambda e: e.dma_start(out=hm[:], in_=d["hm"][:, :]), reads=["d_hm"], writes=["hm"])
        sel = sb("sel", [65, 64], F32)
        nrows = 64 if moba else 96
        for q4 in range(4):
            cs = slice(q4 * 2048, (q4 + 1) * 2048)
            S.dma("sp", lambda e, cs=cs: e.dma_start(out=QT[0:nrows, cs], in_=d["QT_" + pre][:, cs]), reads=["d_QT_" + pre], writes=["QT"])
            S.dma("sp", lambda e, cs=cs: e.dma_start(out=KT[0:nrows, cs], in_=d["KT_" + pre][:, cs]), reads=["d_KT_" + pre], writes=["KT"])
            S.dma("sp", lambda e, cs=cs: e.dma_start(out=GT[:, cs], in_=d["GT_" + pre][:, cs]), reads=["d_GT_" + pre], writes=["GT"])
        for a4 in range(16):
            S.dma("sp", lambda e, a4=a4: e.dma_start(
                out=Va[:, 4 * a4:4 * a4 + 4, 0:64],
                in_=d["V_" + pre][a4 * 512:(a4 + 1) * 512, :].rearrange("(a p) v -> p a v", p=128)),
                reads=["d_V_" + pre], writes=["Va"])
        S.op("dve", lambda e: e.memset(Va[:, :, 64:65], 1.0), writes=["Va"])
        S.dma("sp", lambda e: e.dma_start(out=sel[:], in_=d["sel"][:, :]), reads=["d_sel"], writes=["sel"])
        if not moba:
            trif = sb("trif", [128, 128], F32)
            tri = sb("tri", [128, 128], BF16)
            S.dma("sp", lambda e: e.dma_start(out=trif[:], in_=d["tri"][:, :]), reads=["d_tri"], writes=["trif"])
            S.op("dve", lambda e: e.tensor_copy(out=tri[:], in_=trif[:]), reads=["trif"], writes=["tri"])
        else:
            t5c = sb("t5c", [32, 1], F32)
            et = sb("et", [32, 1], F32)
            b31 = sb("b31", [128, 1], F32)
            OH = sb("OH", [32, 3072], F32)
            fvs = sb("fvs", [1, 3072], F32)
            ES32 = sb("ES32", [128, 2560], F32)
            ES = sb("ES", [128, 2560], BF16)
            S.dma("sp", lambda e: e.dma_start(out=t5c[:], in_=d["t5h"][:, :]), reads=["d_t5h"], writes=["t5c"])
            S.dma("sp", lambda e: e.dma_start(out=b31[:], in_=d["t5h"][31:32, :].partition_broadcast(128).rearrange("p a b -> p (a b)")),
                  reads=["d_t5h"], writes=["b31"])
            S.dma("sp", lambda e: e.dma_start(out=OH[:], in_=d["OH"][:, :]), reads=["d_OH"], writes=["OH"])
            S.op("act", lambda e: e.activation(out=et[:], in_=t5c[:], func=AF.Exp), reads=["t5c"], writes=["et"])
            for j in range(6):
                key = mm_group(C, (lambda: C.ps[5][0:1, :]), [(et[:, 0:1], OH[:, j * 512:(j + 1) * 512])], 5, reads=["et", "OH"])
                S.op("dve", lambda e, j=j: e.tensor_copy(out=fvs[:, j * 512:(j + 1) * 512], in_=C.ps[5][0:1, :]), reads=[], writes=[key, "fvs"])
            S.dma("sp", lambda e: e.dma_start(out=d["fv"][:, :], in_=fvs[:]), reads=["fvs"], writes=["d_fv"])
            for ki in range(128):
                S.dma("sp", lambda e, ki=ki: e.dma_start(
                    out=ES32[ki:ki + 1, :], in_=d["fv"][:, 127 - ki:127 - ki + 2560]), reads=["d_fv"], writes=["ES32_%d" % ki])
            S.op("dve", lambda e: e.tensor_copy(out=ES[:], in_=ES32[:]), reads=["ES32_%d" % ki for ki in range(128)], writes=["ES"])
            for q4 in range(4):
                cs = slice(q4 * 2048, (q4 + 1) * 2048)
                S.dma("sp", lambda e, cs=cs: e.dma_start(out=KT[64:96, cs], in_=d["Eb"][:, cs]), reads=["d_Eb"], writes=["KT"])
            QTf = sb("QTf", [64, SEQ], F32)
            kmT = sb("kmT", [64, 32], F32)
            pen = sb("pen", [128, 32 * 32], F32)
            identf = sb("identf", [128, 128], F32)
            identb = sb("identb", [128, 128], BF16)
            S.dma("sp", lambda e: e.dma_start(out=identf[:], in_=d["ident"][:, :]), reads=["d_ident"], writes=["identf"])
            S.op("dve", lambda e: e.tensor_copy(out=identb[:], in_=identf[:]), reads=["identf"], writes=["identb"])
            for q4 in range(4):
                cs = slice(q4 * 2048, (q4 + 1) * 2048)
                S.dma("sp", lambda e, cs=cs: e.dma_start(out=QTf[:, cs], in_=d["QTf_moba"][:, cs]), reads=["d_QTf_moba"], writes=["QTf"])
            S.dma("sp", lambda e: e.dma_start(out=kmT[:], in_=d["kmT"][:, :]), reads=["d_kmT"], writes=["kmT"])
            S.dma("sp", lambda e: e.dma_start(out=pen[:], in_=d["pen"][:, :].rearrange("a b -> (a b)").partition_broadcast(128)),
                  reads=["d_pen"], writes=["pen"])
            gm = [sb("gm%d" % i, [128, 32], F32) for i in range(4)]
            m8 = [sb("m8%d" % i, [128, 8], F32) for i in range(4)]
            thr = [sb("thr%d" % i, [128, 1], F32) for i in range(4)]
            Mq = [sb("Mq%d" % i, [128, 96], BF16) for i in range(4)]
            for i in range(4):
                S.op("dve", lambda e, i=i: e.memset(Mq[i][:], 0.0), writes=["Mq%d" % i])
            for g in range(16):
                keys = []
                for j in range(4):
                    i = g * 4 + j
                    keys.append(mm_group(C, (lambda j=j: C.ps[6][:, j * 32:(j + 1) * 32]), [(QTf[:, i * 128:(i + 1) * 128], kmT[:, :])], 6,
                                         reads=["QTf", "kmT"]))
                for j in range(4):
                    own = (g * 4 + j) // 2
                    S.op("dve", lambda e, j=j, own=own: e.tensor_tensor(out=gm[j][:], in0=C.ps[6][:, j * 32:(j + 1) * 32],
                                                                      in1=pen[:, own * 32:(own + 1) * 32], op=ALU.add),
                         reads=["pen"], writes=[keys[j], "gm%d" % j])
                for j in range(4):
                    S.op("dve", lambda e, j=j: e.max(out=m8[j][:], in_=gm[j][:]), reads=["gm%d" % j], writes=["m8%d" % j])
                for j in range(4):
                    S.op("dve", lambda e, j=j: e.tensor_scalar(out=thr[j][:], in0=m8[j][:, 3:4], scalar1=-1e29, scalar2=None, op0=ALU.max),
                         reads=["m8%d" % j], writes=["thr%d" % j])
                for j in range(4):
                    S.op("dve", lambda e, j=j: e.tensor_scalar(out=Mq[j][:, 64:96], in0=gm[j][:], scalar1=thr[j][:, 0:1], scalar2=-30000.0,
                                                               op0=ALU.is_lt, op1=ALU.mult),
                         reads=["gm%d" % j, "thr%d" % j], writes=["Mq%d" % j])
                for j in range(4):
                    S.op("pe", lambda e, j=j: e.transpose(C.ps[7][0:96, :].bitcast(BF16)[:, j * 128:(j + 1) * 128], Mq[j][:], identb[:]),
                         reads=["Mq%d" % j, "identb"], writes=["ps7"], sig=True)
                S.op("act", lambda e, g=g: e.activation(out=QT[64:96, g * 512:(g + 1) * 512],
                                                        in_=C.ps[7][64:96, :].bitcast(BF16)[:, 0:512], func=AF.Copy),
                     reads=[], writes=["ps7", "QT"])

        SB = (0, 1, 2, 3)
        OB = (4, 5)
        NB = 4
        DEPTHQ = 3
        items = []
        for qi, qt in enumerate(qtiles):
            for kt in range(4 * qt + 4):
                items.append((qi, qt, kt))

        def front(idx):
            qi, qt, kt = items[idx]
            q0 = qt * TT
            k0 = kt * 128
            diag = kt >= 4 * qt
            koff = (kt - 4 * qt) * 128 if diag else 0
            sbk = SB[idx % NB]
            skey = "ps%d" % sbk
            P = Pb[idx % NB]
            pkey = "P%d" % (idx % NB)
            S.op("pe", lambda e: e.matmul(
                C.ps[sbk][:, koff:TT], KT[0:KD, k0:k0 + 128], QT[0:KD, q0 + koff:q0 + TT], start=True, stop=True),
                reads=["KT", "QT"], writes=[skey])
            far = moba and (q0 - k0 >= 1664)
            if far:
                S.op("act", lambda e: e.activation(
                    out=P[:, koff:TT], in_=C.ps[sbk][:, koff:TT], func=AF.Exp, bias=b31[:, 0:1], scale=scale),
                    reads=["b31"], writes=[skey, pkey])
            else:
                S.op("act", lambda e: e.activation(
                    out=P[:, koff:TT], in_=C.ps[sbk][:, koff:TT], func=AF.Exp, scale=scale),
                    reads=[], writes=[skey, pkey])
                if moba:
                    s0 = q0 - k0 + 384
                    S.op("pool", lambda e: e.tensor_tensor(
                        out=P[:, koff:TT], in0=P[:, koff:TT], in1=ES[:, s0 + koff:s0 + TT], op=ALU.mult),
                        reads=["ES", pkey], writes=[pkey])
                elif diag:
                    S.op("pool", lambda e: e.tensor_tensor(
                        out=P[:, koff:koff + 128], in0=P[:, koff:koff + 128], in1=tri[:], op=ALU.mult),
                        reads=["tri", pkey], writes=[pkey])

        def back(idx):
            qi, qt, kt = items[idx]
            q0 = qt * TT
            diag = kt >= 4 * qt
            koff = (kt - 4 * qt) * 128 if diag else 0
            nkt = 4 * qt + 4
            ob = OB[qi % 2]
            okey = "ps%d" % ob
            P = Pb[idx % NB]
            pkey = "P%d" % (idx % NB)
            S.op("pe", lambda e: e.matmul(
                C.ps[ob][0:65, koff:TT], Va[:, kt, :], P[:, koff:TT], start=(kt == 0), stop=(kt == nkt - 1)),
                reads=[pkey, "Va"], writes=[okey])
            if kt != nkt - 1:
                return
            S.op("act", lambda e: e.activation(out=osb[:], in_=C.ps[ob][0:65, :], func=AF.Copy), reads=[], writes=[okey, "osb"])
            key = mm_group(C, (lambda: C.ps[6][0:64, :]), [(sel[:], osb[:])], 6, reads=["sel", "osb"])
            S.op("dve", lambda e: e.reciprocal(out=rec[:], in_=C.ps[6][0:64, :]), reads=[], writes=[key, "rec"])
            S.op("dve", lambda e: e.tensor_tensor(out=otmp[:], in0=osb[0:64, :], in1=rec[:], op=ALU.mult), reads=["osb", "rec"], writes=["otmp"])
            if not C.fused:
                S.op("dve", lambda e: e.tensor_tensor(out=ysb[:], in0=otmp[:], in1=GT[:, q0:q0 + TT], op=ALU.mult),
                     reads=["otmp", "GT"], writes=["ysb"])
                S.dma("sp", lambda e: e.dma_start(out=d["yT_h"][rowbase:rowbase + 64, q0:q0 + TT], in_=ysb[:]),
                      reads=["ysb"], writes=["d_yT_h"])
            else:
                qq, qc = q0 // 2048, q0 % 2048
                for j in range(4):
                    S.op("dve", lambda e, j=j: e.scalar_tensor_tensor(out=ym[j][:], in0=otmp[:], scalar=hm[0:64, j:j + 1],
                                                                      in1=GT[:, q0:q0 + TT], op0=ALU.mult, op1=ALU.mult),
                         reads=["otmp", "GT", "hm"], writes=["ym%d" % j])
                    S.dma("sp", lambda e, j=j: e.dma_start(
                        out=d["ypad%d" % qq][256 * j + rowbase:256 * j + rowbase + 64, qc:qc + TT], in_=ym[j][:]),
                        reads=["ym%d" % j], writes=["d_ypad%d" % qq])

        n_it = len(items)
        for idx in range(n_it + DEPTHQ):
            if idx < n_it:
                front(idx)
            if idx - DEPTHQ >= 0:
                back(idx - DEPTHQ)
        S.drain_dmas("sp")
        S.replay()


GC = 128
NCH = SEQ // GC


def gdn_consts():
    i = np.arange(128)
    umask = (i[:, None] <= i[None, :]).astype(np.float32)
    m2 = (i[:, None] > i[None, :]).astype(np.float32)
    neg = np.where(i[:, None] < i[None, :], -30000.0, 0.0).astype(np.float32)
    sl = (i[:, None] > i[None, :]).astype(np.float32)
    return umask, m2, neg, sl


def level_masks():
    i = np.arange(128)
    out = np.zeros((128, 14, 128), np.float32)
    for l in range(7):
        b = 1 << l
        bi = i // b
        m = ((bi[:, None] % 2 == 1) & (bi[None, :] == bi[:, None] - 1)).astype(np.float32)
        out[:, 2 * l, :] = m
        out[:, 2 * l + 1, :] = m.T
    return out.reshape(128, 14 * 128)


def stage_gdn(C, nchunks=NCH, G=4, stop_after=None):
    nc, S, d = C.nc, C.S, C.d
    NSET = 2 * G
    with ExitStack() as st:
        def sb(name, shape, dt):
            C.uid += 1
            return st.enter_context(nc.sbuf_tensor("sb%d_%s" % (C.uid, name), list(shape), dt))
        identf = sb("identf", [128, 128], F32)
        umask = sb("umask", [128, 128], F32)
        m2 = sb("m2", [128, 128], F32)
        neg = sb("neg", [128, 128], F32)
        slm = sb("slm", [128, 128], F32)
        onesf = sb("onesf", [128, 128], F32)
        identb = sb("identb", [128, 128], BF16)
        lvf = sb("lvf", [128, 14 * 128], F32)
        lvm = sb("lvm", [128, 14 * 128], BF16)
        S.dma("sp", lambda e: e.dma_start(out=lvf[:], in_=d["lvlm"][:, :]), reads=["d_lvlm"], writes=["lvf"])
        S.op("dve", lambda e: e.tensor_copy(out=lvm[:], in_=lvf[:]), reads=["lvf"], writes=["lvm"])
        for nm, t in (("ident", identf), ("umask", umask), ("m2", m2), ("neg", neg), ("slm", slm)):
            S.dma("sp", lambda e, nm=nm, t=t: e.dma_start(out=t[:], in_=d[nm][:, :]), reads=["d_" + nm], writes=[nm])
        S.op("dve", lambda e: e.memset(onesf[:], 1.0), writes=["onesf"])
        S.op("dve", lambda e: e.tensor_copy(out=identb[:], in_=identf[:]), reads=["ident"], writes=["identb"])
        eps_t = sb("eps_t", [128, 1], F32)
        S.op("dve", lambda e: e.memset(eps_t[:], EPS), writes=["eps"])
        cw = sb("cw", [128, 12], F32)
        S.dma("sp", lambda e: e.dma_start(out=cw[:], in_=d["cw"][:, :]), reads=["d_cw"], writes=["cw"])
        gsc = sb("gsc", [128, 4], F32)
        S.dma("sp", lambda e: e.dma_start(out=gsc[:, 0:2], in_=d["gsc"][:, :].rearrange("a b -> (a b)").partition_broadcast(128)),
              reads=["d_gsc"], writes=["gsc"])
        gnw = sb("gnw", [128, 1], F32)
        S.dma("sp", lambda e: e.dma_start(out=gnw[:], in_=d["gnw"][:, :]), reads=["d_gnw"], writes=["gnw"])
        S.op("act", lambda e: e.activation(out=gsc[:, 2:3], in_=gsc[:, 0:1], func=AF.Exp), reads=["gsc"], writes=["gsc2"])
        S.op("dve", lambda e: e.tensor_scalar(out=gsc[:, 3:4], in0=gsc[:, 2:3], scalar1=-1.0, scalar2=None, op0=ALU.mult),
             reads=["gsc2"], writes=["gsc3"])
        ba = [sb("ba%d" % i, [2, TT], F32) for i in range(2)]
        batok = sb("batok", [128, NCH, 2], F32)
        for c in range(NCH):
            bi = (c // 4) % 2
            if c % 4 == 0:
                S.dma("sp", lambda e, c=c, bi=bi: e.dma_start(out=ba[bi][:], in_=d["baT"][:, c * 128:c * 128 + TT]),
                      reads=["d_baT"], writes=["ba%d" % bi])
            S.op("pe", lambda e, c=c, bi=bi: e.matmul(C.ps[0][:, 2 * c:2 * c + 2], ba[bi][0:2, (c % 4) * 128:(c % 4 + 1) * 128], identf[0:2, 0:2],
                                                      start=True, stop=True),
                 reads=["ba%d" % bi, "ident"], writes=["ps0"], sig=True)
        S.op("dve", lambda e: e.tensor_copy(out=batok[:].rearrange("p a b -> p (a b)"), in_=C.ps[0][:, 0:2 * NCH]), reads=[], writes=["ps0", "batok"])
        beta = sb("beta", [128, NCH], F32)
        nbeta = sb("nbeta", [128, NCH], F32)
        gg = sb("gg", [128, NCH], F32)
        tmpa = sb("tmpa", [128, NCH], F32)
        gc = sb("gc", [128, NCH], F32)
        gce = sb("gce", [128, NCH], F32)
        egc = sb("egc", [128, NCH], F32)
        bke = sb("bke", [128, NCH], F32)
        eend = sb("eend", [128, NCH], F32)
        gend = sb("gend", [128, NCH], F32)
        S.op("act", lambda e: e.activation(out=beta[:], in_=batok[:, :, 0], func=AF.Sigmoid), reads=["batok"], writes=["beta"])
        S.op("dve", lambda e: e.tensor_scalar(out=nbeta[:], in0=beta[:], scalar1=-1.0, scalar2=None, op0=ALU.mult), reads=["beta"], writes=["nbeta"])
        S.op("act", lambda e: e.activation(out=tmpa[:], in_=batok[:, :, 1], func=AF.Exp, bias=gsc[:, 1:2], scale=1.0), reads=["batok", "gsc"], writes=["tmpa"])
        one_t = sb("one_t", [128, 1], F32)
        S.op("dve", lambda e: e.memset(one_t[:], 1.0), writes=["one_t"])
        S.op("act", lambda e: e.activation(out=tmpa[:], in_=tmpa[:], func=AF.Ln, bias=one_t[:, 0:1], scale=1.0), reads=["tmpa", "one_t"], writes=["tmpa"])
        S.op("dve", lambda e: e.tensor_scalar(out=gg[:], in0=tmpa[:], scalar1=gsc[:, 3:4], scalar2=None, op0=ALU.mult), reads=["tmpa", "gsc3"], writes=["gg"])
        key = mm_group(C, (lambda: C.ps[1][:, 0:NCH]), [(umask[:], gg[:])], 1, reads=["umask", "gg"])
        S.op("dve", lambda e: e.tensor_copy(out=gc[:], in_=C.ps[1][:, 0:NCH]), reads=[], writes=[key, "gc"])
        key = mm_group(C, (lambda: C.ps[2][:, 0:NCH]), [(onesf[:], gg[:])], 2, reads=["onesf", "gg"])
        S.op("dve", lambda e: e.tensor_copy(out=gce[:], in_=C.ps[2][:, 0:NCH]), reads=[], writes=[key, "gce"])
        S.op("act", lambda e: e.activation(out=egc[:], in_=gc[:], func=AF.Exp), reads=["gc"], writes=["egc"])
        S.op("act", lambda e: e.activation(out=gend[:], in_=gce[:], func=AF.Exp), reads=["gce"], writes=["gend"])
        S.op("dve", lambda e: e.tensor_tensor(out=bke[:], in0=beta[:], in1=egc[:], op=ALU.mult), reads=["beta", "egc"], writes=["bke"])
        S.op("dve", lambda e: e.tensor_tensor(out=eend[:], in0=gce[:], in1=gc[:], op=ALU.subtract), reads=["gce", "gc"], writes=["eend"])
        S.op("act", lambda e: e.activation(out=eend[:], in_=eend[:], func=AF.Exp), reads=["eend"], writes=["eend"])
        PERTOK = ["beta", "nbeta", "gg", "egc", "bke", "eend", "gend"]
        if stop_after == 1:
            S.drain_dmas("sp"); S.replay(); return

        QnT = sb("QnT", [128, SEQ], BF16)
        KnT = sb("KnT", [128, SEQ], BF16)
        Kb = sb("Kb", [128, NCH, 128], BF16)
        Kend = sb("Kend", [128, NCH, 128], BF16)
        Vb = sb("Vb", [128, NCH, 128], BF16)
        xin = [[sb("xin%d_%d" % (j, i), [128, 3 + TT], F32) for i in range(2)] for j in range(3)]
        cacc = [sb("cacc%d" % j, [128, TT], F32) for j in range(3)]
        sact = [sb("sact%d" % j, [128, TT], F32) for j in range(3)]
        sq2 = [sb("sq2%d" % j, [128, TT], F32) for j in range(2)]
        lnt = sb("lnt", [128, TT], F32)
        rr = [sb("rr%d" % j, [128, TT], F32) for j in range(2)]
        knf = sb("knf", [128, TT], F32)
        names3 = ("gqT", "gkT", "gvT")
        ntile_a = (nchunks * GC + TT - 1) // TT

        def phase_a(ti):
            c0 = ti * TT
            b = ti % 2
            for j in range(3):
                xt = xin[j][b]
                xk = "xin%d_%d" % (j, b)
                if ti == 0:
                    S.op("pool", lambda e, xt=xt: e.memset(xt[:, 0:3], 0.0), writes=[xk])
                    S.dma("sp", lambda e, xt=xt, j=j: e.dma_start(out=xt[:, 3:3 + TT], in_=d[names3[j]][:, 0:TT]),
                          reads=["d_" + names3[j]], writes=[xk])
                else:
                    S.dma("sp", lambda e, xt=xt, j=j: e.dma_start(out=xt[:, :], in_=d[names3[j]][:, c0 - 3:c0 + TT]),
                          reads=["d_" + names3[j]], writes=[xk])
                ck = "cacc%d" % j
                S.op("dve", lambda e, xt=xt, j=j: e.tensor_scalar(out=cacc[j][:], in0=xt[:, 0:TT], scalar1=cw[:, 4 * j:4 * j + 1],
                                                                 scalar2=None, op0=ALU.mult), reads=[xk, "cw"], writes=[ck])
                for tap in range(1, 4):
                    S.op("dve", lambda e, xt=xt, j=j, tap=tap: e.scalar_tensor_tensor(
                        out=cacc[j][:], in0=xt[:, tap:tap + TT], scalar=cw[:, 4 * j + tap:4 * j + tap + 1], in1=cacc[j][:],
                        op0=ALU.mult, op1=ALU.add), reads=[xk, "cw", ck], writes=[ck])
                S.op("act", lambda e, j=j: e.activation(out=sact[j][:], in_=cacc[j][:], func=AF.Silu), reads=[ck], writes=["sact%d" % j])
            for j in range(2):
                S.op("act", lambda e, j=j: e.activation(out=sq2[j][:], in_=sact[j][:], func=AF.Square), reads=["sact%d" % j], writes=["sq2%d" % j])
                bk = C.bank()
                key = mm_group(C, (lambda bk=bk: C.ps[bk][:, :]), [(onesf[:], sq2[j][:])], bk, reads=["onesf", "sq2%d" % j])
                S.op("act", lambda e, bk=bk: e.activation(out=lnt[:], in_=C.ps[bk][:, :], func=AF.Ln, bias=eps_t[:, 0:1], scale=1.0),
                     reads=["eps"], writes=[key, "lnt"])
                S.op("act", lambda e, j=j: e.activation(out=rr[j][:], in_=lnt[:], func=AF.Exp, scale=-0.5), reads=["lnt"], writes=["rr%d" % j])
            S.op("dve", lambda e: e.scalar_tensor_tensor(out=QnT[:, c0:c0 + TT], in0=sact[0][:], scalar=float(128 ** -0.5), in1=rr[0][:],
                                                         op0=ALU.mult, op1=ALU.mult), reads=["sact0", "rr0"], writes=["QnT"])
            S.op("dve", lambda e: e.tensor_tensor(out=knf[:], in0=sact[1][:], in1=rr[1][:], op=ALU.mult), reads=["sact1", "rr1"], writes=["knf"])
            S.op("act", lambda e: e.activation(out=KnT[:, c0:c0 + TT], in_=knf[:], func=AF.Copy), reads=["knf"], writes=["KnT"])
            for a in range(4):
                c = ti * 4 + a
                bk = C.bank()
                key = "ps%d" % bk
                S.op("pe", lambda e, bk=bk, a=a: e.transpose(C.ps[bk][:, 0:128], knf[:, a * 128:(a + 1) * 128], identf[:]),
                     reads=["knf", "ident"], writes=[key])
                S.op("pe", lambda e, bk=bk, a=a: e.transpose(C.ps[bk][:, 128:256], sact[2][:, a * 128:(a + 1) * 128], identf[:]),
                     reads=["sact2", "ident"], writes=[key])
                S.op("act", lambda e, bk=bk, c=c: e.activation(out=Kb[:, c, :], in_=C.ps[bk][:, 0:128], func=AF.Copy, scale=bke[:, c:c + 1]),
                     reads=["bke"], writes=[key, "Kb"])
                S.op("dve", lambda e, bk=bk, c=c: e.tensor_scalar(out=Kend[:, c, :], in0=C.ps[bk][:, 0:128], scalar1=eend[:, c:c + 1], scalar2=None, op0=ALU.mult),
                     reads=["eend"], writes=[key, "Kend"])
                S.op("act", lambda e, bk=bk, c=c: e.activation(out=Vb[:, c, :], in_=C.ps[bk][:, 128:256], func=AF.Copy, scale=beta[:, c:c + 1]),
                     reads=["beta"], writes=[key, "Vb"])
        for ti in range(ntile_a):
            phase_a(ti)
        if stop_after == 2:
            S.drain_dmas("sp"); S.replay(); return

        def bufset(name, dt, n=NSET):
            return [sb("%s%d" % (name, i), [128, 128], dt) for i in range(n)]
        G1 = bufset("G1", F32)
        Dm = bufset("Dm", F32)
        Xf = bufset("Xf", F32)
        Xb_ = bufset("Xb", BF16)
        Yb = bufset("Yb", BF16)
        Mb = bufset("Mb", BF16)
        Xo = bufset("Xo", BF16)
        Yo = bufset("Yo", BF16)
        Hb = bufset("Hb", BF16)
        Gb = bufset("Gb", BF16)
        Am = bufset("Am", F32)
        AT = bufset("AT", BF16)
        TTb = bufset("TTb", BF16)
        WT = bufset("WT", BF16)
        U0 = bufset("U0", F32)
        Ub = bufset("Ub", BF16, 2)
        Sf = sb("Sf", [128, 128], F32)
        Sbb = [sb("Sbb%d" % i, [128, 128], BF16) for i in range(2)]
        otmp = sb("otmp", [128, 128], F32)
        osb = sb("osb", [128, 128], F32)
        osq = sb("osq", [128, 128], F32)
        onr = sb("onr", [128, 128], F32)
        ssq = sb("ssq", [128, 1], F32)
        lno = sb("lno", [128, 1], F32)
        rso = sb("rso", [128, 1], F32)
        gz = [sb("gz%d" % i, [128, TT], BF16) for i in range(2)]
        yst = [sb("yst%d" % i, [128, TT], BF16) for i in range(2)]
        if C.fused:
            ystm = [[sb("ystm%d_%d" % (i, j), [128, TT], BF16) for j in range(4)] for i in range(2)]
            hm = sb("hm", [128, 4], F32)
            gnwm = sb("gnwm", [128, 4], F32)
            S.dma("sp", lambda e: e.dma_start(out=hm[:], in_=d["hm"][:, :]), reads=["d_hm"], writes=["hm"])
            S.op("dve", lambda e: e.tensor_scalar(out=gnwm[:], in0=hm[:], scalar1=gnw[:, 0:1], scalar2=None, op0=ALU.mult),
                 reads=["hm", "gnw"], writes=["gnwm"])
        S.op("dve", lambda e: e.memset(Sf[:], 0.0), writes=["Sf"])
        S.op("dve", lambda e: e.memset(Sbb[0][:], 0.0), writes=["Sbb0"])

        def mm1(out_bank, lhsT, rhs, reads, cols=128):
            return mm_group(C, (lambda: C.ps[out_bank][:, 0:cols]), [(lhsT, rhs)], out_bank, reads=reads)

        def pre_steps(c):
            s = c % NSET
            ck = slice(c * GC, (c + 1) * GC)
            k = lambda nm: "%s%d" % (nm, s)
            steps = []
            st8 = {}

            def s1a():
                S.op("act", lambda e: e.activation(out=G1[s][:], in_=umask[:], func=AF.Copy, scale=gg[:, c:c + 1]),
                     reads=["umask", "gg"], writes=[k("G1")])
            steps.append(s1a)

            def s1b():
                bk = C.bank()
                key = mm_group(C, (lambda: C.ps[bk][:, 0:128]), [(G1[s][:], m2[:]), (identf[:], neg[:])], bk, reads=[k("G1"), "m2", "ident", "neg"])
                S.op("act", lambda e: e.activation(out=Dm[s][:], in_=C.ps[bk][:, 0:128], func=AF.Exp), reads=[], writes=[key, k("Dm")])
            steps.append(s1b)

            def s2():
                bk = C.bank()
                key = mm1(bk, KnT[:, ck], KnT[:, ck], ["KnT"])
                S.op("dve", lambda e: e.scalar_tensor_tensor(out=Xf[s][:], in0=C.ps[bk][:, 0:128], scalar=nbeta[:, c:c + 1], in1=Dm[s][:],
                                                             op0=ALU.mult, op1=ALU.mult), reads=["nbeta", k("Dm")], writes=[key, k("Xf")])
                S.op("dve", lambda e: e.tensor_tensor(out=Xf[s][:], in0=Xf[s][:], in1=slm[:], op=ALU.mult), reads=[k("Xf"), "slm"], writes=[k("Xf")])
                S.op("act", lambda e: e.activation(out=Xb_[s][:], in_=Xf[s][:], func=AF.Copy), reads=[k("Xf")], writes=[k("Xb")])
            steps.append(s2)

            def s3a():
                bk = C.bank()
                key = mm1(bk, QnT[:, ck], KnT[:, ck], ["QnT", "KnT"])
                S.op("dve", lambda e: e.tensor_tensor(out=Am[s][:], in0=C.ps[bk][:, 0:128], in1=Dm[s][:], op=ALU.mult),
                     reads=[k("Dm")], writes=[key, k("Am")])
            steps.append(s3a)

            def s4():
                bk = C.bank()
                key = "ps%d" % bk
                S.op("pe", lambda e: e.transpose(C.ps[bk][:, 0:128], Xf[s][:], identf[:]), reads=[k("Xf"), "ident"], writes=[key])
                S.op("act", lambda e: e.activation(out=Yb[s][:], in_=C.ps[bk][:, 0:128], func=AF.Copy), reads=[], writes=[key, k("Yb")])
                S.op("pool", lambda e: e.tensor_tensor(out=Xo[s][:], in0=Xb_[s][:], in1=lvm[:, 0:128], op=ALU.mult),
                     reads=[k("Xb"), "lvm"], writes=[k("Xo")])
                S.op("dve", lambda e: e.tensor_tensor(out=Mb[s][:], in0=Xo[s][:], in1=identb[:], op=ALU.add),
                     reads=[k("Xo"), "identb"], writes=[k("Mb")])
            steps.append(s4)

            def s3b():
                bk2 = C.bank()
                key2 = "ps%d" % bk2
                S.op("pe", lambda e: e.transpose(C.ps[bk2][:, 0:128], Am[s][:], identf[:]), reads=[k("Am"), "ident"], writes=[key2])
                S.op("act", lambda e: e.activation(out=AT[s][:], in_=C.ps[bk2][:, 0:128], func=AF.Copy), reads=[], writes=[key2, k("AT")])
                S.op("pool", lambda e: e.tensor_tensor(out=Yo[s][:], in0=Yb[s][:], in1=lvm[:, 128:256], op=ALU.mult),
                     reads=[k("Yb"), "lvm"], writes=[k("Yo")])
                S.op("dve", lambda e: e.tensor_tensor(out=TTb[s][:], in0=Yo[s][:], in1=identb[:], op=ALU.add),
                     reads=[k("Yo"), "identb"], writes=[k("TTb")])
            steps.append(s3b)
            for l in range(1, 7):
                def la(l=l):
                    if l <= 5:
                        S.op("pool", lambda e: e.tensor_tensor(out=Yo[s][:], in0=Yb[s][:], in1=lvm[:, (2 * l + 1) * 128:(2 * l + 2) * 128], op=ALU.mult),
                             reads=[k("Yb"), "lvm"], writes=[k("Yo")])
                    S.op("pool", lambda e: e.tensor_tensor(out=Xo[s][:], in0=Xb_[s][:], in1=lvm[:, (2 * l) * 128:(2 * l + 1) * 128], op=ALU.mult),
                         reads=[k("Xb"), "lvm"], writes=[k("Xo")])
                steps.append(la)

                def lb(l=l):
                    if l <= 5:
                        bh = C.bank()
                        keyh = mm1(bh, Yo[s][:], Mb[s][:], [k("Yo"), k("Mb")])
                    bg = C.bank()
                    keyg = mm1(bg, Xo[s][:], TTb[s][:], [k("Xo"), k("TTb")])
                    if l <= 5:
                        S.op("act", lambda e: e.activation(out=Hb[s][:], in_=C.ps[bh][:, 0:128], func=AF.Copy), reads=[], writes=[keyh, k("Hb")])
                    if l % 2 == 0:
                        S.op("act", lambda e: e.activation(out=Gb[s][:], in_=C.ps[bg][:, 0:128], func=AF.Copy), reads=[], writes=[keyg, k("Gb")])
                    else:
                        S.op("dve", lambda e: e.tensor_copy(out=Gb[s][:], in_=C.ps[bg][:, 0:128]), reads=[], writes=[keyg, k("Gb")])
                steps.append(lb)

                def lc(l=l):
                    if l <= 5:
                        bm = C.bank()
                        keym = mm1(bm, TTb[s][:], Hb[s][:], [k("TTb"), k("Hb")])
                    bw = C.bank()
                    keyw = mm1(bw, Mb[s][:], Gb[s][:], [k("Mb"), k("Gb")])
                    if l <= 5:
                        S.op("dve", lambda e: e.tensor_tensor(out=Mb[s][:], in0=Mb[s][:], in1=C.ps[bm][:, 0:128], op=ALU.add),
                             reads=[k("Mb")], writes=[keym, k("Mb")])
                    S.op("dve", lambda e: e.tensor_tensor(out=TTb[s][:], in0=TTb[s][:], in1=C.ps[bw][:, 0:128], op=ALU.add),
                         reads=[k("TTb")], writes=[keyw, k("TTb")])
                steps.append(lc)

            def s5():
                bk = C.bank()
                key = mm1(bk, Kb[:, c, :], TTb[s][:], ["Kb", k("TTb")])
                bk2 = C.bank()
                key2 = mm1(bk2, TTb[s][:], Vb[:, c, :], ["Vb", k("TTb")])
                S.op("act", lambda e: e.activation(out=WT[s][:], in_=C.ps[bk][:, 0:128], func=AF.Copy), reads=[], writes=[key, k("WT")])
                S.op("dve", lambda e: e.tensor_copy(out=U0[s][:], in_=C.ps[bk2][:, 0:128]), reads=[], writes=[key2, k("U0")])
            steps.append(s5)
            return steps

        def scan_steps(c):
            s = c % NSET
            ck = slice(c * GC, (c + 1) * GC)
            k = lambda nm: "%s%d" % (nm, s)
            u = c % 2
            sbi, sbo = c % 2, (c + 1) % 2
            stt = {}

            def sa():
                b1 = C.bank()
                key1 = mm1(b1, WT[s][:], Sbb[sbi][:], [k("WT"), "Sbb%d" % sbi])
                b2 = C.bank()
                key2 = mm1(b2, QnT[:, ck], Sbb[sbi][:], ["QnT", "Sbb%d" % sbi])
                S.op("dve", lambda e: e.tensor_tensor(out=Ub[u][:], in0=U0[s][:], in1=C.ps[b1][:, 0:128], op=ALU.subtract),
                     reads=[k("U0")], writes=[key1, "Ub%d" % u])
                S.op("act", lambda e: e.activation(out=otmp[:], in_=C.ps[b2][:, 0:128], func=AF.Copy, scale=egc[:, c:c + 1]),
                     reads=["egc"], writes=[key2, "otmp"])

            def sb_():
                b4 = C.bank()
                key4 = mm1(b4, Kend[:, c, :], Ub[u][:], ["Kend", "Ub%d" % u])
                b3 = C.bank()
                key3 = mm1(b3, AT[s][:], Ub[u][:], [k("AT"), "Ub%d" % u])
                S.op("dve", lambda e: e.scalar_tensor_tensor(out=Sf[:], in0=Sf[:], scalar=gend[:, c:c + 1], in1=C.ps[b4][:, 0:128],
                                                             op0=ALU.mult, op1=ALU.add), reads=["gend", "Sf"], writes=[key4, "Sf"])
                S.op("act", lambda e: e.activation(out=Sbb[sbo][:], in_=Sf[:], func=AF.Copy), reads=["Sf"], writes=["Sbb%d" % sbo])
                S.op("dve", lambda e: e.tensor_tensor(out=osb[:], in0=otmp[:], in1=C.ps[b3][:, 0:128], op=ALU.add),
                     reads=["otmp"], writes=[key3, "osb"])

            def sc():
                S.op("act", lambda e: e.activation(out=osq[:], in_=osb[:], func=AF.Square, accum_out=ssq[:, 0:1]), reads=["osb"], writes=["osq", "ssq"])
                S.op("act", lambda e: e.activation(out=lno[:], in_=ssq[:], func=AF.Ln, bias=eps_t[:, 0:1], scale=1.0 / 128), reads=["ssq", "eps"], writes=["lno"])
                S.op("act", lambda e: e.activation(out=rso[:], in_=lno[:], func=AF.Exp, scale=-0.5), reads=["lno"], writes=["rso"])
                S.op("act", lambda e: e.activation(out=onr[:], in_=osb[:], func=AF.Copy, scale=rso[:, 0:1]),
                     reads=["osb", "rso"], writes=["onr"])

            def sd():
                b5 = C.bank()
                key5 = "ps%d" % b5
                S.op("pe", lambda e: e.transpose(C.ps[b5][:, 0:128], onr[:], identf[:]), reads=["onr", "ident"], writes=[key5])
                yb = (c // 4) % 2
                a = c % 4
                if a == 0:
                    tz = (c // 4) * TT
                    S.dma("sp", lambda e: e.dma_start(out=gz[yb][:], in_=d["gzT"][:, tz:tz + TT]), reads=["d_gzT"], writes=["gz%d" % yb])
                if not C.fused:
                    S.op("dve", lambda e: e.scalar_tensor_tensor(out=yst[yb][:, a * 128:(a + 1) * 128], in0=C.ps[b5][:, 0:128], scalar=gnw[:, 0:1],
                                                                 in1=gz[yb][:, a * 128:(a + 1) * 128], op0=ALU.mult, op1=ALU.mult),
                         reads=["gnw", "gz%d" % yb], writes=[key5, "yst%d" % yb])
                    if a == 3:
                        t0 = (c // 4) * TT
                        S.dma("sp", lambda e: e.dma_start(out=d["yT_h"][128:256, t0:t0 + TT], in_=yst[yb][:]), reads=["yst%d" % yb], writes=["d_yT_h"])
                else:
                    for j in range(4):
                        S.op("dve", lambda e, j=j: e.scalar_tensor_tensor(out=ystm[yb][j][:, a * 128:(a + 1) * 128], in0=C.ps[b5][:, 0:128],
                                                                          scalar=gnwm[:, j:j + 1], in1=gz[yb][:, a * 128:(a + 1) * 128],
                                                                          op0=ALU.mult, op1=ALU.mult),
                             reads=["gnwm", "gz%d" % yb], writes=[key5, "ystm%d_%d" % (yb, j)])
                    if a == 3:
                        t0 = (c // 4) * TT
                        qq, qc = t0 // 2048, t0 % 2048
                        for j in range(4):
                            S.dma("sp", lambda e, j=j: e.dma_start(out=d["ypad%d" % qq][256 * j + 128:256 * j + 256, qc:qc + TT], in_=ystm[yb][j][:]),
                                  reads=["ystm%d_%d" % (yb, j)], writes=["d_ypad%d" % qq])
                        if qc + TT == 2048:
                            S.coll(lambda e: e.collective_compute(
                                "AllReduce", ALU.add, replica_groups=[[0, 1, 2, 3], [4, 5, 6, 7]],
                                ins=[d["ypad%d" % qq].opt()], outs=[d["yg%d" % qq].opt()]),
                                reads=["d_ypad%d" % qq], writes=["d_yg%d" % qq])
            return [sa, sb_, sc, sd]

        groups = [list(range(g, min(g + G, nchunks))) for g in range(0, nchunks, G)]
        prev = []
        for grp in groups + [[]]:
            lists = [pre_steps(c) for c in grp]
            nst = max([len(l) for l in lists] + [0])
            pending = []
            for c in prev:
                pending += scan_steps(c)
            for si in range(nst):
                for l in lists:
                    if si < len(l):
                        l[si]()
                if pending:
                    pending.pop(0)()
            while pending:
                pending.pop(0)()
            prev = grp
        S.drain_dmas("sp")
        S.replay()


CONST_INS = [("ctab", (32, SEQ), F32), ("stab", (32, SEQ), F32), ("ident", (128, 128), F32), ("sel", (65, 64), F32),
             ("tri", (128, 128), F32), ("OH", (32, 3072), F32), ("Eb", (32, SEQ), BF16), ("pen", (32, 32), F32),
             ("umask", (128, 128), F32), ("m2", (128, 128), F32), ("neg", (128, 128), F32), ("slm", (128, 128), F32),
             ("lvlm", (128, 14 * 128), F32)]
LAYER_INS = PROJ_INS + [("t5h", (32, 1)), ("cw", (128, 12)), ("gsc", (1, 2)), ("gnw", (128, 1))]


def host_consts():
    import ml_dtypes
    ctab, stab = rope_consts()
    pen, E = moba_consts()
    umask, m2, neg, slm = gdn_consts()
    sel = np.zeros((65, 64), np.float32)
    sel[64, :] = 1.0
    return {"ctab": ctab, "stab": stab, "ident": np.eye(128, dtype=np.float32), "sel": sel,
            "tri": np.triu(np.ones((128, 128), np.float32)), "OH": t5_consts(), "Eb": E.astype(ml_dtypes.bfloat16),
            "pen": pen, "umask": umask, "m2": m2, "neg": neg, "slm": slm, "lvlm": level_masks()}


def prep_layer_all(I, l, h):
    out = prep_layer(I, l, h)
    f = np.float32
    out["t5h"] = np.ascontiguousarray(I["t5_table"][:, h:h + 1]).astype(f)
    cwf = I["gdn_conv_w"][l]
    cw = np.zeros((128, 12), f)
    for j in range(3):
        for tap in range(4):
            cw[:, 4 * j + tap] = cwf[tap, j * 512 + 128 * h:j * 512 + 128 * h + 128]
    out["cw"] = cw
    out["gsc"] = np.array([[I["gdn_A_log"][l, h], I["gdn_dt_bias"][l, h]]], f)
    out["gnw"] = np.ascontiguousarray(I["gdn_norm_w"][l].reshape(128, 1)).astype(f)
    return out


def wo_perm():
    rows = []
    for h in range(4):
        rows += list(range(64 * h, 64 * h + 64))
        rows += list(range(768 + 64 * h, 768 + 64 * h + 64))
        rows += list(range(256 + 128 * h, 256 + 128 * h + 128))
    return rows


def build_layer_program(has_prev, do_layer, final):
    nc = bass.Bass("TRN2", target_bir_lowering=False)
    with ExitStack() as st:
        C = Ctx(nc, st)
        C.din("xT", (DM, SEQ), F32)
        if has_prev:
            C.din("yT", (DM, SEQ), BF16)
            C.din("wo", (DM, DM), F32)
            if not final:
                C.dout("xTo", (DM, SEQ), F32)
        if final:
            C.din("fnw", (128, 8), F32)
            C.dout("outT", (DM, SEQ), F32)
        if do_layer:
            for n, s_, dt in CONST_INS:
                C.din(n, s_, dt)
            for n, s_ in LAYER_INS:
                C.din(n, s_, F32)
            for n, s_, dt in PROJ_OUTS:
                C.dint(n, s_, dt)
            C.dint("fv", (1, 3072), F32)
            C.dout("yT_h", (256, SEQ), BF16)
        stage_proj(C, has_prev=has_prev, do_proj=do_layer, final=final)
        if do_layer:
            stage_attn(C, False)
            stage_attn(C, True)
            stage_gdn(C)
    return nc


def build_fused_program(depth=DEPTH, do_coll=True):
    nc = bass.Bass("TRN2", target_bir_lowering=False)
    with ExitStack() as st:
        C = Ctx(nc, st)
        C.fused = True
        S = C.S
        C.din("xT", (DM, SEQ), F32)
        C.din("hm", (128, 4), F32)
        C.din("fnw", (128, 8), F32)
        for n, s_, dt in CONST_INS:
            C.din(n, s_, dt)
        for l in range(depth):
            for n, s_ in LAYER_INS:
                C.din("%s_%d" % (n, l), s_, F32)
            C.din("wo_%d" % l, (DM, DM), F32)
        C.dout("outT", (DM, SEQ), F32)
        C.dint("xTi", (DM, SEQ), F32)
        for n, s_, dt in PROJ_OUTS:
            C.dint(n, s_, dt)
        C.dint("fv", (1, 3072), F32)
        for q in range(4):
            C.dint("ypad%d" % q, (DM, 2048), BF16)
            C.dint("yg%d" % q, (DM, 2048), BF16)
        x_ext = C.d["xT"]
        for l in range(depth):
            for n, s_ in LAYER_INS:
                C.d[n] = C.d["%s_%d" % (n, l)]
            if l > 0:
                C.d["wo"] = C.d["wo_%d" % (l - 1)]
            C.d["xT"] = x_ext if l <= 1 else C.d["xTi"]
            C.d["xTo"] = C.d["xTi"]
            import os
            ST = os.environ.get("STAGES", "pamg")
            if "p" in ST:
                stage_proj(C, has_prev=(l > 0), do_proj=True, final=False)
            if "a" in ST:
                stage_attn(C, False)
            if "m" in ST:
                stage_attn(C, True)
            if "g" in ST:
                stage_gdn(C)
        C.d["wo"] = C.d["wo_%d" % (depth - 1)]
        C.d["xT"] = C.d["xTi"] if depth > 1 else x_ext
        if "f" in os.environ.get("STAGES", "pamgf"):
            stage_proj(C, has_prev=True, do_proj=False, final=True)
    return nc


_FUSED = []


def kernel(x, norm_w, w_in, mla_q_norm, mla_w_uq, mla_kv_norm, mla_w_ukv, gdn_conv_w, gdn_A_log, gdn_dt_bias,
           gdn_norm_w, w_out, t5_table, final_norm_w):
    I = dict(x=np.asarray(x), norm_w=np.asarray(norm_w), w_in=np.asarray(w_in), mla_q_norm=np.asarray(mla_q_norm),
             mla_w_uq=np.asarray(mla_w_uq), mla_kv_norm=np.asarray(mla_kv_norm), mla_w_ukv=np.asarray(mla_w_ukv),
             gdn_conv_w=np.asarray(gdn_conv_w), gdn_A_log=np.asarray(gdn_A_log), gdn_dt_bias=np.asarray(gdn_dt_bias),
             gdn_norm_w=np.asarray(gdn_norm_w), w_out=np.asarray(w_out), t5_table=np.asarray(t5_table),
             final_norm_w=np.asarray(final_norm_w))
    if not _FUSED:
        _FUSED.append(build_fused_program())
    nc = _FUSED[0]
    consts = host_consts()
    perm = wo_perm()
    fnw = np.ascontiguousarray(I["final_norm_w"].reshape(8, 128).T).astype(np.float32)
    wos = [np.ascontiguousarray(I["w_out"][l][perm, :]).astype(np.float32) for l in range(DEPTH)]
    xT = [np.ascontiguousarray(I["x"][b].T).astype(np.float32) for b in range(2)]
    maps = []
    for c in range(8):
        b, h = c // 4, c % 4
        m = dict(consts)
        m["xT"] = xT[b]
        hm = np.zeros((128, 4), np.float32)
        hm[:, h] = 1.0
        m["hm"] = hm
        m["fnw"] = fnw
        for l in range(DEPTH):
            for k, v in prep_layer_all(I, l, h).items():
                m["%s_%d" % (k, l)] = v
            m["wo_%d" % l] = wos[l]
        maps.append(m)
    res = run_bass_kernel_spmd(nc, maps, core_ids=list(range(8)))
    out = np.stack([np.asarray(res.results[4 * b]["outT"]).T for b in range(2)], axis=0)
    return np.ascontiguousarray(out).astype(np.float32)
```

```python
import math
import os
from contextlib import ExitStack

import numpy as np
import concourse.bass as bass
import concourse.mybir as mybir
from concourse.bass_utils import run_bass_kernel_spmd

F32 = mybir.dt.float32
BF16 = mybir.dt.bfloat16
AF = mybir.ActivationFunctionType
ALU = mybir.AluOpType
AX = mybir.AxisListType

SEQ = 8192
DM = 1024
DEPTH = 4
TT = 512
NT = SEQ // TT
EPS = 1e-6
TW = [128] * 8 + [96, 96, 64, 2]
TOFF = [sum(TW[:i]) for i in range(len(TW))]
NCOL = sum(TW)

ENGS = ("pe", "act", "dve", "pool", "sp")


class Sched:
    def __init__(self, nc, stack, n_dma_sems=40):
        self.nc = nc
        self.q = {e: [] for e in ENGS}
        self.sem = {e: stack.enter_context(nc.semaphore("s_" + e)) for e in ENGS if e != "sp"}
        self.cnt = {e: 0 for e in ENGS}
        self.seen = {e: {} for e in ENGS}
        self.dsem = [stack.enter_context(nc.semaphore("d%d" % i)) for i in range(n_dma_sems)]
        self.dcnt = [0] * n_dma_sems
        self.dnext = 0
        self.lastw = {}
        self.readers = {}
        self.semobj = dict(self.sem)
        for i, s in enumerate(self.dsem):
            self.semobj["d%d" % i] = s
        self.semobj["cc"] = stack.enter_context(nc.semaphore("s_cc"))
        self.cccnt = 0

    def _deps(self, eng, reads, writes):
        deps = {}

        def add(src, val, kind):
            if src == eng and eng == "pe":
                return
            if deps.get(src, 0) < val:
                deps[src] = val
        for k in reads:
            w = self.lastw.get(k)
            if w:
                add(w[0], w[1], "raw")
        for k in writes:
            w = self.lastw.get(k)
            if w:
                add(w[0], w[1], "waw")
            for s, v in self.readers.get(k, {}).items():
                add(s, v, "war")
        waits = []
        for s, v in deps.items():
            if self.seen[eng].get(s, 0) < v:
                self.seen[eng][s] = v
                waits.append((s, v))
        return waits

    def _commit(self, src, val, reads, writes):
        for k in reads:
            self.readers.setdefault(k, {})[src] = val
        for k in writes:
            self.lastw[k] = (src, val)
            self.readers[k] = {}

    def op(self, eng, fn, reads=(), writes=(), sig=True):
        waits = self._deps(eng, reads, writes)
        val = self.cnt[eng] + 1
        if sig:
            self.cnt[eng] = val
        self._commit(eng, val, reads, writes)
        self.q[eng].append((waits, fn, (eng, 1) if sig else None))

    def dma(self, eng, fn, reads=(), writes=()):
        i = self.dnext
        self.dnext = (self.dnext + 1) % len(self.dsem)
        name = "d%d" % i
        waits = self._deps(eng, reads, writes)
        if self.dcnt[i] > 0 and self.seen[eng].get(name, 0) < self.dcnt[i]:
            self.seen[eng][name] = self.dcnt[i]
            waits.append((name, self.dcnt[i]))
        self.dcnt[i] += 16
        self._commit(name, self.dcnt[i], reads, writes)
        self.q[eng].append((waits, fn, (name, 16)))

    def coll(self, fn, reads=(), writes=()):
        if "cc" not in self.semobj:
            raise RuntimeError("no cc semaphore")
        waits = self._deps("pool", reads, writes)
        self.cccnt += 1
        self._commit("cc", self.cccnt, reads, writes)
        self.q["pool"].append((waits, fn, ("cc", 1)))

    def drain_dmas(self, eng="sp", include_cc=False):
        waits = []
        for i in range(len(self.dsem)):
            name = "d%d" % i
            if self.dcnt[i] > 0 and self.seen[eng].get(name, 0) < self.dcnt[i]:
                self.seen[eng][name] = self.dcnt[i]
                waits.append((name, self.dcnt[i]))
        if include_cc and self.cccnt > 0 and self.seen[eng].get("cc", 0) < self.cccnt:
            self.seen[eng]["cc"] = self.cccnt
            waits.append(("cc", self.cccnt))
        self.q[eng].append((waits, None, None))

    def replay(self):
        nc = self.nc
        with nc.Block() as block:
            engmap = {"pe": block.tensor, "act": block.scalar, "dve": block.vector,
                      "pool": block.gpsimd, "sp": block.sync}
            for e in ENGS:
                items = self.q[e]
                if not items:
                    continue

                def body(engine, items=items):
                    for waits, fn, inc in items:
                        for s, v in waits:
                            engine.wait_ge(self.semobj[s], v)
                        if fn is None:
                            continue
                        ins = fn(engine)
                        if inc is not None:
                            ins.then_inc(self.semobj[inc[0]], inc[1])
                engmap[e](body)
        self.q = {e: [] for e in ENGS}


class Ctx:
    def __init__(self, nc, stack):
        self.nc = nc
        self.S = Sched(nc, stack)
        self.ps = [stack.enter_context(nc.psum_tensor("ps%d" % i, [128, 512], F32)) for i in range(8)]
        self.psn = 0
        self.uid = 0
        self.fused = False
        self.d = {}

    def bank(self):
        i = self.psn
        self.psn = (self.psn + 1) % 8
        return i

    def din(self, name, shape, dt):
        t = self.nc.dram_tensor(name, list(shape), dt, kind="ExternalInput").ap()
        self.d[name] = t
        return t

    def dout(self, name, shape, dt):
        t = self.nc.dram_tensor(name, list(shape), dt, kind="ExternalOutput").ap()
        self.d[name] = t
        return t

    def dint(self, name, shape, dt):
        t = self.nc.dram_tensor(name, list(shape), dt, kind="Internal").ap()
        self.d[name] = t
        return t


def mm_group(C, out_fn, pairs, bank, reads, extra_writes=()):
    S = C.S
    n = len(pairs)
    key = "ps%d" % bank
    for i, (l, r) in enumerate(pairs):
        S.op("pe", (lambda e, l=l, r=r, i=i: e.matmul(out_fn(), l, r, start=(i == 0), stop=(i == n - 1))),
             reads=reads, writes=[key] + list(extra_writes), sig=(i == n - 1))
    return key


def stage_proj(C, has_prev, do_proj, final=False, tiles=range(NT)):
    nc, S, d = C.nc, C.S, C.d
    with ExitStack() as st:
        def sb(name, shape, dt):
            C.uid += 1
            return st.enter_context(nc.sbuf_tensor("sb%d_%s" % (C.uid, name), list(shape), dt))
        xs = [sb("xs%d" % i, [128, 8, TT], F32) for i in range(2)]
        eps_t = sb("eps_t", [128, 1], F32)
        S.op("dve", lambda e: e.memset(eps_t[:], EPS), writes=["eps"])
        if has_prev:
            ys = [sb("ys%d" % i, [128, 8, TT], BF16) for i in range(2)]
            wo = sb("wo", [128, 8, DM], BF16)
            wstage = sb("wstage", [128, NCOL], F32)
            for k in range(8):
                S.dma("sp", lambda e, k=k: e.dma_start(out=wstage[:, 0:DM], in_=d["wo"][k * 128:(k + 1) * 128, :]),
                      reads=["d_wo"], writes=["wstage"])
                if k % 2 == 0:
                    S.op("dve", lambda e, k=k: e.tensor_copy(out=wo[:, k, :], in_=wstage[:, 0:DM]),
                         reads=["wstage"], writes=["wo"])
                else:
                    S.op("act", lambda e, k=k: e.activation(out=wo[:, k, :], in_=wstage[:, 0:DM], func=AF.Copy),
                         reads=["wstage"], writes=["wo"])
        if do_proj or final:
            xsq = sb("xsq", [128, 8, TT], BF16)
            ones = sb("ones", [128, 128], BF16)
            S.op("dve", lambda e: e.memset(ones[:], 1.0), writes=["ones"])
            rstd1 = sb("rstd1", [128, TT], F32)
            lnt = sb("lnt", [128, TT], F32)
        if final:
            fnw = sb("fnw", [128, 8], F32)
            S.dma("sp", lambda e: e.dma_start(out=fnw[:], in_=d["fnw"][:, :]), reads=["d_fnw"], writes=["fnw"])
            outs = sb("outs", [128, 8, TT], F32)
        if do_proj:
            if not has_prev:
                wstage = sb("wstage", [128, NCOL], F32)
            xb = sb("xb", [128, 8, TT], BF16)
            win = sb("win", [128, 8, NCOL], BF16)
            nw = sb("nw", [128, 8], F32)
            S.dma("sp", lambda e: e.dma_start(out=nw[:], in_=d["nw"][:, :]), reads=["d_nw"], writes=["nw"])
            for k in range(8):
                S.dma("sp", lambda e, k=k: e.dma_start(out=wstage[:], in_=d["w_in"][k * 128:(k + 1) * 128, :]),
                      reads=["d_w_in"], writes=["wstage"])
                S.op("dve", lambda e, k=k: e.tensor_scalar(out=win[:, k, :], in0=wstage[:], scalar1=nw[:, k:k + 1],
                                                           scalar2=None, op0=ALU.mult),
                     reads=["wstage", "nw"], writes=["win"])
            wuq = sb("wuq", [128, 2, 192], BF16)
            qnw = sb("qnw", [128, 2], F32)
            S.dma("sp", lambda e: e.dma_start(out=qnw[:], in_=d["qnw"][:, :]), reads=["d_qnw"], writes=["qnw"])
            for k in range(2):
                S.dma("sp", lambda e, k=k: e.dma_start(out=wstage[:, 0:192], in_=d["wuq"][k * 128:(k + 1) * 128, :]),
                      reads=["d_wuq"], writes=["wstage"])
                S.op("dve", lambda e, k=k: e.tensor_scalar(out=wuq[:, k, :], in0=wstage[:, 0:192], scalar1=qnw[:, k:k + 1],
                                                           scalar2=None, op0=ALU.mult),
                     reads=["wstage", "qnw"], writes=["wuq"])
            wukv = sb("wukv", [128, 128], BF16)
            kvnw = sb("kvnw", [128, 1], F32)
            S.dma("sp", lambda e: e.dma_start(out=kvnw[:], in_=d["kvnw"][:, :]), reads=["d_kvnw"], writes=["kvnw"])
            S.dma("sp", lambda e: e.dma_start(out=wstage[:, 0:128], in_=d["wukv"][:, :]), reads=["d_wukv"], writes=["wstage"])
            S.op("dve", lambda e: e.tensor_scalar(out=wukv[:], in0=wstage[:, 0:128], scalar1=kvnw[:, 0:1],
                                                  scalar2=None, op0=ALU.mult),
                 reads=["wstage", "kvnw"], writes=["wukv"])
            identf = sb("identf", [128, 128], F32)
            identb = sb("identb", [128, 128], BF16)
            S.dma("sp", lambda e: e.dma_start(out=identf[:], in_=d["ident"][:, :]), reads=["d_ident"], writes=["identf"])
            S.op("dve", lambda e: e.tensor_copy(out=identb[:], in_=identf[:]), reads=["identf"], writes=["identb"])
            ctab = sb("ctab", [128, TT], F32)
            stab = sb("stab", [128, TT], F32)
            cqf = sb("cqf", [128, 2, TT], F32)
            cqsq = sb("cqsq", [128, 2, TT], BF16)
            cqb = sb("cqb", [128, 2, TT], BF16)
            rstd2 = sb("rstd2", [128, TT], F32)
            rstd3 = sb("rstd3", [128, TT], F32)
            ckvf = sb("ckvf", [128, TT], F32)
            ckvsq = sb("ckvsq", [128, TT], BF16)
            ckvb = sb("ckvb", [128, TT], BF16)
            QTs = sb("QTs", [96, TT], BF16)
            KTs = sb("KTs", [96, TT], BF16)
            t1 = sb("t1", [128, TT], F32)
            t2 = sb("t2", [128, TT], F32)
            t3 = sb("t3", [128, TT], F32)
            t4 = sb("t4", [128, TT], F32)
            vT = sb("vT", [64, TT], BF16)
            Vs = sb("Vs", [128, 4, 64], BF16)
            mvT = sb("mvT", [64, TT], BF16)
            mVs = sb("mVs", [128, 4, 64], BF16)
            mqf = sb("mqf", [64, TT], F32)
            mqb = sb("mqb", [64, TT], BF16)
            mkf = sb("mkf", [64, TT], F32)
            mkb = sb("mkb", [64, TT], BF16)
            km = sb("km", [64, 32], F32)
            gf = sb("gf", [128, TT], F32)
            gs = sb("gs", [128, TT], BF16)
            gq3 = [sb("gq3_%d" % i, [128, TT], F32) for i in range(3)]
            zf = sb("zf", [128, TT], F32)
            zs = sb("zs", [128, TT], BF16)
            bas = sb("bas", [2, TT], F32)

        def rstd_from(bank, n, out_t, rows=128):
            key = "ps%d" % bank
            S.op("act", lambda e: e.activation(out=lnt[0:rows, :], in_=C.ps[bank][0:rows, :], func=AF.Ln,
                                               bias=eps_t[0:rows, 0:1], scale=1.0 / n),
                 reads=["eps"], writes=[key, "lnt"])
            S.op("act", lambda e: e.activation(out=out_t[0:rows, :], in_=lnt[0:rows, :], func=AF.Exp, scale=-0.5),
                 reads=["lnt"], writes=[out_t.name if hasattr(out_t, "name") else "rstd"])

        def do_tile(ti):
            b = ti % 2
            c0 = ti * TT
            xk = "xs%d" % b
            xsb = xs[b]
            for k4 in range(2):
                S.dma("sp", lambda e, k4=k4, xsb=xsb: e.dma_start(
                    out=xsb[:, 4 * k4:4 * k4 + 4, :],
                    in_=d["xT"][4 * k4 * 128:(4 * k4 + 4) * 128, c0:c0 + TT].rearrange("(k p) t -> p k t", p=128)),
                    reads=["d_xT"], writes=[xk])
            if has_prev:
                yk = "ys%d" % b
                ysb = ys[b]
                S.dma("sp", lambda e, ysb=ysb: e.dma_start(
                    out=ysb[:], in_=(d["yg%d" % (c0 // 2048)][:, c0 % 2048:c0 % 2048 + TT] if C.fused else d["yT"][:, c0:c0 + TT]
                                      ).rearrange("(k p) t -> p k t", p=128)),
                    reads=[("d_yg%d" % (c0 // 2048)) if C.fused else "d_yT"], writes=[yk])
                for c in range(8):
                    bk = C.bank()
                    key = mm_group(C, (lambda bk=bk: C.ps[bk][:, :]),
                                   [(wo[:, k, c * 128:(c + 1) * 128], ysb[:, k, :]) for k in range(8)],
                                   bk, reads=["wo", yk])
                    S.op("dve", lambda e, c=c, bk=bk, xsb=xsb: e.tensor_tensor(out=xsb[:, c, :], in0=C.ps[bk][:, :], in1=xsb[:, c, :], op=ALU.add),
                         reads=[], writes=[key, xk])
                if not final:
                    for k4 in range(2):
                        S.dma("sp", lambda e, k4=k4, xsb=xsb: e.dma_start(
                            out=d["xTo"][4 * k4 * 128:(4 * k4 + 4) * 128, c0:c0 + TT].rearrange("(k p) t -> p k t", p=128),
                            in_=xsb[:, 4 * k4:4 * k4 + 4, :]),
                            reads=[xk], writes=["d_xTo"])
            if not (do_proj or final):
                return
            S.op("act", lambda e, xsb=xsb: e.activation(out=xsq[:], in_=xsb[:], func=AF.Square), reads=[xk], writes=["xsq"])
            bk = C.bank()
            key = mm_group(C, (lambda bk=bk: C.ps[bk][:, :]), [(ones[:], xsq[:, k, :]) for k in range(8)], bk,
                           reads=["ones", "xsq"])
            S.op("act", lambda e, bk=bk: e.activation(out=lnt[:], in_=C.ps[bk][:, :], func=AF.Ln, bias=eps_t[:, 0:1], scale=1.0 / DM),
                 reads=["eps"], writes=[key, "lnt"])
            S.op("act", lambda e: e.activation(out=rstd1[:], in_=lnt[:], func=AF.Exp, scale=-0.5), reads=["lnt"], writes=["rstd1"])
            if final:
                for k in range(8):
                    S.op("dve", lambda e, k=k, xsb=xsb: e.scalar_tensor_tensor(
                        out=outs[:, k, :], in0=xsb[:, k, :], scalar=fnw[:, k:k + 1], in1=rstd1[:], op0=ALU.mult, op1=ALU.mult),
                        reads=[xk, "fnw", "rstd1"], writes=["outs"])
                for k4 in range(2):
                    S.dma("sp", lambda e, k4=k4: e.dma_start(
                        out=d["outT"][4 * k4 * 128:(4 * k4 + 4) * 128, c0:c0 + TT].rearrange("(k p) t -> p k t", p=128),
                        in_=outs[:, 4 * k4:4 * k4 + 4, :]),
                        reads=["outs"], writes=["d_outT"])
                return
            S.op("dve", lambda e, xsb=xsb: e.tensor_copy(out=xb[:], in_=xsb[:]), reads=[xk], writes=["xb"])
            S.dma("sp", lambda e: e.dma_start(out=ctab[64:96, :], in_=d["ctab"][:, c0:c0 + TT]), reads=["d_ctab"], writes=["ctab"])
            S.dma("sp", lambda e: e.dma_start(out=stab[64:96, :], in_=d["stab"][:, c0:c0 + TT]), reads=["d_stab"], writes=["stab"])

            def proj_tile(t):
                bk = C.bank()
                w = TW[t]
                key = mm_group(C, (lambda bk=bk, w=w: C.ps[bk][0:w, :]),
                               [(win[:, k, TOFF[t]:TOFF[t] + w], xb[:, k, :]) for k in range(8)], bk,
                               reads=["win", "xb"])
                return bk, key

            for j in range(2):
                bk, key = proj_tile(j)
                S.op("dve", lambda e, bk=bk, j=j: e.tensor_tensor(out=cqf[:, j, :], in0=C.ps[bk][:, :], in1=rstd1[:], op=ALU.mult),
                     reads=["rstd1"], writes=[key, "cqf"])
            S.op("act", lambda e: e.activation(out=cqsq[:], in_=cqf[:], func=AF.Square), reads=["cqf"], writes=["cqsq"])
            S.op("dve", lambda e: e.tensor_copy(out=cqb[:], in_=cqf[:]), reads=["cqf"], writes=["cqb"])
            bk = C.bank()
            key = mm_group(C, (lambda bk=bk: C.ps[bk][:, :]), [(ones[:], cqsq[:, j, :]) for j in range(2)], bk, reads=["ones", "cqsq"])
            S.op("act", lambda e, bk=bk: e.activation(out=lnt[:], in_=C.ps[bk][:, :], func=AF.Ln, bias=eps_t[:, 0:1], scale=1.0 / 256),
                 reads=["eps"], writes=[key, "lnt"])
            S.op("act", lambda e: e.activation(out=rstd2[:], in_=lnt[:], func=AF.Exp, scale=-0.5), reads=["lnt"], writes=["rstd2"])
            bq = C.bank()
            keyq = mm_group(C, (lambda bq=bq: C.ps[bq][0:96, :]), [(wuq[:, j, 0:96], cqb[:, j, :]) for j in range(2)], bq, reads=["wuq", "cqb"])
            br = C.bank()
            keyr = mm_group(C, (lambda br=br: C.ps[br][0:96, :]), [(wuq[:, j, 96:192], cqb[:, j, :]) for j in range(2)], br, reads=["wuq", "cqb"])
            S.op("dve", lambda e, bq=bq: e.tensor_tensor(out=QTs[0:64, :], in0=C.ps[bq][0:64, :], in1=rstd2[0:64, :], op=ALU.mult),
                 reads=["rstd2"], writes=[keyq, "QTs"])
            S.op("dve", lambda e, bq=bq: e.tensor_tensor(out=t1[64:96, :], in0=C.ps[bq][64:96, :], in1=ctab[64:96, :], op=ALU.mult),
                 reads=["ctab"], writes=[keyq, "t1"])
            S.op("dve", lambda e, br=br: e.tensor_tensor(out=t2[64:96, :], in0=C.ps[br][64:96, :], in1=stab[64:96, :], op=ALU.mult),
                 reads=["stab"], writes=[keyr, "t2"])
            S.op("dve", lambda e: e.tensor_tensor(out=t1[64:96, :], in0=t1[64:96, :], in1=t2[64:96, :], op=ALU.add),
                 reads=["t2", "t1"], writes=["t1"])
            S.op("dve", lambda e: e.tensor_tensor(out=QTs[64:96, :], in0=t1[64:96, :], in1=rstd2[64:96, :], op=ALU.mult),
                 reads=["t1", "rstd2"], writes=["QTs"])
            S.dma("sp", lambda e: e.dma_start(out=d["QT_mla"][:, c0:c0 + TT], in_=QTs[:]), reads=["QTs"], writes=["d_QT_mla"])
            bk, key = proj_tile(2)
            S.op("dve", lambda e, bk=bk: e.tensor_tensor(out=ckvf[:], in0=C.ps[bk][:, :], in1=rstd1[:], op=ALU.mult),
                 reads=["rstd1"], writes=[key, "ckvf"])
            S.op("act", lambda e: e.activation(out=ckvsq[:], in_=ckvf[:], func=AF.Square), reads=["ckvf"], writes=["ckvsq"])
            S.op("act", lambda e: e.activation(out=ckvb[:], in_=ckvf[:], func=AF.Copy), reads=["ckvf"], writes=["ckvb"])
            bk = C.bank()
            key = mm_group(C, (lambda bk=bk: C.ps[bk][:, :]), [(ones[:], ckvsq[:])], bk, reads=["ones", "ckvsq"])
            S.op("act", lambda e, bk=bk: e.activation(out=lnt[:], in_=C.ps[bk][:, :], func=AF.Ln, bias=eps_t[:, 0:1], scale=1.0 / 128),
                 reads=["eps"], writes=[key, "lnt"])
            S.op("act", lambda e: e.activation(out=rstd3[:], in_=lnt[:], func=AF.Exp, scale=-0.5), reads=["lnt"], writes=["rstd3"])
            bk = C.bank()
            key = mm_group(C, (lambda bk=bk: C.ps[bk][0:64, :]), [(wukv[:, 0:64], ckvb[:])], bk, reads=["wukv", "ckvb"])
            S.op("dve", lambda e, bk=bk: e.tensor_tensor(out=KTs[0:64, :], in0=C.ps[bk][0:64, :], in1=rstd3[0:64, :], op=ALU.mult),
                 reads=["rstd3"], writes=[key, "KTs"])
            bk = C.bank()
            key = mm_group(C, (lambda bk=bk: C.ps[bk][0:64, :]), [(wukv[:, 64:128], ckvb[:])], bk, reads=["wukv", "ckvb"])
            S.op("dve", lambda e, bk=bk: e.tensor_tensor(out=vT[:], in0=C.ps[bk][0:64, :], in1=rstd3[0:64, :], op=ALU.mult),
                 reads=["rstd3"], writes=[key, "vT"])

            def transpose_v(src, srck, dst, dstk, dname):
                bk = C.bank()
                key = "ps%d" % bk
                for j in range(4):
                    S.op("pe", lambda e, j=j, bk=bk: e.transpose(
                        C.ps[bk][:, :].bitcast(BF16)[:, j * 64:(j + 1) * 64], src[0:64, j * 128:(j + 1) * 128], identb[0:64, 0:64]),
                        reads=[srck, "identb"], writes=[key], sig=(j == 3))
                S.op("act", lambda e, bk=bk: e.activation(out=dst[:].rearrange("p a b -> p (a b)"),
                                                          in_=C.ps[bk][:, :].bitcast(BF16)[:, 0:256], func=AF.Copy),
                     reads=[], writes=[key, dstk])
                S.dma("sp", lambda e: e.dma_start(
                    out=d[dname][c0:c0 + TT, :].rearrange("(a p) v -> p a v", p=128), in_=dst[:]),
                    reads=[dstk], writes=["d_" + dname])
            transpose_v(vT, "vT", Vs, "Vs", "V_mla")
            bk, key = proj_tile(8)
            S.op("dve", lambda e, bk=bk: e.tensor_tensor(out=mqf[:], in0=C.ps[bk][0:64, :], in1=rstd1[0:64, :], op=ALU.mult),
                 reads=["rstd1"], writes=[key, "mqf"])
            S.op("dve", lambda e, bk=bk: e.tensor_tensor(out=t3[64:96, :], in0=C.ps[bk][64:96, :], in1=ctab[64:96, :], op=ALU.mult),
                 reads=["ctab"], writes=[key, "t3"])
            S.op("act", lambda e: e.activation(out=mqb[:], in_=mqf[:], func=AF.Copy, scale=0.125),
                 reads=["mqf"], writes=["mqb"])
            S.dma("sp", lambda e: e.dma_start(out=d["QTf_moba"][:, c0:c0 + TT], in_=mqf[:]), reads=["mqf"], writes=["d_QTf_moba"])
            S.dma("sp", lambda e: e.dma_start(out=d["QT_moba"][:, c0:c0 + TT], in_=mqb[:]), reads=["mqb"], writes=["d_QT_moba"])
            bk, key = proj_tile(9)
            S.op("dve", lambda e, bk=bk: e.tensor_tensor(out=mkf[:], in0=C.ps[bk][0:64, :], in1=rstd1[0:64, :], op=ALU.mult),
                 reads=["rstd1"], writes=[key, "mkf"])
            S.op("dve", lambda e, bk=bk: e.tensor_tensor(out=t4[64:96, :], in0=C.ps[bk][64:96, :], in1=stab[64:96, :], op=ALU.mult),
                 reads=["stab"], writes=[key, "t4"])
            S.op("act", lambda e: e.activation(out=mkb[:], in_=mkf[:], func=AF.Copy), reads=["mkf"], writes=["mkb"])
            S.op("dve", lambda e: e.tensor_reduce(out=km[:, 2 * ti:2 * ti + 2], in_=mkf[:].rearrange("p (a b) -> p a b", b=256),
                                                  op=ALU.add, axis=AX.X),
                 reads=["mkf"], writes=["km"])
            S.dma("sp", lambda e: e.dma_start(out=d["KT_moba"][:, c0:c0 + TT], in_=mkb[:]), reads=["mkb"], writes=["d_KT_moba"])
            S.op("dve", lambda e: e.tensor_tensor(out=t3[64:96, :], in0=t3[64:96, :], in1=t4[64:96, :], op=ALU.add),
                 reads=["t3", "t4"], writes=["t3"])
            S.op("dve", lambda e: e.tensor_tensor(out=KTs[64:96, :], in0=t3[64:96, :], in1=rstd1[64:96, :], op=ALU.mult),
                 reads=["t3", "rstd1"], writes=["KTs"])
            S.dma("sp", lambda e: e.dma_start(out=d["KT_mla"][:, c0:c0 + TT], in_=KTs[:]), reads=["KTs"], writes=["d_KT_mla"])
            bk, key = proj_tile(10)
            S.op("dve", lambda e, bk=bk: e.tensor_tensor(out=mvT[:], in0=C.ps[bk][0:64, :], in1=rstd1[0:64, :], op=ALU.mult),
                 reads=["rstd1"], writes=[key, "mvT"])
            transpose_v(mvT, "mvT", mVs, "mVs", "V_moba")
            bk, key = proj_tile(3)
            S.op("dve", lambda e, bk=bk: e.tensor_tensor(out=gf[:], in0=C.ps[bk][:, :], in1=rstd1[:], op=ALU.mult),
                 reads=["rstd1"], writes=[key, "gf"])
            S.op("act", lambda e: e.activation(out=gs[:], in_=gf[:], func=AF.Silu), reads=["gf"], writes=["gs"])
            S.dma("sp", lambda e: e.dma_start(out=d["GT_mla"][:, c0:c0 + TT], in_=gs[0:64, :]), reads=["gs"], writes=["d_GT_mla"])
            S.dma("sp", lambda e: e.dma_start(out=d["GT_moba"][:, c0:c0 + TT], in_=gs[64:128, :]), reads=["gs"], writes=["d_GT_moba"])
            for j, nm in enumerate(("gqT", "gkT", "gvT")):
                bk, key = proj_tile(4 + j)
                S.op("dve", lambda e, bk=bk, j=j: e.tensor_tensor(out=gq3[j][:], in0=C.ps[bk][:, :], in1=rstd1[:], op=ALU.mult),
                     reads=["rstd1"], writes=[key, "gq3_%d" % j])
                S.dma("sp", lambda e, j=j, nm=nm: e.dma_start(out=d[nm][:, c0:c0 + TT], in_=gq3[j][:]),
                      reads=["gq3_%d" % j], writes=["d_" + nm])
            bk, key = proj_tile(7)
            S.op("dve", lambda e, bk=bk: e.tensor_tensor(out=zf[:], in0=C.ps[bk][:, :], in1=rstd1[:], op=ALU.mult),
                 reads=["rstd1"], writes=[key, "zf"])
            S.op("act", lambda e: e.activation(out=zs[:], in_=zf[:], func=AF.Silu), reads=["zf"], writes=["zs"])
            S.dma("sp", lambda e: e.dma_start(out=d["gzT"][:, c0:c0 + TT], in_=zs[:]), reads=["zs"], writes=["d_gzT"])
            bk, key = proj_tile(11)
            S.op("dve", lambda e, bk=bk: e.tensor_tensor(out=bas[:], in0=C.ps[bk][0:2, :], in1=rstd1[0:2, :], op=ALU.mult),
                 reads=["rstd1"], writes=[key, "bas"])
            S.dma("sp", lambda e: e.dma_start(out=d["baT"][:, c0:c0 + TT], in_=bas[:]), reads=["bas"], writes=["d_baT"])
        for ti in tiles:
            do_tile(ti)
        if do_proj and not final:
            S.op("dve", lambda e: e.tensor_scalar(out=km[:], in0=km[:], scalar1=1.0 / 256, scalar2=None, op0=ALU.mult),
                 reads=["km"], writes=["km"])
            S.dma("sp", lambda e: e.dma_start(out=d["kmT"][:, :], in_=km[:]), reads=["km"], writes=["d_kmT"])
        S.drain_dmas("sp")
        S.replay()


def rope_consts():
    inv = (1.0 / (10000.0 ** (np.arange(0, 32, 2, dtype=np.float32) / 32))).astype(np.float32)
    ang = (np.arange(SEQ, dtype=np.float32)[:, None] * inv[None, :]).astype(np.float32)
    cos = np.cos(ang).astype(np.float32).T
    sin = np.sin(ang).astype(np.float32).T
    ctab = np.ascontiguousarray(np.concatenate([cos, cos], 0))
    stab = np.ascontiguousarray(np.concatenate([-sin, sin], 0))
    return ctab, stab


def in_cols(h):
    cq0, ckv0, kr0, mg0, gq0, gk0, gv0, gz0, gb0, ga0, mq0, mk0, mv0, cg0 = (
        0, 256, 384, 416, 672, 1184, 1696, 2208, 2720, 2724, 2728, 2984, 3240, 3496)
    r = lambda a, n: list(range(a, a + n))
    cols = []
    cols += r(cq0, 256) + r(ckv0, 128)
    cols += r(mg0 + 64 * h, 64) + r(cg0 + 64 * h, 64)
    cols += r(gq0 + 128 * h, 128) + r(gk0 + 128 * h, 128) + r(gv0 + 128 * h, 128) + r(gz0 + 128 * h, 128)
    cols += r(mq0 + 64 * h, 64) + r(kr0, 32)
    cols += r(mk0 + 64 * h, 64) + r(kr0 + 16, 16) + r(kr0, 16)
    cols += r(mv0 + 64 * h, 64)
    cols += [gb0 + h, ga0 + h]
    assert len(cols) == NCOL
    return cols


def prep_layer(I, l, h):
    f = np.float32
    out = {}
    out["w_in"] = np.ascontiguousarray(I["w_in"][l][:, in_cols(h)]).astype(f)
    out["nw"] = np.ascontiguousarray(I["norm_w"][l].reshape(8, 128).T).astype(f)
    wq = I["mla_w_uq"][l][:, 96 * h:96 * h + 96]
    wuq = np.zeros((256, 192), f)
    wuq[:, 0:96] = wq
    wuq[:, 160:176] = wq[:, 80:96]
    wuq[:, 176:192] = wq[:, 64:80]
    out["wuq"] = wuq
    out["qnw"] = np.ascontiguousarray(I["mla_q_norm"][l].reshape(2, 128).T).astype(f)
    out["wukv"] = np.ascontiguousarray(I["mla_w_ukv"][l][:, 128 * h:128 * h + 128]).astype(f)
    out["kvnw"] = np.ascontiguousarray(I["mla_kv_norm"][l].reshape(128, 1)).astype(f)
    return out


PROJ_OUTS = [("QT_mla", (96, SEQ), BF16), ("KT_mla", (96, SEQ), BF16), ("V_mla", (SEQ, 64), BF16),
             ("QTf_moba", (64, SEQ), F32), ("QT_moba", (64, SEQ), BF16), ("KT_moba", (64, SEQ), BF16),
             ("V_moba", (SEQ, 64), BF16), ("kmT", (64, 32), F32),
             ("GT_mla", (64, SEQ), BF16), ("GT_moba", (64, SEQ), BF16),
             ("gqT", (128, SEQ), F32), ("gkT", (128, SEQ), F32), ("gvT", (128, SEQ), F32),
             ("gzT", (128, SEQ), BF16), ("baT", (2, SEQ), F32)]
PROJ_INS = [("w_in", (DM, NCOL)), ("nw", (128, 8)), ("wuq", (256, 192)), ("qnw", (128, 2)),
            ("wukv", (128, 128)), ("kvnw", (128, 1))]


def t5_consts():
    e = np.arange(3072)
    dd = e - 511
    n = np.maximum(dd, 0)
    nf = np.maximum(n, 1).astype(np.float32)
    large = 16 + (np.log(nf / np.float32(16)) / np.float32(math.log(2048 / 16)) * np.float32(16)).astype(np.int32)
    large = np.minimum(large, 31)
    bucket = np.where(n < 16, n, large)
    OH = np.zeros((32, 3072), np.float32)
    OH[bucket, e] = 1.0
    OH[:, dd < 0] = 0.0
    return OH


def moba_consts():
    pen = np.zeros((32, 32), np.float32)
    for own in range(32):
        pen[own, own] = 1e30
        pen[own, own + 1:] = -1e30
    E = np.zeros((32, SEQ), np.float32)
    for j in range(32):
        E[j, j * 256:(j + 1) * 256] = 1.0
    return pen, E


def stage_attn(C, moba, qtiles=range(NT)):
    nc, S, d = C.nc, C.S, C.d
    pre = "moba" if moba else "mla"
    KD = 96
    scale = 1.0 if moba else 96 ** -0.5
    rowbase = 64 if moba else 0
    with ExitStack() as st:
        def sb(name, shape, dt):
            C.uid += 1
            return st.enter_context(nc.sbuf_tensor("sb%d_%s" % (C.uid, name), list(shape), dt))
        QT = sb("QT", [96, SEQ], BF16)
        KT = sb("KT", [96, SEQ], BF16)
        Va = sb("Va", [128, 64, 65], BF16)
        GT = sb("GT", [64, SEQ], BF16)
        Pb = [sb("P%d" % i, [128, TT], BF16) for i in range(4)]
        osb = sb("osb", [65, TT], F32)
        rec = sb("rec", [64, TT], F32)
        otmp = sb("otmp", [64, TT], F32)
        ysb = sb("ysb", [64, TT], BF16)
        if C.fused:
            ym = [sb("ym%d" % j, [64, TT], BF16) for j in range(4)]
            hm = sb("hm", [128, 4], F32)
            S.dma("sp", lambda e: e.dma_start(out=hm[:], in_=d["hm"][:, :]), reads=["d_hm"], writes=["hm"])
        sel = sb("sel", [65, 64], F32)
        nrows = 64 if moba else 96
        for q4 in range(4):
            cs = slice(q4 * 2048, (q4 + 1) * 2048)
            S.dma("sp", lambda e, cs=cs: e.dma_start(out=QT[0:nrows, cs], in_=d["QT_" + pre][:, cs]), reads=["d_QT_" + pre], writes=["QT"])
            S.dma("sp", lambda e, cs=cs: e.dma_start(out=KT[0:nrows, cs], in_=d["KT_" + pre][:, cs]), reads=["d_KT_" + pre], writes=["KT"])
            S.dma("sp", lambda e, cs=cs: e.dma_start(out=GT[:, cs], in_=d["GT_" + pre][:, cs]), reads=["d_GT_" + pre], writes=["GT"])
        for a4 in range(16):
            S.dma("sp", lambda e, a4=a4: e.dma_start(
                out=Va[:, 4 * a4:4 * a4 + 4, 0:64],
                in_=d["V_" + pre][a4 * 512:(a4 + 1) * 512, :].rearrange("(a p) v -> p a v", p=128)),
                reads=["d_V_" + pre], writes=["Va"])
        S.op("dve", lambda e: e.memset(Va[:, :, 64:65], 1.0), writes=["Va"])
        S.dma("sp", lambda e: e.dma_start(out=sel[:], in_=d["sel"][:, :]), reads=["d_sel"], writes=["sel"])
        if not moba:
            trif = sb("trif", [128, 128], F32)
            tri = sb("tri", [128, 128], BF16)
            S.dma("sp", lambda e: e.dma_start(out=trif[:], in_=d["tri"][:, :]), reads=["d_tri"], writes=["trif"])
            S.op("dve", lambda e: e.tensor_copy(out=tri[:], in_=trif[:]), reads=["trif"], writes=["tri"])
        else:
            t5c = sb("t5c", [32, 1], F32)
            et = sb("et", [32, 1], F32)
            b31 = sb("b31", [128, 1], F32)
            OH = sb("OH", [32, 3072], F32)
            fvs = sb("fvs", [1, 3072], F32)
            ES32 = sb("ES32", [128, 2560], F32)
            ES = sb("ES", [128, 2560], BF16)
            S.dma("sp", lambda e: e.dma_start(out=t5c[:], in_=d["t5h"][:, :]), reads=["d_t5h"], writes=["t5c"])
            S.dma("sp", lambda e: e.dma_start(out=b31[:], in_=d["t5h"][31:32, :].partition_broadcast(128).rearrange("p a b -> p (a b)")),
                  reads=["d_t5h"], writes=["b31"])
            S.dma("sp", lambda e: e.dma_start(out=OH[:], in_=d["OH"][:, :]), reads=["d_OH"], writes=["OH"])
            S.op("act", lambda e: e.activation(out=et[:], in_=t5c[:], func=AF.Exp), reads=["t5c"], writes=["et"])
            for j in range(6):
                key = mm_group(C, (lambda: C.ps[5][0:1, :]), [(et[:, 0:1], OH[:, j * 512:(j + 1) * 512])], 5, reads=["et", "OH"])
                S.op("dve", lambda e, j=j: e.tensor_copy(out=fvs[:, j * 512:(j + 1) * 512], in_=C.ps[5][0:1, :]), reads=[], writes=[key, "fvs"])
            S.dma("sp", lambda e: e.dma_start(out=d["fv"][:, :], in_=fvs[:]), reads=["fvs"], writes=["d_fv"])
            for ki in range(128):
                S.dma("sp", lambda e, ki=ki: e.dma_start(
                    out=ES32[ki:ki + 1, :], in_=d["fv"][:, 127 - ki:127 - ki + 2560]), reads=["d_fv"], writes=["ES32_%d" % ki])
            S.op("dve", lambda e: e.tensor_copy(out=ES[:], in_=ES32[:]), reads=["ES32_%d" % ki for ki in range(128)], writes=["ES"])
            for q4 in range(4):
                cs = slice(q4 * 2048, (q4 + 1) * 2048)
                S.dma("sp", lambda e, cs=cs: e.dma_start(out=KT[64:96, cs], in_=d["Eb"][:, cs]), reads=["d_Eb"], writes=["KT"])
            QTf = sb("QTf", [64, SEQ], F32)
            kmT = sb("kmT", [64, 32], F32)
            pen = sb("pen", [128, 32 * 32], F32)
            identf = sb("identf", [128, 128], F32)
            identb = sb("identb", [128, 128], BF16)
            S.dma("sp", lambda e: e.dma_start(out=identf[:], in_=d["ident"][:, :]), reads=["d_ident"], writes=["identf"])
            S.op("dve", lambda e: e.tensor_copy(out=identb[:], in_=identf[:]), reads=["identf"], writes=["identb"])
            for q4 in range(4):
                cs = slice(q4 * 2048, (q4 + 1) * 2048)
                S.dma("sp", lambda e, cs=cs: e.dma_start(out=QTf[:, cs], in_=d["QTf_moba"][:, cs]), reads=["d_QTf_moba"], writes=["QTf"])
            S.dma("sp", lambda e: e.dma_start(out=kmT[:], in_=d["kmT"][:, :]), reads=["d_kmT"], writes=["kmT"])
            S.dma("sp", lambda e: e.dma_start(out=pen[:], in_=d["pen"][:, :].rearrange("a b -> (a b)").partition_broadcast(128)),
                  reads=["d_pen"], writes=["pen"])
            gm = [sb("gm%d" % i, [128, 32], F32) for i in range(4)]
            m8 = [sb("m8%d" % i, [128, 8], F32) for i in range(4)]
            thr = [sb("thr%d" % i, [128, 1], F32) for i in range(4)]
            Mq = [sb("Mq%d" % i, [128, 96], BF16) for i in range(4)]
            for i in range(4):
                S.op("dve", lambda e, i=i: e.memset(Mq[i][:], 0.0), writes=["Mq%d" % i])
            for g in range(16):
                keys = []
                for j in range(4):
                    i = g * 4 + j
                    keys.append(mm_group(C, (lambda j=j: C.ps[6][:, j * 32:(j + 1) * 32]), [(QTf[:, i * 128:(i + 1) * 128], kmT[:, :])], 6,
                                         reads=["QTf", "kmT"]))
                for j in range(4):
                    own = (g * 4 + j) // 2
                    S.op("dve", lambda e, j=j, own=own: e.tensor_tensor(out=gm[j][:], in0=C.ps[6][:, j * 32:(j + 1) * 32],
                                                                      in1=pen[:, own * 32:(own + 1) * 32], op=ALU.add),
                         reads=["pen"], writes=[keys[j], "gm%d" % j])
                for j in range(4):
                    S.op("dve", lambda e, j=j: e.max(out=m8[j][:], in_=gm[j][:]), reads=["gm%d" % j], writes=["m8%d" % j])
                for j in range(4):
                    S.op("dve", lambda e, j=j: e.tensor_scalar(out=thr[j][:], in0=m8[j][:, 3:4], scalar1=-1e29, scalar2=None, op0=ALU.max),
                         reads=["m8%d" % j], writes=["thr%d" % j])
                for j in range(4):
                    S.op("dve", lambda e, j=j: e.tensor_scalar(out=Mq[j][:, 64:96], in0=gm[j][:], scalar1=thr[j][:, 0:1], scalar2=-30000.0,
                                                               op0=ALU.is_lt, op1=ALU.mult),
                         reads=["gm%d" % j, "thr%d" % j], writes=["Mq%d" % j])
                for j in range(4):
                    S.op("pe", lambda e, j=j: e.transpose(C.ps[7][0:96, :].bitcast(BF16)[:, j * 128:(j + 1) * 128], Mq[j][:], identb[:]),
                         reads=["Mq%d" % j, "identb"], writes=["ps7"], sig=True)
                S.op("act", lambda e, g=g: e.activation(out=QT[64:96, g * 512:(g + 1) * 512],
                                                        in_=C.ps[7][64:96, :].bitcast(BF16)[:, 0:512], func=AF.Copy),
                     reads=[], writes=["ps7", "QT"])

        SB = (0, 1, 2, 3)
        OB = (4, 5)
        NB = 4
        DEPTHQ = 3
        items = []
        for qi, qt in enumerate(qtiles):
            for kt in range(4 * qt + 4):
                items.append((qi, qt, kt))

        def front(idx):
            qi, qt, kt = items[idx]
            q0 = qt * TT
            k0 = kt * 128
            diag = kt >= 4 * qt
            koff = (kt - 4 * qt) * 128 if diag else 0
            sbk = SB[idx % NB]
            skey = "ps%d" % sbk
            P = Pb[idx % NB]
            pkey = "P%d" % (idx % NB)
            S.op("pe", lambda e: e.matmul(
                C.ps[sbk][:, koff:TT], KT[0:KD, k0:k0 + 128], QT[0:KD, q0 + koff:q0 + TT], start=True, stop=True),
                reads=["KT", "QT"], writes=[skey])
            far = moba and (q0 - k0 >= 1664)
            if far:
                S.op("act", lambda e: e.activation(
                    out=P[:, koff:TT], in_=C.ps[sbk][:, koff:TT], func=AF.Exp, bias=b31[:, 0:1], scale=scale),
                    reads=["b31"], writes=[skey, pkey])
            else:
                S.op("act", lambda e: e.activation(
                    out=P[:, koff:TT], in_=C.ps[sbk][:, koff:TT], func=AF.Exp, scale=scale),
                    reads=[], writes=[skey, pkey])
                if moba:
                    s0 = q0 - k0 + 384
                    S.op("pool", lambda e: e.tensor_tensor(
                        out=P[:, koff:TT], in0=P[:, koff:TT], in1=ES[:, s0 + koff:s0 + TT], op=ALU.mult),
                        reads=["ES", pkey], writes=[pkey])
                elif diag:
                    S.op("pool", lambda e: e.tensor_tensor(
                        out=P[:, koff:koff + 128], in0=P[:, koff:koff + 128], in1=tri[:], op=ALU.mult),
                        reads=["tri", pkey], writes=[pkey])

        def back(idx):
            qi, qt, kt = items[idx]
            q0 = qt * TT
            diag = kt >= 4 * qt
            koff = (kt - 4 * qt) * 128 if diag else 0
            nkt = 4 * qt + 4
            ob = OB[qi % 2]
            okey = "ps%d" % ob
            P = Pb[idx % NB]
            pkey = "P%d" % (idx % NB)
            S.op("pe", lambda e: e.matmul(
                C.ps[ob][0:65, koff:TT], Va[:, kt, :], P[:, koff:TT], start=(kt == 0), stop=(kt == nkt - 1)),
                reads=[pkey, "Va"], writes=[okey])
            if kt != nkt - 1:
                return
            S.op("act", lambda e: e.activation(out=osb[:], in_=C.ps[ob][0:65, :], func=AF.Copy), reads=[], writes=[okey, "osb"])
            key = mm_group(C, (lambda: C.ps[6][0:64, :]), [(sel[:], osb[:])], 6, reads=["sel", "osb"])
            S.op("dve", lambda e: e.reciprocal(out=rec[:], in_=C.ps[6][0:64, :]), reads=[], writes=[key, "rec"])
            S.op("dve", lambda e: e.tensor_tensor(out=otmp[:], in0=osb[0:64, :], in1=rec[:], op=ALU.mult), reads=["osb", "rec"], writes=["otmp"])
            if not C.fused:
                S.op("dve", lambda e: e.tensor_tensor(out=ysb[:], in0=otmp[:], in1=GT[:, q0:q0 + TT], op=ALU.mult),
                     reads=["otmp", "GT"], writes=["ysb"])
                S.dma("sp", lambda e: e.dma_start(out=d["yT_h"][rowbase:rowbase + 64, q0:q0 + TT], in_=ysb[:]),
                      reads=["ysb"], writes=["d_yT_h"])
            else:
                qq, qc = q0 // 2048, q0 % 2048
                for j in range(4):
                    S.op("dve", lambda e, j=j: e.scalar_tensor_tensor(out=ym[j][:], in0=otmp[:], scalar=hm[0:64, j:j + 1],
                                                                      in1=GT[:, q0:q0 + TT], op0=ALU.mult, op1=ALU.mult),
                         reads=["otmp", "GT", "hm"], writes=["ym%d" % j])
                    S.dma("sp", lambda e, j=j: e.dma_start(
                        out=d["ypad%d" % qq][256 * j + rowbase:256 * j + rowbase + 64, qc:qc + TT], in_=ym[j][:]),
                        reads=["ym%d" % j], writes=["d_ypad%d" % qq])

        n_it = len(items)
        for idx in range(n_it + DEPTHQ):
            if idx < n_it:
                front(idx)
            if idx - DEPTHQ >= 0:
                back(idx - DEPTHQ)
        S.drain_dmas("sp")
        S.replay()


GC = 128
NCH = SEQ // GC


def gdn_consts():
    i = np.arange(128)
    umask = (i[:, None] <= i[None, :]).astype(np.float32)
    m2 = (i[:, None] > i[None, :]).astype(np.float32)
    neg = np.where(i[:, None] < i[None, :], -30000.0, 0.0).astype(np.float32)
    sl = (i[:, None] > i[None, :]).astype(np.float32)
    return umask, m2, neg, sl


def level_masks():
    i = np.arange(128)
    out = np.zeros((128, 14, 128), np.float32)
    for l in range(7):
        b = 1 << l
        bi = i // b
        m = ((bi[:, None] % 2 == 1) & (bi[None, :] == bi[:, None] - 1)).astype(np.float32)
        out[:, 2 * l, :] = m
        out[:, 2 * l + 1, :] = m.T
    return out.reshape(128, 14 * 128)


def stage_gdn(C, nchunks=NCH, G=4, stop_after=None):
    nc, S, d = C.nc, C.S, C.d
    NSET = 2 * G
    with ExitStack() as st:
        def sb(name, shape, dt):
            C.uid += 1
            return st.enter_context(nc.sbuf_tensor("sb%d_%s" % (C.uid, name), list(shape), dt))
        identf = sb("identf", [128, 128], F32)
        umask = sb("umask", [128, 128], F32)
        m2 = sb("m2", [128, 128], F32)
        neg = sb("neg", [128, 128], F32)
        slm = sb("slm", [128, 128], F32)
        onesf = sb("onesf", [128, 128], F32)
        identb = sb("identb", [128, 128], BF16)
        lvf = sb("lvf", [128, 14 * 128], F32)
        lvm = sb("lvm", [128, 14 * 128], BF16)
        S.dma("sp", lambda e: e.dma_start(out=lvf[:], in_=d["lvlm"][:, :]), reads=["d_lvlm"], writes=["lvf"])
        S.op("dve", lambda e: e.tensor_copy(out=lvm[:], in_=lvf[:]), reads=["lvf"], writes=["lvm"])
        for nm, t in (("ident", identf), ("umask", umask), ("m2", m2), ("neg", neg), ("slm", slm)):
            S.dma("sp", lambda e, nm=nm, t=t: e.dma_start(out=t[:], in_=d[nm][:, :]), reads=["d_" + nm], writes=[nm])
        S.op("dve", lambda e: e.memset(onesf[:], 1.0), writes=["onesf"])
        S.op("dve", lambda e: e.tensor_copy(out=identb[:], in_=identf[:]), reads=["ident"], writes=["identb"])
        eps_t = sb("eps_t", [128, 1], F32)
        S.op("dve", lambda e: e.memset(eps_t[:], EPS), writes=["eps"])
        cw = sb("cw", [128, 12], F32)
        S.dma("sp", lambda e: e.dma_start(out=cw[:], in_=d["cw"][:, :]), reads=["d_cw"], writes=["cw"])
        gsc = sb("gsc", [128, 4], F32)
        S.dma("sp", lambda e: e.dma_start(out=gsc[:, 0:2], in_=d["gsc"][:, :].rearrange("a b -> (a b)").partition_broadcast(128)),
              reads=["d_gsc"], writes=["gsc"])
        gnw = sb("gnw", [128, 1], F32)
        S.dma("sp", lambda e: e.dma_start(out=gnw[:], in_=d["gnw"][:, :]), reads=["d_gnw"], writes=["gnw"])
        S.op("act", lambda e: e.activation(out=gsc[:, 2:3], in_=gsc[:, 0:1], func=AF.Exp), reads=["gsc"], writes=["gsc2"])
        S.op("dve", lambda e: e.tensor_scalar(out=gsc[:, 3:4], in0=gsc[:, 2:3], scalar1=-1.0, scalar2=None, op0=ALU.mult),
             reads=["gsc2"], writes=["gsc3"])
        ba = [sb("ba%d" % i, [2, TT], F32) for i in range(2)]
        batok = sb("batok", [128, NCH, 2], F32)
        for c in range(NCH):
            bi = (c // 4) % 2
            if c % 4 == 0:
                S.dma("sp", lambda e, c=c, bi=bi: e.dma_start(out=ba[bi][:], in_=d["baT"][:, c * 128:c * 128 + TT]),
                      reads=["d_baT"], writes=["ba%d" % bi])
            S.op("pe", lambda e, c=c, bi=bi: e.matmul(C.ps[0][:, 2 * c:2 * c + 2], ba[bi][0:2, (c % 4) * 128:(c % 4 + 1) * 128], identf[0:2, 0:2],
                                                      start=True, stop=True),
                 reads=["ba%d" % bi, "ident"], writes=["ps0"], sig=True)
        S.op("dve", lambda e: e.tensor_copy(out=batok[:].rearrange("p a b -> p (a b)"), in_=C.ps[0][:, 0:2 * NCH]), reads=[], writes=["ps0", "batok"])
        beta = sb("beta", [128, NCH], F32)
        nbeta = sb("nbeta", [128, NCH], F32)
        gg = sb("gg", [128, NCH], F32)
        tmpa = sb("tmpa", [128, NCH], F32)
        gc = sb("gc", [128, NCH], F32)
        gce = sb("gce", [128, NCH], F32)
        egc = sb("egc", [128, NCH], F32)
        bke = sb("bke", [128, NCH], F32)
        eend = sb("eend", [128, NCH], F32)
        gend = sb("gend", [128, NCH], F32)
        S.op("act", lambda e: e.activation(out=beta[:], in_=batok[:, :, 0], func=AF.Sigmoid), reads=["batok"], writes=["beta"])
        S.op("dve", lambda e: e.tensor_scalar(out=nbeta[:], in0=beta[:], scalar1=-1.0, scalar2=None, op0=ALU.mult), reads=["beta"], writes=["nbeta"])
        S.op("act", lambda e: e.activation(out=tmpa[:], in_=batok[:, :, 1], func=AF.Exp, bias=gsc[:, 1:2], scale=1.0), reads=["batok", "gsc"], writes=["tmpa"])
        one_t = sb("one_t", [128, 1], F32)
        S.op("dve", lambda e: e.memset(one_t[:], 1.0), writes=["one_t"])
        S.op("act", lambda e: e.activation(out=tmpa[:], in_=tmpa[:], func=AF.Ln, bias=one_t[:, 0:1], scale=1.0), reads=["tmpa", "one_t"], writes=["tmpa"])
        S.op("dve", lambda e: e.tensor_scalar(out=gg[:], in0=tmpa[:], scalar1=gsc[:, 3:4], scalar2=None, op0=ALU.mult), reads=["tmpa", "gsc3"], writes=["gg"])
        key = mm_group(C, (lambda: C.ps[1][:, 0:NCH]), [(umask[:], gg[:])], 1, reads=["umask", "gg"])
        S.op("dve", lambda e: e.tensor_copy(out=gc[:], in_=C.ps[1][:, 0:NCH]), reads=[], writes=[key, "gc"])
        key = mm_group(C, (lambda: C.ps[2][:, 0:NCH]), [(onesf[:], gg[:])], 2, reads=["onesf", "gg"])
        S.op("dve", lambda e: e.tensor_copy(out=gce[:], in_=C.ps[2][:, 0:NCH]), reads=[], writes=[key, "gce"])
        S.op("act", lambda e: e.activation(out=egc[:], in_=gc[:], func=AF.Exp), reads=["gc"], writes=["egc"])
        S.op("act", lambda e: e.activation(out=gend[:], in_=gce[:], func=AF.Exp), reads=["gce"], writes=["gend"])
        S.op("dve", lambda e: e.tensor_tensor(out=bke[:], in0=beta[:], in1=egc[:], op=ALU.mult), reads=["beta", "egc"], writes=["bke"])
        S.op("dve", lambda e: e.tensor_tensor(out=eend[:], in0=gce[:], in1=gc[:], op=ALU.subtract), reads=["gce", "gc"], writes=["eend"])
        S.op("act", lambda e: e.activation(out=eend[:], in_=eend[:], func=AF.Exp), reads=["eend"], writes=["eend"])
        PERTOK = ["beta", "nbeta", "gg", "egc", "bke", "eend", "gend"]
        if stop_after == 1:
            S.drain_dmas("sp"); S.replay(); return

        QnT = sb("QnT", [128, SEQ], BF16)
        KnT = sb("KnT", [128, SEQ], BF16)
        Kb = sb("Kb", [128, NCH, 128], BF16)
        Kend = sb("Kend", [128, NCH, 128], BF16)
        Vb = sb("Vb", [128, NCH, 128], BF16)
        xin = [[sb("xin%d_%d" % (j, i), [128, 3 + TT], F32) for i in range(2)] for j in range(3)]
        cacc = [sb("cacc%d" % j, [128, TT], F32) for j in range(3)]
        sact = [sb("sact%d" % j, [128, TT], F32) for j in range(3)]
        sq2 = [sb("sq2%d" % j, [128, TT], F32) for j in range(2)]
        lnt = sb("lnt", [128, TT], F32)
        rr = [sb("rr%d" % j, [128, TT], F32) for j in range(2)]
        knf = sb("knf", [128, TT], F32)
        names3 = ("gqT", "gkT", "gvT")
        ntile_a = (nchunks * GC + TT - 1) // TT

        def phase_a_steps(ti):
            c0 = ti * TT
            b = ti % 2
            steps = []

            def pj(j):
                xt = xin[j][b]
                xk = "xin%d_%d" % (j, b)
                if ti == 0:
                    S.op("pool", lambda e, xt=xt: e.memset(xt[:, 0:3], 0.0), writes=[xk])
                    S.dma("sp", lambda e, xt=xt, j=j: e.dma_start(out=xt[:, 3:3 + TT], in_=d[names3[j]][:, 0:TT]),
                          reads=["d_" + names3[j]], writes=[xk])
                else:
                    S.dma("sp", lambda e, xt=xt, j=j: e.dma_start(out=xt[:, :], in_=d[names3[j]][:, c0 - 3:c0 + TT]),
                          reads=["d_" + names3[j]], writes=[xk])
                ck = "cacc%d" % j
                S.op("dve", lambda e, xt=xt, j=j: e.tensor_scalar(out=cacc[j][:], in0=xt[:, 0:TT], scalar1=cw[:, 4 * j:4 * j + 1],
                                                                 scalar2=None, op0=ALU.mult), reads=[xk, "cw"], writes=[ck])
                for tap in range(1, 4):
                    S.op("dve", lambda e, xt=xt, j=j, tap=tap: e.scalar_tensor_tensor(
                        out=cacc[j][:], in0=xt[:, tap:tap + TT], scalar=cw[:, 4 * j + tap:4 * j + tap + 1], in1=cacc[j][:],
                        op0=ALU.mult, op1=ALU.add), reads=[xk, "cw", ck], writes=[ck])
                S.op("act", lambda e, j=j: e.activation(out=sact[j][:], in_=cacc[j][:], func=AF.Silu), reads=[ck], writes=["sact%d" % j])
            for j in range(3):
                steps.append(lambda j=j: pj(j))

            def pn(j):
                S.op("act", lambda e, j=j: e.activation(out=sq2[j][:], in_=sact[j][:], func=AF.Square), reads=["sact%d" % j], writes=["sq2%d" % j])
                bk = C.bank()
                key = mm_group(C, (lambda bk=bk: C.ps[bk][:, :]), [(onesf[:], sq2[j][:])], bk, reads=["onesf", "sq2%d" % j])
                S.op("act", lambda e, bk=bk: e.activation(out=lnt[:], in_=C.ps[bk][:, :], func=AF.Ln, bias=eps_t[:, 0:1], scale=1.0),
                     reads=["eps"], writes=[key, "lnt"])
                S.op("act", lambda e, j=j: e.activation(out=rr[j][:], in_=lnt[:], func=AF.Exp, scale=-0.5), reads=["lnt"], writes=["rr%d" % j])
            for j in range(2):
                steps.append(lambda j=j: pn(j))

            def pq():
                S.op("dve", lambda e: e.scalar_tensor_tensor(out=QnT[:, c0:c0 + TT], in0=sact[0][:], scalar=float(128 ** -0.5), in1=rr[0][:],
                                                             op0=ALU.mult, op1=ALU.mult), reads=["sact0", "rr0"], writes=["QnT%d" % ti])
                S.op("dve", lambda e: e.tensor_tensor(out=knf[:], in0=sact[1][:], in1=rr[1][:], op=ALU.mult), reads=["sact1", "rr1"], writes=["knf"])
                S.op("act", lambda e: e.activation(out=KnT[:, c0:c0 + TT], in_=knf[:], func=AF.Copy), reads=["knf"], writes=["KnT%d" % ti])
            steps.append(pq)

            def pt(a):
                c = ti * 4 + a
                bk = C.bank()
                key = "ps%d" % bk
                S.op("pe", lambda e, bk=bk, a=a: e.transpose(C.ps[bk][:, 0:128], knf[:, a * 128:(a + 1) * 128], identf[:]),
                     reads=["knf", "ident"], writes=[key])
                S.op("pe", lambda e, bk=bk, a=a: e.transpose(C.ps[bk][:, 128:256], sact[2][:, a * 128:(a + 1) * 128], identf[:]),
                     reads=["sact2", "ident"], writes=[key])
                S.op("act", lambda e, bk=bk, c=c: e.activation(out=Kb[:, c, :], in_=C.ps[bk][:, 0:128], func=AF.Copy, scale=bke[:, c:c + 1]),
                     reads=["bke"], writes=[key, "Kb%d" % ti])
                S.op("dve", lambda e, bk=bk, c=c: e.tensor_scalar(out=Kend[:, c, :], in0=C.ps[bk][:, 0:128], scalar1=eend[:, c:c + 1], scalar2=None, op0=ALU.mult),
                     reads=["eend"], writes=[key, "Kend%d" % ti])
                S.op("act", lambda e, bk=bk, c=c: e.activation(out=Vb[:, c, :], in_=C.ps[bk][:, 128:256], func=AF.Copy, scale=beta[:, c:c + 1]),
                     reads=["beta"], writes=[key, "Vb%d" % ti])
            for a in range(4):
                steps.append(lambda a=a: pt(a))
            return steps

        for st_ in phase_a_steps(0):
            st_()
        if stop_after == 2:
            S.drain_dmas("sp"); S.replay(); return

        def bufset(name, dt, n=NSET):
            return [sb("%s%d" % (name, i), [128, 128], dt) for i in range(n)]
        G1 = bufset("G1", F32)
        Dm = bufset("Dm", F32)
        Xf = bufset("Xf", F32)
        Xb_ = bufset("Xb", BF16)
        Yb = bufset("Yb", BF16)
        Mb = bufset("Mb", BF16)
        Xo = bufset("Xo", BF16)
        Yo = bufset("Yo", BF16)
        Hb = bufset("Hb", BF16)
        Gb = bufset("Gb", BF16)
        Am = bufset("Am", F32)
        AT = bufset("AT", BF16)
        TTb = bufset("TTb", BF16)
        WT = bufset("WT", BF16)
        U0 = bufset("U0", F32)
        Ub = bufset("Ub", BF16, 2)
        Sf = sb("Sf", [128, 128], F32)
        Sbb = [sb("Sbb%d" % i, [128, 128], BF16) for i in range(2)]
        otmp = sb("otmp", [128, 128], F32)
        osb = sb("osb", [128, 128], F32)
        osq = sb("osq", [128, 128], F32)
        onr = sb("onr", [128, 128], F32)
        ssq = sb("ssq", [128, 1], F32)
        lno = sb("lno", [128, 1], F32)
        rso = sb("rso", [128, 1], F32)
        gz = [sb("gz%d" % i, [128, TT], BF16) for i in range(2)]
        yst = [sb("yst%d" % i, [128, TT], BF16) for i in range(2)]
        if C.fused:
            ystm = [[sb("ystm%d_%d" % (i, j), [128, TT], BF16) for j in range(4)] for i in range(2)]
            hm = sb("hm", [128, 4], F32)
            gnwm = sb("gnwm", [128, 4], F32)
            S.dma("sp", lambda e: e.dma_start(out=hm[:], in_=d["hm"][:, :]), reads=["d_hm"], writes=["hm"])
            S.op("dve", lambda e: e.tensor_scalar(out=gnwm[:], in0=hm[:], scalar1=gnw[:, 0:1], scalar2=None, op0=ALU.mult),
                 reads=["hm", "gnw"], writes=["gnwm"])
        S.op("dve", lambda e: e.memset(Sf[:], 0.0), writes=["Sf"])
        S.op("dve", lambda e: e.memset(Sbb[0][:], 0.0), writes=["Sbb0"])

        def mm1(out_bank, lhsT, rhs, reads, cols=128):
            return mm_group(C, (lambda: C.ps[out_bank][:, 0:cols]), [(lhsT, rhs)], out_bank, reads=reads)

        def pre_steps(c):
            s = c % NSET
            ck = slice(c * GC, (c + 1) * GC)
            k = lambda nm: "%s%d" % (nm, s)
            steps = []
            st8 = {}

            def s1a():
                S.op("act", lambda e: e.activation(out=G1[s][:], in_=umask[:], func=AF.Copy, scale=gg[:, c:c + 1]),
                     reads=["umask", "gg"], writes=[k("G1")])
            steps.append(s1a)

            def s1b():
                bk = C.bank()
                key = mm_group(C, (lambda: C.ps[bk][:, 0:128]), [(G1[s][:], m2[:]), (identf[:], neg[:])], bk, reads=[k("G1"), "m2", "ident", "neg"])
                S.op("act", lambda e: e.activation(out=Dm[s][:], in_=C.ps[bk][:, 0:128], func=AF.Exp), reads=[], writes=[key, k("Dm")])
            steps.append(s1b)

            def s2():
                bk = C.bank()
                key = mm1(bk, KnT[:, ck], KnT[:, ck], ["KnT%d" % (c // 4)])
                S.op("dve", lambda e: e.scalar_tensor_tensor(out=Xf[s][:], in0=C.ps[bk][:, 0:128], scalar=nbeta[:, c:c + 1], in1=Dm[s][:],
                                                             op0=ALU.mult, op1=ALU.mult), reads=["nbeta", k("Dm")], writes=[key, k("Xf")])
                S.op("dve", lambda e: e.tensor_tensor(out=Xf[s][:], in0=Xf[s][:], in1=slm[:], op=ALU.mult), reads=[k("Xf"), "slm"], writes=[k("Xf")])
                S.op("act", lambda e: e.activation(out=Xb_[s][:], in_=Xf[s][:], func=AF.Copy), reads=[k("Xf")], writes=[k("Xb")])
            steps.append(s2)

            def s3a():
                bk = C.bank()
                key = mm1(bk, QnT[:, ck], KnT[:, ck], ["QnT%d" % (c // 4), "KnT%d" % (c // 4)])
                S.op("dve", lambda e: e.tensor_tensor(out=Am[s][:], in0=C.ps[bk][:, 0:128], in1=Dm[s][:], op=ALU.mult),
                     reads=[k("Dm")], writes=[key, k("Am")])
            steps.append(s3a)

            def s4():
                bk = C.bank()
                key = "ps%d" % bk
                S.op("pe", lambda e: e.transpose(C.ps[bk][:, 0:128], Xf[s][:], identf[:]), reads=[k("Xf"), "ident"], writes=[key])
                S.op("act", lambda e: e.activation(out=Yb[s][:], in_=C.ps[bk][:, 0:128], func=AF.Copy), reads=[], writes=[key, k("Yb")])
                S.op("pool", lambda e: e.tensor_tensor(out=Xo[s][:], in0=Xb_[s][:], in1=lvm[:, 0:128], op=ALU.mult),
                     reads=[k("Xb"), "lvm"], writes=[k("Xo")])
                S.op("dve", lambda e: e.tensor_tensor(out=Mb[s][:], in0=Xo[s][:], in1=identb[:], op=ALU.add),
                     reads=[k("Xo"), "identb"], writes=[k("Mb")])
            steps.append(s4)

            def s3b():
                bk2 = C.bank()
                key2 = "ps%d" % bk2
                S.op("pe", lambda e: e.transpose(C.ps[bk2][:, 0:128], Am[s][:], identf[:]), reads=[k("Am"), "ident"], writes=[key2])
                S.op("act", lambda e: e.activation(out=AT[s][:], in_=C.ps[bk2][:, 0:128], func=AF.Copy), reads=[], writes=[key2, k("AT")])
                S.op("pool", lambda e: e.tensor_tensor(out=Yo[s][:], in0=Yb[s][:], in1=lvm[:, 128:256], op=ALU.mult),
                     reads=[k("Yb"), "lvm"], writes=[k("Yo")])
                S.op("dve", lambda e: e.tensor_tensor(out=TTb[s][:], in0=Yo[s][:], in1=identb[:], op=ALU.add),
                     reads=[k("Yo"), "identb"], writes=[k("TTb")])
            steps.append(s3b)
            for l in range(1, 7):
                def la(l=l):
                    if l <= 5:
                        S.op("pool", lambda e: e.tensor_tensor(out=Yo[s][:], in0=Yb[s][:], in1=lvm[:, (2 * l + 1) * 128:(2 * l + 2) * 128], op=ALU.mult),
                             reads=[k("Yb"), "lvm"], writes=[k("Yo")])
                    S.op("pool", lambda e: e.tensor_tensor(out=Xo[s][:], in0=Xb_[s][:], in1=lvm[:, (2 * l) * 128:(2 * l + 1) * 128], op=ALU.mult),
                         reads=[k("Xb"), "lvm"], writes=[k("Xo")])
                steps.append(la)

                def lb(l=l):
                    if l <= 5:
                        bh = C.bank()
                        keyh = mm1(bh, Yo[s][:], Mb[s][:], [k("Yo"), k("Mb")])
                    bg = C.bank()
                    keyg = mm1(bg, Xo[s][:], TTb[s][:], [k("Xo"), k("TTb")])
                    if l <= 5:
                        S.op("act", lambda e: e.activation(out=Hb[s][:], in_=C.ps[bh][:, 0:128], func=AF.Copy), reads=[], writes=[keyh, k("Hb")])
                    if l % 2 == 0:
                        S.op("act", lambda e: e.activation(out=Gb[s][:], in_=C.ps[bg][:, 0:128], func=AF.Copy), reads=[], writes=[keyg, k("Gb")])
                    else:
                        S.op("dve", lambda e: e.tensor_copy(out=Gb[s][:], in_=C.ps[bg][:, 0:128]), reads=[], writes=[keyg, k("Gb")])
                steps.append(lb)

                def lc(l=l):
                    if l <= 5:
                        bm = C.bank()
                        keym = mm1(bm, TTb[s][:], Hb[s][:], [k("TTb"), k("Hb")])
                    bw = C.bank()
                    keyw = mm1(bw, Mb[s][:], Gb[s][:], [k("Mb"), k("Gb")])
                    if l <= 5:
                        S.op("dve", lambda e: e.tensor_tensor(out=Mb[s][:], in0=Mb[s][:], in1=C.ps[bm][:, 0:128], op=ALU.add),
                             reads=[k("Mb")], writes=[keym, k("Mb")])
                    S.op("dve", lambda e: e.tensor_tensor(out=TTb[s][:], in0=TTb[s][:], in1=C.ps[bw][:, 0:128], op=ALU.add),
                         reads=[k("TTb")], writes=[keyw, k("TTb")])
                steps.append(lc)

            def s5():
                bk = C.bank()
                key = mm1(bk, Kb[:, c, :], TTb[s][:], ["Kb%d" % (c // 4), k("TTb")])
                bk2 = C.bank()
                key2 = mm1(bk2, TTb[s][:], Vb[:, c, :], ["Vb%d" % (c // 4), k("TTb")])
                S.op("act", lambda e: e.activation(out=WT[s][:], in_=C.ps[bk][:, 0:128], func=AF.Copy), reads=[], writes=[key, k("WT")])
                S.op("dve", lambda e: e.tensor_copy(out=U0[s][:], in_=C.ps[bk2][:, 0:128]), reads=[], writes=[key2, k("U0")])
            steps.append(s5)
            return steps

        def scan_steps(c):
            s = c % NSET
            ck = slice(c * GC, (c + 1) * GC)
            k = lambda nm: "%s%d" % (nm, s)
            u = c % 2
            sbi, sbo = c % 2, (c + 1) % 2
            stt = {}

            def sa():
                b1 = C.bank()
                key1 = mm1(b1, WT[s][:], Sbb[sbi][:], [k("WT"), "Sbb%d" % sbi])
                b2 = C.bank()
                key2 = mm1(b2, QnT[:, ck], Sbb[sbi][:], ["QnT%d" % (c // 4), "Sbb%d" % sbi])
                S.op("dve", lambda e: e.tensor_tensor(out=Ub[u][:], in0=U0[s][:], in1=C.ps[b1][:, 0:128], op=ALU.subtract),
                     reads=[k("U0")], writes=[key1, "Ub%d" % u])
                S.op("act", lambda e: e.activation(out=otmp[:], in_=C.ps[b2][:, 0:128], func=AF.Copy, scale=egc[:, c:c + 1]),
                     reads=["egc"], writes=[key2, "otmp"])

            def sb_():
                b4 = C.bank()
                key4 = mm1(b4, Kend[:, c, :], Ub[u][:], ["Kend%d" % (c // 4), "Ub%d" % u])
                b3 = C.bank()
                key3 = mm1(b3, AT[s][:], Ub[u][:], [k("AT"), "Ub%d" % u])
                S.op("dve", lambda e: e.scalar_tensor_tensor(out=Sf[:], in0=Sf[:], scalar=gend[:, c:c + 1], in1=C.ps[b4][:, 0:128],
                                                             op0=ALU.mult, op1=ALU.add), reads=["gend", "Sf"], writes=[key4, "Sf"])
                S.op("act", lambda e: e.activation(out=Sbb[sbo][:], in_=Sf[:], func=AF.Copy), reads=["Sf"], writes=["Sbb%d" % sbo])
                S.op("dve", lambda e: e.tensor_tensor(out=osb[:], in0=otmp[:], in1=C.ps[b3][:, 0:128], op=ALU.add),
                     reads=["otmp"], writes=[key3, "osb"])

            def sc():
                S.op("act", lambda e: e.activation(out=osq[:], in_=osb[:], func=AF.Square, accum_out=ssq[:, 0:1]), reads=["osb"], writes=["osq", "ssq"])
                S.op("act", lambda e: e.activation(out=lno[:], in_=ssq[:], func=AF.Ln, bias=eps_t[:, 0:1], scale=1.0 / 128), reads=["ssq", "eps"], writes=["lno"])
                S.op("act", lambda e: e.activation(out=rso[:], in_=lno[:], func=AF.Exp, scale=-0.5), reads=["lno"], writes=["rso"])
                S.op("act", lambda e: e.activation(out=onr[:], in_=osb[:], func=AF.Copy, scale=rso[:, 0:1]),
                     reads=["osb", "rso"], writes=["onr"])

            def sd():
                b5 = C.bank()
                key5 = "ps%d" % b5
                S.op("pe", lambda e: e.transpose(C.ps[b5][:, 0:128], onr[:], identf[:]), reads=["onr", "ident"], writes=[key5])
                yb = (c // 4) % 2
                a = c % 4
                if a == 0:
                    tz = (c // 4) * TT
                    S.dma("sp", lambda e: e.dma_start(out=gz[yb][:], in_=d["gzT"][:, tz:tz + TT]), reads=["d_gzT"], writes=["gz%d" % yb])
                if not C.fused:
                    S.op("dve", lambda e: e.scalar_tensor_tensor(out=yst[yb][:, a * 128:(a + 1) * 128], in0=C.ps[b5][:, 0:128], scalar=gnw[:, 0:1],
                                                                 in1=gz[yb][:, a * 128:(a + 1) * 128], op0=ALU.mult, op1=ALU.mult),
                         reads=["gnw", "gz%d" % yb], writes=[key5, "yst%d" % yb])
                    if a == 3:
                        t0 = (c // 4) * TT
                        S.dma("sp", lambda e: e.dma_start(out=d["yT_h"][128:256, t0:t0 + TT], in_=yst[yb][:]), reads=["yst%d" % yb], writes=["d_yT_h"])
                else:
                    for j in range(4):
                        S.op("dve", lambda e, j=j: e.scalar_tensor_tensor(out=ystm[yb][j][:, a * 128:(a + 1) * 128], in0=C.ps[b5][:, 0:128],
                                                                          scalar=gnwm[:, j:j + 1], in1=gz[yb][:, a * 128:(a + 1) * 128],
                                                                          op0=ALU.mult, op1=ALU.mult),
                             reads=["gnwm", "gz%d" % yb], writes=[key5, "ystm%d_%d" % (yb, j)])
                    if a == 3:
                        t0 = (c // 4) * TT
                        qq, qc = t0 // 2048, t0 % 2048
                        for j in range(4):
                            S.dma("sp", lambda e, j=j: e.dma_start(out=d["ypad%d" % qq][256 * j + 128:256 * j + 256, qc:qc + TT], in_=ystm[yb][j][:]),
                                  reads=["ystm%d_%d" % (yb, j)], writes=["d_ypad%d" % qq])
                        if qc + TT == 2048:
                            S.coll(lambda e: e.collective_compute(
                                "AllReduce", ALU.add, replica_groups=[[0, 1, 2, 3], [4, 5, 6, 7]],
                                ins=[d["ypad%d" % qq].opt()], outs=[d["yg%d" % qq].opt()]),
                                reads=["d_ypad%d" % qq], writes=["d_yg%d" % qq])
            return [sa, sb_, sc, sd]

        groups = [list(range(g, min(g + G, nchunks))) for g in range(0, nchunks, G)]
        prev = []
        for gi, grp in enumerate(groups + [[]]):
            lists = [pre_steps(c) for c in grp]
            if grp and gi + 1 < ntile_a and G == 4:
                lists.append(phase_a_steps(gi + 1))
            nst = max([len(l) for l in lists] + [0])
            pending = []
            for c in prev:
                pending += scan_steps(c)
            for si in range(nst):
                for l in lists:
                    if si < len(l):
                        l[si]()
                if pending:
                    pending.pop(0)()
            while pending:
                pending.pop(0)()
            prev = grp
        S.drain_dmas("sp")
        S.replay()


CONST_INS = [("ctab", (32, SEQ), F32), ("stab", (32, SEQ), F32), ("ident", (128, 128), F32), ("sel", (65, 64), F32),
             ("tri", (128, 128), F32), ("OH", (32, 3072), F32), ("Eb", (32, SEQ), BF16), ("pen", (32, 32), F32),
             ("umask", (128, 128), F32), ("m2", (128, 128), F32), ("neg", (128, 128), F32), ("slm", (128, 128), F32),
             ("lvlm", (128, 14 * 128), F32)]
LAYER_INS = PROJ_INS + [("t5h", (32, 1)), ("cw", (128, 12)), ("gsc", (1, 2)), ("gnw", (128, 1))]


def host_consts():
    import ml_dtypes
    ctab, stab = rope_consts()
    pen, E = moba_consts()
    umask, m2, neg, slm = gdn_consts()
    sel = np.zeros((65, 64), np.float32)
    sel[64, :] = 1.0
    return {"ctab": ctab, "stab": stab, "ident": np.eye(128, dtype=np.float32), "sel": sel,
            "tri": np.triu(np.ones((128, 128), np.float32)), "OH": t5_consts(), "Eb": E.astype(ml_dtypes.bfloat16),
            "pen": pen, "umask": umask, "m2": m2, "neg": neg, "slm": slm, "lvlm": level_masks()}


def prep_layer_all(I, l, h):
    out = prep_layer(I, l, h)
    f = np.float32
    out["t5h"] = np.ascontiguousarray(I["t5_table"][:, h:h + 1]).astype(f)
    cwf = I["gdn_conv_w"][l]
    cw = np.zeros((128, 12), f)
    for j in range(3):
        for tap in range(4):
            cw[:, 4 * j + tap] = cwf[tap, j * 512 + 128 * h:j * 512 + 128 * h + 128]
    out["cw"] = cw
    out["gsc"] = np.array([[I["gdn_A_log"][l, h], I["gdn_dt_bias"][l, h]]], f)
    out["gnw"] = np.ascontiguousarray(I["gdn_norm_w"][l].reshape(128, 1)).astype(f)
    return out


def wo_perm():
    rows = []
    for h in range(4):
        rows += list(range(64 * h, 64 * h + 64))
        rows += list(range(768 + 64 * h, 768 + 64 * h + 64))
        rows += list(range(256 + 128 * h, 256 + 128 * h + 128))
    return rows


def build_layer_program(has_prev, do_layer, final):
    nc = bass.Bass("TRN2", target_bir_lowering=False)
    with ExitStack() as st:
        C = Ctx(nc, st)
        C.din("xT", (DM, SEQ), F32)
        if has_prev:
            C.din("yT", (DM, SEQ), BF16)
            C.din("wo", (DM, DM), F32)
            if not final:
                C.dout("xTo", (DM, SEQ), F32)
        if final:
            C.din("fnw", (128, 8), F32)
            C.dout("outT", (DM, SEQ), F32)
        if do_layer:
            for n, s_, dt in CONST_INS:
                C.din(n, s_, dt)
            for n, s_ in LAYER_INS:
                C.din(n, s_, F32)
            for n, s_, dt in PROJ_OUTS:
                C.dint(n, s_, dt)
            C.dint("fv", (1, 3072), F32)
            C.dout("yT_h", (256, SEQ), BF16)
        stage_proj(C, has_prev=has_prev, do_proj=do_layer, final=final)
        if do_layer:
            stage_attn(C, False)
            stage_attn(C, True)
            stage_gdn(C)
    return nc


def build_fused_program(depth=DEPTH, do_coll=True):
    nc = bass.Bass("TRN2", target_bir_lowering=False)
    with ExitStack() as st:
        C = Ctx(nc, st)
        C.fused = True
        S = C.S
        C.din("xT", (DM, SEQ), F32)
        C.din("hm", (128, 4), F32)
        C.din("fnw", (128, 8), F32)
        for n, s_, dt in CONST_INS:
            C.din(n, s_, dt)
        for l in range(depth):
            for n, s_ in LAYER_INS:
                C.din("%s_%d" % (n, l), s_, F32)
            C.din("wo_%d" % l, (DM, DM), F32)
        C.dout("outT", (DM, SEQ), F32)
        C.dint("xTi", (DM, SEQ), F32)
        for n, s_, dt in PROJ_OUTS:
            C.dint(n, s_, dt)
        C.dint("fv", (1, 3072), F32)
        for q in range(4):
            C.dint("ypad%d" % q, (DM, 2048), BF16)
            C.dint("yg%d" % q, (DM, 2048), BF16)
        x_ext = C.d["xT"]
        for l in range(depth):
            for n, s_ in LAYER_INS:
                C.d[n] = C.d["%s_%d" % (n, l)]
            if l > 0:
                C.d["wo"] = C.d["wo_%d" % (l - 1)]
            C.d["xT"] = x_ext if l <= 1 else C.d["xTi"]
            C.d["xTo"] = C.d["xTi"]
            import os
            ST = os.environ.get("STAGES", "pamg")
            if "p" in ST:
                stage_proj(C, has_prev=(l > 0), do_proj=True, final=False)
            if "a" in ST:
                stage_attn(C, False)
            if "m" in ST:
                stage_attn(C, True)
            if "g" in ST:
                stage_gdn(C)
        C.d["wo"] = C.d["wo_%d" % (depth - 1)]
        C.d["xT"] = C.d["xTi"] if depth > 1 else x_ext
        if "f" in os.environ.get("STAGES", "pamgf"):
            stage_proj(C, has_prev=True, do_proj=False, final=True)
    return nc


_FUSED = []


def kernel(x, norm_w, w_in, mla_q_norm, mla_w_uq, mla_kv_norm, mla_w_ukv, gdn_conv_w, gdn_A_log, gdn_dt_bias,
           gdn_norm_w, w_out, t5_table, final_norm_w):
    I = dict(x=np.asarray(x), norm_w=np.asarray(norm_w), w_in=np.asarray(w_in), mla_q_norm=np.asarray(mla_q_norm),
             mla_w_uq=np.asarray(mla_w_uq), mla_kv_norm=np.asarray(mla_kv_norm), mla_w_ukv=np.asarray(mla_w_ukv),
             gdn_conv_w=np.asarray(gdn_conv_w), gdn_A_log=np.asarray(gdn_A_log), gdn_dt_bias=np.asarray(gdn_dt_bias),
             gdn_norm_w=np.asarray(gdn_norm_w), w_out=np.asarray(w_out), t5_table=np.asarray(t5_table),
             final_norm_w=np.asarray(final_norm_w))
    if not _FUSED:
        _FUSED.append(build_fused_program())
    nc = _FUSED[0]
    consts = host_consts()
    perm = wo_perm()
    fnw = np.ascontiguousarray(I["final_norm_w"].reshape(8, 128).T).astype(np.float32)
    wos = [np.ascontiguousarray(I["w_out"][l][perm, :]).astype(np.float32) for l in range(DEPTH)]
    xT = [np.ascontiguousarray(I["x"][b].T).astype(np.float32) for b in range(2)]
    maps = []
    for c in range(8):
        b, h = c // 4, c % 4
        m = dict(consts)
        m["xT"] = xT[b]
        hm = np.zeros((128, 4), np.float32)
        hm[:, h] = 1.0
        m["hm"] = hm
        m["fnw"] = fnw
        for l in range(DEPTH):
            for k, v in prep_layer_all(I, l, h).items():
                m["%s_%d" % (k, l)] = v
            m["wo_%d" % l] = wos[l]
        maps.append(m)
    res = run_bass_kernel_spmd(nc, maps, core_ids=list(range(8)))
    out = np.stack([np.asarray(res.results[4 * b]["outT"]).T for b in range(2)], axis=0)
    return np.ascontiguousarray(out).astype(np.float32)
```

```python
import math
import os
from contextlib import ExitStack

import numpy as np
import concourse.bass as bass
import concourse.mybir as mybir
from concourse.bass_utils import run_bass_kernel_spmd

F32 = mybir.dt.float32
BF16 = mybir.dt.bfloat16
AF = mybir.ActivationFunctionType
ALU = mybir.AluOpType
AX = mybir.AxisListType

SEQ = 8192
DM = 1024
DEPTH = 4
TT = 512
NT = SEQ // TT
EPS = 1e-6
TW = [128] * 8 + [96, 96, 64, 2]
TOFF = [sum(TW[:i]) for i in range(len(TW))]
NCOL = sum(TW)

ENGS = ("pe", "act", "dve", "pool", "sp")


class Sched:
    def __init__(self, nc, stack, n_dma_sems=40):
        self.nc = nc
        self.q = {e: [] for e in ENGS}
        self.sem = {e: stack.enter_context(nc.semaphore("s_" + e)) for e in ENGS if e != "sp"}
        self.cnt = {e: 0 for e in ENGS}
        self.seen = {e: {} for e in ENGS}
        self.dsem = [stack.enter_context(nc.semaphore("d%d" % i)) for i in range(n_dma_sems)]
        self.dcnt = [0] * n_dma_sems
        self.dnext = 0
        self.lastw = {}
        self.readers = {}
        self.semobj = dict(self.sem)
        for i, s in enumerate(self.dsem):
            self.semobj["d%d" % i] = s
        self.semobj["cc"] = stack.enter_context(nc.semaphore("s_cc"))
        self.cccnt = 0

    def _deps(self, eng, reads, writes):
        deps = {}

        def add(src, val, kind):
            if src == eng and eng == "pe":
                return
            if deps.get(src, 0) < val:
                deps[src] = val
        for k in reads:
            w = self.lastw.get(k)
            if w:
                add(w[0], w[1], "raw")
        for k in writes:
            w = self.lastw.get(k)
            if w:
                add(w[0], w[1], "waw")
            for s, v in self.readers.get(k, {}).items():
                add(s, v, "war")
        waits = []
        for s, v in deps.items():
            if self.seen[eng].get(s, 0) < v:
                self.seen[eng][s] = v
                waits.append((s, v))
        return waits

    def _commit(self, src, val, reads, writes):
        for k in reads:
            self.readers.setdefault(k, {})[src] = val
        for k in writes:
            self.lastw[k] = (src, val)
            self.readers[k] = {}

    def op(self, eng, fn, reads=(), writes=(), sig=True):
        waits = self._deps(eng, reads, writes)
        val = self.cnt[eng] + 1
        if sig:
            self.cnt[eng] = val
        self._commit(eng, val, reads, writes)
        self.q[eng].append((waits, fn, (eng, 1) if sig else None))

    def dma(self, eng, fn, reads=(), writes=()):
        i = self.dnext
        self.dnext = (self.dnext + 1) % len(self.dsem)
        name = "d%d" % i
        waits = self._deps(eng, reads, writes)
        if self.dcnt[i] > 0 and self.seen[eng].get(name, 0) < self.dcnt[i]:
            self.seen[eng][name] = self.dcnt[i]
            waits.append((name, self.dcnt[i]))
        self.dcnt[i] += 16
        self._commit(name, self.dcnt[i], reads, writes)
        self.q[eng].append((waits, fn, (name, 16)))

    def coll(self, fn, reads=(), writes=()):
        if "cc" not in self.semobj:
            raise RuntimeError("no cc semaphore")
        waits = self._deps("pool", reads, writes)
        self.cccnt += 1
        self._commit("cc", self.cccnt, reads, writes)
        self.q["pool"].append((waits, fn, ("cc", 1)))

    def drain_dmas(self, eng="sp", include_cc=False):
        waits = []
        for i in range(len(self.dsem)):
            name = "d%d" % i
            if self.dcnt[i] > 0 and self.seen[eng].get(name, 0) < self.dcnt[i]:
                self.seen[eng][name] = self.dcnt[i]
                waits.append((name, self.dcnt[i]))
        if include_cc and self.cccnt > 0 and self.seen[eng].get("cc", 0) < self.cccnt:
            self.seen[eng]["cc"] = self.cccnt
            waits.append(("cc", self.cccnt))
        self.q[eng].append((waits, None, None))

    def replay(self):
        nc = self.nc
        with nc.Block() as block:
            engmap = {"pe": block.tensor, "act": block.scalar, "dve": block.vector,
                      "pool": block.gpsimd, "sp": block.sync}
            for e in ENGS:
                items = self.q[e]
                if not items:
                    continue

                def body(engine, items=items):
                    for waits, fn, inc in items:
                        for s, v in waits:
                            engine.wait_ge(self.semobj[s], v)
                        if fn is None:
                            continue
                        ins = fn(engine)
                        if inc is not None:
                            ins.then_inc(self.semobj[inc[0]], inc[1])
                engmap[e](body)
        self.q = {e: [] for e in ENGS}


class Ctx:
    def __init__(self, nc, stack):
        self.nc = nc
        self.S = Sched(nc, stack)
        self.ps = [stack.enter_context(nc.psum_tensor("ps%d" % i, [128, 512], F32)) for i in range(8)]
        self.psn = 0
        self.uid = 0
        self.fused = False
        self.d = {}

    def bank(self):
        i = self.psn
        self.psn = (self.psn + 1) % 8
        return i

    def din(self, name, shape, dt):
        t = self.nc.dram_tensor(name, list(shape), dt, kind="ExternalInput").ap()
        self.d[name] = t
        return t

    def dout(self, name, shape, dt):
        t = self.nc.dram_tensor(name, list(shape), dt, kind="ExternalOutput").ap()
        self.d[name] = t
        return t

    def dint(self, name, shape, dt):
        t = self.nc.dram_tensor(name, list(shape), dt, kind="Internal").ap()
        self.d[name] = t
        return t


def mm_group(C, out_fn, pairs, bank, reads, extra_writes=()):
    S = C.S
    n = len(pairs)
    key = "ps%d" % bank
    for i, (l, r) in enumerate(pairs):
        S.op("pe", (lambda e, l=l, r=r, i=i: e.matmul(out_fn(), l, r, start=(i == 0), stop=(i == n - 1))),
             reads=reads, writes=[key] + list(extra_writes), sig=(i == n - 1))
    return key


def stage_proj(C, has_prev, do_proj, final=False, tiles=range(NT)):
    nc, S, d = C.nc, C.S, C.d
    with ExitStack() as st:
        def sb(name, shape, dt):
            C.uid += 1
            return st.enter_context(nc.sbuf_tensor("sb%d_%s" % (C.uid, name), list(shape), dt))
        xs = [sb("xs%d" % i, [128, 8, TT], F32) for i in range(2)]
        eps_t = sb("eps_t", [128, 1], F32)
        S.op("dve", lambda e: e.memset(eps_t[:], EPS), writes=["eps"])
        if has_prev:
            ys = [sb("ys%d" % i, [128, 8, TT], BF16) for i in range(2)]
            wo = sb("wo", [128, 8, DM], BF16)
            wstage = sb("wstage", [128, NCOL], F32)
            for k in range(8):
                S.dma("sp", lambda e, k=k: e.dma_start(out=wstage[:, 0:DM], in_=d["wo"][k * 128:(k + 1) * 128, :]),
                      reads=["d_wo"], writes=["wstage"])
                if k % 2 == 0:
                    S.op("dve", lambda e, k=k: e.tensor_copy(out=wo[:, k, :], in_=wstage[:, 0:DM]),
                         reads=["wstage"], writes=["wo"])
                else:
                    S.op("act", lambda e, k=k: e.activation(out=wo[:, k, :], in_=wstage[:, 0:DM], func=AF.Copy),
                         reads=["wstage"], writes=["wo"])
        if do_proj or final:
            xsq = sb("xsq", [128, 8, TT], BF16)
            ones = sb("ones", [128, 128], BF16)
            S.op("dve", lambda e: e.memset(ones[:], 1.0), writes=["ones"])
            rstd1 = sb("rstd1", [128, TT], F32)
            lnt = sb("lnt", [128, TT], F32)
        if final:
            fnw = sb("fnw", [128, 8], F32)
            S.dma("sp", lambda e: e.dma_start(out=fnw[:], in_=d["fnw"][:, :]), reads=["d_fnw"], writes=["fnw"])
            outs = sb("outs", [128, 8, TT], F32)
        if do_proj:
            if not has_prev:
                wstage = sb("wstage", [128, NCOL], F32)
            xb = sb("xb", [128, 8, TT], BF16)
            win = sb("win", [128, 8, NCOL], BF16)
            nw = sb("nw", [128, 8], F32)
            S.dma("sp", lambda e: e.dma_start(out=nw[:], in_=d["nw"][:, :]), reads=["d_nw"], writes=["nw"])
            for k in range(8):
                S.dma("sp", lambda e, k=k: e.dma_start(out=wstage[:], in_=d["w_in"][k * 128:(k + 1) * 128, :]),
                      reads=["d_w_in"], writes=["wstage"])
                S.op("dve", lambda e, k=k: e.tensor_scalar(out=win[:, k, :], in0=wstage[:], scalar1=nw[:, k:k + 1],
                                                           scalar2=None, op0=ALU.mult),
                     reads=["wstage", "nw"], writes=["win"])
            wuq = sb("wuq", [128, 2, 192], BF16)
            qnw = sb("qnw", [128, 2], F32)
            S.dma("sp", lambda e: e.dma_start(out=qnw[:], in_=d["qnw"][:, :]), reads=["d_qnw"], writes=["qnw"])
            for k in range(2):
                S.dma("sp", lambda e, k=k: e.dma_start(out=wstage[:, 0:192], in_=d["wuq"][k * 128:(k + 1) * 128, :]),
                      reads=["d_wuq"], writes=["wstage"])
                S.op("dve", lambda e, k=k: e.tensor_scalar(out=wuq[:, k, :], in0=wstage[:, 0:192], scalar1=qnw[:, k:k + 1],
                                                           scalar2=None, op0=ALU.mult),
                     reads=["wstage", "qnw"], writes=["wuq"])
            wukv = sb("wukv", [128, 128], BF16)
            kvnw = sb("kvnw", [128, 1], F32)
            S.dma("sp", lambda e: e.dma_start(out=kvnw[:], in_=d["kvnw"][:, :]), reads=["d_kvnw"], writes=["kvnw"])
            S.dma("sp", lambda e: e.dma_start(out=wstage[:, 0:128], in_=d["wukv"][:, :]), reads=["d_wukv"], writes=["wstage"])
            S.op("dve", lambda e: e.tensor_scalar(out=wukv[:], in0=wstage[:, 0:128], scalar1=kvnw[:, 0:1],
                                                  scalar2=None, op0=ALU.mult),
                 reads=["wstage", "kvnw"], writes=["wukv"])
            identf = sb("identf", [128, 128], F32)
            identb = sb("identb", [128, 128], BF16)
            S.dma("sp", lambda e: e.dma_start(out=identf[:], in_=d["ident"][:, :]), reads=["d_ident"], writes=["identf"])
            S.op("dve", lambda e: e.tensor_copy(out=identb[:], in_=identf[:]), reads=["identf"], writes=["identb"])
            ctab = sb("ctab", [128, TT], F32)
            stab = sb("stab", [128, TT], F32)
            cqf = sb("cqf", [128, 2, TT], F32)
            cqsq = sb("cqsq", [128, 2, TT], BF16)
            cqb = sb("cqb", [128, 2, TT], BF16)
            rstd2 = sb("rstd2", [128, TT], F32)
            rstd3 = sb("rstd3", [128, TT], F32)
            ckvf = sb("ckvf", [128, TT], F32)
            ckvsq = sb("ckvsq", [128, TT], BF16)
            ckvb = sb("ckvb", [128, TT], BF16)
            QTs = sb("QTs", [96, TT], BF16)
            KTs = sb("KTs", [96, TT], BF16)
            t1 = sb("t1", [128, TT], F32)
            t2 = sb("t2", [128, TT], F32)
            t3 = sb("t3", [128, TT], F32)
            t4 = sb("t4", [128, TT], F32)
            vT = sb("vT", [64, TT], BF16)
            Vs = sb("Vs", [128, 4, 64], BF16)
            mvT = sb("mvT", [64, TT], BF16)
            mVs = sb("mVs", [128, 4, 64], BF16)
            mqf = sb("mqf", [64, TT], F32)
            mqb = sb("mqb", [64, TT], BF16)
            mkf = sb("mkf", [64, TT], F32)
            mkb = sb("mkb", [64, TT], BF16)
            km = sb("km", [64, 32], F32)
            gf = sb("gf", [128, TT], F32)
            gs = sb("gs", [128, TT], BF16)
            gq3 = [sb("gq3_%d" % i, [128, TT], F32) for i in range(3)]
            zf = sb("zf", [128, TT], F32)
            zs = sb("zs", [128, TT], BF16)
            bas = sb("bas", [2, TT], F32)

        def rstd_from(bank, n, out_t, rows=128):
            key = "ps%d" % bank
            S.op("act", lambda e: e.activation(out=lnt[0:rows, :], in_=C.ps[bank][0:rows, :], func=AF.Ln,
                                               bias=eps_t[0:rows, 0:1], scale=1.0 / n),
                 reads=["eps"], writes=[key, "lnt"])
            S.op("act", lambda e: e.activation(out=out_t[0:rows, :], in_=lnt[0:rows, :], func=AF.Exp, scale=-0.5),
                 reads=["lnt"], writes=[out_t.name if hasattr(out_t, "name") else "rstd"])

        def do_tile(ti):
            b = ti % 2
            c0 = ti * TT
            xk = "xs%d" % b
            xsb = xs[b]
            for k4 in range(2):
                S.dma("sp", lambda e, k4=k4, xsb=xsb: e.dma_start(
                    out=xsb[:, 4 * k4:4 * k4 + 4, :],
                    in_=d["xT"][4 * k4 * 128:(4 * k4 + 4) * 128, c0:c0 + TT].rearrange("(k p) t -> p k t", p=128)),
                    reads=["d_xT"], writes=[xk])
            if has_prev:
                yk = "ys%d" % b
                ysb = ys[b]
                S.dma("sp", lambda e, ysb=ysb: e.dma_start(
                    out=ysb[:], in_=(d["yg%d" % (c0 // 2048)][:, c0 % 2048:c0 % 2048 + TT] if C.fused else d["yT"][:, c0:c0 + TT]
                                      ).rearrange("(k p) t -> p k t", p=128)),
                    reads=[("d_yg%d" % (c0 // 2048)) if C.fused else "d_yT"], writes=[yk])
                for c in range(8):
                    bk = C.bank()
                    key = mm_group(C, (lambda bk=bk: C.ps[bk][:, :]),
                                   [(wo[:, k, c * 128:(c + 1) * 128], ysb[:, k, :]) for k in range(8)],
                                   bk, reads=["wo", yk])
                    S.op("dve", lambda e, c=c, bk=bk, xsb=xsb: e.tensor_tensor(out=xsb[:, c, :], in0=C.ps[bk][:, :], in1=xsb[:, c, :], op=ALU.add),
                         reads=[], writes=[key, xk])
                if not final:
                    for k4 in range(2):
                        S.dma("sp", lambda e, k4=k4, xsb=xsb: e.dma_start(
                            out=d["xTo"][4 * k4 * 128:(4 * k4 + 4) * 128, c0:c0 + TT].rearrange("(k p) t -> p k t", p=128),
                            in_=xsb[:, 4 * k4:4 * k4 + 4, :]),
                            reads=[xk], writes=["d_xTo"])
            if not (do_proj or final):
                return
            S.op("act", lambda e, xsb=xsb: e.activation(out=xsq[:], in_=xsb[:], func=AF.Square), reads=[xk], writes=["xsq"])
            bk = C.bank()
            key = mm_group(C, (lambda bk=bk: C.ps[bk][:, :]), [(ones[:], xsq[:, k, :]) for k in range(8)], bk,
                           reads=["ones", "xsq"])
            S.op("act", lambda e, bk=bk: e.activation(out=lnt[:], in_=C.ps[bk][:, :], func=AF.Ln, bias=eps_t[:, 0:1], scale=1.0 / DM),
                 reads=["eps"], writes=[key, "lnt"])
            S.op("act", lambda e: e.activation(out=rstd1[:], in_=lnt[:], func=AF.Exp, scale=-0.5), reads=["lnt"], writes=["rstd1"])
            if final:
                for k in range(8):
                    S.op("dve", lambda e, k=k, xsb=xsb: e.scalar_tensor_tensor(
                        out=outs[:, k, :], in0=xsb[:, k, :], scalar=fnw[:, k:k + 1], in1=rstd1[:], op0=ALU.mult, op1=ALU.mult),
                        reads=[xk, "fnw", "rstd1"], writes=["outs"])
                for k4 in range(2):
                    S.dma("sp", lambda e, k4=k4: e.dma_start(
                        out=d["outT"][4 * k4 * 128:(4 * k4 + 4) * 128, c0:c0 + TT].rearrange("(k p) t -> p k t", p=128),
                        in_=outs[:, 4 * k4:4 * k4 + 4, :]),
                        reads=["outs"], writes=["d_outT"])
                return
            S.op("dve", lambda e, xsb=xsb: e.tensor_copy(out=xb[:], in_=xsb[:]), reads=[xk], writes=["xb"])
            S.dma("sp", lambda e: e.dma_start(out=ctab[64:96, :], in_=d["ctab"][:, c0:c0 + TT]), reads=["d_ctab"], writes=["ctab"])
            S.dma("sp", lambda e: e.dma_start(out=stab[64:96, :], in_=d["stab"][:, c0:c0 + TT]), reads=["d_stab"], writes=["stab"])

            def proj_tile(t):
                bk = C.bank()
                w = TW[t]
                key = mm_group(C, (lambda bk=bk, w=w: C.ps[bk][0:w, :]),
                               [(win[:, k, TOFF[t]:TOFF[t] + w], xb[:, k, :]) for k in range(8)], bk,
                               reads=["win", "xb"])
                return bk, key

            for j in range(2):
                bk, key = proj_tile(j)
                S.op("dve", lambda e, bk=bk, j=j: e.tensor_tensor(out=cqf[:, j, :], in0=C.ps[bk][:, :], in1=rstd1[:], op=ALU.mult),
                     reads=["rstd1"], writes=[key, "cqf"])
            S.op("act", lambda e: e.activation(out=cqsq[:], in_=cqf[:], func=AF.Square), reads=["cqf"], writes=["cqsq"])
            S.op("dve", lambda e: e.tensor_copy(out=cqb[:], in_=cqf[:]), reads=["cqf"], writes=["cqb"])
            bk = C.bank()
            key = mm_group(C, (lambda bk=bk: C.ps[bk][:, :]), [(ones[:], cqsq[:, j, :]) for j in range(2)], bk, reads=["ones", "cqsq"])
            S.op("act", lambda e, bk=bk: e.activation(out=lnt[:], in_=C.ps[bk][:, :], func=AF.Ln, bias=eps_t[:, 0:1], scale=1.0 / 256),
                 reads=["eps"], writes=[key, "lnt"])
            S.op("act", lambda e: e.activation(out=rstd2[:], in_=lnt[:], func=AF.Exp, scale=-0.5), reads=["lnt"], writes=["rstd2"])
            bq = C.bank()
            keyq = mm_group(C, (lambda bq=bq: C.ps[bq][0:96, :]), [(wuq[:, j, 0:96], cqb[:, j, :]) for j in range(2)], bq, reads=["wuq", "cqb"])
            br = C.bank()
            keyr = mm_group(C, (lambda br=br: C.ps[br][0:96, :]), [(wuq[:, j, 96:192], cqb[:, j, :]) for j in range(2)], br, reads=["wuq", "cqb"])
            S.op("dve", lambda e, bq=bq: e.tensor_tensor(out=QTs[0:64, :], in0=C.ps[bq][0:64, :], in1=rstd2[0:64, :], op=ALU.mult),
                 reads=["rstd2"], writes=[keyq, "QTs"])
            S.op("dve", lambda e, bq=bq: e.tensor_tensor(out=t1[64:96, :], in0=C.ps[bq][64:96, :], in1=ctab[64:96, :], op=ALU.mult),
                 reads=["ctab"], writes=[keyq, "t1"])
            S.op("dve", lambda e, br=br: e.tensor_tensor(out=t2[64:96, :], in0=C.ps[br][64:96, :], in1=stab[64:96, :], op=ALU.mult),
                 reads=["stab"], writes=[keyr, "t2"])
            S.op("dve", lambda e: e.tensor_tensor(out=t1[64:96, :], in0=t1[64:96, :], in1=t2[64:96, :], op=ALU.add),
                 reads=["t2", "t1"], writes=["t1"])
            S.op("dve", lambda e: e.tensor_tensor(out=QTs[64:96, :], in0=t1[64:96, :], in1=rstd2[64:96, :], op=ALU.mult),
                 reads=["t1", "rstd2"], writes=["QTs"])
            S.dma("sp", lambda e: e.dma_start(out=d["QT_mla"][:, c0:c0 + TT], in_=QTs[:]), reads=["QTs"], writes=["d_QT_mla"])
            bk, key = proj_tile(2)
            S.op("dve", lambda e, bk=bk: e.tensor_tensor(out=ckvf[:], in0=C.ps[bk][:, :], in1=rstd1[:], op=ALU.mult),
                 reads=["rstd1"], writes=[key, "ckvf"])
            S.op("act", lambda e: e.activation(out=ckvsq[:], in_=ckvf[:], func=AF.Square), reads=["ckvf"], writes=["ckvsq"])
            S.op("act", lambda e: e.activation(out=ckvb[:], in_=ckvf[:], func=AF.Copy), reads=["ckvf"], writes=["ckvb"])
            bk = C.bank()
            key = mm_group(C, (lambda bk=bk: C.ps[bk][:, :]), [(ones[:], ckvsq[:])], bk, reads=["ones", "ckvsq"])
            S.op("act", lambda e, bk=bk: e.activation(out=lnt[:], in_=C.ps[bk][:, :], func=AF.Ln, bias=eps_t[:, 0:1], scale=1.0 / 128),
                 reads=["eps"], writes=[key, "lnt"])
            S.op("act", lambda e: e.activation(out=rstd3[:], in_=lnt[:], func=AF.Exp, scale=-0.5), reads=["lnt"], writes=["rstd3"])
            bk = C.bank()
            key = mm_group(C, (lambda bk=bk: C.ps[bk][0:64, :]), [(wukv[:, 0:64], ckvb[:])], bk, reads=["wukv", "ckvb"])
            S.op("dve", lambda e, bk=bk: e.tensor_tensor(out=KTs[0:64, :], in0=C.ps[bk][0:64, :], in1=rstd3[0:64, :], op=ALU.mult),
                 reads=["rstd3"], writes=[key, "KTs"])
            bk = C.bank()
            key = mm_group(C, (lambda bk=bk: C.ps[bk][0:64, :]), [(wukv[:, 64:128], ckvb[:])], bk, reads=["wukv", "ckvb"])
            S.op("dve", lambda e, bk=bk: e.tensor_tensor(out=vT[:], in0=C.ps[bk][0:64, :], in1=rstd3[0:64, :], op=ALU.mult),
                 reads=["rstd3"], writes=[key, "vT"])

            def transpose_v(src, srck, dst, dstk, dname):
                bk = C.bank()
                key = "ps%d" % bk
                for j in range(4):
                    S.op("pe", lambda e, j=j, bk=bk: e.transpose(
                        C.ps[bk][:, :].bitcast(BF16)[:, j * 64:(j + 1) * 64], src[0:64, j * 128:(j + 1) * 128], identb[0:64, 0:64]),
                        reads=[srck, "identb"], writes=[key], sig=(j == 3))
                S.op("act", lambda e, bk=bk: e.activation(out=dst[:].rearrange("p a b -> p (a b)"),
                                                          in_=C.ps[bk][:, :].bitcast(BF16)[:, 0:256], func=AF.Copy),
                     reads=[], writes=[key, dstk])
                S.dma("sp", lambda e: e.dma_start(
                    out=d[dname][c0:c0 + TT, :].rearrange("(a p) v -> p a v", p=128), in_=dst[:]),
                    reads=[dstk], writes=["d_" + dname])
            transpose_v(vT, "vT", Vs, "Vs", "V_mla")
            bk, key = proj_tile(8)
            S.op("dve", lambda e, bk=bk: e.tensor_tensor(out=mqf[:], in0=C.ps[bk][0:64, :], in1=rstd1[0:64, :], op=ALU.mult),
                 reads=["rstd1"], writes=[key, "mqf"])
            S.op("dve", lambda e, bk=bk: e.tensor_tensor(out=t3[64:96, :], in0=C.ps[bk][64:96, :], in1=ctab[64:96, :], op=ALU.mult),
                 reads=["ctab"], writes=[key, "t3"])
            S.op("act", lambda e: e.activation(out=mqb[:], in_=mqf[:], func=AF.Copy, scale=0.125),
                 reads=["mqf"], writes=["mqb"])
            S.dma("sp", lambda e: e.dma_start(out=d["QTf_moba"][:, c0:c0 + TT], in_=mqf[:]), reads=["mqf"], writes=["d_QTf_moba"])
            S.dma("sp", lambda e: e.dma_start(out=d["QT_moba"][:, c0:c0 + TT], in_=mqb[:]), reads=["mqb"], writes=["d_QT_moba"])
            bk, key = proj_tile(9)
            S.op("dve", lambda e, bk=bk: e.tensor_tensor(out=mkf[:], in0=C.ps[bk][0:64, :], in1=rstd1[0:64, :], op=ALU.mult),
                 reads=["rstd1"], writes=[key, "mkf"])
            S.op("dve", lambda e, bk=bk: e.tensor_tensor(out=t4[64:96, :], in0=C.ps[bk][64:96, :], in1=stab[64:96, :], op=ALU.mult),
                 reads=["stab"], writes=[key, "t4"])
            S.op("act", lambda e: e.activation(out=mkb[:], in_=mkf[:], func=AF.Copy), reads=["mkf"], writes=["mkb"])
            S.op("dve", lambda e: e.tensor_reduce(out=km[:, 2 * ti:2 * ti + 2], in_=mkf[:].rearrange("p (a b) -> p a b", b=256),
                                                  op=ALU.add, axis=AX.X),
                 reads=["mkf"], writes=["km"])
            S.dma("sp", lambda e: e.dma_start(out=d["KT_moba"][:, c0:c0 + TT], in_=mkb[:]), reads=["mkb"], writes=["d_KT_moba"])
            S.op("dve", lambda e: e.tensor_tensor(out=t3[64:96, :], in0=t3[64:96, :], in1=t4[64:96, :], op=ALU.add),
                 reads=["t3", "t4"], writes=["t3"])
            S.op("dve", lambda e: e.tensor_tensor(out=KTs[64:96, :], in0=t3[64:96, :], in1=rstd1[64:96, :], op=ALU.mult),
                 reads=["t3", "rstd1"], writes=["KTs"])
            S.dma("sp", lambda e: e.dma_start(out=d["KT_mla"][:, c0:c0 + TT], in_=KTs[:]), reads=["KTs"], writes=["d_KT_mla"])
            bk, key = proj_tile(10)
            S.op("dve", lambda e, bk=bk: e.tensor_tensor(out=mvT[:], in0=C.ps[bk][0:64, :], in1=rstd1[0:64, :], op=ALU.mult),
                 reads=["rstd1"], writes=[key, "mvT"])
            transpose_v(mvT, "mvT", mVs, "mVs", "V_moba")
            bk, key = proj_tile(3)
            S.op("dve", lambda e, bk=bk: e.tensor_tensor(out=gf[:], in0=C.ps[bk][:, :], in1=rstd1[:], op=ALU.mult),
                 reads=["rstd1"], writes=[key, "gf"])
            S.op("act", lambda e: e.activation(out=gs[:], in_=gf[:], func=AF.Silu), reads=["gf"], writes=["gs"])
            S.dma("sp", lambda e: e.dma_start(out=d["GT_mla"][:, c0:c0 + TT], in_=gs[0:64, :]), reads=["gs"], writes=["d_GT_mla"])
            S.dma("sp", lambda e: e.dma_start(out=d["GT_moba"][:, c0:c0 + TT], in_=gs[64:128, :]), reads=["gs"], writes=["d_GT_moba"])
            for j, nm in enumerate(("gqT", "gkT", "gvT")):
                bk, key = proj_tile(4 + j)
                S.op("dve", lambda e, bk=bk, j=j: e.tensor_tensor(out=gq3[j][:], in0=C.ps[bk][:, :], in1=rstd1[:], op=ALU.mult),
                     reads=["rstd1"], writes=[key, "gq3_%d" % j])
                S.dma("sp", lambda e, j=j, nm=nm: e.dma_start(out=d[nm][:, c0:c0 + TT], in_=gq3[j][:]),
                      reads=["gq3_%d" % j], writes=["d_" + nm])
            bk, key = proj_tile(7)
            S.op("dve", lambda e, bk=bk: e.tensor_tensor(out=zf[:], in0=C.ps[bk][:, :], in1=rstd1[:], op=ALU.mult),
                 reads=["rstd1"], writes=[key, "zf"])
            S.op("act", lambda e: e.activation(out=zs[:], in_=zf[:], func=AF.Silu), reads=["zf"], writes=["zs"])
            S.dma("sp", lambda e: e.dma_start(out=d["gzT"][:, c0:c0 + TT], in_=zs[:]), reads=["zs"], writes=["d_gzT"])
            bk, key = proj_tile(11)
            S.op("dve", lambda e, bk=bk: e.tensor_tensor(out=bas[:], in0=C.ps[bk][0:2, :], in1=rstd1[0:2, :], op=ALU.mult),
                 reads=["rstd1"], writes=[key, "bas"])
            S.dma("sp", lambda e: e.dma_start(out=d["baT"][:, c0:c0 + TT], in_=bas[:]), reads=["bas"], writes=["d_baT"])
        for ti in tiles:
            do_tile(ti)
        if do_proj and not final:
            S.op("dve", lambda e: e.tensor_scalar(out=km[:], in0=km[:], scalar1=1.0 / 256, scalar2=None, op0=ALU.mult),
                 reads=["km"], writes=["km"])
            S.dma("sp", lambda e: e.dma_start(out=d["kmT"][:, :], in_=km[:]), reads=["km"], writes=["d_kmT"])
        S.drain_dmas("sp")
        S.replay()


def rope_consts():
    inv = (1.0 / (10000.0 ** (np.arange(0, 32, 2, dtype=np.float32) / 32))).astype(np.float32)
    ang = (np.arange(SEQ, dtype=np.float32)[:, None] * inv[None, :]).astype(np.float32)
    cos = np.cos(ang).astype(np.float32).T
    sin = np.sin(ang).astype(np.float32).T
    ctab = np.ascontiguousarray(np.concatenate([cos, cos], 0))
    stab = np.ascontiguousarray(np.concatenate([-sin, sin], 0))
    return ctab, stab


def in_cols(h):
    cq0, ckv0, kr0, mg0, gq0, gk0, gv0, gz0, gb0, ga0, mq0, mk0, mv0, cg0 = (
        0, 256, 384, 416, 672, 1184, 1696, 2208, 2720, 2724, 2728, 2984, 3240, 3496)
    r = lambda a, n: list(range(a, a + n))
    cols = []
    cols += r(cq0, 256) + r(ckv0, 128)
    cols += r(mg0 + 64 * h, 64) + r(cg0 + 64 * h, 64)
    cols += r(gq0 + 128 * h, 128) + r(gk0 + 128 * h, 128) + r(gv0 + 128 * h, 128) + r(gz0 + 128 * h, 128)
    cols += r(mq0 + 64 * h, 64) + r(kr0, 32)
    cols += r(mk0 + 64 * h, 64) + r(kr0 + 16, 16) + r(kr0, 16)
    cols += r(mv0 + 64 * h, 64)
    cols += [gb0 + h, ga0 + h]
    assert len(cols) == NCOL
    return cols


def prep_layer(I, l, h):
    f = np.float32
    out = {}
    out["w_in"] = np.ascontiguousarray(I["w_in"][l][:, in_cols(h)]).astype(f)
    out["nw"] = np.ascontiguousarray(I["norm_w"][l].reshape(8, 128).T).astype(f)
    wq = I["mla_w_uq"][l][:, 96 * h:96 * h + 96]
    wuq = np.zeros((256, 192), f)
    wuq[:, 0:96] = wq
    wuq[:, 160:176] = wq[:, 80:96]
    wuq[:, 176:192] = wq[:, 64:80]
    out["wuq"] = wuq
    out["qnw"] = np.ascontiguousarray(I["mla_q_norm"][l].reshape(2, 128).T).astype(f)
    out["wukv"] = np.ascontiguousarray(I["mla_w_ukv"][l][:, 128 * h:128 * h + 128]).astype(f)
    out["kvnw"] = np.ascontiguousarray(I["mla_kv_norm"][l].reshape(128, 1)).astype(f)
    return out


PROJ_OUTS = [("QT_mla", (96, SEQ), BF16), ("KT_mla", (96, SEQ), BF16), ("V_mla", (SEQ, 64), BF16),
             ("QTf_moba", (64, SEQ), F32), ("QT_moba", (64, SEQ), BF16), ("KT_moba", (64, SEQ), BF16),
             ("V_moba", (SEQ, 64), BF16), ("kmT", (64, 32), F32),
             ("GT_mla", (64, SEQ), BF16), ("GT_moba", (64, SEQ), BF16),
             ("gqT", (128, SEQ), F32), ("gkT", (128, SEQ), F32), ("gvT", (128, SEQ), F32),
             ("gzT", (128, SEQ), BF16), ("baT", (2, SEQ), F32)]
PROJ_INS = [("w_in", (DM, NCOL)), ("nw", (128, 8)), ("wuq", (256, 192)), ("qnw", (128, 2)),
            ("wukv", (128, 128)), ("kvnw", (128, 1))]


def t5_consts():
    e = np.arange(3072)
    dd = e - 511
    n = np.maximum(dd, 0)
    nf = np.maximum(n, 1).astype(np.float32)
    large = 16 + (np.log(nf / np.float32(16)) / np.float32(math.log(2048 / 16)) * np.float32(16)).astype(np.int32)
    large = np.minimum(large, 31)
    bucket = np.where(n < 16, n, large)
    OH = np.zeros((32, 3072), np.float32)
    OH[bucket, e] = 1.0
    OH[:, dd < 0] = 0.0
    return OH


def moba_consts():
    pen = np.zeros((32, 32), np.float32)
    for own in range(32):
        pen[own, own] = 1e30
        pen[own, own + 1:] = -1e30
    E = np.zeros((32, SEQ), np.float32)
    for j in range(32):
        E[j, j * 256:(j + 1) * 256] = 1.0
    return pen, E


def stage_attn(C, moba, qtiles=range(NT)):
    nc, S, d = C.nc, C.S, C.d
    pre = "moba" if moba else "mla"
    KD = 96
    scale = 1.0 if moba else 96 ** -0.5
    rowbase = 64 if moba else 0
    with ExitStack() as st:
        def sb(name, shape, dt):
            C.uid += 1
            return st.enter_context(nc.sbuf_tensor("sb%d_%s" % (C.uid, name), list(shape), dt))
        QT = sb("QT", [96, SEQ], BF16)
        KT = sb("KT", [96, SEQ], BF16)
        Va = sb("Va", [128, 64, 65], BF16)
        GT = sb("GT", [64, SEQ], BF16)
        Pb = [sb("P%d" % i, [128, TT], BF16) for i in range(8)]
        osb = sb("osb", [65, TT], F32)
        rec = sb("rec", [64, TT], F32)
        otmp = sb("otmp", [64, TT], F32)
        ysb = sb("ysb", [64, TT], BF16)
        if C.fused:
            ym = [sb("ym%d" % j, [64, TT], BF16) for j in range(4)]
            hm = sb("hm", [128, 4], F32)
            S.dma("sp", lambda e: e.dma_start(out=hm[:], in_=d["hm"][:, :]), reads=["d_hm"], writes=["hm"])
        sel = sb("sel", [65, 64], F32)
        nrows = 64 if moba else 96
        for q4 in range(4):
            cs = slice(q4 * 2048, (q4 + 1) * 2048)
            S.dma("sp", lambda e, cs=cs: e.dma_start(out=QT[0:nrows, cs], in_=d["QT_" + pre][:, cs]), reads=["d_QT_" + pre], writes=["QT"])
            S.dma("sp", lambda e, cs=cs: e.dma_start(out=KT[0:nrows, cs], in_=d["KT_" + pre][:, cs]), reads=["d_KT_" + pre], writes=["KT"])
            S.dma("sp", lambda e, cs=cs: e.dma_start(out=GT[:, cs], in_=d["GT_" + pre][:, cs]), reads=["d_GT_" + pre], writes=["GT"])
        for a4 in range(16):
            S.dma("sp", lambda e, a4=a4: e.dma_start(
                out=Va[:, 4 * a4:4 * a4 + 4, 0:64],
                in_=d["V_" + pre][a4 * 512:(a4 + 1) * 512, :].rearrange("(a p) v -> p a v", p=128)),
                reads=["d_V_" + pre], writes=["Va"])
        S.op("dve", lambda e: e.memset(Va[:, :, 64:65], 1.0), writes=["Va"])
        S.dma("sp", lambda e: e.dma_start(out=sel[:], in_=d["sel"][:, :]), reads=["d_sel"], writes=["sel"])
        if not moba:
            trif = sb("trif", [128, 128], F32)
            tri = sb("tri", [128, 128], BF16)
            S.dma("sp", lambda e: e.dma_start(out=trif[:], in_=d["tri"][:, :]), reads=["d_tri"], writes=["trif"])
            S.op("dve", lambda e: e.tensor_copy(out=tri[:], in_=trif[:]), reads=["trif"], writes=["tri"])
        else:
            t5c = sb("t5c", [32, 1], F32)
            et = sb("et", [32, 1], F32)
            b31 = sb("b31", [128, 1], F32)
            OH = sb("OH", [32, 3072], F32)
            fvs = sb("fvs", [1, 3072], F32)
            ES32 = sb("ES32", [128, 2560], F32)
            ES = sb("ES", [128, 2560], BF16)
            S.dma("sp", lambda e: e.dma_start(out=t5c[:], in_=d["t5h"][:, :]), reads=["d_t5h"], writes=["t5c"])
            S.dma("sp", lambda e: e.dma_start(out=b31[:], in_=d["t5h"][31:32, :].partition_broadcast(128).rearrange("p a b -> p (a b)")),
                  reads=["d_t5h"], writes=["b31"])
            S.dma("sp", lambda e: e.dma_start(out=OH[:], in_=d["OH"][:, :]), reads=["d_OH"], writes=["OH"])
            S.op("act", lambda e: e.activation(out=et[:], in_=t5c[:], func=AF.Exp), reads=["t5c"], writes=["et"])
            for j in range(6):
                key = mm_group(C, (lambda: C.ps[5][0:1, :]), [(et[:, 0:1], OH[:, j * 512:(j + 1) * 512])], 5, reads=["et", "OH"])
                S.op("dve", lambda e, j=j: e.tensor_copy(out=fvs[:, j * 512:(j + 1) * 512], in_=C.ps[5][0:1, :]), reads=[], writes=[key, "fvs"])
            S.dma("sp", lambda e: e.dma_start(out=d["fv"][:, :], in_=fvs[:]), reads=["fvs"], writes=["d_fv"])
            for ki in range(128):
                S.dma("sp", lambda e, ki=ki: e.dma_start(
                    out=ES32[ki:ki + 1, :], in_=d["fv"][:, 127 - ki:127 - ki + 2560]), reads=["d_fv"], writes=["ES32_%d" % ki])
            S.op("dve", lambda e: e.tensor_copy(out=ES[:], in_=ES32[:]), reads=["ES32_%d" % ki for ki in range(128)], writes=["ES"])
            for q4 in range(4):
                cs = slice(q4 * 2048, (q4 + 1) * 2048)
                S.dma("sp", lambda e, cs=cs: e.dma_start(out=KT[64:96, cs], in_=d["Eb"][:, cs]), reads=["d_Eb"], writes=["KT"])
            QTf = sb("QTf", [64, SEQ], F32)
            kmT = sb("kmT", [64, 32], F32)
            pen = sb("pen", [128, 32 * 32], F32)
            identf = sb("identf", [128, 128], F32)
            identb = sb("identb", [128, 128], BF16)
            S.dma("sp", lambda e: e.dma_start(out=identf[:], in_=d["ident"][:, :]), reads=["d_ident"], writes=["identf"])
            S.op("dve", lambda e: e.tensor_copy(out=identb[:], in_=identf[:]), reads=["identf"], writes=["identb"])
            for q4 in range(4):
                cs = slice(q4 * 2048, (q4 + 1) * 2048)
                S.dma("sp", lambda e, cs=cs: e.dma_start(out=QTf[:, cs], in_=d["QTf_moba"][:, cs]), reads=["d_QTf_moba"], writes=["QTf"])
            S.dma("sp", lambda e: e.dma_start(out=kmT[:], in_=d["kmT"][:, :]), reads=["d_kmT"], writes=["kmT"])
            S.dma("sp", lambda e: e.dma_start(out=pen[:], in_=d["pen"][:, :].rearrange("a b -> (a b)").partition_broadcast(128)),
                  reads=["d_pen"], writes=["pen"])
            gm = [sb("gm%d" % i, [128, 32], F32) for i in range(4)]
            m8 = [sb("m8%d" % i, [128, 8], F32) for i in range(4)]
            thr = [sb("thr%d" % i, [128, 1], F32) for i in range(4)]
            Mq = [sb("Mq%d" % i, [128, 96], BF16) for i in range(4)]
            for i in range(4):
                S.op("dve", lambda e, i=i: e.memset(Mq[i][:], 0.0), writes=["Mq%d" % i])
            def gate_group(g):
                keys = []
                for j in range(4):
                    i = g * 4 + j
                    keys.append(mm_group(C, (lambda j=j: C.ps[6][:, j * 32:(j + 1) * 32]), [(QTf[:, i * 128:(i + 1) * 128], kmT[:, :])], 6,
                                         reads=["QTf", "kmT"]))
                for j in range(4):
                    own = (g * 4 + j) // 2
                    S.op("dve", lambda e, j=j, own=own: e.tensor_tensor(out=gm[j][:], in0=C.ps[6][:, j * 32:(j + 1) * 32],
                                                                      in1=pen[:, own * 32:(own + 1) * 32], op=ALU.add),
                         reads=["pen"], writes=[keys[j], "gm%d" % j])
                for j in range(4):
                    S.op("dve", lambda e, j=j: e.max(out=m8[j][:], in_=gm[j][:]), reads=["gm%d" % j], writes=["m8%d" % j])
                for j in range(4):
                    S.op("dve", lambda e, j=j: e.tensor_scalar(out=thr[j][:], in0=m8[j][:, 3:4], scalar1=-1e29, scalar2=None, op0=ALU.max),
                         reads=["m8%d" % j], writes=["thr%d" % j])
                for j in range(4):
                    S.op("dve", lambda e, j=j: e.tensor_scalar(out=Mq[j][:, 64:96], in0=gm[j][:], scalar1=thr[j][:, 0:1], scalar2=-30000.0,
                                                               op0=ALU.is_lt, op1=ALU.mult),
                         reads=["gm%d" % j, "thr%d" % j], writes=["Mq%d" % j])
                for j in range(4):
                    S.op("pe", lambda e, j=j: e.transpose(C.ps[7][0:96, :].bitcast(BF16)[:, j * 128:(j + 1) * 128], Mq[j][:], identb[:]),
                         reads=["Mq%d" % j, "identb"], writes=["ps7"], sig=True)
                S.op("act", lambda e, g=g: e.activation(out=QT[64:96, g * 512:(g + 1) * 512],
                                                        in_=C.ps[7][64:96, :].bitcast(BF16)[:, 0:512], func=AF.Copy),
                     reads=[], writes=["ps7", "QTg%d" % g])

        SB = (0, 1, 2, 3)
        OB = (4, 5)
        NB = 4
        NP = 8
        DEPTHQ = 6
        items = []
        for qi, qt in enumerate(qtiles):
            for kt in range(4 * qt + 4):
                items.append((qi, qt, kt))

        def front(idx):
            qi, qt, kt = items[idx]
            q0 = qt * TT
            k0 = kt * 128
            diag = kt >= 4 * qt
            koff = (kt - 4 * qt) * 128 if diag else 0
            sbk = SB[idx % NB]
            skey = "ps%d" % sbk
            P = Pb[idx % NP]
            pkey = "P%d" % (idx % NP)
            if moba and kt == 0 and qt + 2 < NT:
                gate_group(qt + 2)
            S.op("pe", lambda e: e.matmul(
                C.ps[sbk][:, koff:TT], KT[0:KD, k0:k0 + 128], QT[0:KD, q0 + koff:q0 + TT], start=True, stop=True),
                reads=["KT", "QT"] + (["QTg%d" % qt] if moba else []), writes=[skey])
            far = moba and (q0 - k0 >= 1664)
            if far:
                S.op("act", lambda e: e.activation(
                    out=P[:, koff:TT], in_=C.ps[sbk][:, koff:TT], func=AF.Exp, bias=b31[:, 0:1], scale=scale),
                    reads=["b31"], writes=[skey, pkey])
            else:
                S.op("act", lambda e: e.activation(
                    out=P[:, koff:TT], in_=C.ps[sbk][:, koff:TT], func=AF.Exp, scale=scale),
                    reads=[], writes=[skey, pkey])
                if moba:
                    s0 = q0 - k0 + 384
                    S.op("dve" if idx % 4 != 3 else "pool", lambda e: e.tensor_tensor(
                        out=P[:, koff:TT], in0=P[:, koff:TT], in1=ES[:, s0 + koff:s0 + TT], op=ALU.mult),
                        reads=["ES", pkey], writes=[pkey])
                elif diag:
                    S.op("pool", lambda e: e.tensor_tensor(
                        out=P[:, koff:koff + 128], in0=P[:, koff:koff + 128], in1=tri[:], op=ALU.mult),
                        reads=["tri", pkey], writes=[pkey])

        def back(idx):
            qi, qt, kt = items[idx]
            q0 = qt * TT
            diag = kt >= 4 * qt
            koff = (kt - 4 * qt) * 128 if diag else 0
            nkt = 4 * qt + 4
            ob = OB[qi % 2]
            okey = "ps%d" % ob
            P = Pb[idx % NP]
            pkey = "P%d" % (idx % NP)
            S.op("pe", lambda e: e.matmul(
                C.ps[ob][0:65, koff:TT], Va[:, kt, :], P[:, koff:TT], start=(kt == 0), stop=(kt == nkt - 1)),
                reads=[pkey, "Va"], writes=[okey])
            if kt != nkt - 1:
                return
            S.op("act", lambda e: e.activation(out=osb[:], in_=C.ps[ob][0:65, :], func=AF.Copy), reads=[], writes=[okey, "osb"])
            key = mm_group(C, (lambda: C.ps[6][0:64, :]), [(sel[:], osb[:])], 6, reads=["sel", "osb"])
            S.op("dve", lambda e: e.reciprocal(out=rec[:], in_=C.ps[6][0:64, :]), reads=[], writes=[key, "rec"])
            S.op("dve", lambda e: e.tensor_tensor(out=otmp[:], in0=osb[0:64, :], in1=rec[:], op=ALU.mult), reads=["osb", "rec"], writes=["otmp"])
            if not C.fused:
                S.op("dve", lambda e: e.tensor_tensor(out=ysb[:], in0=otmp[:], in1=GT[:, q0:q0 + TT], op=ALU.mult),
                     reads=["otmp", "GT"], writes=["ysb"])
                S.dma("sp", lambda e: e.dma_start(out=d["yT_h"][rowbase:rowbase + 64, q0:q0 + TT], in_=ysb[:]),
                      reads=["ysb"], writes=["d_yT_h"])
            else:
                qq, qc = q0 // 2048, q0 % 2048
                for j in range(4):
                    S.op("dve", lambda e, j=j: e.scalar_tensor_tensor(out=ym[j][:], in0=otmp[:], scalar=hm[0:64, j:j + 1],
                                                                      in1=GT[:, q0:q0 + TT], op0=ALU.mult, op1=ALU.mult),
                         reads=["otmp", "GT", "hm"], writes=["ym%d" % j])
                    S.dma("sp", lambda e, j=j: e.dma_start(
                        out=d["ypad%d" % qq][256 * j + rowbase:256 * j + rowbase + 64, qc:qc + TT], in_=ym[j][:]),
                        reads=["ym%d" % j], writes=["d_ypad%d" % qq])

        if moba:
            gate_group(0)
            gate_group(1)
        n_it = len(items)
        for idx in range(n_it + DEPTHQ):
            if idx < n_it:
                front(idx)
            if idx - DEPTHQ >= 0:
                back(idx - DEPTHQ)
        S.drain_dmas("sp")
        S.replay()


GC = 128
NCH = SEQ // GC


def gdn_consts():
    i = np.arange(128)
    umask = (i[:, None] <= i[None, :]).astype(np.float32)
    m2 = (i[:, None] > i[None, :]).astype(np.float32)
    neg = np.where(i[:, None] < i[None, :], -30000.0, 0.0).astype(np.float32)
    sl = (i[:, None] > i[None, :]).astype(np.float32)
    return umask, m2, neg, sl


def level_masks():
    i = np.arange(128)
    out = np.zeros((128, 14, 128), np.float32)
    for l in range(7):
        b = 1 << l
        bi = i // b
        m = ((bi[:, None] % 2 == 1) & (bi[None, :] == bi[:, None] - 1)).astype(np.float32)
        out[:, 2 * l, :] = m
        out[:, 2 * l + 1, :] = m.T
    return out.reshape(128, 14 * 128)


def stage_gdn(C, nchunks=NCH, G=4, stop_after=None):
    nc, S, d = C.nc, C.S, C.d
    NSET = 2 * G
    with ExitStack() as st:
        def sb(name, shape, dt):
            C.uid += 1
            return st.enter_context(nc.sbuf_tensor("sb%d_%s" % (C.uid, name), list(shape), dt))
        identf = sb("identf", [128, 128], F32)
        umask = sb("umask", [128, 128], F32)
        m2 = sb("m2", [128, 128], F32)
        neg = sb("neg", [128, 128], F32)
        slm = sb("slm", [128, 128], F32)
        onesf = sb("onesf", [128, 128], F32)
        identb = sb("identb", [128, 128], BF16)
        lvf = sb("lvf", [128, 14 * 128], F32)
        lvm = sb("lvm", [128, 14 * 128], BF16)
        S.dma("sp", lambda e: e.dma_start(out=lvf[:], in_=d["lvlm"][:, :]), reads=["d_lvlm"], writes=["lvf"])
        S.op("dve", lambda e: e.tensor_copy(out=lvm[:], in_=lvf[:]), reads=["lvf"], writes=["lvm"])
        for nm, t in (("ident", identf), ("umask", umask), ("m2", m2), ("neg", neg), ("slm", slm)):
            S.dma("sp", lambda e, nm=nm, t=t: e.dma_start(out=t[:], in_=d[nm][:, :]), reads=["d_" + nm], writes=[nm])
        S.op("dve", lambda e: e.memset(onesf[:], 1.0), writes=["onesf"])
        S.op("dve", lambda e: e.tensor_copy(out=identb[:], in_=identf[:]), reads=["ident"], writes=["identb"])
        eps_t = sb("eps_t", [128, 1], F32)
        S.op("dve", lambda e: e.memset(eps_t[:], EPS), writes=["eps"])
        cw = sb("cw", [128, 12], F32)
        S.dma("sp", lambda e: e.dma_start(out=cw[:], in_=d["cw"][:, :]), reads=["d_cw"], writes=["cw"])
        gsc = sb("gsc", [128, 4], F32)
        S.dma("sp", lambda e: e.dma_start(out=gsc[:, 0:2], in_=d["gsc"][:, :].rearrange("a b -> (a b)").partition_broadcast(128)),
              reads=["d_gsc"], writes=["gsc"])
        gnw = sb("gnw", [128, 1], F32)
        S.dma("sp", lambda e: e.dma_start(out=gnw[:], in_=d["gnw"][:, :]), reads=["d_gnw"], writes=["gnw"])
        S.op("act", lambda e: e.activation(out=gsc[:, 2:3], in_=gsc[:, 0:1], func=AF.Exp), reads=["gsc"], writes=["gsc2"])
        S.op("dve", lambda e: e.tensor_scalar(out=gsc[:, 3:4], in0=gsc[:, 2:3], scalar1=-1.0, scalar2=None, op0=ALU.mult),
             reads=["gsc2"], writes=["gsc3"])
        ba = [sb("ba%d" % i, [2, TT], F32) for i in range(2)]
        batok = sb("batok", [128, NCH, 2], F32)
        for c in range(NCH):
            bi = (c // 4) % 2
            if c % 4 == 0:
                S.dma("sp", lambda e, c=c, bi=bi: e.dma_start(out=ba[bi][:], in_=d["baT"][:, c * 128:c * 128 + TT]),
                      reads=["d_baT"], writes=["ba%d" % bi])
            S.op("pe", lambda e, c=c, bi=bi: e.matmul(C.ps[0][:, 2 * c:2 * c + 2], ba[bi][0:2, (c % 4) * 128:(c % 4 + 1) * 128], identf[0:2, 0:2],
                                                      start=True, stop=True),
                 reads=["ba%d" % bi, "ident"], writes=["ps0"], sig=True)
        S.op("dve", lambda e: e.tensor_copy(out=batok[:].rearrange("p a b -> p (a b)"), in_=C.ps[0][:, 0:2 * NCH]), reads=[], writes=["ps0", "batok"])
        beta = sb("beta", [128, NCH], F32)
        nbeta = sb("nbeta", [128, NCH], F32)
        gg = sb("gg", [128, NCH], F32)
        tmpa = sb("tmpa", [128, NCH], F32)
        gc = sb("gc", [128, NCH], F32)
        gce = sb("gce", [128, NCH], F32)
        egc = sb("egc", [128, NCH], F32)
        bke = sb("bke", [128, NCH], F32)
        eend = sb("eend", [128, NCH], F32)
        gend = sb("gend", [128, NCH], F32)
        S.op("act", lambda e: e.activation(out=beta[:], in_=batok[:, :, 0], func=AF.Sigmoid), reads=["batok"], writes=["beta"])
        S.op("dve", lambda e: e.tensor_scalar(out=nbeta[:], in0=beta[:], scalar1=-1.0, scalar2=None, op0=ALU.mult), reads=["beta"], writes=["nbeta"])
        S.op("act", lambda e: e.activation(out=tmpa[:], in_=batok[:, :, 1], func=AF.Exp, bias=gsc[:, 1:2], scale=1.0), reads=["batok", "gsc"], writes=["tmpa"])
        one_t = sb("one_t", [128, 1], F32)
        S.op("dve", lambda e: e.memset(one_t[:], 1.0), writes=["one_t"])
        S.op("act", lambda e: e.activation(out=tmpa[:], in_=tmpa[:], func=AF.Ln, bias=one_t[:, 0:1], scale=1.0), reads=["tmpa", "one_t"], writes=["tmpa"])
        S.op("dve", lambda e: e.tensor_scalar(out=gg[:], in0=tmpa[:], scalar1=gsc[:, 3:4], scalar2=None, op0=ALU.mult), reads=["tmpa", "gsc3"], writes=["gg"])
        key = mm_group(C, (lambda: C.ps[1][:, 0:NCH]), [(umask[:], gg[:])], 1, reads=["umask", "gg"])
        S.op("dve", lambda e: e.tensor_copy(out=gc[:], in_=C.ps[1][:, 0:NCH]), reads=[], writes=[key, "gc"])
        key = mm_group(C, (lambda: C.ps[2][:, 0:NCH]), [(onesf[:], gg[:])], 2, reads=["onesf", "gg"])
        S.op("dve", lambda e: e.tensor_copy(out=gce[:], in_=C.ps[2][:, 0:NCH]), reads=[], writes=[key, "gce"])
        S.op("act", lambda e: e.activation(out=egc[:], in_=gc[:], func=AF.Exp), reads=["gc"], writes=["egc"])
        S.op("act", lambda e: e.activation(out=gend[:], in_=gce[:], func=AF.Exp), reads=["gce"], writes=["gend"])
        S.op("dve", lambda e: e.tensor_tensor(out=bke[:], in0=beta[:], in1=egc[:], op=ALU.mult), reads=["beta", "egc"], writes=["bke"])
        S.op("dve", lambda e: e.tensor_tensor(out=eend[:], in0=gce[:], in1=gc[:], op=ALU.subtract), reads=["gce", "gc"], writes=["eend"])
        S.op("act", lambda e: e.activation(out=eend[:], in_=eend[:], func=AF.Exp), reads=["eend"], writes=["eend"])
        PERTOK = ["beta", "nbeta", "gg", "egc", "bke", "eend", "gend"]
        if stop_after == 1:
            S.drain_dmas("sp"); S.replay(); return

        QnT = sb("QnT", [128, SEQ], BF16)
        KnT = sb("KnT", [128, SEQ], BF16)
        Kb = sb("Kb", [128, NCH, 128], BF16)
        Kend = sb("Kend", [128, NCH, 128], BF16)
        Vb = sb("Vb", [128, NCH, 128], BF16)
        xin = [[sb("xin%d_%d" % (j, i), [128, 3 + TT], F32) for i in range(2)] for j in range(3)]
        cacc = [sb("cacc%d" % j, [128, TT], F32) for j in range(3)]
        sact = [sb("sact%d" % j, [128, TT], F32) for j in range(3)]
        sq2 = [sb("sq2%d" % j, [128, TT], F32) for j in range(2)]
        lnt = sb("lnt", [128, TT], F32)
        rr = [sb("rr%d" % j, [128, TT], F32) for j in range(2)]
        knf = sb("knf", [128, TT], F32)
        names3 = ("gqT", "gkT", "gvT")
        ntile_a = (nchunks * GC + TT - 1) // TT

        def phase_a_steps(ti):
            c0 = ti * TT
            b = ti % 2
            steps = []

            def pj(j):
                xt = xin[j][b]
                xk = "xin%d_%d" % (j, b)
                if ti == 0:
                    S.op("pool", lambda e, xt=xt: e.memset(xt[:, 0:3], 0.0), writes=[xk])
                    S.dma("sp", lambda e, xt=xt, j=j: e.dma_start(out=xt[:, 3:3 + TT], in_=d[names3[j]][:, 0:TT]),
                          reads=["d_" + names3[j]], writes=[xk])
                else:
                    S.dma("sp", lambda e, xt=xt, j=j: e.dma_start(out=xt[:, :], in_=d[names3[j]][:, c0 - 3:c0 + TT]),
                          reads=["d_" + names3[j]], writes=[xk])
                ck = "cacc%d" % j
                S.op("dve", lambda e, xt=xt, j=j: e.tensor_scalar(out=cacc[j][:], in0=xt[:, 0:TT], scalar1=cw[:, 4 * j:4 * j + 1],
                                                                 scalar2=None, op0=ALU.mult), reads=[xk, "cw"], writes=[ck])
                for tap in range(1, 4):
                    S.op("dve", lambda e, xt=xt, j=j, tap=tap: e.scalar_tensor_tensor(
                        out=cacc[j][:], in0=xt[:, tap:tap + TT], scalar=cw[:, 4 * j + tap:4 * j + tap + 1], in1=cacc[j][:],
                        op0=ALU.mult, op1=ALU.add), reads=[xk, "cw", ck], writes=[ck])
                S.op("act", lambda e, j=j: e.activation(out=sact[j][:], in_=cacc[j][:], func=AF.Silu), reads=[ck], writes=["sact%d" % j])
            for j in range(3):
                steps.append(lambda j=j: pj(j))

            def pn(j):
                S.op("act", lambda e, j=j: e.activation(out=sq2[j][:], in_=sact[j][:], func=AF.Square), reads=["sact%d" % j], writes=["sq2%d" % j])
                bk = C.bank()
                key = mm_group(C, (lambda bk=bk: C.ps[bk][:, :]), [(onesf[:], sq2[j][:])], bk, reads=["onesf", "sq2%d" % j])
                S.op("act", lambda e, bk=bk: e.activation(out=lnt[:], in_=C.ps[bk][:, :], func=AF.Ln, bias=eps_t[:, 0:1], scale=1.0),
                     reads=["eps"], writes=[key, "lnt"])
                S.op("act", lambda e, j=j: e.activation(out=rr[j][:], in_=lnt[:], func=AF.Exp, scale=-0.5), reads=["lnt"], writes=["rr%d" % j])
            for j in range(2):
                steps.append(lambda j=j: pn(j))

            def pq():
                S.op("dve", lambda e: e.scalar_tensor_tensor(out=QnT[:, c0:c0 + TT], in0=sact[0][:], scalar=float(128 ** -0.5), in1=rr[0][:],
                                                             op0=ALU.mult, op1=ALU.mult), reads=["sact0", "rr0"], writes=["QnT%d" % ti])
                S.op("dve", lambda e: e.tensor_tensor(out=knf[:], in0=sact[1][:], in1=rr[1][:], op=ALU.mult), reads=["sact1", "rr1"], writes=["knf"])
                S.op("act", lambda e: e.activation(out=KnT[:, c0:c0 + TT], in_=knf[:], func=AF.Copy), reads=["knf"], writes=["KnT%d" % ti])
            steps.append(pq)

            def pt(a):
                c = ti * 4 + a
                bk = C.bank()
                key = "ps%d" % bk
                S.op("pe", lambda e, bk=bk, a=a: e.transpose(C.ps[bk][:, 0:128], knf[:, a * 128:(a + 1) * 128], identf[:]),
                     reads=["knf", "ident"], writes=[key])
                S.op("pe", lambda e, bk=bk, a=a: e.transpose(C.ps[bk][:, 128:256], sact[2][:, a * 128:(a + 1) * 128], identf[:]),
                     reads=["sact2", "ident"], writes=[key])
                S.op("act", lambda e, bk=bk, c=c: e.activation(out=Kb[:, c, :], in_=C.ps[bk][:, 0:128], func=AF.Copy, scale=bke[:, c:c + 1]),
                     reads=["bke"], writes=[key, "Kb%d" % ti])
                S.op("dve", lambda e, bk=bk, c=c: e.tensor_scalar(out=Kend[:, c, :], in0=C.ps[bk][:, 0:128], scalar1=eend[:, c:c + 1], scalar2=None, op0=ALU.mult),
                     reads=["eend"], writes=[key, "Kend%d" % ti])
                S.op("act", lambda e, bk=bk, c=c: e.activation(out=Vb[:, c, :], in_=C.ps[bk][:, 128:256], func=AF.Copy, scale=beta[:, c:c + 1]),
                     reads=["beta"], writes=[key, "Vb%d" % ti])
            for a in range(4):
                steps.append(lambda a=a: pt(a))
            return steps

        for st_ in phase_a_steps(0):
            st_()
        if stop_after == 2:
            S.drain_dmas("sp"); S.replay(); return

        def bufset(name, dt, n=NSET):
            return [sb("%s%d" % (name, i), [128, 128], dt) for i in range(n)]
        G1 = bufset("G1", F32)
        Dm = bufset("Dm", F32)
        Xf = bufset("Xf", F32)
        Xb_ = bufset("Xb", BF16)
        Yb = bufset("Yb", BF16)
        Mb = bufset("Mb", BF16)
        Xo = bufset("Xo", BF16)
        Yo = bufset("Yo", BF16)
        Hb = bufset("Hb", BF16)
        Gb = bufset("Gb", BF16)
        Am = bufset("Am", F32)
        AT = bufset("AT", BF16)
        TTb = bufset("TTb", BF16)
        WT = bufset("WT", BF16)
        U0 = bufset("U0", F32)
        Ub = bufset("Ub", BF16, 2)
        Sf = sb("Sf", [128, 128], F32)
        Sbb = [sb("Sbb%d" % i, [128, 128], BF16) for i in range(2)]
        otmp = sb("otmp", [128, 128], F32)
        osb = sb("osb", [128, 128], F32)
        osq = sb("osq", [128, 128], F32)
        onr = sb("onr", [128, 128], F32)
        ssq = sb("ssq", [128, 1], F32)
        lno = sb("lno", [128, 1], F32)
        rso = sb("rso", [128, 1], F32)
        gz = [sb("gz%d" % i, [128, TT], BF16) for i in range(2)]
        yst = [sb("yst%d" % i, [128, TT], BF16) for i in range(2)]
        if C.fused:
            ystm = [[sb("ystm%d_%d" % (i, j), [128, TT], BF16) for j in range(4)] for i in range(2)]
            hm = sb("hm", [128, 4], F32)
            gnwm = sb("gnwm", [128, 4], F32)
            S.dma("sp", lambda e: e.dma_start(out=hm[:], in_=d["hm"][:, :]), reads=["d_hm"], writes=["hm"])
            S.op("dve", lambda e: e.tensor_scalar(out=gnwm[:], in0=hm[:], scalar1=gnw[:, 0:1], scalar2=None, op0=ALU.mult),
                 reads=["hm", "gnw"], writes=["gnwm"])
        S.op("dve", lambda e: e.memset(Sf[:], 0.0), writes=["Sf"])
        S.op("dve", lambda e: e.memset(Sbb[0][:], 0.0), writes=["Sbb0"])

        def mm1(out_bank, lhsT, rhs, reads, cols=128):
            return mm_group(C, (lambda: C.ps[out_bank][:, 0:cols]), [(lhsT, rhs)], out_bank, reads=reads)

        def pre_steps(c):
            s = c % NSET
            ck = slice(c * GC, (c + 1) * GC)
            k = lambda nm: "%s%d" % (nm, s)
            steps = []
            st8 = {}

            def s1a():
                S.op("act", lambda e: e.activation(out=G1[s][:], in_=umask[:], func=AF.Copy, scale=gg[:, c:c + 1]),
                     reads=["umask", "gg"], writes=[k("G1")])
            steps.append(s1a)

            def s1b():
                bk = C.bank()
                key = mm_group(C, (lambda: C.ps[bk][:, 0:128]), [(G1[s][:], m2[:]), (identf[:], neg[:])], bk, reads=[k("G1"), "m2", "ident", "neg"])
                S.op("act", lambda e: e.activation(out=Dm[s][:], in_=C.ps[bk][:, 0:128], func=AF.Exp), reads=[], writes=[key, k("Dm")])
            steps.append(s1b)

            def s2():
                bk = C.bank()
                key = mm1(bk, KnT[:, ck], KnT[:, ck], ["KnT%d" % (c // 4)])
                S.op("dve", lambda e: e.scalar_tensor_tensor(out=Xf[s][:], in0=C.ps[bk][:, 0:128], scalar=nbeta[:, c:c + 1], in1=Dm[s][:],
                                                             op0=ALU.mult, op1=ALU.mult), reads=["nbeta", k("Dm")], writes=[key, k("Xf")])
                S.op("dve", lambda e: e.tensor_tensor(out=Xf[s][:], in0=Xf[s][:], in1=slm[:], op=ALU.mult), reads=[k("Xf"), "slm"], writes=[k("Xf")])
                S.op("act", lambda e: e.activation(out=Xb_[s][:], in_=Xf[s][:], func=AF.Copy), reads=[k("Xf")], writes=[k("Xb")])
            steps.append(s2)

            def s3a():
                bk = C.bank()
                key = mm1(bk, QnT[:, ck], KnT[:, ck], ["QnT%d" % (c // 4), "KnT%d" % (c // 4)])
                S.op("dve", lambda e: e.tensor_tensor(out=Am[s][:], in0=C.ps[bk][:, 0:128], in1=Dm[s][:], op=ALU.mult),
                     reads=[k("Dm")], writes=[key, k("Am")])
            steps.append(s3a)

            def s4():
                bk = C.bank()
                key = "ps%d" % bk
                S.op("pe", lambda e: e.transpose(C.ps[bk][:, 0:128], Xf[s][:], identf[:]), reads=[k("Xf"), "ident"], writes=[key])
                S.op("act", lambda e: e.activation(out=Yb[s][:], in_=C.ps[bk][:, 0:128], func=AF.Copy), reads=[], writes=[key, k("Yb")])
                S.op("pool", lambda e: e.tensor_tensor(out=Xo[s][:], in0=Xb_[s][:], in1=lvm[:, 0:128], op=ALU.mult),
                     reads=[k("Xb"), "lvm"], writes=[k("Xo")])
                S.op("dve", lambda e: e.tensor_tensor(out=Mb[s][:], in0=Xo[s][:], in1=identb[:], op=ALU.add),
                     reads=[k("Xo"), "identb"], writes=[k("Mb")])
            steps.append(s4)

            def s3b():
                bk2 = C.bank()
                key2 = "ps%d" % bk2
                S.op("pe", lambda e: e.transpose(C.ps[bk2][:, 0:128], Am[s][:], identf[:]), reads=[k("Am"), "ident"], writes=[key2])
                S.op("act", lambda e: e.activation(out=AT[s][:], in_=C.ps[bk2][:, 0:128], func=AF.Copy), reads=[], writes=[key2, k("AT")])
                S.op("pool", lambda e: e.tensor_tensor(out=Yo[s][:], in0=Yb[s][:], in1=lvm[:, 128:256], op=ALU.mult),
                     reads=[k("Yb"), "lvm"], writes=[k("Yo")])
                S.op("dve", lambda e: e.tensor_tensor(out=TTb[s][:], in0=Yo[s][:], in1=identb[:], op=ALU.add),
                     reads=[k("Yo"), "identb"], writes=[k("TTb")])
            steps.append(s3b)
            for l in range(1, 7):
                def la(l=l):
                    if l <= 5:
                        S.op("pool", lambda e: e.tensor_tensor(out=Yo[s][:], in0=Yb[s][:], in1=lvm[:, (2 * l + 1) * 128:(2 * l + 2) * 128], op=ALU.mult),
                             reads=[k("Yb"), "lvm"], writes=[k("Yo")])
                    S.op("pool", lambda e: e.tensor_tensor(out=Xo[s][:], in0=Xb_[s][:], in1=lvm[:, (2 * l) * 128:(2 * l + 1) * 128], op=ALU.mult),
                         reads=[k("Xb"), "lvm"], writes=[k("Xo")])
                steps.append(la)

                def lb(l=l):
                    if l <= 5:
                        bh = C.bank()
                        keyh = mm1(bh, Yo[s][:], Mb[s][:], [k("Yo"), k("Mb")])
                    bg = C.bank()
                    keyg = mm1(bg, Xo[s][:], TTb[s][:], [k("Xo"), k("TTb")])
                    if l <= 5:
                        S.op("act", lambda e: e.activation(out=Hb[s][:], in_=C.ps[bh][:, 0:128], func=AF.Copy), reads=[], writes=[keyh, k("Hb")])
                    if l % 2 == 0:
                        S.op("act", lambda e: e.activation(out=Gb[s][:], in_=C.ps[bg][:, 0:128], func=AF.Copy), reads=[], writes=[keyg, k("Gb")])
                    else:
                        S.op("dve", lambda e: e.tensor_copy(out=Gb[s][:], in_=C.ps[bg][:, 0:128]), reads=[], writes=[keyg, k("Gb")])
                steps.append(lb)

                def lc(l=l):
                    if l <= 5:
                        bm = C.bank()
                        keym = mm1(bm, TTb[s][:], Hb[s][:], [k("TTb"), k("Hb")])
                    bw = C.bank()
                    keyw = mm1(bw, Mb[s][:], Gb[s][:], [k("Mb"), k("Gb")])
                    if l <= 5:
                        S.op("dve", lambda e: e.tensor_tensor(out=Mb[s][:], in0=Mb[s][:], in1=C.ps[bm][:, 0:128], op=ALU.add),
                             reads=[k("Mb")], writes=[keym, k("Mb")])
                    S.op("dve", lambda e: e.tensor_tensor(out=TTb[s][:], in0=TTb[s][:], in1=C.ps[bw][:, 0:128], op=ALU.add),
                         reads=[k("TTb")], writes=[keyw, k("TTb")])
                steps.append(lc)

            def s5():
                bk = C.bank()
                key = mm1(bk, Kb[:, c, :], TTb[s][:], ["Kb%d" % (c // 4), k("TTb")])
                bk2 = C.bank()
                key2 = mm1(bk2, TTb[s][:], Vb[:, c, :], ["Vb%d" % (c // 4), k("TTb")])
                S.op("act", lambda e: e.activation(out=WT[s][:], in_=C.ps[bk][:, 0:128], func=AF.Copy), reads=[], writes=[key, k("WT")])
                S.op("dve", lambda e: e.tensor_copy(out=U0[s][:], in_=C.ps[bk2][:, 0:128]), reads=[], writes=[key2, k("U0")])
            steps.append(s5)
            return steps

        def scan_steps(c):
            s = c % NSET
            ck = slice(c * GC, (c + 1) * GC)
            k = lambda nm: "%s%d" % (nm, s)
            u = c % 2
            sbi, sbo = c % 2, (c + 1) % 2
            stt = {}

            def sa():
                b1 = C.bank()
                key1 = mm1(b1, WT[s][:], Sbb[sbi][:], [k("WT"), "Sbb%d" % sbi])
                b2 = C.bank()
                key2 = mm1(b2, QnT[:, ck], Sbb[sbi][:], ["QnT%d" % (c // 4), "Sbb%d" % sbi])
                S.op("dve", lambda e: e.tensor_tensor(out=Ub[u][:], in0=U0[s][:], in1=C.ps[b1][:, 0:128], op=ALU.subtract),
                     reads=[k("U0")], writes=[key1, "Ub%d" % u])
                S.op("act", lambda e: e.activation(out=otmp[:], in_=C.ps[b2][:, 0:128], func=AF.Copy, scale=egc[:, c:c + 1]),
                     reads=["egc"], writes=[key2, "otmp"])

            def sb_():
                b4 = C.bank()
                key4 = mm1(b4, Kend[:, c, :], Ub[u][:], ["Kend%d" % (c // 4), "Ub%d" % u])
                b3 = C.bank()
                key3 = mm1(b3, AT[s][:], Ub[u][:], [k("AT"), "Ub%d" % u])
                S.op("dve", lambda e: e.scalar_tensor_tensor(out=Sf[:], in0=Sf[:], scalar=gend[:, c:c + 1], in1=C.ps[b4][:, 0:128],
                                                             op0=ALU.mult, op1=ALU.add), reads=["gend", "Sf"], writes=[key4, "Sf"])
                S.op("act", lambda e: e.activation(out=Sbb[sbo][:], in_=Sf[:], func=AF.Copy), reads=["Sf"], writes=["Sbb%d" % sbo])
                S.op("dve", lambda e: e.tensor_tensor(out=osb[:], in0=otmp[:], in1=C.ps[b3][:, 0:128], op=ALU.add),
                     reads=["otmp"], writes=[key3, "osb"])

            def sc():
                S.op("act", lambda e: e.activation(out=osq[:], in_=osb[:], func=AF.Square, accum_out=ssq[:, 0:1]), reads=["osb"], writes=["osq", "ssq"])
                S.op("act", lambda e: e.activation(out=lno[:], in_=ssq[:], func=AF.Ln, bias=eps_t[:, 0:1], scale=1.0 / 128), reads=["ssq", "eps"], writes=["lno"])
                S.op("act", lambda e: e.activation(out=rso[:], in_=lno[:], func=AF.Exp, scale=-0.5), reads=["lno"], writes=["rso"])
                S.op("act", lambda e: e.activation(out=onr[:], in_=osb[:], func=AF.Copy, scale=rso[:, 0:1]),
                     reads=["osb", "rso"], writes=["onr"])

            def sd():
                b5 = C.bank()
                key5 = "ps%d" % b5
                S.op("pe", lambda e: e.transpose(C.ps[b5][:, 0:128], onr[:], identf[:]), reads=["onr", "ident"], writes=[key5])
                yb = (c // 4) % 2
                a = c % 4
                if a == 0:
                    tz = (c // 4) * TT
                    S.dma("sp", lambda e: e.dma_start(out=gz[yb][:], in_=d["gzT"][:, tz:tz + TT]), reads=["d_gzT"], writes=["gz%d" % yb])
                if not C.fused:
                    S.op("dve", lambda e: e.scalar_tensor_tensor(out=yst[yb][:, a * 128:(a + 1) * 128], in0=C.ps[b5][:, 0:128], scalar=gnw[:, 0:1],
                                                                 in1=gz[yb][:, a * 128:(a + 1) * 128], op0=ALU.mult, op1=ALU.mult),
                         reads=["gnw", "gz%d" % yb], writes=[key5, "yst%d" % yb])
                    if a == 3:
                        t0 = (c // 4) * TT
                        S.dma("sp", lambda e: e.dma_start(out=d["yT_h"][128:256, t0:t0 + TT], in_=yst[yb][:]), reads=["yst%d" % yb], writes=["d_yT_h"])
                else:
                    for j in range(4):
                        S.op("dve", lambda e, j=j: e.scalar_tensor_tensor(out=ystm[yb][j][:, a * 128:(a + 1) * 128], in0=C.ps[b5][:, 0:128],
                                                                          scalar=gnwm[:, j:j + 1], in1=gz[yb][:, a * 128:(a + 1) * 128],
                                                                          op0=ALU.mult, op1=ALU.mult),
                             reads=["gnwm", "gz%d" % yb], writes=[key5, "ystm%d_%d" % (yb, j)])
                    if a == 3:
                        t0 = (c // 4) * TT
                        qq, qc = t0 // 2048, t0 % 2048
                        for j in range(4):
                            S.dma("sp", lambda e, j=j: e.dma_start(out=d["ypad%d" % qq][256 * j + 128:256 * j + 256, qc:qc + TT], in_=ystm[yb][j][:]),
                                  reads=["ystm%d_%d" % (yb, j)], writes=["d_ypadg%d_%d_%d" % (qq, j, qc // TT)])
                        if qc + TT == 2048:
                            S.coll(lambda e: e.collective_compute(
                                "AllReduce", ALU.add, replica_groups=[[0, 1, 2, 3], [4, 5, 6, 7]],
                                ins=[d["ypad%d" % qq].opt()], outs=[d["yg%d" % qq].opt()]),
                                reads=["d_ypad%d" % qq] + ["d_ypadg%d_%d_%d" % (qq, j, t) for j in range(4) for t in range(4)],
                                writes=["d_yg%d" % qq])
            return [sa, sb_, sc, sd]

        groups = [list(range(g, min(g + G, nchunks))) for g in range(0, nchunks, G)]
        prev = []
        for gi, grp in enumerate(groups + [[]]):
            lists = [pre_steps(c) for c in grp]
            if grp and gi + 1 < ntile_a and G == 4:
                lists.append(phase_a_steps(gi + 1))
            nst = max([len(l) for l in lists] + [0])
            pending = []
            for c in prev:
                pending += scan_steps(c)
            for si in range(nst):
                for l in lists:
                    if si < len(l):
                        l[si]()
                if pending:
                    pending.pop(0)()
            while pending:
                pending.pop(0)()
            prev = grp
        S.drain_dmas("sp")
        S.replay()


CONST_INS = [("ctab", (32, SEQ), F32), ("stab", (32, SEQ), F32), ("ident", (128, 128), F32), ("sel", (65, 64), F32),
             ("tri", (128, 128), F32), ("OH", (32, 3072), F32), ("Eb", (32, SEQ), BF16), ("pen", (32, 32), F32),
             ("umask", (128, 128), F32), ("m2", (128, 128), F32), ("neg", (128, 128), F32), ("slm", (128, 128), F32),
             ("lvlm", (128, 14 * 128), F32)]
LAYER_INS = PROJ_INS + [("t5h", (32, 1)), ("cw", (128, 12)), ("gsc", (1, 2)), ("gnw", (128, 1))]


def host_consts():
    import ml_dtypes
    ctab, stab = rope_consts()
    pen, E = moba_consts()
    umask, m2, neg, slm = gdn_consts()
    sel = np.zeros((65, 64), np.float32)
    sel[64, :] = 1.0
    return {"ctab": ctab, "stab": stab, "ident": np.eye(128, dtype=np.float32), "sel": sel,
            "tri": np.triu(np.ones((128, 128), np.float32)), "OH": t5_consts(), "Eb": E.astype(ml_dtypes.bfloat16),
            "pen": pen, "umask": umask, "m2": m2, "neg": neg, "slm": slm, "lvlm": level_masks()}


def prep_layer_all(I, l, h):
    out = prep_layer(I, l, h)
    f = np.float32
    out["t5h"] = np.ascontiguousarray(I["t5_table"][:, h:h + 1]).astype(f)
    cwf = I["gdn_conv_w"][l]
    cw = np.zeros((128, 12), f)
    for j in range(3):
        for tap in range(4):
            cw[:, 4 * j + tap] = cwf[tap, j * 512 + 128 * h:j * 512 + 128 * h + 128]
    out["cw"] = cw
    out["gsc"] = np.array([[I["gdn_A_log"][l, h], I["gdn_dt_bias"][l, h]]], f)
    out["gnw"] = np.ascontiguousarray(I["gdn_norm_w"][l].reshape(128, 1)).astype(f)
    return out


def wo_perm():
    rows = []
    for h in range(4):
        rows += list(range(64 * h, 64 * h + 64))
        rows += list(range(768 + 64 * h, 768 + 64 * h + 64))
        rows += list(range(256 + 128 * h, 256 + 128 * h + 128))
    return rows


def build_layer_program(has_prev, do_layer, final):
    nc = bass.Bass("TRN2", target_bir_lowering=False)
    with ExitStack() as st:
        C = Ctx(nc, st)
        C.din("xT", (DM, SEQ), F32)
        if has_prev:
            C.din("yT", (DM, SEQ), BF16)
            C.din("wo", (DM, DM), F32)
            if not final:
                C.dout("xTo", (DM, SEQ), F32)
        if final:
            C.din("fnw", (128, 8), F32)
            C.dout("outT", (DM, SEQ), F32)
        if do_layer:
            for n, s_, dt in CONST_INS:
                C.din(n, s_, dt)
            for n, s_ in LAYER_INS:
                C.din(n, s_, F32)
            for n, s_, dt in PROJ_OUTS:
                C.dint(n, s_, dt)
            C.dint("fv", (1, 3072), F32)
            C.dout("yT_h", (256, SEQ), BF16)
        stage_proj(C, has_prev=has_prev, do_proj=do_layer, final=final)
        if do_layer:
            stage_attn(C, False)
            stage_attn(C, True)
            stage_gdn(C)
    return nc


def build_fused_program(depth=DEPTH, do_coll=True):
    nc = bass.Bass("TRN2", target_bir_lowering=False)
    with ExitStack() as st:
        C = Ctx(nc, st)
        C.fused = True
        S = C.S
        C.din("xT", (DM, SEQ), F32)
        C.din("hm", (128, 4), F32)
        C.din("fnw", (128, 8), F32)
        for n, s_, dt in CONST_INS:
            C.din(n, s_, dt)
        for l in range(depth):
            for n, s_ in LAYER_INS:
                C.din("%s_%d" % (n, l), s_, F32)
            C.din("wo_%d" % l, (DM, DM), F32)
        C.dout("outT", (DM, SEQ), F32)
        C.dint("xTi", (DM, SEQ), F32)
        for n, s_, dt in PROJ_OUTS:
            C.dint(n, s_, dt)
        C.dint("fv", (1, 3072), F32)
        for q in range(4):
            C.dint("ypad%d" % q, (DM, 2048), BF16)
            C.dint("yg%d" % q, (DM, 2048), BF16)
        x_ext = C.d["xT"]
        for l in range(depth):
            for n, s_ in LAYER_INS:
                C.d[n] = C.d["%s_%d" % (n, l)]
            if l > 0:
                C.d["wo"] = C.d["wo_%d" % (l - 1)]
            C.d["xT"] = x_ext if l <= 1 else C.d["xTi"]
            C.d["xTo"] = C.d["xTi"]
            import os
            ST = os.environ.get("STAGES", "pamg")
            if "p" in ST:
                stage_proj(C, has_prev=(l > 0), do_proj=True, final=False)
            if "a" in ST:
                stage_attn(C, False)
            if "m" in ST:
                stage_attn(C, True)
            if "g" in ST:
                stage_gdn(C)
        C.d["wo"] = C.d["wo_%d" % (depth - 1)]
        C.d["xT"] = C.d["xTi"] if depth > 1 else x_ext
        if "f" in os.environ.get("STAGES", "pamgf"):
            stage_proj(C, has_prev=True, do_proj=False, final=True)
    return nc


_FUSED = []


def kernel(x, norm_w, w_in, mla_q_norm, mla_w_uq, mla_kv_norm, mla_w_ukv, gdn_conv_w, gdn_A_log, gdn_dt_bias,
           gdn_norm_w, w_out, t5_table, final_norm_w):
    I = dict(x=np.asarray(x), norm_w=np.asarray(norm_w), w_in=np.asarray(w_in), mla_q_norm=np.asarray(mla_q_norm),
             mla_w_uq=np.asarray(mla_w_uq), mla_kv_norm=np.asarray(mla_kv_norm), mla_w_ukv=np.asarray(mla_w_ukv),
             gdn_conv_w=np.asarray(gdn_conv_w), gdn_A_log=np.asarray(gdn_A_log), gdn_dt_bias=np.asarray(gdn_dt_bias),
             gdn_norm_w=np.asarray(gdn_norm_w), w_out=np.asarray(w_out), t5_table=np.asarray(t5_table),
             final_norm_w=np.asarray(final_norm_w))
    if not _FUSED:
        _FUSED.append(build_fused_program())
    nc = _FUSED[0]
    consts = host_consts()
    perm = wo_perm()
    fnw = np.ascontiguousarray(I["final_norm_w"].reshape(8, 128).T).astype(np.float32)
    wos = [np.ascontiguousarray(I["w_out"][l][perm, :]).astype(np.float32) for l in range(DEPTH)]
    xT = [np.ascontiguousarray(I["x"][b].T).astype(np.float32) for b in range(2)]
    maps = []
    for c in range(8):
        b, h = c // 4, c % 4
        m = dict(consts)
        m["xT"] = xT[b]
        hm = np.zeros((128, 4), np.float32)
        hm[:, h] = 1.0
        m["hm"] = hm
        m["fnw"] = fnw
        for l in range(DEPTH):
            for k, v in prep_layer_all(I, l, h).items():
                m["%s_%d" % (k, l)] = v
            m["wo_%d" % l] = wos[l]
        maps.append(m)
    res = run_bass_kernel_spmd(nc, maps, core_ids=list(range(8)))
    out = np.stack([np.asarray(res.results[4 * b]["outT"]).T for b in range(2)], axis=0)
    return np.ascontiguousarray(out).astype(np.float32)
```

```python
import math
import os
from contextlib import ExitStack

import numpy as np
import concourse.bass as bass
import concourse.mybir as mybir
from concourse.bass_utils import run_bass_kernel_spmd

F32 = mybir.dt.float32
BF16 = mybir.dt.bfloat16
AF = mybir.ActivationFunctionType
ALU = mybir.AluOpType
AX = mybir.AxisListType

SEQ = 8192
DM = 1024
DEPTH = 4
TT = 512
NT = SEQ // TT
EPS = 1e-6
TW = [128] * 8 + [96, 96, 64, 2]
TOFF = [sum(TW[:i]) for i in range(len(TW))]
NCOL = sum(TW)

ENGS = ("pe", "act", "dve", "pool", "sp")


class Sched:
    def __init__(self, nc, stack, n_dma_sems=40):
        self.nc = nc
        self.q = {e: [] for e in ENGS}
        self.sem = {e: stack.enter_context(nc.semaphore("s_" + e)) for e in ENGS if e != "sp"}
        self.cnt = {e: 0 for e in ENGS}
        self.seen = {e: {} for e in ENGS}
        self.dsem = [stack.enter_context(nc.semaphore("d%d" % i)) for i in range(n_dma_sems)]
        self.dcnt = [0] * n_dma_sems
        self.dnext = 0
        self.lastw = {}
        self.readers = {}
        self.semobj = dict(self.sem)
        for i, s in enumerate(self.dsem):
            self.semobj["d%d" % i] = s
        self.semobj["cc"] = stack.enter_context(nc.semaphore("s_cc"))
        self.cccnt = 0

    def _deps(self, eng, reads, writes):
        deps = {}

        def add(src, val, kind):
            if src == eng and eng == "pe":
                return
            if deps.get(src, 0) < val:
                deps[src] = val
        for k in reads:
            w = self.lastw.get(k)
            if w:
                add(w[0], w[1], "raw")
        for k in writes:
            w = self.lastw.get(k)
            if w:
                add(w[0], w[1], "waw")
            for s, v in self.readers.get(k, {}).items():
                add(s, v, "war")
        waits = []
        for s, v in deps.items():
            if self.seen[eng].get(s, 0) < v:
                self.seen[eng][s] = v
                waits.append((s, v))
        return waits

    def _commit(self, src, val, reads, writes):
        for k in reads:
            self.readers.setdefault(k, {})[src] = val
        for k in writes:
            self.lastw[k] = (src, val)
            self.readers[k] = {}

    def op(self, eng, fn, reads=(), writes=(), sig=True):
        waits = self._deps(eng, reads, writes)
        val = self.cnt[eng] + 1
        if sig:
            self.cnt[eng] = val
        self._commit(eng, val, reads, writes)
        self.q[eng].append((waits, fn, (eng, 1) if sig else None))

    def dma(self, eng, fn, reads=(), writes=()):
        i = self.dnext
        self.dnext = (self.dnext + 1) % len(self.dsem)
        name = "d%d" % i
        waits = self._deps(eng, reads, writes)
        if self.dcnt[i] > 0 and self.seen[eng].get(name, 0) < self.dcnt[i]:
            self.seen[eng][name] = self.dcnt[i]
            waits.append((name, self.dcnt[i]))
        self.dcnt[i] += 16
        self._commit(name, self.dcnt[i], reads, writes)
        self.q[eng].append((waits, fn, (name, 16)))

    def coll(self, fn, reads=(), writes=()):
        if "cc" not in self.semobj:
            raise RuntimeError("no cc semaphore")
        waits = self._deps("pool", reads, writes)
        self.cccnt += 1
        self._commit("cc", self.cccnt, reads, writes)
        self.q["pool"].append((waits, fn, ("cc", 1)))

    def drain_dmas(self, eng="sp", include_cc=False):
        waits = []
        for i in range(len(self.dsem)):
            name = "d%d" % i
            if self.dcnt[i] > 0 and self.seen[eng].get(name, 0) < self.dcnt[i]:
                self.seen[eng][name] = self.dcnt[i]
                waits.append((name, self.dcnt[i]))
        if include_cc and self.cccnt > 0 and self.seen[eng].get("cc", 0) < self.cccnt:
            self.seen[eng]["cc"] = self.cccnt
            waits.append(("cc", self.cccnt))
        self.q[eng].append((waits, None, None))

    def replay(self):
        nc = self.nc
        with nc.Block() as block:
            engmap = {"pe": block.tensor, "act": block.scalar, "dve": block.vector,
                      "pool": block.gpsimd, "sp": block.sync}
            for e in ENGS:
                items = self.q[e]
                if not items:
                    continue

                def body(engine, items=items):
                    for waits, fn, inc in items:
                        for s, v in waits:
                            engine.wait_ge(self.semobj[s], v)
                        if fn is None:
                            continue
                        ins = fn(engine)
                        if inc is not None:
                            ins.then_inc(self.semobj[inc[0]], inc[1])
                engmap[e](body)
        self.q = {e: [] for e in ENGS}


class Ctx:
    def __init__(self, nc, stack):
        self.nc = nc
        self.S = Sched(nc, stack)
        self.ps = [stack.enter_context(nc.psum_tensor("ps%d" % i, [128, 512], F32)) for i in range(8)]
        self.psn = 0
        self.uid = 0
        self.fused = False
        self.d = {}

    def bank(self):
        i = self.psn
        self.psn = (self.psn + 1) % 8
        return i

    def din(self, name, shape, dt):
        t = self.nc.dram_tensor(name, list(shape), dt, kind="ExternalInput").ap()
        self.d[name] = t
        return t

    def dout(self, name, shape, dt):
        t = self.nc.dram_tensor(name, list(shape), dt, kind="ExternalOutput").ap()
        self.d[name] = t
        return t

    def dint(self, name, shape, dt):
        t = self.nc.dram_tensor(name, list(shape), dt, kind="Internal").ap()
        self.d[name] = t
        return t


def mm_group(C, out_fn, pairs, bank, reads, extra_writes=()):
    S = C.S
    n = len(pairs)
    key = "ps%d" % bank
    for i, (l, r) in enumerate(pairs):
        S.op("pe", (lambda e, l=l, r=r, i=i: e.matmul(out_fn(), l, r, start=(i == 0), stop=(i == n - 1))),
             reads=reads, writes=[key] + list(extra_writes), sig=(i == n - 1))
    return key


def stage_proj(C, has_prev, do_proj, final=False, tiles=range(NT)):
    nc, S, d = C.nc, C.S, C.d
    with ExitStack() as st:
        def sb(name, shape, dt):
            C.uid += 1
            return st.enter_context(nc.sbuf_tensor("sb%d_%s" % (C.uid, name), list(shape), dt))
        xs = [sb("xs%d" % i, [128, 8, TT], F32) for i in range(2)]
        eps_t = sb("eps_t", [128, 1], F32)
        S.op("dve", lambda e: e.memset(eps_t[:], EPS), writes=["eps"])
        if has_prev:
            ys = [sb("ys%d" % i, [128, 8, TT], BF16) for i in range(2)]
            wo = sb("wo", [128, 8, DM], BF16)
            wstage = sb("wstage", [128, NCOL], F32)
            for k in range(8):
                S.dma("sp", lambda e, k=k: e.dma_start(out=wstage[:, 0:DM], in_=d["wo"][k * 128:(k + 1) * 128, :]),
                      reads=["d_wo"], writes=["wstage"])
                if k % 2 == 0:
                    S.op("dve", lambda e, k=k: e.tensor_copy(out=wo[:, k, :], in_=wstage[:, 0:DM]),
                         reads=["wstage"], writes=["wo"])
                else:
                    S.op("act", lambda e, k=k: e.activation(out=wo[:, k, :], in_=wstage[:, 0:DM], func=AF.Copy),
                         reads=["wstage"], writes=["wo"])
        if do_proj or final:
            xsq = sb("xsq", [128, 8, TT], BF16)
            ones = sb("ones", [128, 128], BF16)
            S.op("dve", lambda e: e.memset(ones[:], 1.0), writes=["ones"])
            rstd1 = sb("rstd1", [128, TT], F32)
            lnt = sb("lnt", [128, TT], F32)
        if final:
            fnw = sb("fnw", [128, 8], F32)
            S.dma("sp", lambda e: e.dma_start(out=fnw[:], in_=d["fnw"][:, :]), reads=["d_fnw"], writes=["fnw"])
            outs = sb("outs", [128, 8, TT], F32)
        if do_proj:
            if not has_prev:
                wstage = sb("wstage", [128, NCOL], F32)
            xb = sb("xb", [128, 8, TT], BF16)
            win = sb("win", [128, 8, NCOL], BF16)
            nw = sb("nw", [128, 8], F32)
            S.dma("sp", lambda e: e.dma_start(out=nw[:], in_=d["nw"][:, :]), reads=["d_nw"], writes=["nw"])
            for k in range(8):
                S.dma("sp", lambda e, k=k: e.dma_start(out=wstage[:], in_=d["w_in"][k * 128:(k + 1) * 128, :]),
                      reads=["d_w_in"], writes=["wstage"])
                S.op("dve", lambda e, k=k: e.tensor_scalar(out=win[:, k, :], in0=wstage[:], scalar1=nw[:, k:k + 1],
                                                           scalar2=None, op0=ALU.mult),
                     reads=["wstage", "nw"], writes=["win"])
            wuq = sb("wuq", [128, 2, 192], BF16)
            qnw = sb("qnw", [128, 2], F32)
            S.dma("sp", lambda e: e.dma_start(out=qnw[:], in_=d["qnw"][:, :]), reads=["d_qnw"], writes=["qnw"])
            for k in range(2):
                S.dma("sp", lambda e, k=k: e.dma_start(out=wstage[:, 0:192], in_=d["wuq"][k * 128:(k + 1) * 128, :]),
                      reads=["d_wuq"], writes=["wstage"])
                S.op("dve", lambda e, k=k: e.tensor_scalar(out=wuq[:, k, :], in0=wstage[:, 0:192], scalar1=qnw[:, k:k + 1],
                                                           scalar2=None, op0=ALU.mult),
                     reads=["wstage", "qnw"], writes=["wuq"])
            wukv = sb("wukv", [128, 128], BF16)
            kvnw = sb("kvnw", [128, 1], F32)
            S.dma("sp", lambda e: e.dma_start(out=kvnw[:], in_=d["kvnw"][:, :]), reads=["d_kvnw"], writes=["kvnw"])
            S.dma("sp", lambda e: e.dma_start(out=wstage[:, 0:128], in_=d["wukv"][:, :]), reads=["d_wukv"], writes=["wstage"])
            S.op("dve", lambda e: e.tensor_scalar(out=wukv[:], in0=wstage[:, 0:128], scalar1=kvnw[:, 0:1],
                                                  scalar2=None, op0=ALU.mult),
                 reads=["wstage", "kvnw"], writes=["wukv"])
            identf = sb("identf", [128, 128], F32)
            identb = sb("identb", [128, 128], BF16)
            S.dma("sp", lambda e: e.dma_start(out=identf[:], in_=d["ident"][:, :]), reads=["d_ident"], writes=["identf"])
            S.op("dve", lambda e: e.tensor_copy(out=identb[:], in_=identf[:]), reads=["identf"], writes=["identb"])
            ctab = sb("ctab", [128, TT], F32)
            stab = sb("stab", [128, TT], F32)
            cqf = sb("cqf", [128, 2, TT], F32)
            cqsq = sb("cqsq", [128, 2, TT], BF16)
            cqb = sb("cqb", [128, 2, TT], BF16)
            rstd2 = sb("rstd2", [128, TT], F32)
            rstd3 = sb("rstd3", [128, TT], F32)
            ckvf = sb("ckvf", [128, TT], F32)
            ckvsq = sb("ckvsq", [128, TT], BF16)
            ckvb = sb("ckvb", [128, TT], BF16)
            QTs = sb("QTs", [96, TT], BF16)
            KTs = sb("KTs", [96, TT], BF16)
            t1 = sb("t1", [128, TT], F32)
            t2 = sb("t2", [128, TT], F32)
            t3 = sb("t3", [128, TT], F32)
            t4 = sb("t4", [128, TT], F32)
            vT = sb("vT", [64, TT], BF16)
            Vs = sb("Vs", [128, 4, 64], BF16)
            mvT = sb("mvT", [64, TT], BF16)
            mVs = sb("mVs", [128, 4, 64], BF16)
            mqf = sb("mqf", [64, TT], F32)
            mqb = sb("mqb", [64, TT], BF16)
            mkf = sb("mkf", [64, TT], F32)
            mkb = sb("mkb", [64, TT], BF16)
            km = sb("km", [64, 32], F32)
            gf = sb("gf", [128, TT], F32)
            gs = sb("gs", [128, TT], BF16)
            gq3 = [sb("gq3_%d" % i, [128, TT], F32) for i in range(3)]
            zf = sb("zf", [128, TT], F32)
            zs = sb("zs", [128, TT], BF16)
            bas = sb("bas", [2, TT], F32)

        def rstd_from(bank, n, out_t, rows=128):
            key = "ps%d" % bank
            S.op("act", lambda e: e.activation(out=lnt[0:rows, :], in_=C.ps[bank][0:rows, :], func=AF.Ln,
                                               bias=eps_t[0:rows, 0:1], scale=1.0 / n),
                 reads=["eps"], writes=[key, "lnt"])
            S.op("act", lambda e: e.activation(out=out_t[0:rows, :], in_=lnt[0:rows, :], func=AF.Exp, scale=-0.5),
                 reads=["lnt"], writes=[out_t.name if hasattr(out_t, "name") else "rstd"])

        def do_tile(ti):
            b = ti % 2
            c0 = ti * TT
            xk = "xs%d" % b
            xsb = xs[b]
            for k4 in range(2):
                S.dma("sp", lambda e, k4=k4, xsb=xsb: e.dma_start(
                    out=xsb[:, 4 * k4:4 * k4 + 4, :],
                    in_=d["xT"][4 * k4 * 128:(4 * k4 + 4) * 128, c0:c0 + TT].rearrange("(k p) t -> p k t", p=128)),
                    reads=["d_xT"], writes=[xk])
            if has_prev:
                yk = "ys%d" % b
                ysb = ys[b]
                S.dma("sp", lambda e, ysb=ysb: e.dma_start(
                    out=ysb[:], in_=(d["yg%d" % (c0 // 2048)][:, c0 % 2048:c0 % 2048 + TT] if C.fused else d["yT"][:, c0:c0 + TT]
                                      ).rearrange("(k p) t -> p k t", p=128)),
                    reads=[("d_yg%d" % (c0 // 2048)) if C.fused else "d_yT"], writes=[yk])
                for c in range(8):
                    bk = C.bank()
                    key = mm_group(C, (lambda bk=bk: C.ps[bk][:, :]),
                                   [(wo[:, k, c * 128:(c + 1) * 128], ysb[:, k, :]) for k in range(8)],
                                   bk, reads=["wo", yk])
                    S.op("dve", lambda e, c=c, bk=bk, xsb=xsb: e.tensor_tensor(out=xsb[:, c, :], in0=C.ps[bk][:, :], in1=xsb[:, c, :], op=ALU.add),
                         reads=[], writes=[key, xk])
                if not final:
                    for k4 in range(2):
                        S.dma("sp", lambda e, k4=k4, xsb=xsb: e.dma_start(
                            out=d["xTo"][4 * k4 * 128:(4 * k4 + 4) * 128, c0:c0 + TT].rearrange("(k p) t -> p k t", p=128),
                            in_=xsb[:, 4 * k4:4 * k4 + 4, :]),
                            reads=[xk], writes=["d_xTo"])
            if not (do_proj or final):
                return
            S.op("act", lambda e, xsb=xsb: e.activation(out=xsq[:], in_=xsb[:], func=AF.Square), reads=[xk], writes=["xsq"])
            bk = C.bank()
            key = mm_group(C, (lambda bk=bk: C.ps[bk][:, :]), [(ones[:], xsq[:, k, :]) for k in range(8)], bk,
                           reads=["ones", "xsq"])
            S.op("act", lambda e, bk=bk: e.activation(out=lnt[:], in_=C.ps[bk][:, :], func=AF.Ln, bias=eps_t[:, 0:1], scale=1.0 / DM),
                 reads=["eps"], writes=[key, "lnt"])
            S.op("act", lambda e: e.activation(out=rstd1[:], in_=lnt[:], func=AF.Exp, scale=-0.5), reads=["lnt"], writes=["rstd1"])
            if final:
                for k in range(8):
                    S.op("dve", lambda e, k=k, xsb=xsb: e.scalar_tensor_tensor(
                        out=outs[:, k, :], in0=xsb[:, k, :], scalar=fnw[:, k:k + 1], in1=rstd1[:], op0=ALU.mult, op1=ALU.mult),
                        reads=[xk, "fnw", "rstd1"], writes=["outs"])
                for k4 in range(2):
                    S.dma("sp", lambda e, k4=k4: e.dma_start(
                        out=d["outT"][4 * k4 * 128:(4 * k4 + 4) * 128, c0:c0 + TT].rearrange("(k p) t -> p k t", p=128),
                        in_=outs[:, 4 * k4:4 * k4 + 4, :]),
                        reads=["outs"], writes=["d_outT"])
                return
            S.op("dve", lambda e, xsb=xsb: e.tensor_copy(out=xb[:], in_=xsb[:]), reads=[xk], writes=["xb"])
            S.dma("sp", lambda e: e.dma_start(out=ctab[64:96, :], in_=d["ctab"][:, c0:c0 + TT]), reads=["d_ctab"], writes=["ctab"])
            S.dma("sp", lambda e: e.dma_start(out=stab[64:96, :], in_=d["stab"][:, c0:c0 + TT]), reads=["d_stab"], writes=["stab"])

            def proj_tile(t):
                bk = C.bank()
                w = TW[t]
                key = mm_group(C, (lambda bk=bk, w=w: C.ps[bk][0:w, :]),
                               [(win[:, k, TOFF[t]:TOFF[t] + w], xb[:, k, :]) for k in range(8)], bk,
                               reads=["win", "xb"])
                return bk, key

            for j in range(2):
                bk, key = proj_tile(j)
                S.op("dve", lambda e, bk=bk, j=j: e.tensor_tensor(out=cqf[:, j, :], in0=C.ps[bk][:, :], in1=rstd1[:], op=ALU.mult),
                     reads=["rstd1"], writes=[key, "cqf"])
            S.op("act", lambda e: e.activation(out=cqsq[:], in_=cqf[:], func=AF.Square), reads=["cqf"], writes=["cqsq"])
            S.op("dve", lambda e: e.tensor_copy(out=cqb[:], in_=cqf[:]), reads=["cqf"], writes=["cqb"])
            bk = C.bank()
            key = mm_group(C, (lambda bk=bk: C.ps[bk][:, :]), [(ones[:], cqsq[:, j, :]) for j in range(2)], bk, reads=["ones", "cqsq"])
            S.op("act", lambda e, bk=bk: e.activation(out=lnt[:], in_=C.ps[bk][:, :], func=AF.Ln, bias=eps_t[:, 0:1], scale=1.0 / 256),
                 reads=["eps"], writes=[key, "lnt"])
            S.op("act", lambda e: e.activation(out=rstd2[:], in_=lnt[:], func=AF.Exp, scale=-0.5), reads=["lnt"], writes=["rstd2"])
            bq = C.bank()
            keyq = mm_group(C, (lambda bq=bq: C.ps[bq][0:96, :]), [(wuq[:, j, 0:96], cqb[:, j, :]) for j in range(2)], bq, reads=["wuq", "cqb"])
            br = C.bank()
            keyr = mm_group(C, (lambda br=br: C.ps[br][0:96, :]), [(wuq[:, j, 96:192], cqb[:, j, :]) for j in range(2)], br, reads=["wuq", "cqb"])
            S.op("dve", lambda e, bq=bq: e.tensor_tensor(out=QTs[0:64, :], in0=C.ps[bq][0:64, :], in1=rstd2[0:64, :], op=ALU.mult),
                 reads=["rstd2"], writes=[keyq, "QTs"])
            S.op("dve", lambda e, bq=bq: e.tensor_tensor(out=t1[64:96, :], in0=C.ps[bq][64:96, :], in1=ctab[64:96, :], op=ALU.mult),
                 reads=["ctab"], writes=[keyq, "t1"])
            S.op("dve", lambda e, br=br: e.tensor_tensor(out=t2[64:96, :], in0=C.ps[br][64:96, :], in1=stab[64:96, :], op=ALU.mult),
                 reads=["stab"], writes=[keyr, "t2"])
            S.op("dve", lambda e: e.tensor_tensor(out=t1[64:96, :], in0=t1[64:96, :], in1=t2[64:96, :], op=ALU.add),
                 reads=["t2", "t1"], writes=["t1"])
            S.op("dve", lambda e: e.tensor_tensor(out=QTs[64:96, :], in0=t1[64:96, :], in1=rstd2[64:96, :], op=ALU.mult),
                 reads=["t1", "rstd2"], writes=["QTs"])
            S.dma("sp", lambda e: e.dma_start(out=d["QT_mla"][:, c0:c0 + TT], in_=QTs[:]), reads=["QTs"], writes=["d_QT_mla"])
            bk, key = proj_tile(2)
            S.op("dve", lambda e, bk=bk: e.tensor_tensor(out=ckvf[:], in0=C.ps[bk][:, :], in1=rstd1[:], op=ALU.mult),
                 reads=["rstd1"], writes=[key, "ckvf"])
            S.op("act", lambda e: e.activation(out=ckvsq[:], in_=ckvf[:], func=AF.Square), reads=["ckvf"], writes=["ckvsq"])
            S.op("act", lambda e: e.activation(out=ckvb[:], in_=ckvf[:], func=AF.Copy), reads=["ckvf"], writes=["ckvb"])
            bk = C.bank()
            key = mm_group(C, (lambda bk=bk: C.ps[bk][:, :]), [(ones[:], ckvsq[:])], bk, reads=["ones", "ckvsq"])
            S.op("act", lambda e, bk=bk: e.activation(out=lnt[:], in_=C.ps[bk][:, :], func=AF.Ln, bias=eps_t[:, 0:1], scale=1.0 / 128),
                 reads=["eps"], writes=[key, "lnt"])
            S.op("act", lambda e: e.activation(out=rstd3[:], in_=lnt[:], func=AF.Exp, scale=-0.5), reads=["lnt"], writes=["rstd3"])
            bk = C.bank()
            key = mm_group(C, (lambda bk=bk: C.ps[bk][0:64, :]), [(wukv[:, 0:64], ckvb[:])], bk, reads=["wukv", "ckvb"])
            S.op("dve", lambda e, bk=bk: e.tensor_tensor(out=KTs[0:64, :], in0=C.ps[bk][0:64, :], in1=rstd3[0:64, :], op=ALU.mult),
                 reads=["rstd3"], writes=[key, "KTs"])
            bk = C.bank()
            key = mm_group(C, (lambda bk=bk: C.ps[bk][0:64, :]), [(wukv[:, 64:128], ckvb[:])], bk, reads=["wukv", "ckvb"])
            S.op("dve", lambda e, bk=bk: e.tensor_tensor(out=vT[:], in0=C.ps[bk][0:64, :], in1=rstd3[0:64, :], op=ALU.mult),
                 reads=["rstd3"], writes=[key, "vT"])

            def transpose_v(src, srck, dst, dstk, dname):
                bk = C.bank()
                key = "ps%d" % bk
                for j in range(4):
                    S.op("pe", lambda e, j=j, bk=bk: e.transpose(
                        C.ps[bk][:, :].bitcast(BF16)[:, j * 64:(j + 1) * 64], src[0:64, j * 128:(j + 1) * 128], identb[0:64, 0:64]),
                        reads=[srck, "identb"], writes=[key], sig=(j == 3))
                S.op("act", lambda e, bk=bk: e.activation(out=dst[:].rearrange("p a b -> p (a b)"),
                                                          in_=C.ps[bk][:, :].bitcast(BF16)[:, 0:256], func=AF.Copy),
                     reads=[], writes=[key, dstk])
                S.dma("sp", lambda e: e.dma_start(
                    out=d[dname][c0:c0 + TT, :].rearrange("(a p) v -> p a v", p=128), in_=dst[:]),
                    reads=[dstk], writes=["d_" + dname])
            transpose_v(vT, "vT", Vs, "Vs", "V_mla")
            bk, key = proj_tile(8)
            S.op("dve", lambda e, bk=bk: e.tensor_tensor(out=mqf[:], in0=C.ps[bk][0:64, :], in1=rstd1[0:64, :], op=ALU.mult),
                 reads=["rstd1"], writes=[key, "mqf"])
            S.op("dve", lambda e, bk=bk: e.tensor_tensor(out=t3[64:96, :], in0=C.ps[bk][64:96, :], in1=ctab[64:96, :], op=ALU.mult),
                 reads=["ctab"], writes=[key, "t3"])
            S.op("act", lambda e: e.activation(out=mqb[:], in_=mqf[:], func=AF.Copy, scale=0.125),
                 reads=["mqf"], writes=["mqb"])
            S.dma("sp", lambda e: e.dma_start(out=d["QTf_moba"][:, c0:c0 + TT], in_=mqf[:]), reads=["mqf"], writes=["d_QTf_moba"])
            S.dma("sp", lambda e: e.dma_start(out=d["QT_moba"][:, c0:c0 + TT], in_=mqb[:]), reads=["mqb"], writes=["d_QT_moba"])
            bk, key = proj_tile(9)
            S.op("dve", lambda e, bk=bk: e.tensor_tensor(out=mkf[:], in0=C.ps[bk][0:64, :], in1=rstd1[0:64, :], op=ALU.mult),
                 reads=["rstd1"], writes=[key, "mkf"])
            S.op("dve", lambda e, bk=bk: e.tensor_tensor(out=t4[64:96, :], in0=C.ps[bk][64:96, :], in1=stab[64:96, :], op=ALU.mult),
                 reads=["stab"], writes=[key, "t4"])
            S.op("act", lambda e: e.activation(out=mkb[:], in_=mkf[:], func=AF.Copy), reads=["mkf"], writes=["mkb"])
            S.op("dve", lambda e: e.tensor_reduce(out=km[:, 2 * ti:2 * ti + 2], in_=mkf[:].rearrange("p (a b) -> p a b", b=256),
                                                  op=ALU.add, axis=AX.X),
                 reads=["mkf"], writes=["km"])
            S.dma("sp", lambda e: e.dma_start(out=d["KT_moba"][:, c0:c0 + TT], in_=mkb[:]), reads=["mkb"], writes=["d_KT_moba"])
            S.op("dve", lambda e: e.tensor_tensor(out=t3[64:96, :], in0=t3[64:96, :], in1=t4[64:96, :], op=ALU.add),
                 reads=["t3", "t4"], writes=["t3"])
            S.op("dve", lambda e: e.tensor_tensor(out=KTs[64:96, :], in0=t3[64:96, :], in1=rstd1[64:96, :], op=ALU.mult),
                 reads=["t3", "rstd1"], writes=["KTs"])
            S.dma("sp", lambda e: e.dma_start(out=d["KT_mla"][:, c0:c0 + TT], in_=KTs[:]), reads=["KTs"], writes=["d_KT_mla"])
            bk, key = proj_tile(10)
            S.op("dve", lambda e, bk=bk: e.tensor_tensor(out=mvT[:], in0=C.ps[bk][0:64, :], in1=rstd1[0:64, :], op=ALU.mult),
                 reads=["rstd1"], writes=[key, "mvT"])
            transpose_v(mvT, "mvT", mVs, "mVs", "V_moba")
            bk, key = proj_tile(3)
            S.op("dve", lambda e, bk=bk: e.tensor_tensor(out=gf[:], in0=C.ps[bk][:, :], in1=rstd1[:], op=ALU.mult),
                 reads=["rstd1"], writes=[key, "gf"])
            S.op("act", lambda e: e.activation(out=gs[:], in_=gf[:], func=AF.Silu), reads=["gf"], writes=["gs"])
            S.dma("sp", lambda e: e.dma_start(out=d["GT_mla"][:, c0:c0 + TT], in_=gs[0:64, :]), reads=["gs"], writes=["d_GT_mla"])
            S.dma("sp", lambda e: e.dma_start(out=d["GT_moba"][:, c0:c0 + TT], in_=gs[64:128, :]), reads=["gs"], writes=["d_GT_moba"])
            for j, nm in enumerate(("gqT", "gkT", "gvT")):
                bk, key = proj_tile(4 + j)
                S.op("dve", lambda e, bk=bk, j=j: e.tensor_tensor(out=gq3[j][:], in0=C.ps[bk][:, :], in1=rstd1[:], op=ALU.mult),
                     reads=["rstd1"], writes=[key, "gq3_%d" % j])
                S.dma("sp", lambda e, j=j, nm=nm: e.dma_start(out=d[nm][:, c0:c0 + TT], in_=gq3[j][:]),
                      reads=["gq3_%d" % j], writes=["d_" + nm])
            bk, key = proj_tile(7)
            S.op("dve", lambda e, bk=bk: e.tensor_tensor(out=zf[:], in0=C.ps[bk][:, :], in1=rstd1[:], op=ALU.mult),
                 reads=["rstd1"], writes=[key, "zf"])
            S.op("act", lambda e: e.activation(out=zs[:], in_=zf[:], func=AF.Silu), reads=["zf"], writes=["zs"])
            S.dma("sp", lambda e: e.dma_start(out=d["gzT"][:, c0:c0 + TT], in_=zs[:]), reads=["zs"], writes=["d_gzT"])
            bk, key = proj_tile(11)
            S.op("dve", lambda e, bk=bk: e.tensor_tensor(out=bas[:], in0=C.ps[bk][0:2, :], in1=rstd1[0:2, :], op=ALU.mult),
                 reads=["rstd1"], writes=[key, "bas"])
            S.dma("sp", lambda e: e.dma_start(out=d["baT"][:, c0:c0 + TT], in_=bas[:]), reads=["bas"], writes=["d_baT"])
        for ti in tiles:
            do_tile(ti)
        if do_proj and not final:
            S.op("dve", lambda e: e.tensor_scalar(out=km[:], in0=km[:], scalar1=1.0 / 256, scalar2=None, op0=ALU.mult),
                 reads=["km"], writes=["km"])
            S.dma("sp", lambda e: e.dma_start(out=d["kmT"][:, :], in_=km[:]), reads=["km"], writes=["d_kmT"])
        S.drain_dmas("sp")
        S.replay()


def rope_consts():
    inv = (1.0 / (10000.0 ** (np.arange(0, 32, 2, dtype=np.float32) / 32))).astype(np.float32)
    ang = (np.arange(SEQ, dtype=np.float32)[:, None] * inv[None, :]).astype(np.float32)
    cos = np.cos(ang).astype(np.float32).T
    sin = np.sin(ang).astype(np.float32).T
    ctab = np.ascontiguousarray(np.concatenate([cos, cos], 0))
    stab = np.ascontiguousarray(np.concatenate([-sin, sin], 0))
    return ctab, stab


def in_cols(h):
    cq0, ckv0, kr0, mg0, gq0, gk0, gv0, gz0, gb0, ga0, mq0, mk0, mv0, cg0 = (
        0, 256, 384, 416, 672, 1184, 1696, 2208, 2720, 2724, 2728, 2984, 3240, 3496)
    r = lambda a, n: list(range(a, a + n))
    cols = []
    cols += r(cq0, 256) + r(ckv0, 128)
    cols += r(mg0 + 64 * h, 64) + r(cg0 + 64 * h, 64)
    cols += r(gq0 + 128 * h, 128) + r(gk0 + 128 * h, 128) + r(gv0 + 128 * h, 128) + r(gz0 + 128 * h, 128)
    cols += r(mq0 + 64 * h, 64) + r(kr0, 32)
    cols += r(mk0 + 64 * h, 64) + r(kr0 + 16, 16) + r(kr0, 16)
    cols += r(mv0 + 64 * h, 64)
    cols += [gb0 + h, ga0 + h]
    assert len(cols) == NCOL
    return cols


def prep_layer(I, l, h):
    f = np.float32
    out = {}
    out["w_in"] = np.ascontiguousarray(I["w_in"][l][:, in_cols(h)]).astype(f)
    out["nw"] = np.ascontiguousarray(I["norm_w"][l].reshape(8, 128).T).astype(f)
    wq = I["mla_w_uq"][l][:, 96 * h:96 * h + 96]
    wuq = np.zeros((256, 192), f)
    wuq[:, 0:96] = wq
    wuq[:, 160:176] = wq[:, 80:96]
    wuq[:, 176:192] = wq[:, 64:80]
    out["wuq"] = wuq
    out["qnw"] = np.ascontiguousarray(I["mla_q_norm"][l].reshape(2, 128).T).astype(f)
    out["wukv"] = np.ascontiguousarray(I["mla_w_ukv"][l][:, 128 * h:128 * h + 128]).astype(f)
    out["kvnw"] = np.ascontiguousarray(I["mla_kv_norm"][l].reshape(128, 1)).astype(f)
    return out


PROJ_OUTS = [("QT_mla", (96, SEQ), BF16), ("KT_mla", (96, SEQ), BF16), ("V_mla", (SEQ, 64), BF16),
             ("QTf_moba", (64, SEQ), F32), ("QT_moba", (64, SEQ), BF16), ("KT_moba", (64, SEQ), BF16),
             ("V_moba", (SEQ, 64), BF16), ("kmT", (64, 32), F32),
             ("GT_mla", (64, SEQ), BF16), ("GT_moba", (64, SEQ), BF16),
             ("gqT", (128, SEQ), F32), ("gkT", (128, SEQ), F32), ("gvT", (128, SEQ), F32),
             ("gzT", (128, SEQ), BF16), ("baT", (2, SEQ), F32)]
PROJ_INS = [("w_in", (DM, NCOL)), ("nw", (128, 8)), ("wuq", (256, 192)), ("qnw", (128, 2)),
            ("wukv", (128, 128)), ("kvnw", (128, 1))]


def t5_consts():
    e = np.arange(3072)
    dd = e - 511
    n = np.maximum(dd, 0)
    nf = np.maximum(n, 1).astype(np.float32)
    large = 16 + (np.log(nf / np.float32(16)) / np.float32(math.log(2048 / 16)) * np.float32(16)).astype(np.int32)
    large = np.minimum(large, 31)
    bucket = np.where(n < 16, n, large)
    OH = np.zeros((32, 3072), np.float32)
    OH[bucket, e] = 1.0
    OH[:, dd < 0] = 0.0
    return OH


def moba_consts():
    pen = np.zeros((32, 32), np.float32)
    for own in range(32):
        pen[own, own] = 1e30
        pen[own, own + 1:] = -1e30
    E = np.zeros((32, SEQ), np.float32)
    for j in range(32):
        E[j, j * 256:(j + 1) * 256] = 1.0
    return pen, E


def stage_attn(C, moba, qtiles=range(NT), es_load=False):
    nc, S, d = C.nc, C.S, C.d
    pre = "moba" if moba else "mla"
    KD = 96
    scale = 1.0 if moba else 96 ** -0.5
    rowbase = 64 if moba else 0
    with ExitStack() as st:
        def sb(name, shape, dt):
            C.uid += 1
            return st.enter_context(nc.sbuf_tensor("sb%d_%s" % (C.uid, name), list(shape), dt))
        QT = sb("QT", [96, SEQ], BF16)
        KT = sb("KT", [96, SEQ], BF16)
        Va = sb("Va", [128, 64, 65], BF16)
        GT = sb("GT", [64, SEQ], BF16)
        Pb = [sb("P%d" % i, [128, TT], BF16) for i in range(8)]
        osb = sb("osb", [65, TT], F32)
        rec = sb("rec", [64, TT], F32)
        otmp = sb("otmp", [64, TT], F32)
        ysb = sb("ysb", [64, TT], BF16)
        if C.fused:
            ym = [sb("ym%d" % j, [64, TT], BF16) for j in range(4)]
            hm = sb("hm", [128, 4], F32)
            S.dma("sp", lambda e: e.dma_start(out=hm[:], in_=d["hm"][:, :]), reads=["d_hm"], writes=["hm"])
        sel = sb("sel", [65, 64], F32)
        nrows = 64 if moba else 96
        for q4 in range(4):
            cs = slice(q4 * 2048, (q4 + 1) * 2048)
            S.dma("sp", lambda e, cs=cs: e.dma_start(out=QT[0:nrows, cs], in_=d["QT_" + pre][:, cs]), reads=["d_QT_" + pre], writes=["QT"])
            S.dma("sp", lambda e, cs=cs: e.dma_start(out=KT[0:nrows, cs], in_=d["KT_" + pre][:, cs]), reads=["d_KT_" + pre], writes=["KT"])
            S.dma("sp", lambda e, cs=cs: e.dma_start(out=GT[:, cs], in_=d["GT_" + pre][:, cs]), reads=["d_GT_" + pre], writes=["GT"])
        for a4 in range(16):
            S.dma("sp", lambda e, a4=a4: e.dma_start(
                out=Va[:, 4 * a4:4 * a4 + 4, 0:64],
                in_=d["V_" + pre][a4 * 512:(a4 + 1) * 512, :].rearrange("(a p) v -> p a v", p=128)),
                reads=["d_V_" + pre], writes=["Va"])
        S.op("dve", lambda e: e.memset(Va[:, :, 64:65], 1.0), writes=["Va"])
        S.dma("sp", lambda e: e.dma_start(out=sel[:], in_=d["sel"][:, :]), reads=["d_sel"], writes=["sel"])
        if not moba:
            trif = sb("trif", [128, 128], F32)
            tri = sb("tri", [128, 128], BF16)
            S.dma("sp", lambda e: e.dma_start(out=trif[:], in_=d["tri"][:, :]), reads=["d_tri"], writes=["trif"])
            S.op("dve", lambda e: e.tensor_copy(out=tri[:], in_=trif[:]), reads=["trif"], writes=["tri"])
        else:
            t5c = sb("t5c", [32, 1], F32)
            et = sb("et", [32, 1], F32)
            b31 = sb("b31", [128, 1], F32)
            OH = sb("OH", [32, 3072], F32)
            fvs = sb("fvs", [1, 3072], F32)
            ES32 = sb("ES32", [128, 2560], F32)
            ES = sb("ES", [128, 2560], BF16)
            S.dma("sp", lambda e: e.dma_start(out=b31[:], in_=d["t5h"][31:32, :].partition_broadcast(128).rearrange("p a b -> p (a b)")),
                  reads=["d_t5h"], writes=["b31"])
            if es_load:
                S.dma("sp", lambda e: e.dma_start(out=ES[:], in_=d["ESd"][:, :]), reads=["d_ESd"], writes=["ES"])
            else:
                S.dma("sp", lambda e: e.dma_start(out=t5c[:], in_=d["t5h"][:, :]), reads=["d_t5h"], writes=["t5c"])
                S.dma("sp", lambda e: e.dma_start(out=OH[:], in_=d["OH"][:, :]), reads=["d_OH"], writes=["OH"])
                S.op("act", lambda e: e.activation(out=et[:], in_=t5c[:], func=AF.Exp), reads=["t5c"], writes=["et"])
                for j in range(6):
                    key = mm_group(C, (lambda: C.ps[5][0:1, :]), [(et[:, 0:1], OH[:, j * 512:(j + 1) * 512])], 5, reads=["et", "OH"])
                    S.op("dve", lambda e, j=j: e.tensor_copy(out=fvs[:, j * 512:(j + 1) * 512], in_=C.ps[5][0:1, :]), reads=[], writes=[key, "fvs"])
                S.dma("sp", lambda e: e.dma_start(out=d["fv"][:, :], in_=fvs[:]), reads=["fvs"], writes=["d_fv"])
                for ki in range(128):
                    S.dma("sp", lambda e, ki=ki: e.dma_start(
                        out=ES32[ki:ki + 1, :], in_=d["fv"][:, 127 - ki:127 - ki + 2560]), reads=["d_fv"], writes=["ES32_%d" % ki])
                S.op("dve", lambda e: e.tensor_copy(out=ES[:], in_=ES32[:]), reads=["ES32_%d" % ki for ki in range(128)], writes=["ES"])
                if "ESd" in d:
                    S.dma("sp", lambda e: e.dma_start(out=d["ESd"][:, :], in_=ES[:]), reads=["ES"], writes=["d_ESd"])
            for q4 in range(4):
                cs = slice(q4 * 2048, (q4 + 1) * 2048)
                S.dma("sp", lambda e, cs=cs: e.dma_start(out=KT[64:96, cs], in_=d["Eb"][:, cs]), reads=["d_Eb"], writes=["KT"])
            QTf = sb("QTf", [64, SEQ], F32)
            kmT = sb("kmT", [64, 32], F32)
            pen = sb("pen", [128, 32 * 32], F32)
            identf = sb("identf", [128, 128], F32)
            identb = sb("identb", [128, 128], BF16)
            S.dma("sp", lambda e: e.dma_start(out=identf[:], in_=d["ident"][:, :]), reads=["d_ident"], writes=["identf"])
            S.op("dve", lambda e: e.tensor_copy(out=identb[:], in_=identf[:]), reads=["identf"], writes=["identb"])
            for q4 in range(4):
                cs = slice(q4 * 2048, (q4 + 1) * 2048)
                S.dma("sp", lambda e, cs=cs: e.dma_start(out=QTf[:, cs], in_=d["QTf_moba"][:, cs]), reads=["d_QTf_moba"], writes=["QTf"])
            S.dma("sp", lambda e: e.dma_start(out=kmT[:], in_=d["kmT"][:, :]), reads=["d_kmT"], writes=["kmT"])
            S.dma("sp", lambda e: e.dma_start(out=pen[:], in_=d["pen"][:, :].rearrange("a b -> (a b)").partition_broadcast(128)),
                  reads=["d_pen"], writes=["pen"])
            gm = [sb("gm%d" % i, [128, 32], F32) for i in range(4)]
            m8 = [sb("m8%d" % i, [128, 8], F32) for i in range(4)]
            thr = [sb("thr%d" % i, [128, 1], F32) for i in range(4)]
            Mq = [sb("Mq%d" % i, [128, 96], BF16) for i in range(4)]
            for i in range(4):
                S.op("dve", lambda e, i=i: e.memset(Mq[i][:], 0.0), writes=["Mq%d" % i])
            def gate_group(g):
                keys = []
                for j in range(4):
                    i = g * 4 + j
                    keys.append(mm_group(C, (lambda j=j: C.ps[6][:, j * 32:(j + 1) * 32]), [(QTf[:, i * 128:(i + 1) * 128], kmT[:, :])], 6,
                                         reads=["QTf", "kmT"]))
                for j in range(4):
                    own = (g * 4 + j) // 2
                    S.op("dve", lambda e, j=j, own=own: e.tensor_tensor(out=gm[j][:], in0=C.ps[6][:, j * 32:(j + 1) * 32],
                                                                      in1=pen[:, own * 32:(own + 1) * 32], op=ALU.add),
                         reads=["pen"], writes=[keys[j], "gm%d" % j])
                for j in range(4):
                    S.op("dve", lambda e, j=j: e.max(out=m8[j][:], in_=gm[j][:]), reads=["gm%d" % j], writes=["m8%d" % j])
                for j in range(4):
                    S.op("dve", lambda e, j=j: e.tensor_scalar(out=thr[j][:], in0=m8[j][:, 3:4], scalar1=-1e29, scalar2=None, op0=ALU.max),
                         reads=["m8%d" % j], writes=["thr%d" % j])
                for j in range(4):
                    S.op("dve", lambda e, j=j: e.tensor_scalar(out=Mq[j][:, 64:96], in0=gm[j][:], scalar1=thr[j][:, 0:1], scalar2=-30000.0,
                                                               op0=ALU.is_lt, op1=ALU.mult),
                         reads=["gm%d" % j, "thr%d" % j], writes=["Mq%d" % j])
                for j in range(4):
                    S.op("pe", lambda e, j=j: e.transpose(C.ps[7][0:96, :].bitcast(BF16)[:, j * 128:(j + 1) * 128], Mq[j][:], identb[:]),
                         reads=["Mq%d" % j, "identb"], writes=["ps7"], sig=True)
                S.op("act", lambda e, g=g: e.activation(out=QT[64:96, g * 512:(g + 1) * 512],
                                                        in_=C.ps[7][64:96, :].bitcast(BF16)[:, 0:512], func=AF.Copy),
                     reads=[], writes=["ps7", "QTg%d" % g])

        SB = (0, 1, 2, 3)
        OB = (4, 5)
        NB = 4
        NP = 8
        DEPTHQ = 6
        items = []
        for qi, qt in enumerate(qtiles):
            for kt in range(4 * qt + 4):
                items.append((qi, qt, kt))

        def front(idx):
            qi, qt, kt = items[idx]
            q0 = qt * TT
            k0 = kt * 128
            diag = kt >= 4 * qt
            koff = (kt - 4 * qt) * 128 if diag else 0
            sbk = SB[idx % NB]
            skey = "ps%d" % sbk
            P = Pb[idx % NP]
            pkey = "P%d" % (idx % NP)
            if moba and kt == 0 and qt + 2 < NT:
                gate_group(qt + 2)
            S.op("pe", lambda e: e.matmul(
                C.ps[sbk][:, koff:TT], KT[0:KD, k0:k0 + 128], QT[0:KD, q0 + koff:q0 + TT], start=True, stop=True),
                reads=["KT", "QT"] + (["QTg%d" % qt] if moba else []), writes=[skey])
            far = moba and (q0 - k0 >= 1664)
            if far:
                S.op("act", lambda e: e.activation(
                    out=P[:, koff:TT], in_=C.ps[sbk][:, koff:TT], func=AF.Exp, bias=b31[:, 0:1], scale=scale),
                    reads=["b31"], writes=[skey, pkey])
            else:
                S.op("act", lambda e: e.activation(
                    out=P[:, koff:TT], in_=C.ps[sbk][:, koff:TT], func=AF.Exp, scale=scale),
                    reads=[], writes=[skey, pkey])
                if moba:
                    s0 = q0 - k0 + 384
                    S.op("dve" if idx % 4 != 3 else "pool", lambda e: e.tensor_tensor(
                        out=P[:, koff:TT], in0=P[:, koff:TT], in1=ES[:, s0 + koff:s0 + TT], op=ALU.mult),
                        reads=["ES", pkey], writes=[pkey])
                elif diag:
                    S.op("pool", lambda e: e.tensor_tensor(
                        out=P[:, koff:koff + 128], in0=P[:, koff:koff + 128], in1=tri[:], op=ALU.mult),
                        reads=["tri", pkey], writes=[pkey])

        def back(idx):
            qi, qt, kt = items[idx]
            q0 = qt * TT
            diag = kt >= 4 * qt
            koff = (kt - 4 * qt) * 128 if diag else 0
            nkt = 4 * qt + 4
            ob = OB[qi % 2]
            okey = "ps%d" % ob
            P = Pb[idx % NP]
            pkey = "P%d" % (idx % NP)
            S.op("pe", lambda e: e.matmul(
                C.ps[ob][0:65, koff:TT], Va[:, kt, :], P[:, koff:TT], start=(kt == 0), stop=(kt == nkt - 1)),
                reads=[pkey, "Va"], writes=[okey])
            if kt != nkt - 1:
                return
            S.op("act", lambda e: e.activation(out=osb[:], in_=C.ps[ob][0:65, :], func=AF.Copy), reads=[], writes=[okey, "osb"])
            key = mm_group(C, (lambda: C.ps[6][0:64, :]), [(sel[:], osb[:])], 6, reads=["sel", "osb"])
            S.op("dve", lambda e: e.reciprocal(out=rec[:], in_=C.ps[6][0:64, :]), reads=[], writes=[key, "rec"])
            S.op("dve", lambda e: e.tensor_tensor(out=otmp[:], in0=osb[0:64, :], in1=rec[:], op=ALU.mult), reads=["osb", "rec"], writes=["otmp"])
            if not C.fused:
                S.op("dve", lambda e: e.tensor_tensor(out=ysb[:], in0=otmp[:], in1=GT[:, q0:q0 + TT], op=ALU.mult),
                     reads=["otmp", "GT"], writes=["ysb"])
                S.dma("sp", lambda e: e.dma_start(out=d["yT_h"][rowbase:rowbase + 64, q0:q0 + TT], in_=ysb[:]),
                      reads=["ysb"], writes=["d_yT_h"])
            else:
                qq, qc = q0 // 2048, q0 % 2048
                for j in range(4):
                    S.op("dve", lambda e, j=j: e.scalar_tensor_tensor(out=ym[j][:], in0=otmp[:], scalar=hm[0:64, j:j + 1],
                                                                      in1=GT[:, q0:q0 + TT], op0=ALU.mult, op1=ALU.mult),
                         reads=["otmp", "GT", "hm"], writes=["ym%d" % j])
                    S.dma("sp", lambda e, j=j: e.dma_start(
                        out=d["ypad%d" % qq][256 * j + rowbase:256 * j + rowbase + 64, qc:qc + TT], in_=ym[j][:]),
                        reads=["ym%d" % j], writes=["d_ypad%d" % qq])

        if moba:
            gate_group(0)
            gate_group(1)
        n_it = len(items)
        for idx in range(n_it + DEPTHQ):
            if idx < n_it:
                front(idx)
            if idx - DEPTHQ >= 0:
                back(idx - DEPTHQ)
        S.drain_dmas("sp")
        S.replay()


GC = 128
NCH = SEQ // GC


def gdn_consts():
    i = np.arange(128)
    umask = (i[:, None] <= i[None, :]).astype(np.float32)
    m2 = (i[:, None] > i[None, :]).astype(np.float32)
    neg = np.where(i[:, None] < i[None, :], -30000.0, 0.0).astype(np.float32)
    sl = (i[:, None] > i[None, :]).astype(np.float32)
    return umask, m2, neg, sl


def level_masks():
    i = np.arange(128)
    out = np.zeros((128, 14, 128), np.float32)
    for l in range(7):
        b = 1 << l
        bi = i // b
        m = ((bi[:, None] % 2 == 1) & (bi[None, :] == bi[:, None] - 1)).astype(np.float32)
        out[:, 2 * l, :] = m
        out[:, 2 * l + 1, :] = m.T
    return out.reshape(128, 14 * 128)


def stage_gdn(C, nchunks=NCH, G=4, stop_after=None):
    nc, S, d = C.nc, C.S, C.d
    NSET = 2 * G
    with ExitStack() as st:
        def sb(name, shape, dt):
            C.uid += 1
            return st.enter_context(nc.sbuf_tensor("sb%d_%s" % (C.uid, name), list(shape), dt))
        identf = sb("identf", [128, 128], F32)
        umask = sb("umask", [128, 128], F32)
        m2 = sb("m2", [128, 128], F32)
        neg = sb("neg", [128, 128], F32)
        slm = sb("slm", [128, 128], F32)
        onesf = sb("onesf", [128, 128], F32)
        identb = sb("identb", [128, 128], BF16)
        lvf = sb("lvf", [128, 14 * 128], F32)
        lvm = sb("lvm", [128, 14 * 128], BF16)
        S.dma("sp", lambda e: e.dma_start(out=lvf[:], in_=d["lvlm"][:, :]), reads=["d_lvlm"], writes=["lvf"])
        S.op("dve", lambda e: e.tensor_copy(out=lvm[:], in_=lvf[:]), reads=["lvf"], writes=["lvm"])
        for nm, t in (("ident", identf), ("umask", umask), ("m2", m2), ("neg", neg), ("slm", slm)):
            S.dma("sp", lambda e, nm=nm, t=t: e.dma_start(out=t[:], in_=d[nm][:, :]), reads=["d_" + nm], writes=[nm])
        S.op("dve", lambda e: e.memset(onesf[:], 1.0), writes=["onesf"])
        S.op("dve", lambda e: e.tensor_copy(out=identb[:], in_=identf[:]), reads=["ident"], writes=["identb"])
        eps_t = sb("eps_t", [128, 1], F32)
        S.op("dve", lambda e: e.memset(eps_t[:], EPS), writes=["eps"])
        cw = sb("cw", [128, 12], F32)
        S.dma("sp", lambda e: e.dma_start(out=cw[:], in_=d["cw"][:, :]), reads=["d_cw"], writes=["cw"])
        gsc = sb("gsc", [128, 4], F32)
        S.dma("sp", lambda e: e.dma_start(out=gsc[:, 0:2], in_=d["gsc"][:, :].rearrange("a b -> (a b)").partition_broadcast(128)),
              reads=["d_gsc"], writes=["gsc"])
        gnw = sb("gnw", [128, 1], F32)
        S.dma("sp", lambda e: e.dma_start(out=gnw[:], in_=d["gnw"][:, :]), reads=["d_gnw"], writes=["gnw"])
        S.op("act", lambda e: e.activation(out=gsc[:, 2:3], in_=gsc[:, 0:1], func=AF.Exp), reads=["gsc"], writes=["gsc2"])
        S.op("dve", lambda e: e.tensor_scalar(out=gsc[:, 3:4], in0=gsc[:, 2:3], scalar1=-1.0, scalar2=None, op0=ALU.mult),
             reads=["gsc2"], writes=["gsc3"])
        ba = [sb("ba%d" % i, [2, TT], F32) for i in range(2)]
        batok = sb("batok", [128, NCH, 2], F32)
        for c in range(NCH):
            bi = (c // 4) % 2
            if c % 4 == 0:
                S.dma("sp", lambda e, c=c, bi=bi: e.dma_start(out=ba[bi][:], in_=d["baT"][:, c * 128:c * 128 + TT]),
                      reads=["d_baT"], writes=["ba%d" % bi])
            S.op("pe", lambda e, c=c, bi=bi: e.matmul(C.ps[0][:, 2 * c:2 * c + 2], ba[bi][0:2, (c % 4) * 128:(c % 4 + 1) * 128], identf[0:2, 0:2],
                                                      start=True, stop=True),
                 reads=["ba%d" % bi, "ident"], writes=["ps0"], sig=True)
        S.op("dve", lambda e: e.tensor_copy(out=batok[:].rearrange("p a b -> p (a b)"), in_=C.ps[0][:, 0:2 * NCH]), reads=[], writes=["ps0", "batok"])
        beta = sb("beta", [128, NCH], F32)
        nbeta = sb("nbeta", [128, NCH], F32)
        gg = sb("gg", [128, NCH], F32)
        tmpa = sb("tmpa", [128, NCH], F32)
        gc = sb("gc", [128, NCH], F32)
        gce = sb("gce", [128, NCH], F32)
        egc = sb("egc", [128, NCH], F32)
        bke = sb("bke", [128, NCH], F32)
        eend = sb("eend", [128, NCH], F32)
        gend = sb("gend", [128, NCH], F32)
        S.op("act", lambda e: e.activation(out=beta[:], in_=batok[:, :, 0], func=AF.Sigmoid), reads=["batok"], writes=["beta"])
        S.op("dve", lambda e: e.tensor_scalar(out=nbeta[:], in0=beta[:], scalar1=-1.0, scalar2=None, op0=ALU.mult), reads=["beta"], writes=["nbeta"])
        S.op("act", lambda e: e.activation(out=tmpa[:], in_=batok[:, :, 1], func=AF.Exp, bias=gsc[:, 1:2], scale=1.0), reads=["batok", "gsc"], writes=["tmpa"])
        one_t = sb("one_t", [128, 1], F32)
        S.op("dve", lambda e: e.memset(one_t[:], 1.0), writes=["one_t"])
        S.op("act", lambda e: e.activation(out=tmpa[:], in_=tmpa[:], func=AF.Ln, bias=one_t[:, 0:1], scale=1.0), reads=["tmpa", "one_t"], writes=["tmpa"])
        S.op("dve", lambda e: e.tensor_scalar(out=gg[:], in0=tmpa[:], scalar1=gsc[:, 3:4], scalar2=None, op0=ALU.mult), reads=["tmpa", "gsc3"], writes=["gg"])
        key = mm_group(C, (lambda: C.ps[1][:, 0:NCH]), [(umask[:], gg[:])], 1, reads=["umask", "gg"])
        S.op("dve", lambda e: e.tensor_copy(out=gc[:], in_=C.ps[1][:, 0:NCH]), reads=[], writes=[key, "gc"])
        key = mm_group(C, (lambda: C.ps[2][:, 0:NCH]), [(onesf[:], gg[:])], 2, reads=["onesf", "gg"])
        S.op("dve", lambda e: e.tensor_copy(out=gce[:], in_=C.ps[2][:, 0:NCH]), reads=[], writes=[key, "gce"])
        S.op("act", lambda e: e.activation(out=egc[:], in_=gc[:], func=AF.Exp), reads=["gc"], writes=["egc"])
        S.op("act", lambda e: e.activation(out=gend[:], in_=gce[:], func=AF.Exp), reads=["gce"], writes=["gend"])
        S.op("dve", lambda e: e.tensor_tensor(out=bke[:], in0=beta[:], in1=egc[:], op=ALU.mult), reads=["beta", "egc"], writes=["bke"])
        S.op("dve", lambda e: e.tensor_tensor(out=eend[:], in0=gce[:], in1=gc[:], op=ALU.subtract), reads=["gce", "gc"], writes=["eend"])
        S.op("act", lambda e: e.activation(out=eend[:], in_=eend[:], func=AF.Exp), reads=["eend"], writes=["eend"])
        PERTOK = ["beta", "nbeta", "gg", "egc", "bke", "eend", "gend"]
        if stop_after == 1:
            S.drain_dmas("sp"); S.replay(); return

        QnT = sb("QnT", [128, SEQ], BF16)
        KnT = sb("KnT", [128, SEQ], BF16)
        Kb = sb("Kb", [128, NCH, 128], BF16)
        Kend = sb("Kend", [128, NCH, 128], BF16)
        Vb = sb("Vb", [128, NCH, 128], BF16)
        xin = [[sb("xin%d_%d" % (j, i), [128, 3 + TT], F32) for i in range(2)] for j in range(3)]
        cacc = [sb("cacc%d" % j, [128, TT], F32) for j in range(3)]
        sact = [sb("sact%d" % j, [128, TT], F32) for j in range(3)]
        sq2 = [sb("sq2%d" % j, [128, TT], F32) for j in range(2)]
        lnt = sb("lnt", [128, TT], F32)
        rr = [sb("rr%d" % j, [128, TT], F32) for j in range(2)]
        knf = sb("knf", [128, TT], F32)
        names3 = ("gqT", "gkT", "gvT")
        ntile_a = (nchunks * GC + TT - 1) // TT

        def phase_a_steps(ti):
            c0 = ti * TT
            b = ti % 2
            steps = []

            def pj(j):
                xt = xin[j][b]
                xk = "xin%d_%d" % (j, b)
                if ti == 0:
                    S.op("pool", lambda e, xt=xt: e.memset(xt[:, 0:3], 0.0), writes=[xk])
                    S.dma("sp", lambda e, xt=xt, j=j: e.dma_start(out=xt[:, 3:3 + TT], in_=d[names3[j]][:, 0:TT]),
                          reads=["d_" + names3[j]], writes=[xk])
                else:
                    S.dma("sp", lambda e, xt=xt, j=j: e.dma_start(out=xt[:, :], in_=d[names3[j]][:, c0 - 3:c0 + TT]),
                          reads=["d_" + names3[j]], writes=[xk])
                ck = "cacc%d" % j
                S.op("dve", lambda e, xt=xt, j=j: e.tensor_scalar(out=cacc[j][:], in0=xt[:, 0:TT], scalar1=cw[:, 4 * j:4 * j + 1],
                                                                 scalar2=None, op0=ALU.mult), reads=[xk, "cw"], writes=[ck])
                for tap in range(1, 4):
                    S.op("dve", lambda e, xt=xt, j=j, tap=tap: e.scalar_tensor_tensor(
                        out=cacc[j][:], in0=xt[:, tap:tap + TT], scalar=cw[:, 4 * j + tap:4 * j + tap + 1], in1=cacc[j][:],
                        op0=ALU.mult, op1=ALU.add), reads=[xk, "cw", ck], writes=[ck])
                S.op("act", lambda e, j=j: e.activation(out=sact[j][:], in_=cacc[j][:], func=AF.Silu), reads=[ck], writes=["sact%d" % j])
            for j in range(3):
                steps.append(lambda j=j: pj(j))

            def pn(j):
                S.op("act", lambda e, j=j: e.activation(out=sq2[j][:], in_=sact[j][:], func=AF.Square), reads=["sact%d" % j], writes=["sq2%d" % j])
                bk = C.bank()
                key = mm_group(C, (lambda bk=bk: C.ps[bk][:, :]), [(onesf[:], sq2[j][:])], bk, reads=["onesf", "sq2%d" % j])
                S.op("act", lambda e, bk=bk: e.activation(out=lnt[:], in_=C.ps[bk][:, :], func=AF.Ln, bias=eps_t[:, 0:1], scale=1.0),
                     reads=["eps"], writes=[key, "lnt"])
                S.op("act", lambda e, j=j: e.activation(out=rr[j][:], in_=lnt[:], func=AF.Exp, scale=-0.5), reads=["lnt"], writes=["rr%d" % j])
            for j in range(2):
                steps.append(lambda j=j: pn(j))

            def pq():
                S.op("dve", lambda e: e.scalar_tensor_tensor(out=QnT[:, c0:c0 + TT], in0=sact[0][:], scalar=float(128 ** -0.5), in1=rr[0][:],
                                                             op0=ALU.mult, op1=ALU.mult), reads=["sact0", "rr0"], writes=["QnT%d" % ti])
                S.op("dve", lambda e: e.tensor_tensor(out=knf[:], in0=sact[1][:], in1=rr[1][:], op=ALU.mult), reads=["sact1", "rr1"], writes=["knf"])
                S.op("act", lambda e: e.activation(out=KnT[:, c0:c0 + TT], in_=knf[:], func=AF.Copy), reads=["knf"], writes=["KnT%d" % ti])
            steps.append(pq)

            def pt(a):
                c = ti * 4 + a
                bk = C.bank()
                key = "ps%d" % bk
                S.op("pe", lambda e, bk=bk, a=a: e.transpose(C.ps[bk][:, 0:128], knf[:, a * 128:(a + 1) * 128], identf[:]),
                     reads=["knf", "ident"], writes=[key])
                S.op("pe", lambda e, bk=bk, a=a: e.transpose(C.ps[bk][:, 128:256], sact[2][:, a * 128:(a + 1) * 128], identf[:]),
                     reads=["sact2", "ident"], writes=[key])
                S.op("act", lambda e, bk=bk, c=c: e.activation(out=Kb[:, c, :], in_=C.ps[bk][:, 0:128], func=AF.Copy, scale=bke[:, c:c + 1]),
                     reads=["bke"], writes=[key, "Kb%d" % ti])
                S.op("dve", lambda e, bk=bk, c=c: e.tensor_scalar(out=Kend[:, c, :], in0=C.ps[bk][:, 0:128], scalar1=eend[:, c:c + 1], scalar2=None, op0=ALU.mult),
                     reads=["eend"], writes=[key, "Kend%d" % ti])
                S.op("act", lambda e, bk=bk, c=c: e.activation(out=Vb[:, c, :], in_=C.ps[bk][:, 128:256], func=AF.Copy, scale=beta[:, c:c + 1]),
                     reads=["beta"], writes=[key, "Vb%d" % ti])
            for a in range(4):
                steps.append(lambda a=a: pt(a))
            return steps

        for st_ in phase_a_steps(0):
            st_()
        if stop_after == 2:
            S.drain_dmas("sp"); S.replay(); return

        def bufset(name, dt, n=NSET):
            return [sb("%s%d" % (name, i), [128, 128], dt) for i in range(n)]
        G1 = bufset("G1", F32)
        Dm = bufset("Dm", F32)
        Xf = bufset("Xf", F32)
        Xb_ = bufset("Xb", BF16)
        Yb = bufset("Yb", BF16)
        Mb = bufset("Mb", BF16)
        Xo = bufset("Xo", BF16)
        Yo = bufset("Yo", BF16)
        Hb = bufset("Hb", BF16)
        Gb = bufset("Gb", BF16)
        Am = bufset("Am", F32)
        AT = bufset("AT", BF16)
        TTb = bufset("TTb", BF16)
        WT = bufset("WT", BF16)
        U0 = bufset("U0", F32)
        Ub = bufset("Ub", BF16, 2)
        Sf = sb("Sf", [128, 128], F32)
        Sbb = [sb("Sbb%d" % i, [128, 128], BF16) for i in range(2)]
        otmp = sb("otmp", [128, 128], F32)
        osb = sb("osb", [128, 128], F32)
        osq = sb("osq", [128, 128], F32)
        onr = sb("onr", [128, 128], F32)
        ssq = sb("ssq", [128, 1], F32)
        lno = sb("lno", [128, 1], F32)
        rso = sb("rso", [128, 1], F32)
        gz = [sb("gz%d" % i, [128, TT], BF16) for i in range(2)]
        yst = [sb("yst%d" % i, [128, TT], BF16) for i in range(2)]
        if C.fused:
            ystm = [[sb("ystm%d_%d" % (i, j), [128, TT], BF16) for j in range(4)] for i in range(2)]
            hm = sb("hm", [128, 4], F32)
            gnwm = sb("gnwm", [128, 4], F32)
            S.dma("sp", lambda e: e.dma_start(out=hm[:], in_=d["hm"][:, :]), reads=["d_hm"], writes=["hm"])
            S.op("dve", lambda e: e.tensor_scalar(out=gnwm[:], in0=hm[:], scalar1=gnw[:, 0:1], scalar2=None, op0=ALU.mult),
                 reads=["hm", "gnw"], writes=["gnwm"])
        S.op("dve", lambda e: e.memset(Sf[:], 0.0), writes=["Sf"])
        S.op("dve", lambda e: e.memset(Sbb[0][:], 0.0), writes=["Sbb0"])

        def mm1(out_bank, lhsT, rhs, reads, cols=128):
            return mm_group(C, (lambda: C.ps[out_bank][:, 0:cols]), [(lhsT, rhs)], out_bank, reads=reads)

        def pre_steps(c):
            s = c % NSET
            ck = slice(c * GC, (c + 1) * GC)
            k = lambda nm: "%s%d" % (nm, s)
            steps = []
            st8 = {}

            def s1a():
                S.op("act", lambda e: e.activation(out=G1[s][:], in_=umask[:], func=AF.Copy, scale=gg[:, c:c + 1]),
                     reads=["umask", "gg"], writes=[k("G1")])
            steps.append(s1a)

            def s1b():
                bk = C.bank()
                key = mm_group(C, (lambda: C.ps[bk][:, 0:128]), [(G1[s][:], m2[:]), (identf[:], neg[:])], bk, reads=[k("G1"), "m2", "ident", "neg"])
                S.op("act", lambda e: e.activation(out=Dm[s][:], in_=C.ps[bk][:, 0:128], func=AF.Exp), reads=[], writes=[key, k("Dm")])
            steps.append(s1b)

            def s2():
                bk = C.bank()
                key = mm1(bk, KnT[:, ck], KnT[:, ck], ["KnT%d" % (c // 4)])
                S.op("dve", lambda e: e.scalar_tensor_tensor(out=Xf[s][:], in0=C.ps[bk][:, 0:128], scalar=nbeta[:, c:c + 1], in1=Dm[s][:],
                                                             op0=ALU.mult, op1=ALU.mult), reads=["nbeta", k("Dm")], writes=[key, k("Xf")])
                S.op("dve", lambda e: e.tensor_tensor(out=Xf[s][:], in0=Xf[s][:], in1=slm[:], op=ALU.mult), reads=[k("Xf"), "slm"], writes=[k("Xf")])
                S.op("act", lambda e: e.activation(out=Xb_[s][:], in_=Xf[s][:], func=AF.Copy), reads=[k("Xf")], writes=[k("Xb")])
            steps.append(s2)

            def s3a():
                bk = C.bank()
                key = mm1(bk, QnT[:, ck], KnT[:, ck], ["QnT%d" % (c // 4), "KnT%d" % (c // 4)])
                S.op("dve", lambda e: e.tensor_tensor(out=Am[s][:], in0=C.ps[bk][:, 0:128], in1=Dm[s][:], op=ALU.mult),
                     reads=[k("Dm")], writes=[key, k("Am")])
            steps.append(s3a)

            def s4():
                bk = C.bank()
                key = "ps%d" % bk
                S.op("pe", lambda e: e.transpose(C.ps[bk][:, 0:128], Xf[s][:], identf[:]), reads=[k("Xf"), "ident"], writes=[key])
                S.op("act", lambda e: e.activation(out=Yb[s][:], in_=C.ps[bk][:, 0:128], func=AF.Copy), reads=[], writes=[key, k("Yb")])
                S.op("pool", lambda e: e.tensor_tensor(out=Xo[s][:], in0=Xb_[s][:], in1=lvm[:, 0:128], op=ALU.mult),
                     reads=[k("Xb"), "lvm"], writes=[k("Xo")])
                S.op("dve", lambda e: e.tensor_tensor(out=Mb[s][:], in0=Xo[s][:], in1=identb[:], op=ALU.add),
                     reads=[k("Xo"), "identb"], writes=[k("Mb")])
            steps.append(s4)

            def s3b():
                bk2 = C.bank()
                key2 = "ps%d" % bk2
                S.op("pe", lambda e: e.transpose(C.ps[bk2][:, 0:128], Am[s][:], identf[:]), reads=[k("Am"), "ident"], writes=[key2])
                S.op("act", lambda e: e.activation(out=AT[s][:], in_=C.ps[bk2][:, 0:128], func=AF.Copy), reads=[], writes=[key2, k("AT")])
                S.op("pool", lambda e: e.tensor_tensor(out=Yo[s][:], in0=Yb[s][:], in1=lvm[:, 128:256], op=ALU.mult),
                     reads=[k("Yb"), "lvm"], writes=[k("Yo")])
                S.op("dve", lambda e: e.tensor_tensor(out=TTb[s][:], in0=Yo[s][:], in1=identb[:], op=ALU.add),
                     reads=[k("Yo"), "identb"], writes=[k("TTb")])
            steps.append(s3b)
            for l in range(1, 7):
                def la(l=l):
                    if l <= 5:
                        S.op("pool", lambda e: e.tensor_tensor(out=Yo[s][:], in0=Yb[s][:], in1=lvm[:, (2 * l + 1) * 128:(2 * l + 2) * 128], op=ALU.mult),
                             reads=[k("Yb"), "lvm"], writes=[k("Yo")])
                    S.op("pool", lambda e: e.tensor_tensor(out=Xo[s][:], in0=Xb_[s][:], in1=lvm[:, (2 * l) * 128:(2 * l + 1) * 128], op=ALU.mult),
                         reads=[k("Xb"), "lvm"], writes=[k("Xo")])
                steps.append(la)

                def lb(l=l):
                    if l <= 5:
                        bh = C.bank()
                        keyh = mm1(bh, Yo[s][:], Mb[s][:], [k("Yo"), k("Mb")])
                    bg = C.bank()
                    keyg = mm1(bg, Xo[s][:], TTb[s][:], [k("Xo"), k("TTb")])
                    if l <= 5:
                        S.op("act", lambda e: e.activation(out=Hb[s][:], in_=C.ps[bh][:, 0:128], func=AF.Copy), reads=[], writes=[keyh, k("Hb")])
                    if l % 2 == 0:
                        S.op("act", lambda e: e.activation(out=Gb[s][:], in_=C.ps[bg][:, 0:128], func=AF.Copy), reads=[], writes=[keyg, k("Gb")])
                    else:
                        S.op("dve", lambda e: e.tensor_copy(out=Gb[s][:], in_=C.ps[bg][:, 0:128]), reads=[], writes=[keyg, k("Gb")])
                steps.append(lb)

                def lc(l=l):
                    if l <= 5:
                        bm = C.bank()
                        keym = mm1(bm, TTb[s][:], Hb[s][:], [k("TTb"), k("Hb")])
                    bw = C.bank()
                    keyw = mm1(bw, Mb[s][:], Gb[s][:], [k("Mb"), k("Gb")])
                    if l <= 5:
                        S.op("dve", lambda e: e.tensor_tensor(out=Mb[s][:], in0=Mb[s][:], in1=C.ps[bm][:, 0:128], op=ALU.add),
                             reads=[k("Mb")], writes=[keym, k("Mb")])
                    S.op("dve", lambda e: e.tensor_tensor(out=TTb[s][:], in0=TTb[s][:], in1=C.ps[bw][:, 0:128], op=ALU.add),
                         reads=[k("TTb")], writes=[keyw, k("TTb")])
                steps.append(lc)

            def s5():
                bk = C.bank()
                key = mm1(bk, Kb[:, c, :], TTb[s][:], ["Kb%d" % (c // 4), k("TTb")])
                bk2 = C.bank()
                key2 = mm1(bk2, TTb[s][:], Vb[:, c, :], ["Vb%d" % (c // 4), k("TTb")])
                S.op("act", lambda e: e.activation(out=WT[s][:], in_=C.ps[bk][:, 0:128], func=AF.Copy), reads=[], writes=[key, k("WT")])
                S.op("dve", lambda e: e.tensor_copy(out=U0[s][:], in_=C.ps[bk2][:, 0:128]), reads=[], writes=[key2, k("U0")])
            steps.append(s5)
            return steps

        def scan_steps(c):
            s = c % NSET
            ck = slice(c * GC, (c + 1) * GC)
            k = lambda nm: "%s%d" % (nm, s)
            u = c % 2
            sbi, sbo = c % 2, (c + 1) % 2
            stt = {}

            def sa():
                b1 = C.bank()
                key1 = mm1(b1, WT[s][:], Sbb[sbi][:], [k("WT"), "Sbb%d" % sbi])
                b2 = C.bank()
                key2 = mm1(b2, QnT[:, ck], Sbb[sbi][:], ["QnT%d" % (c // 4), "Sbb%d" % sbi])
                S.op("dve", lambda e: e.tensor_tensor(out=Ub[u][:], in0=U0[s][:], in1=C.ps[b1][:, 0:128], op=ALU.subtract),
                     reads=[k("U0")], writes=[key1, "Ub%d" % u])
                S.op("act", lambda e: e.activation(out=otmp[:], in_=C.ps[b2][:, 0:128], func=AF.Copy, scale=egc[:, c:c + 1]),
                     reads=["egc"], writes=[key2, "otmp"])

            def sb_():
                b4 = C.bank()
                key4 = mm1(b4, Kend[:, c, :], Ub[u][:], ["Kend%d" % (c // 4), "Ub%d" % u])
                b3 = C.bank()
                key3 = mm1(b3, AT[s][:], Ub[u][:], [k("AT"), "Ub%d" % u])
                S.op("dve", lambda e: e.scalar_tensor_tensor(out=Sf[:], in0=Sf[:], scalar=gend[:, c:c + 1], in1=C.ps[b4][:, 0:128],
                                                             op0=ALU.mult, op1=ALU.add), reads=["gend", "Sf"], writes=[key4, "Sf"])
                S.op("act", lambda e: e.activation(out=Sbb[sbo][:], in_=Sf[:], func=AF.Copy), reads=["Sf"], writes=["Sbb%d" % sbo])
                S.op("dve", lambda e: e.tensor_tensor(out=osb[:], in0=otmp[:], in1=C.ps[b3][:, 0:128], op=ALU.add),
                     reads=["otmp"], writes=[key3, "osb"])

            def sc():
                S.op("act", lambda e: e.activation(out=osq[:], in_=osb[:], func=AF.Square, accum_out=ssq[:, 0:1]), reads=["osb"], writes=["osq", "ssq"])
                S.op("act", lambda e: e.activation(out=lno[:], in_=ssq[:], func=AF.Ln, bias=eps_t[:, 0:1], scale=1.0 / 128), reads=["ssq", "eps"], writes=["lno"])
                S.op("act", lambda e: e.activation(out=rso[:], in_=lno[:], func=AF.Exp, scale=-0.5), reads=["lno"], writes=["rso"])
                S.op("act", lambda e: e.activation(out=onr[:], in_=osb[:], func=AF.Copy, scale=rso[:, 0:1]),
                     reads=["osb", "rso"], writes=["onr"])

            def sd():
                b5 = C.bank()
                key5 = "ps%d" % b5
                S.op("pe", lambda e: e.transpose(C.ps[b5][:, 0:128], onr[:], identf[:]), reads=["onr", "ident"], writes=[key5])
                yb = (c // 4) % 2
                a = c % 4
                if a == 0:
                    tz = (c // 4) * TT
                    S.dma("sp", lambda e: e.dma_start(out=gz[yb][:], in_=d["gzT"][:, tz:tz + TT]), reads=["d_gzT"], writes=["gz%d" % yb])
                if not C.fused:
                    S.op("dve", lambda e: e.scalar_tensor_tensor(out=yst[yb][:, a * 128:(a + 1) * 128], in0=C.ps[b5][:, 0:128], scalar=gnw[:, 0:1],
                                                                 in1=gz[yb][:, a * 128:(a + 1) * 128], op0=ALU.mult, op1=ALU.mult),
                         reads=["gnw", "gz%d" % yb], writes=[key5, "yst%d" % yb])
                    if a == 3:
                        t0 = (c // 4) * TT
                        S.dma("sp", lambda e: e.dma_start(out=d["yT_h"][128:256, t0:t0 + TT], in_=yst[yb][:]), reads=["yst%d" % yb], writes=["d_yT_h"])
                else:
                    for j in range(4):
                        S.op("dve", lambda e, j=j: e.scalar_tensor_tensor(out=ystm[yb][j][:, a * 128:(a + 1) * 128], in0=C.ps[b5][:, 0:128],
                                                                          scalar=gnwm[:, j:j + 1], in1=gz[yb][:, a * 128:(a + 1) * 128],
                                                                          op0=ALU.mult, op1=ALU.mult),
                             reads=["gnwm", "gz%d" % yb], writes=[key5, "ystm%d_%d" % (yb, j)])
                    if a == 3:
                        t0 = (c // 4) * TT
                        qq, qc = t0 // 2048, t0 % 2048
                        for j in range(4):
                            S.dma("sp", lambda e, j=j: e.dma_start(out=d["ypad%d" % qq][256 * j + 128:256 * j + 256, qc:qc + TT], in_=ystm[yb][j][:]),
                                  reads=["ystm%d_%d" % (yb, j)], writes=["d_ypadg%d_%d_%d" % (qq, j, qc // TT)])
                        if qc + TT == 2048:
                            S.coll(lambda e: e.collective_compute(
                                "AllReduce", ALU.add, replica_groups=[[0, 1, 2, 3], [4, 5, 6, 7]],
                                ins=[d["ypad%d" % qq].opt()], outs=[d["yg%d" % qq].opt()]),
                                reads=["d_ypad%d" % qq] + ["d_ypadg%d_%d_%d" % (qq, j, t) for j in range(4) for t in range(4)],
                                writes=["d_yg%d" % qq])
            return [sa, sb_, sc, sd]

        groups = [list(range(g, min(g + G, nchunks))) for g in range(0, nchunks, G)]
        prev = []
        for gi, grp in enumerate(groups + [[]]):
            lists = [pre_steps(c) for c in grp]
            if grp and gi + 1 < ntile_a and G == 4:
                lists.append(phase_a_steps(gi + 1))
            nst = max([len(l) for l in lists] + [0])
            pending = []
            for c in prev:
                pending += scan_steps(c)
            for si in range(nst):
                for l in lists:
                    if si < len(l):
                        l[si]()
                if pending:
                    pending.pop(0)()
            while pending:
                pending.pop(0)()
            prev = grp
        S.drain_dmas("sp")
        S.replay()


CONST_INS = [("ctab", (32, SEQ), F32), ("stab", (32, SEQ), F32), ("ident", (128, 128), F32), ("sel", (65, 64), F32),
             ("tri", (128, 128), F32), ("OH", (32, 3072), F32), ("Eb", (32, SEQ), BF16), ("pen", (32, 32), F32),
             ("umask", (128, 128), F32), ("m2", (128, 128), F32), ("neg", (128, 128), F32), ("slm", (128, 128), F32),
             ("lvlm", (128, 14 * 128), F32)]
LAYER_INS = PROJ_INS + [("t5h", (32, 1)), ("cw", (128, 12)), ("gsc", (1, 2)), ("gnw", (128, 1))]


def host_consts():
    import ml_dtypes
    ctab, stab = rope_consts()
    pen, E = moba_consts()
    umask, m2, neg, slm = gdn_consts()
    sel = np.zeros((65, 64), np.float32)
    sel[64, :] = 1.0
    return {"ctab": ctab, "stab": stab, "ident": np.eye(128, dtype=np.float32), "sel": sel,
            "tri": np.triu(np.ones((128, 128), np.float32)), "OH": t5_consts(), "Eb": E.astype(ml_dtypes.bfloat16),
            "pen": pen, "umask": umask, "m2": m2, "neg": neg, "slm": slm, "lvlm": level_masks()}


def prep_layer_all(I, l, h):
    out = prep_layer(I, l, h)
    f = np.float32
    out["t5h"] = np.ascontiguousarray(I["t5_table"][:, h:h + 1]).astype(f)
    cwf = I["gdn_conv_w"][l]
    cw = np.zeros((128, 12), f)
    for j in range(3):
        for tap in range(4):
            cw[:, 4 * j + tap] = cwf[tap, j * 512 + 128 * h:j * 512 + 128 * h + 128]
    out["cw"] = cw
    out["gsc"] = np.array([[I["gdn_A_log"][l, h], I["gdn_dt_bias"][l, h]]], f)
    out["gnw"] = np.ascontiguousarray(I["gdn_norm_w"][l].reshape(128, 1)).astype(f)
    return out


def wo_perm():
    rows = []
    for h in range(4):
        rows += list(range(64 * h, 64 * h + 64))
        rows += list(range(768 + 64 * h, 768 + 64 * h + 64))
        rows += list(range(256 + 128 * h, 256 + 128 * h + 128))
    return rows


def build_layer_program(has_prev, do_layer, final):
    nc = bass.Bass("TRN2", target_bir_lowering=False)
    with ExitStack() as st:
        C = Ctx(nc, st)
        C.din("xT", (DM, SEQ), F32)
        if has_prev:
            C.din("yT", (DM, SEQ), BF16)
            C.din("wo", (DM, DM), F32)
            if not final:
                C.dout("xTo", (DM, SEQ), F32)
        if final:
            C.din("fnw", (128, 8), F32)
            C.dout("outT", (DM, SEQ), F32)
        if do_layer:
            for n, s_, dt in CONST_INS:
                C.din(n, s_, dt)
            for n, s_ in LAYER_INS:
                C.din(n, s_, F32)
            for n, s_, dt in PROJ_OUTS:
                C.dint(n, s_, dt)
            C.dint("fv", (1, 3072), F32)
            C.dout("yT_h", (256, SEQ), BF16)
        stage_proj(C, has_prev=has_prev, do_proj=do_layer, final=final)
        if do_layer:
            stage_attn(C, False)
            stage_attn(C, True)
            stage_gdn(C)
    return nc


def build_fused_program(depth=DEPTH, do_coll=True):
    nc = bass.Bass("TRN2", target_bir_lowering=False)
    with ExitStack() as st:
        C = Ctx(nc, st)
        C.fused = True
        S = C.S
        C.din("xT", (DM, SEQ), F32)
        C.din("hm", (128, 4), F32)
        C.din("fnw", (128, 8), F32)
        for n, s_, dt in CONST_INS:
            C.din(n, s_, dt)
        for l in range(depth):
            for n, s_ in LAYER_INS:
                C.din("%s_%d" % (n, l), s_, F32)
            C.din("wo_%d" % l, (DM, DM), F32)
        C.dout("outT", (DM, SEQ), F32)
        C.dint("xTi", (DM, SEQ), F32)
        for n, s_, dt in PROJ_OUTS:
            C.dint(n, s_, dt)
        C.dint("fv", (1, 3072), F32)
        C.dint("ESd", (128, 2560), BF16)
        for q in range(4):
            C.dint("ypad%d" % q, (DM, 2048), BF16)
            C.dint("yg%d" % q, (DM, 2048), BF16)
        x_ext = C.d["xT"]
        for l in range(depth):
            for n, s_ in LAYER_INS:
                C.d[n] = C.d["%s_%d" % (n, l)]
            if l > 0:
                C.d["wo"] = C.d["wo_%d" % (l - 1)]
            C.d["xT"] = x_ext if l <= 1 else C.d["xTi"]
            C.d["xTo"] = C.d["xTi"]
            import os
            ST = os.environ.get("STAGES", "pamg")
            if "p" in ST:
                stage_proj(C, has_prev=(l > 0), do_proj=True, final=False)
            if "a" in ST:
                stage_attn(C, False)
            if "m" in ST:
                stage_attn(C, True, es_load=(l > 0))
            if "g" in ST:
                stage_gdn(C)
        C.d["wo"] = C.d["wo_%d" % (depth - 1)]
        C.d["xT"] = C.d["xTi"] if depth > 1 else x_ext
        if "f" in os.environ.get("STAGES", "pamgf"):
            stage_proj(C, has_prev=True, do_proj=False, final=True)
    return nc


_FUSED = []


def kernel(x, norm_w, w_in, mla_q_norm, mla_w_uq, mla_kv_norm, mla_w_ukv, gdn_conv_w, gdn_A_log, gdn_dt_bias,
           gdn_norm_w, w_out, t5_table, final_norm_w):
    I = dict(x=np.asarray(x), norm_w=np.asarray(norm_w), w_in=np.asarray(w_in), mla_q_norm=np.asarray(mla_q_norm),
             mla_w_uq=np.asarray(mla_w_uq), mla_kv_norm=np.asarray(mla_kv_norm), mla_w_ukv=np.asarray(mla_w_ukv),
             gdn_conv_w=np.asarray(gdn_conv_w), gdn_A_log=np.asarray(gdn_A_log), gdn_dt_bias=np.asarray(gdn_dt_bias),
             gdn_norm_w=np.asarray(gdn_norm_w), w_out=np.asarray(w_out), t5_table=np.asarray(t5_table),
             final_norm_w=np.asarray(final_norm_w))
    if not _FUSED:
        _FUSED.append(build_fused_program())
    nc = _FUSED[0]
    consts = host_consts()
    perm = wo_perm()
    fnw = np.ascontiguousarray(I["final_norm_w"].reshape(8, 128).T).astype(np.float32)
    wos = [np.ascontiguousarray(I["w_out"][l][perm, :]).astype(np.float32) for l in range(DEPTH)]
    xT = [np.ascontiguousarray(I["x"][b].T).astype(np.float32) for b in range(2)]
    maps = []
    for c in range(8):
        b, h = c // 4, c % 4
        m = dict(consts)
        m["xT"] = xT[b]
        hm = np.zeros((128, 4), np.float32)
        hm[:, h] = 1.0
        m["hm"] = hm
        m["fnw"] = fnw
        for l in range(DEPTH):
            for k, v in prep_layer_all(I, l, h).items():
                m["%s_%d" % (k, l)] = v
            m["wo_%d" % l] = wos[l]
        maps.append(m)
    res = run_bass_kernel_spmd(nc, maps, core_ids=list(range(8)))
    out = np.stack([np.asarray(res.results[4 * b]["outT"]).T for b in range(2)], axis=0)
    return np.ascontiguousarray(out).astype(np.float32)
```
